# Optimizing a Trainium2 kernel written in Bass

```python
import math
import jax, jax.numpy as jnp
from jax import lax
import numpy as np

D_MODEL = 1024
BATCH = 8
SEQ = 4096
DEPTH = 2

SSD_D_INNER = D_MODEL
SSD_HEAD_DIM = 64
SSD_HEADS = SSD_D_INNER // SSD_HEAD_DIM
SSD_GROUPS = 2
SSD_D_STATE = 128
SSD_CONV = 4
SSD_CHUNK = 128
SSD_CONV_DIM = SSD_D_INNER + 2 * SSD_GROUPS * SSD_D_STATE
DA_HEADS = 8
DA_HEAD_DIM = 64
DA_V_DIM = 2 * DA_HEAD_DIM
DA_QK_WIDTH = DA_HEADS * 2 * DA_HEAD_DIM
DA_V_WIDTH = DA_HEADS * DA_V_DIM
Q_BLOCK = 128
ROPE_THETA = 10000.0
EVEN_SPLITS = [SSD_D_INNER,
               SSD_D_INNER + SSD_CONV_DIM,
               SSD_D_INNER + SSD_CONV_DIM + SSD_HEADS,
               SSD_D_INNER + SSD_CONV_DIM + SSD_HEADS + DA_QK_WIDTH,
               SSD_D_INNER + SSD_CONV_DIM + SSD_HEADS + 2 * DA_QK_WIDTH]
EVEN_IN_WIDTH = SSD_D_INNER + SSD_CONV_DIM + SSD_HEADS + 2 * DA_QK_WIDTH + DA_V_WIDTH
EVEN_OUT_WIDTH = SSD_D_INNER + DA_V_WIDTH
SC_WIDTH = D_MODEL
SC_CONV = 3
D_FF = 2816
FFN_CONV = 3
RMS_EPS = 1e-6
N_EVEN = (DEPTH + 1) // 2
N_ODD = DEPTH // 2

kernel_name = "hybrid_ssd_diffattn_shortconv_convffn"


def rms_norm(x, w):
    xf = x.astype(jnp.float32)
    xf = xf * lax.rsqrt(jnp.mean(xf * xf, axis=-1, keepdims=True) + RMS_EPS)
    return (xf * w.astype(jnp.float32)).astype(x.dtype)


def causal_dwconv(x, w, b=None):
    K, C = w.shape
    y = lax.conv_general_dilated(x, w[:, None, :].astype(x.dtype), window_strides=(1,),
                                 padding=[(K - 1, 0)],
                                 dimension_numbers=('NWC', 'WIO', 'NWC'),
                                 feature_group_count=C)
    if b is not None:
        y = y + b.astype(x.dtype)
    return y


def rope_tables(seq, dim, dtype):
    inv = 1.0 / (ROPE_THETA ** (jnp.arange(0, dim, 2, dtype=jnp.float32) / dim))
    ang = jnp.arange(seq, dtype=jnp.float32)[:, None] * inv[None, :]
    ang = jnp.concatenate([ang, ang], axis=-1)
    return jnp.cos(ang).astype(dtype), jnp.sin(ang).astype(dtype)


def apply_rope(x, cos, sin):
    x1, x2 = jnp.split(x, 2, axis=-1)
    rot = jnp.concatenate([-x2, x1], axis=-1)
    return x * cos[:, None, :] + rot * sin[:, None, :]


def ssd_chunked(x, dt, A, Bm, Cm):
    b, L, H, P = x.shape
    G, N = Bm.shape[2], Bm.shape[3]
    R = H // G
    c = L // SSD_CHUNK
    X = (x * dt[..., None]).reshape(b, c, SSD_CHUNK, G, R, P)
    Adt = (dt * A).reshape(b, c, SSD_CHUNK, G, R)
    Bc = Bm.reshape(b, c, SSD_CHUNK, G, N)
    Cc = Cm.reshape(b, c, SSD_CHUNK, G, N)
    A_cs = jnp.cumsum(Adt, axis=2)
    A_t = A_cs.transpose(0, 1, 3, 4, 2)
    seg = A_t[..., :, None] - A_t[..., None, :]
    tril = jnp.tril(jnp.ones((SSD_CHUNK, SSD_CHUNK), dtype=bool))
    Lmat = jnp.exp(jnp.where(tril, seg, -jnp.inf))
    CB = jnp.einsum('bclgn,bcsgn->bcgls', Cc, Bc)
    Y_diag = jnp.einsum('bcgrls,bcsgrp->bclgrp', CB[:, :, :, None] * Lmat, X)
    decay_states = jnp.exp(A_cs[:, :, -1:] - A_cs)
    states = jnp.einsum('bclgn,bclgrp->bcgrpn', Bc, X * decay_states[..., None])
    chunk_decay = jnp.exp(A_cs[:, :, -1])

    def step(carry, inp):
        st, dec = inp
        return carry * dec[..., None, None] + st, carry

    init = jnp.zeros((b, G, R, P, N), dtype=X.dtype)
    _, prev = lax.scan(step, init, (states.transpose(1, 0, 2, 3, 4, 5),
                                    chunk_decay.transpose(1, 0, 2, 3)))
    prev = prev.transpose(1, 0, 2, 3, 4, 5)
    Y_off = jnp.einsum('bclgn,bcgrpn->bclgrp', Cc, prev) * jnp.exp(A_cs)[..., None]
    return (Y_diag + Y_off).reshape(b, L, H, P)


def diff_attention(q, k, v, lam, lambda_init, subln_w):
    b, L = q.shape[0], q.shape[1]
    nblk = L // Q_BLOCK
    scale = DA_HEAD_DIM ** -0.5
    kh = k.transpose(0, 2, 1, 3)
    vh = v.transpose(0, 2, 1, 3)
    qblocks = q.transpose(0, 2, 1, 3).reshape(b, 2 * DA_HEADS, nblk, Q_BLOCK, DA_HEAD_DIM)
    qblocks = qblocks.transpose(2, 0, 1, 3, 4)
    kpos = jnp.arange(L)

    def block(args):
        qb, i = args
        s = jnp.einsum('bhqd,bhkd->bhqk', qb, kh).astype(jnp.float32) * scale
        qpos = i * Q_BLOCK + jnp.arange(Q_BLOCK)
        s = jnp.where(kpos[None, :] <= qpos[:, None], s, -jnp.inf)
        p = jax.nn.softmax(s, axis=-1).reshape(b, DA_HEADS, 2, Q_BLOCK, L)
        w = (p[:, :, 0] - lam * p[:, :, 1]).astype(vh.dtype)
        return jnp.einsum('bhqk,bhkd->bhqd', w, vh)

    o = lax.map(block, (qblocks, jnp.arange(nblk)))
    o = o.transpose(1, 0, 3, 2, 4).reshape(b, L, DA_HEADS, DA_V_DIM)
    o = rms_norm(o, subln_w) * (1.0 - lambda_init)
    return o.reshape(b, L, DA_V_WIDTH)


def ssd_diffattn_mixer(h, w_in, conv_w, conv_b, dt_bias, a_log, d_skip, ssd_norm,
                       q_norm, k_norm, lq1, lk1, lq2, lk2, subln, w_out,
                       lambda_init, cos, sin):
    b, L, _ = h.shape
    proj = h @ w_in
    z, xbc, dt, q, k, v = jnp.split(proj, EVEN_SPLITS, axis=-1)
    xbc = jax.nn.silu(causal_dwconv(xbc, conv_w, conv_b))
    xs, Bm, Cm = jnp.split(xbc, [SSD_D_INNER, SSD_D_INNER + SSD_GROUPS * SSD_D_STATE], axis=-1)
    xs = xs.reshape(b, L, SSD_HEADS, SSD_HEAD_DIM).astype(jnp.float32)
    Bm = Bm.reshape(b, L, SSD_GROUPS, SSD_D_STATE).astype(jnp.float32)
    Cm = Cm.reshape(b, L, SSD_GROUPS, SSD_D_STATE).astype(jnp.float32)
    dtf = jax.nn.softplus(dt.astype(jnp.float32) + dt_bias.astype(jnp.float32))
    A = -jnp.exp(a_log.astype(jnp.float32))
    y = ssd_chunked(xs, dtf, A, Bm, Cm) + d_skip.astype(jnp.float32)[:, None] * xs
    y = y.reshape(b, L, SSD_D_INNER) * jax.nn.silu(z.astype(jnp.float32))
    yg = y.reshape(b, L, SSD_GROUPS, SSD_D_INNER // SSD_GROUPS)
    yg = yg * lax.rsqrt(jnp.mean(yg * yg, axis=-1, keepdims=True) + RMS_EPS)
    y_ssd = (yg.reshape(b, L, SSD_D_INNER) * ssd_norm.astype(jnp.float32)).astype(h.dtype)
    q = apply_rope(rms_norm(q.reshape(b, L, 2 * DA_HEADS, DA_HEAD_DIM), q_norm), cos, sin)
    k = apply_rope(rms_norm(k.reshape(b, L, 2 * DA_HEADS, DA_HEAD_DIM), k_norm), cos, sin)
    v = v.reshape(b, L, DA_HEADS, DA_V_DIM)
    lam = (jnp.exp(jnp.sum(lq1.astype(jnp.float32) * lk1.astype(jnp.float32)))
           - jnp.exp(jnp.sum(lq2.astype(jnp.float32) * lk2.astype(jnp.float32)))
           + lambda_init)
    y_att = diff_attention(q, k, v, lam, lambda_init, subln)
    return jnp.concatenate([y_ssd, y_att], axis=-1) @ w_out


def short_conv_mixer(h, w_in, conv_w, w_out):
    bg, cg, u = jnp.split(h @ w_in, 3, axis=-1)
    return (bg * causal_dwconv(cg * u, conv_w)) @ w_out


def conv_ffn(h, w_up, conv_w, conv_b, w_down):
    up = causal_dwconv(h @ w_up, conv_w, conv_b)
    gate, val = jnp.split(up, 2, axis=-1)
    return (jax.nn.silu(gate) * val) @ w_down


def setup_inputs(seed: int = 0) -> dict:
    key = jax.random.key(seed)
    ks = iter(jax.random.split(key, 40))

    def nrm(shape, scale):
        return jax.random.normal(next(ks), shape, jnp.float32) * scale

    def gain(shape):
        return 1.0 + nrm(shape, 0.02)

    x = nrm((BATCH, SEQ, D_MODEL), 1.0)
    mix_norm = gain((DEPTH, D_MODEL))
    ffn_norm = gain((DEPTH, D_MODEL))
    hy_w_in = nrm((N_EVEN, D_MODEL, EVEN_IN_WIDTH), D_MODEL ** -0.5)
    hy_conv_w = nrm((N_EVEN, SSD_CONV, SSD_CONV_DIM), SSD_CONV ** -0.5)
    hy_conv_b = nrm((N_EVEN, SSD_CONV_DIM), 0.01)
    u = jax.random.uniform(next(ks), (N_EVEN, SSD_HEADS), jnp.float32)
    dt0 = jnp.exp(u * (math.log(0.1) - math.log(0.001)) + math.log(0.001))
    hy_dt_bias = dt0 + jnp.log(-jnp.expm1(-dt0))
    hy_a_log = jnp.log(jax.random.uniform(next(ks), (N_EVEN, SSD_HEADS), jnp.float32, 1.0, 16.0))
    hy_d_skip = 1.0 + nrm((N_EVEN, SSD_HEADS), 0.1)
    hy_ssd_norm = gain((N_EVEN, SSD_D_INNER))
    hy_q_norm = gain((N_EVEN, DA_HEAD_DIM))
    hy_k_norm = gain((N_EVEN, DA_HEAD_DIM))
    hy_lambda_q1 = nrm((N_EVEN, DA_HEAD_DIM), 0.1)
    hy_lambda_k1 = nrm((N_EVEN, DA_HEAD_DIM), 0.1)
    hy_lambda_q2 = nrm((N_EVEN, DA_HEAD_DIM), 0.1)
    hy_lambda_k2 = nrm((N_EVEN, DA_HEAD_DIM), 0.1)
    hy_subln = gain((N_EVEN, DA_V_DIM))
    hy_w_out = nrm((N_EVEN, EVEN_OUT_WIDTH, D_MODEL), EVEN_OUT_WIDTH ** -0.5)
    sc_w_in = nrm((N_ODD, D_MODEL, 3 * SC_WIDTH), D_MODEL ** -0.5)
    sc_conv_w = nrm((N_ODD, SC_CONV, SC_WIDTH), SC_CONV ** -0.5)
    sc_w_out = nrm((N_ODD, SC_WIDTH, D_MODEL), SC_WIDTH ** -0.5)
    ffn_w_up = nrm((DEPTH, D_MODEL, 2 * D_FF), D_MODEL ** -0.5)
    ffn_conv_w = nrm((DEPTH, FFN_CONV, 2 * D_FF), FFN_CONV ** -0.5)
    ffn_conv_b = nrm((DEPTH, 2 * D_FF), 0.01)
    ffn_w_down = nrm((DEPTH, D_FF, D_MODEL), D_FF ** -0.5)
    return {"x": x, "mix_norm": mix_norm, "ffn_norm": ffn_norm,
            "hy_w_in": hy_w_in, "hy_conv_w": hy_conv_w, "hy_conv_b": hy_conv_b,
            "hy_dt_bias": hy_dt_bias, "hy_a_log": hy_a_log, "hy_d_skip": hy_d_skip,
            "hy_ssd_norm": hy_ssd_norm, "hy_q_norm": hy_q_norm, "hy_k_norm": hy_k_norm,
            "hy_lambda_q1": hy_lambda_q1, "hy_lambda_k1": hy_lambda_k1,
            "hy_lambda_q2": hy_lambda_q2, "hy_lambda_k2": hy_lambda_k2,
            "hy_subln": hy_subln, "hy_w_out": hy_w_out,
            "sc_w_in": sc_w_in, "sc_conv_w": sc_conv_w, "sc_w_out": sc_w_out,
            "ffn_w_up": ffn_w_up, "ffn_conv_w": ffn_conv_w, "ffn_conv_b": ffn_conv_b,
            "ffn_w_down": ffn_w_down}


def reference(x, mix_norm, ffn_norm, hy_w_in, hy_conv_w, hy_conv_b, hy_dt_bias, hy_a_log,
              hy_d_skip, hy_ssd_norm, hy_q_norm, hy_k_norm, hy_lambda_q1, hy_lambda_k1,
              hy_lambda_q2, hy_lambda_k2, hy_subln, hy_w_out, sc_w_in, sc_conv_w, sc_w_out,
              ffn_w_up, ffn_conv_w, ffn_conv_b, ffn_w_down):
    cos, sin = rope_tables(x.shape[1], DA_HEAD_DIM, x.dtype)
    h = x
    for layer in range(DEPTH):
        hn = rms_norm(h, mix_norm[layer])
        if layer % 2 == 0:
            e = layer // 2
            lambda_init = 0.8 - 0.6 * math.exp(-0.3 * layer)
            h = h + ssd_diffattn_mixer(hn, hy_w_in[e], hy_conv_w[e], hy_conv_b[e],
                                       hy_dt_bias[e], hy_a_log[e], hy_d_skip[e],
                                       hy_ssd_norm[e], hy_q_norm[e], hy_k_norm[e],
                                       hy_lambda_q1[e], hy_lambda_k1[e],
                                       hy_lambda_q2[e], hy_lambda_k2[e],
                                       hy_subln[e], hy_w_out[e], lambda_init, cos, sin)
        else:
            o = layer // 2
            h = h + short_conv_mixer(hn, sc_w_in[o], sc_conv_w[o], sc_w_out[o])
        h = h + conv_ffn(rms_norm(h, ffn_norm[layer]), ffn_w_up[layer], ffn_conv_w[layer],
                         ffn_conv_b[layer], ffn_w_down[layer])
    return h
```

```python
import math
from contextlib import ExitStack

import numpy as np
import concourse.bass as bass
import concourse.mybir as mybir
from concourse.bass_utils import run_bass_kernel_spmd

F32 = mybir.dt.float32
BF16 = mybir.dt.bfloat16
I32 = mybir.dt.int32
AF = mybir.ActivationFunctionType
ALU = mybir.AluOpType

L = 4096
DM = 1024
NT = 8
DFF = 2816
EPS = 1e-6
W_IN0 = 5648
C_Z, C_XBC, C_DT, C_Q, C_K, C_V = 0, 1024, 2560, 2576, 3600, 4624


class Ev:
    __slots__ = ("key", "sem", "val", "eng", "clock")

    def __init__(self, key, sem, val, eng, clock):
        self.key, self.sem, self.val, self.eng, self.clock = key, sem, val, eng, clock


class Buf:
    __slots__ = ("name", "w", "rd", "excl")

    def __init__(self, name="", excl=False):
        self.name, self.w, self.rd, self.excl = name, None, {}, excl


class Sched:
    NDMA = 8

    def __init__(self, nc, stack):
        self.nc = nc
        self.engs = {"pe": nc.tensor, "act": nc.scalar, "dve": nc.vector,
                     "pool": nc.gpsimd, "sp": nc.sync}
        self.sem, self.cnt, self.pending, self.last_ins = {}, {}, {}, {}
        self.seen = {e: {} for e in self.engs}
        for e in ("pe", "act", "dve", "pool"):
            self.sem[e] = stack.enter_context(nc.semaphore("s_" + e))
            self.cnt[e] = 0
            self.pending[e] = []
            self.last_ins[e] = None
        self.dsem, self.dcnt, self.drr = {}, {}, {}
        for q in ("sp", "pool"):
            self.dsem[q] = []
            for i in range(self.NDMA):
                k = "d_%s%d" % (q, i)
                self.dsem[q].append((k, stack.enter_context(nc.semaphore(k))))
                self.dcnt[k] = 0
            self.drr[q] = 0
        self.n_wait = 0
        self.n_ins = 0

    def _need(self, e, ev):
        if ev is None:
            return
        if ev.val is None:
            self._force(ev.eng)
        seen = self.seen[e]
        if seen.get(ev.key, 0) >= ev.val:
            return
        self.engs[e].wait_ge(ev.sem, ev.val)
        self.n_wait += 1
        for k, v in ev.clock.items():
            if seen.get(k, 0) < v:
                seen[k] = v

    def _force(self, e):
        if not self.pending[e]:
            return
        self.last_ins[e].then_inc(self.sem[e], 1)
        self._signal(e)

    def _signal(self, e):
        self.cnt[e] += 1
        v = self.cnt[e]
        clock = dict(self.seen[e])
        clock["s_" + e] = v
        for ev in self.pending[e]:
            ev.val = v
            ev.clock = clock
        self.pending[e] = []
        self.last_ins[e] = None

    def _deps(self, e, reads, writes, is_dma):
        for b in reads:
            if b.w is not None:
                self._need(e, b.w)
            if b.excl:
                for ev in list(b.rd.values()):
                    if ev.eng != e:
                        self._need(e, ev)
        for b in writes:
            if b.w is not None and (is_dma or b.w.eng != e or e == "pool"):
                self._need(e, b.w)
            for ev in list(b.rd.values()):
                if is_dma or ev.eng != e or e == "pool":
                    self._need(e, ev)

    def _record(self, ev, reads, writes):
        for b in reads:
            b.rd[ev.key] = ev
        for b in writes:
            b.w = ev
            b.rd = {}

    def op(self, e, fn, reads=(), writes=(), sig=True):
        self._deps(e, reads, writes, False)
        ins = fn()
        self.n_ins += 1
        ev = Ev("s_" + e, self.sem[e], None, e, None)
        self.pending[e].append(ev)
        self.last_ins[e] = ins
        if sig:
            ins.then_inc(self.sem[e], 1)
            self._signal(e)
        self._record(ev, reads, writes)
        return ev

    def dma(self, q, out, in_, reads=(), writes=(), **kw):
        i = self.drr[q]
        self.drr[q] = (i + 1) % self.NDMA
        key, sem = self.dsem[q][i]
        prev = self.dcnt[key]
        seen = self.seen[q]
        if prev > 0 and seen.get(key, 0) < prev:
            self.engs[q].wait_ge(sem, prev)
            self.n_wait += 1
            seen[key] = prev
        self._deps(q, reads, writes, True)
        ins = self.engs[q].dma_start(out=out, in_=in_, **kw)
        ins.then_inc(sem, 16)
        self.n_ins += 1
        self.dcnt[key] = prev + 16
        clock = dict(seen)
        clock[key] = prev + 16
        ev = Ev(key, sem, prev + 16, None, clock)
        self._record(ev, reads, writes)
        return ev

    def barrier(self, engines=("pe", "act", "dve", "pool", "sp")):
        for e in ("pe", "act", "dve", "pool"):
            self._force(e)
        evs = []
        for e in ("pe", "act", "dve", "pool"):
            if self.cnt[e] > 0:
                evs.append(Ev("s_" + e, self.sem[e], self.cnt[e], e, {"s_" + e: self.cnt[e]}))
        for q in self.dsem:
            for key, sem in self.dsem[q]:
                if self.dcnt[key] > 0:
                    evs.append(Ev(key, sem, self.dcnt[key], None, {key: self.dcnt[key]}))
        for e in engines:
            for ev in evs:
                if ev.eng != e:
                    self._need(e, ev)


_UID = [0]


class Ring:
    def __init__(self, nc, st, name, n, shape, dt):
        _UID[0] += 1
        name = "%s_%d_" % (name, _UID[0])
        self.t = [st.enter_context(nc.sbuf_tensor("%s%d" % (name, i), shape, dt)) for i in range(n)]
        self.b = [Buf("%s%d" % (name, i)) for i in range(n)]
        self.i = 0

    def next(self):
        i = self.i
        self.i = (i + 1) % len(self.t)
        return self.t[i], self.b[i]


class PRing:
    def __init__(self, nc, st, name, n, shape, dt=F32):
        _UID[0] += 1
        name = "%s_%d_" % (name, _UID[0])
        self.t = [st.enter_context(nc.psum_tensor("%s%d" % (name, i), shape, dt)) for i in range(n)]
        self.b = [Buf("%s%d" % (name, i), excl=True) for i in range(n)]
        self.i = 0

    def next(self):
        i = self.i
        self.i = (i + 1) % len(self.t)
        return self.t[i], self.b[i]


def bc(ap, shape):
    return ap.to_broadcast(shape)


class _Stop(Exception):
    pass


def build_program(layers=(0, 1), dbg=False, stop=None, hstop=None):
    nc = bass.Bass("TRN2", target_bir_lowering=False)

    def din(name, shape):
        return nc.dram_tensor(name, shape, F32, kind="ExternalInput").ap()

    x = din("x", [L, DM])
    mix_norm = din("mix_norm", [2, DM])
    ffn_norm = din("ffn_norm", [2, DM])
    hy_w_in = din("hy_w_in", [1, DM, W_IN0])
    hy_conv_w = din("hy_conv_w", [1, 4, 1536])
    hy_conv_b = din("hy_conv_b", [1, 1536])
    hy_dt_bias = din("hy_dt_bias", [1, 16])
    hy_a_log = din("hy_a_log", [1, 16])
    hy_d_skip = din("hy_d_skip", [1, 16])
    hy_ssd_norm = din("hy_ssd_norm", [1, 1024])
    hy_q_norm = din("hy_q_norm", [1, 64])
    hy_k_norm = din("hy_k_norm", [1, 64])
    hy_lq1 = din("hy_lambda_q1", [1, 64])
    hy_lk1 = din("hy_lambda_k1", [1, 64])
    hy_lq2 = din("hy_lambda_q2", [1, 64])
    hy_lk2 = din("hy_lambda_k2", [1, 64])
    hy_subln = din("hy_subln", [1, 128])
    hy_w_out = din("hy_w_out", [1, 2048, DM])
    sc_w_in = din("sc_w_in", [1, DM, 3072])
    sc_conv_w = din("sc_conv_w", [1, 3, 1024])
    sc_w_out = din("sc_w_out", [1, 1024, DM])
    ffn_w_up = din("ffn_w_up", [2, DM, 2 * DFF])
    ffn_conv_w = din("ffn_conv_w", [2, 3, 2 * DFF])
    ffn_conv_b = din("ffn_conv_b", [2, 2 * DFF])
    ffn_w_down = din("ffn_w_down", [2, DFF, DM])
    out = nc.dram_tensor("out", [L, DM], F32, kind="ExternalOutput").ap()

    skind = "ExternalOutput" if dbg else "Internal"

    def dscr(name, shape, dt=BF16):
        return nc.dram_tensor(name, shape, dt, kind=skind).ap()

    hA = dscr("hA", [L, DM], F32)
    hB = dscr("hB", [L, DM], F32)
    hC = dscr("hC", [L, DM], F32)
    ACT_S = dscr("ACT_S", [22, 128, L])
    XBC_S = dscr("XBC_S", [12, 128, L])
    QT_S = dscr("QT_S", [8, 128, L])
    KT_S = dscr("KT_S", [8, 128, L])
    ZS_S = dscr("ZS_S", [32, 128, 1024])
    V_S = dscr("V_S", [32, 128, 1024])

    with ExitStack() as top:
        S = Sched(nc, top)
        V, A, G, P = nc.vector, nc.scalar, nc.gpsimd, nc.tensor

        def sbt(st, name, shape, dt=F32):
            _UID[0] += 1
            return st.enter_context(nc.sbuf_tensor("%s_%d" % (name, _UID[0]), shape, dt))

        def pst(st, name, shape, dt=F32):
            _UID[0] += 1
            return st.enter_context(nc.psum_tensor("%s_%d" % (name, _UID[0]), shape, dt))

        hnT = sbt(top, "hnT", [128, 8, L], BF16)
        B_hn = [Buf("hn%d" % i) for i in range(32)]
        WR = {"t": None}
        B_Wres = Buf("Wres")

        def open_wres(st):
            WR["t"] = sbt(st, "Wres", [128, 22, DM], BF16)
        identf = sbt(top, "identf", [128, 128]); B_c = Buf("consts")
        identb = sbt(top, "identb", [128, 128], BF16)
        triT_f = sbt(top, "triT_f", [128, 128])
        triT_b = sbt(top, "triT_b", [128, 128], BF16)
        ones_f = sbt(top, "ones_f", [128, 128])
        epsc = sbt(top, "epsc", [128, 1])
        normw = sbt(top, "normw", [128, 4, 8])
        fcw = sbt(top, "fcw", [128, 2, 3, 44])
        fcb = sbt(top, "fcb", [128, 2, 44])
        scw = sbt(top, "scw", [128, 3, 8])

        S.op("pool", lambda: G.memset(identf[:], 1.0), writes=[B_c])
        S.op("pool", lambda: G.affine_select(out=identf[:], in_=identf[:], pattern=[[-1, 128]],
                                             compare_op=ALU.is_equal, fill=0.0, base=0,
                                             channel_multiplier=1), reads=[B_c], writes=[B_c])
        S.op("pool", lambda: G.memset(triT_f[:], 1.0), writes=[B_c])
        S.op("pool", lambda: G.affine_select(out=triT_f[:], in_=triT_f[:], pattern=[[1, 128]],
                                             compare_op=ALU.is_ge, fill=0.0, base=0,
                                             channel_multiplier=-1), reads=[B_c], writes=[B_c])
        S.op("pool", lambda: G.memset(ones_f[:], 1.0), writes=[B_c])
        S.op("pool", lambda: G.memset(epsc[:], EPS), writes=[B_c])
        S.op("dve", lambda: V.tensor_copy(out=identb[:], in_=identf[:]), reads=[B_c], writes=[B_c])
        S.op("dve", lambda: V.tensor_copy(out=triT_b[:], in_=triT_f[:]), reads=[B_c], writes=[B_c])

        def load_cols(st_ring, ps_ring, dst, src_row, nch):
            stg, B_s = st_ring.next()
            S.dma("sp", stg[0:nch, :], src_row.rearrange("(c p) -> c p", p=128), writes=[B_s])
            ps, B_p = ps_ring.next()
            S.op("pe", lambda: P.matmul(ps[:, 0:nch], lhsT=stg[0:nch, :], rhs=identf[0:nch, 0:nch],
                                        start=True, stop=True), reads=[B_s, B_c], writes=[B_p])
            S.op("dve", lambda: V.tensor_copy(out=dst, in_=ps[:, 0:nch]), reads=[B_p], writes=[B_c])

        with ExitStack() as ph:
            stg_r = Ring(nc, ph, "stg", 3, [64, 128], F32)
            ps_r = PRing(nc, ph, "psc", 2, [128, 512])
            for i, (src, l) in enumerate([(mix_norm, 0), (ffn_norm, 0), (mix_norm, 1), (ffn_norm, 1)]):
                load_cols(stg_r, ps_r, normw[:, i, :], src[l, :], 8)
            for l in range(2):
                for k in range(3):
                    load_cols(stg_r, ps_r, fcw[:, l, k, :], ffn_conv_w[l, k, :], 44)
                load_cols(stg_r, ps_r, fcb[:, l, :], ffn_conv_b[l, :], 44)
            for k in range(3):
                load_cols(stg_r, ps_r, scw[:, k, :], sc_conv_w[0, k, :], 8)
            S.barrier()

        def load_Wres(w2d, nch):
            src = w2d.rearrange("(c p) n -> p c n", p=128)
            step = 4
            for c0 in range(0, nch, step):
                c1 = min(nch, c0 + step)
                S.dma("pool", WR["t"][:, c0:c1, :], src[:, c0:c1, :], writes=[B_Wres])

        class NormRes:
            def __init__(self, st):
                self.junk = sbt(st, "nr_junk", [128, DM], BF16); self.B_junk = Buf()
                self.xn = Ring(nc, st, "nr_xn", 2, [128, DM], BF16)
                self.sm = Ring(nc, st, "nr_sm", 3, [128, 2], F32)
                self.ps = PRing(nc, st, "nr_ps", 2, [128, 8, 128], BF16)

        def norm_to_hnT(nr, h_sb, B_h, widx, t128):
            sm, B_sm = nr.sm.next()
            S.op("act", lambda: A.activation(out=nr.junk[:], in_=h_sb, func=AF.Square,
                                             accum_out=sm[:, 0:1]),
                 reads=[B_h], writes=[nr.B_junk, B_sm])
            S.op("act", lambda: A.activation(out=sm[:, 1:2], in_=sm[:, 0:1], func=AF.Sqrt,
                                             bias=epsc[:, 0:1], scale=1.0 / DM),
                 reads=[B_sm, B_c], writes=[B_sm])
            S.op("dve", lambda: V.reciprocal(out=sm[:, 1:2], in_=sm[:, 1:2]), reads=[B_sm], writes=[B_sm])
            xn, B_xn = nr.xn.next()
            S.op("dve", lambda: V.tensor_scalar(out=xn[:], in0=h_sb, scalar1=sm[:, 1:2], scalar2=None,
                                                op0=ALU.mult), reads=[B_h, B_sm], writes=[B_xn])
            ps, B_ps = nr.ps.next()
            for c in range(8):
                S.op("pe", lambda c=c: P.transpose(out=ps[:, c, :], in_=xn[:, c * 128:(c + 1) * 128],
                                                   identity=identb[:]),
                     reads=[B_xn, B_c], writes=[B_ps], sig=(c == 7))
            S.op("dve", lambda: V.tensor_tensor(out=hnT[:, :, t128 * 128:(t128 + 1) * 128], in0=ps[:],
                                                in1=bc(normw[:, widx, :].unsqueeze(2), [128, 8, 128]),
                                                op=ALU.mult),
                 reads=[B_ps, B_c], writes=[B_hn[t128]])

        def phase_norm_in(h_src, widx):
            with ExitStack() as ph:
                nr = NormRes(ph)
                hr = Ring(nc, ph, "ni_h", 3, [128, DM], F32)
                for t in range(32):
                    h, B_h = hr.next()
                    S.dma("sp", h[:], h_src[t * 128:(t + 1) * 128, :], writes=[B_h])
                    norm_to_hnT(nr, h[:], B_h, widx, t)
                S.barrier()

        def phase_tm(nch, h_src, h_dst, widx_next):
            with ExitStack() as ph:
                nr = NormRes(ph) if widx_next is not None else None
                ar = Ring(nc, ph, "tm_a", 2, [128, nch, 512], BF16)
                hr = Ring(nc, ph, "tm_h", 3, [128, DM], F32)
                hn = Ring(nc, ph, "tm_hn", 3, [128, DM], F32)
                pr = PRing(nc, ph, "tm_ps", 4, [128, 512])
                src = ACT_S.rearrange("c p t -> p c t")
                for tt in range(NT):
                    a, B_a = ar.next()
                    half = (nch + 1) // 2
                    S.dma("sp", a[:, 0:half, :], src[:, 0:half, tt * 512:(tt + 1) * 512], writes=[B_a])
                    S.dma("sp", a[:, half:nch, :], src[:, half:nch, tt * 512:(tt + 1) * 512], writes=[B_a])
                    for sub in range(4):
                        t128 = tt * 4 + sub
                        h, B_h = hr.next()
                        S.dma("sp", h[:], h_src[t128 * 128:(t128 + 1) * 128, :], writes=[B_h])
                        pss = [pr.next(), pr.next()]
                        for c in range(nch):
                            for hf in range(2):
                                ps, B_ps = pss[hf]
                                S.op("pe", lambda c=c, hf=hf, ps=ps: P.matmul(
                                    ps[:], lhsT=a[:, c, sub * 128:(sub + 1) * 128],
                                    rhs=WR["t"][:, c, hf * 512:(hf + 1) * 512],
                                    start=(c == 0), stop=(c == nch - 1)),
                                    reads=[B_a, B_Wres], writes=[B_ps], sig=(c == nch - 1))
                        o, B_o = hn.next()
                        for hf in range(2):
                            ps, B_ps = pss[hf]
                            S.op("dve", lambda hf=hf, ps=ps: V.tensor_tensor(
                                out=o[:, hf * 512:(hf + 1) * 512], in0=ps[:],
                                in1=h[:, hf * 512:(hf + 1) * 512], op=ALU.add),
                                reads=[B_ps, B_h], writes=[B_o])
                        S.dma("sp", h_dst[t128 * 128:(t128 + 1) * 128, :], o[:], reads=[B_o])
                        if nr is not None:
                            norm_to_hnT(nr, o[:], B_o, widx_next, t128)
                S.barrier()

        def fm_matmuls(wt, n0, tt, ps, B_ps, B_w):
            for k in range(8):
                S.op("pe", lambda k=k: P.matmul(ps[:], lhsT=wt[:, k, n0:n0 + 128],
                                                rhs=hnT[:, k, tt * 512:(tt + 1) * 512],
                                                start=(k == 0), stop=(k == 7)),
                     reads=[B_w] + B_hn[tt * 4:tt * 4 + 4], writes=[B_ps], sig=(k == 7))

        def wload(wr, w2d, col0, ncol=128):
            wt, B_w = wr.next()
            S.dma("pool", wt[:, :, 0:ncol], w2d.rearrange("(k p) n -> p k n", p=128)[:, :, col0:col0 + ncol],
                  writes=[B_w])
            return wt, B_w

        def phase_ffn_up(l, w_next, nch_next):
            w2d = ffn_w_up[l]
            with ExitStack() as ph:
                wr = Ring(nc, ph, "fu_w", 6, [128, 8, 128], BF16)
                pr = PRing(nc, ph, "fu_ps", 6, [128, 512])
                xg = Ring(nc, ph, "fu_xg", 1, [128, L + 2], F32)
                xv = Ring(nc, ph, "fu_xv", 1, [128, L + 2], F32)
                yr = Ring(nc, ph, "fu_y", 6, [128, 512], F32)
                sr = Ring(nc, ph, "fu_s", 3, [128, 512], F32)
                gr = Ring(nc, ph, "fu_g", 2, [128, L], BF16)
                for r in (xg, xv):
                    for t_, b_ in zip(r.t, r.b):
                        S.op("pool", lambda t_=t_: G.memset(t_[:, 0:2], 0.0), writes=[b_])
                pend = [(wload(wr, w2d, 0), wload(wr, w2d, DFF))]
                load_Wres(w_next, nch_next)
                for i in range(22):
                    if i + 1 < 22:
                        pend.append((wload(wr, w2d, (i + 1) * 128), wload(wr, w2d, DFF + (i + 1) * 128)))
                    (wg, B_wg), (wv, B_wv) = pend.pop(0)
                    Xg, B_xg = xg.next()
                    Xv, B_xv = xv.next()
                    Gt, B_g = gr.next()
                    cg, cv = i, 22 + i
                    for tt in range(NT):
                        sl = slice(tt * 512, (tt + 1) * 512)
                        psg, B_pg = pr.next()
                        psv, B_pv = pr.next()
                        fm_matmuls(wg, 0, tt, psg, B_pg, B_wg)
                        fm_matmuls(wv, 0, tt, psv, B_pv, B_wv)
                        Yg, B_yg = yr.next()
                        Yv, B_yv = yr.next()
                        S.op("act", lambda: A.activation(out=Yg[:], in_=psg[:], func=AF.Identity,
                                                         bias=fcb[:, l, cg:cg + 1], scale=fcw[:, l, 2, cg:cg + 1]),
                             reads=[B_pg, B_c], writes=[B_yg])
                        S.op("act", lambda: A.copy(out=Xg[:, 2 + tt * 512:2 + (tt + 1) * 512], in_=psg[:]),
                             reads=[B_pg], writes=[B_xg])
                        S.op("dve", lambda: V.tensor_scalar(out=Yv[:], in0=psv[:], scalar1=fcw[:, l, 2, cv:cv + 1],
                                                            scalar2=fcb[:, l, cv:cv + 1], op0=ALU.mult, op1=ALU.add),
                             reads=[B_pv, B_c], writes=[B_yv])
                        S.op("dve", lambda: V.tensor_copy(out=Xv[:, 2 + tt * 512:2 + (tt + 1) * 512], in_=psv[:]),
                             reads=[B_pv], writes=[B_xv])
                        for (e, X_, B_x, Y_, B_y, c_) in (("dve", Xg, B_xg, Yg, B_yg, cg), ("dve", Xv, B_xv, Yv, B_yv, cv)):
                            for k in (1, 0):
                                en = "dve"
                                E_ = G if en == "pool" else V
                                S.op(en, lambda E_=E_, X_=X_, Y_=Y_, k=k, c_=c_: E_.scalar_tensor_tensor(
                                    out=Y_[:], in0=X_[:, tt * 512 + k:tt * 512 + k + 512],
                                    scalar=fcw[:, l, k, c_:c_ + 1], in1=Y_[:], op0=ALU.mult, op1=ALU.add),
                                    reads=[B_x, B_c, B_y], writes=[B_y])
                        St, B_s = sr.next()
                        S.op("act", lambda: A.activation(out=St[:], in_=Yg[:], func=AF.Silu),
                             reads=[B_yg], writes=[B_s])
                        S.op("pool", lambda: G.tensor_tensor(out=Gt[:, sl], in0=St[:], in1=Yv[:], op=ALU.mult),
                             reads=[B_s, B_yv], writes=[B_g])
                    S.dma("sp", ACT_S[i], Gt[:], reads=[B_g])
                S.barrier()

        def phase_shortconv(w_next, nch_next):
            w2d = sc_w_in[0]
            with ExitStack() as ph:
                wr = Ring(nc, ph, "sc_w", 6, [128, 8, 128], BF16)
                pr = PRing(nc, ph, "sc_ps", 6, [128, 512])
                mr = Ring(nc, ph, "sc_m", 2, [128, L + 2], F32)
                ur = Ring(nc, ph, "sc_u", 3, [128, 512], F32)
                yr = Ring(nc, ph, "sc_y", 3, [128, 512], F32)
                rr = Ring(nc, ph, "sc_r", 2, [128, L], BF16)
                for t_, b_ in zip(mr.t, mr.b):
                    S.op("pool", lambda t_=t_: G.memset(t_[:, 0:2], 0.0), writes=[b_])
                load_Wres(w_next, nch_next)
                for i in range(8):
                    wb_, wc_, wu_ = (wload(wr, w2d, i * 128), wload(wr, w2d, 1024 + i * 128),
                                     wload(wr, w2d, 2048 + i * 128))
                    M, B_m = mr.next()
                    R, B_r = rr.next()
                    for tt in range(NT):
                        sl = slice(tt * 512, (tt + 1) * 512)
                        psb, B_pb = pr.next()
                        psc, B_pc = pr.next()
                        psu, B_pu = pr.next()
                        fm_matmuls(wc_[0], 0, tt, psc, B_pc, wc_[1])
                        fm_matmuls(wu_[0], 0, tt, psu, B_pu, wu_[1])
                        fm_matmuls(wb_[0], 0, tt, psb, B_pb, wb_[1])
                        U, B_u = ur.next()
                        S.op("act", lambda: A.copy(out=U[:], in_=psu[:]), reads=[B_pu], writes=[B_u])
                        S.op("dve", lambda: V.tensor_tensor(out=M[:, 2 + tt * 512:2 + (tt + 1) * 512], in0=psc[:],
                                                            in1=U[:], op=ALU.mult),
                             reads=[B_pc, B_u], writes=[B_m])
                        Y, B_y = yr.next()
                        S.op("act", lambda: A.activation(out=Y[:], in_=M[:, 2 + tt * 512:2 + (tt + 1) * 512],
                                                         func=AF.Copy, scale=scw[:, 2, i:i + 1]),
                             reads=[B_m, B_c], writes=[B_y])
                        for k in (1, 0):
                            S.op("dve", lambda k=k: V.scalar_tensor_tensor(
                                out=Y[:], in0=M[:, tt * 512 + k:tt * 512 + k + 512], scalar=scw[:, k, i:i + 1],
                                in1=Y[:], op0=ALU.mult, op1=ALU.add), reads=[B_m, B_c, B_y], writes=[B_y])
                        S.op("dve", lambda: V.tensor_tensor(out=R[:, sl], in0=psb[:], in1=Y[:], op=ALU.mult),
                             reads=[B_pb, B_y], writes=[B_r])
                    S.dma("sp", ACT_S[i], R[:], reads=[B_r])
                S.barrier()

        def phase_hybrid(h_in_ap, h_out_ap, widx_next):
            hyb = {}
            w2d = hy_w_in[0]
            hst = ExitStack()
            cst = ExitStack()
            Bh = Buf("hyb_consts")
            blk = sbt(hst, "blk", [128, 128], BF16)
            prot = sbt(hst, "prot", [128, 128], BF16)
            hcw = sbt(hst, "hcw", [128, 4, 12])
            hcb = sbt(hst, "hcb", [128, 12])
            qkw = sbt(hst, "qkw", [128, 2])
            dt_tm = sbt(hst, "dt_tm", [128, 32, 16])
            a_tm = sbt(hst, "a_tm", [128, 32, 16])
            acs = sbt(hst, "acs", [128, 32, 16])
            eA = sbt(hst, "eA", [128, 32, 16])
            dA = sbt(hst, "dA", [128, 32, 16])
            cdr = sbt(hst, "cdr", [128, 32, 16])
            bc16 = sbt(hst, "bc16", [128, 3, 16])
            ssdw = sbt(hst, "ssdw", [128, 1024])
            sublw = sbt(hst, "sublw", [128, 128])
            nlam = sbt(hst, "nlam", [128, 2])
            B_dt = Buf("dtstuff")
            cosT = sbt(cst, "cosT", [128, L])
            sinT = sbt(cst, "sinT", [128, L])

            with ExitStack() as ph:
                tf = sbt(ph, "c_tf", [128, 128])
                tf2 = sbt(ph, "c_tf2", [128, 128])
                B_t = Buf()
                S.op("pool", lambda: G.memset(tf[:], 0.0), writes=[B_t])
                S.op("pool", lambda: G.memset(tf[0:64, 0:64], 1.0), writes=[B_t])
                S.op("pool", lambda: G.memset(tf[64:128, 64:128], 1.0), writes=[B_t])
                S.op("dve", lambda: V.tensor_copy(out=blk[:], in_=tf[:]), reads=[B_t], writes=[Bh])
                S.op("pool", lambda: G.memset(tf[:], 1.0), reads=[Bh], writes=[B_t])
                S.op("pool", lambda: G.affine_select(out=tf[:], in_=tf[:], pattern=[[-1, 128]], compare_op=ALU.is_equal,
                                                     fill=0.0, base=32, channel_multiplier=1), reads=[B_t], writes=[B_t])
                S.op("pool", lambda: G.memset(tf2[:], -1.0), writes=[B_t])
                S.op("pool", lambda: G.affine_select(out=tf2[:], in_=tf2[:], pattern=[[-1, 128]], compare_op=ALU.is_equal,
                                                     fill=0.0, base=-32, channel_multiplier=1), reads=[B_t], writes=[B_t])
                for c0 in (0, 64):
                    S.op("pool", lambda c0=c0: G.memset(tf[:, c0:c0 + 32], 0.0), reads=[B_t], writes=[B_t])
                    S.op("pool", lambda c0=c0: G.memset(tf2[:, c0 + 32:c0 + 64], 0.0), reads=[B_t], writes=[B_t])
                S.op("dve", lambda: V.tensor_tensor(out=tf[:], in0=tf[:], in1=tf2[:], op=ALU.add), reads=[B_t], writes=[B_t])
                S.op("dve", lambda: V.tensor_copy(out=prot[:], in_=tf[:]), reads=[B_t], writes=[Bh])
                pi_ = sbt(ph, "c_pi", [128, 1], I32)
                pf_ = sbt(ph, "c_pf", [128, 2])
                S.op("pool", lambda: G.iota(pi_[:], pattern=[[0, 1]], base=0, channel_multiplier=1), writes=[B_t])
                S.op("dve", lambda: V.tensor_single_scalar(out=pi_[:], in_=pi_[:], scalar=31, op=ALU.bitwise_and),
                     reads=[B_t], writes=[B_t])
                S.op("dve", lambda: V.tensor_copy(out=pf_[:, 0:1], in_=pi_[:]), reads=[B_t], writes=[B_t])
                S.op("dve", lambda: V.tensor_scalar(out=pf_[:, 0:1], in0=pf_[:, 0:1], scalar1=-math.log(10000.0) / 32.0,
                                                    scalar2=-math.log(2 * math.pi), op0=ALU.mult, op1=ALU.add),
                     reads=[B_t], writes=[B_t])
                S.op("act", lambda: A.activation(out=pf_[:, 1:2], in_=pf_[:, 0:1], func=AF.Exp), reads=[B_t], writes=[B_t])
                ti = sbt(ph, "c_ti", [128, L], I32)
                r_ = sbt(ph, "c_r", [128, L])
                kf = sbt(ph, "c_kf", [128, L])
                S.op("pool", lambda: G.iota(ti[:], pattern=[[1, L]], base=0, channel_multiplier=0), writes=[B_t])
                S.op("dve", lambda: V.tensor_copy(out=r_[:], in_=ti[:]), reads=[B_t], writes=[B_t])
                S.op("dve", lambda: V.tensor_scalar(out=r_[:], in0=r_[:], scalar1=pf_[:, 1:2], scalar2=None, op0=ALU.mult),
                     reads=[B_t], writes=[B_t])
                for (dst, off) in ((sinT, 0.0), (cosT, 0.25)):
                    S.op("dve", lambda off=off: V.tensor_scalar(out=kf[:], in0=r_[:], scalar1=off, scalar2=None, op0=ALU.add),
                         reads=[B_t, Bh], writes=[B_t])
                    S.op("dve", lambda: V.tensor_copy(out=ti[:], in_=kf[:]), reads=[B_t], writes=[B_t])
                    S.op("dve", lambda dst=dst: V.tensor_copy(out=dst[:], in_=ti[:]), reads=[B_t], writes=[Bh])
                    S.op("dve", lambda dst=dst: V.tensor_tensor(out=kf[:], in0=kf[:], in1=dst[:], op=ALU.subtract),
                         reads=[B_t, Bh], writes=[B_t])
                    S.op("dve", lambda dst=dst: V.tensor_single_scalar(out=dst[:], in_=kf[:], scalar=0.5, op=ALU.is_gt),
                         reads=[B_t], writes=[Bh])
                    S.op("dve", lambda dst=dst: V.tensor_tensor(out=kf[:], in0=kf[:], in1=dst[:], op=ALU.subtract),
                         reads=[B_t, Bh], writes=[B_t])
                    S.op("act", lambda dst=dst: A.activation(out=dst[:], in_=kf[:], func=AF.Sin, scale=2 * math.pi),
                         reads=[B_t], writes=[Bh])
                stg_r = Ring(nc, ph, "hstg", 3, [64, 128], F32)
                ps_r = PRing(nc, ph, "hpsc", 2, [128, 512])

                def lc(dst, src_row, nch):
                    stg, B_s = stg_r.next()
                    S.dma("sp", stg[0:nch, :], src_row.rearrange("(c p) -> c p", p=128), writes=[B_s])
                    ps, B_p = ps_r.next()
                    S.op("pe", lambda: P.matmul(ps[:, 0:nch], lhsT=stg[0:nch, :], rhs=identf[0:nch, 0:nch],
                                                start=True, stop=True), reads=[B_s, B_c], writes=[B_p])
                    S.op("dve", lambda: V.tensor_copy(out=dst, in_=ps[:, 0:nch]), reads=[B_p], writes=[Bh])
                for k in range(4):
                    lc(hcw[:, k, :], hy_conv_w[0, k, :], 12)
                lc(hcb[:, :], hy_conv_b[0, :], 12)
                for j, src in enumerate((hy_q_norm, hy_k_norm)):
                    for hf in range(2):
                        S.dma("sp", qkw[hf * 64:(hf + 1) * 64, j:j + 1], src[0, :].rearrange("(p o) -> p o", o=1),
                              writes=[Bh])
                S.op("dve", lambda: V.tensor_scalar(out=qkw[:, 0:1], in0=qkw[:, 0:1], scalar1=0.125, scalar2=None,
                                                    op0=ALU.mult), reads=[Bh], writes=[Bh])
                for j, src in enumerate((hy_dt_bias, hy_a_log, hy_d_skip)):
                    S.dma("sp", bc16[:, j, :], src[0:1, :].partition_broadcast(128), writes=[Bh])
                S.op("act", lambda: A.activation(out=bc16[:, 1, :], in_=bc16[:, 1, :], func=AF.Exp), reads=[Bh], writes=[Bh])
                S.op("dve", lambda: V.tensor_scalar(out=bc16[:, 1, :], in0=bc16[:, 1, :], scalar1=-1.0, scalar2=None,
                                                    op0=ALU.mult), reads=[Bh], writes=[Bh])
                S.dma("sp", ssdw[:], hy_ssd_norm[0:1, :].partition_broadcast(128), writes=[Bh])
                S.dma("sp", sublw[:], hy_subln[0:1, :].partition_broadcast(128), writes=[Bh])
                lam_init = 0.8 - 0.6 * math.exp(-0.3 * 0)
                S.op("dve", lambda: V.tensor_scalar(out=sublw[:], in0=sublw[:], scalar1=(1.0 - lam_init), scalar2=None,
                                                    op0=ALU.mult), reads=[Bh], writes=[Bh])
                lt = sbt(ph, "c_lt", [128, 4, 64])
                ls = sbt(ph, "c_ls", [128, 4])
                for j, src in enumerate((hy_lq1, hy_lk1, hy_lq2, hy_lk2)):
                    S.dma("sp", lt[:, j, :], src[0:1, :].partition_broadcast(128), writes=[B_t])
                for j in range(2):
                    S.op("dve", lambda j=j: V.tensor_tensor(out=lt[:, 2 * j, :], in0=lt[:, 2 * j, :], in1=lt[:, 2 * j + 1, :],
                                                            op=ALU.mult), reads=[B_t], writes=[B_t])
                    S.op("act", lambda j=j: A.activation(out=lt[:, 2 * j + 1, :], in_=lt[:, 2 * j, :], func=AF.Identity,
                                                         accum_out=ls[:, j:j + 1]), reads=[B_t], writes=[B_t])
                S.op("act", lambda: A.activation(out=ls[:, 2:4], in_=ls[:, 0:2], func=AF.Exp), reads=[B_t], writes=[B_t])
                S.op("dve", lambda: V.tensor_tensor(out=nlam[:, 0:1], in0=ls[:, 3:4], in1=ls[:, 2:3], op=ALU.subtract),
                     reads=[B_t], writes=[Bh])
                S.op("dve", lambda: V.tensor_scalar(out=nlam[:, 0:1], in0=nlam[:, 0:1], scalar1=-lam_init, scalar2=None,
                                                    op0=ALU.add), reads=[Bh], writes=[Bh])
                S.barrier()
            if hstop == 1:
                cst.close(); hst.close()
                return True

            with ExitStack() as ph:
                wr = Ring(nc, ph, "hb_w", 4, [128, 8, 128], BF16)
                pr = PRing(nc, ph, "hb_ps", 4, [128, 512])
                pr2 = PRing(nc, ph, "hb_ps2", 4, [128, 512])
                xr = Ring(nc, ph, "hb_x", 2, [128, L + 3], BF16)
                yr = Ring(nc, ph, "hb_y", 4, [128, 512], F32)
                orr = Ring(nc, ph, "hb_o", 2, [128, L], BF16)
                sq = Ring(nc, ph, "hb_sq", 3, [128, 512], BF16)
                rs = Ring(nc, ph, "hb_rs", 3, [128, 512], F32)
                qn = Ring(nc, ph, "hb_qn", 3, [128, 512], BF16)
                t1r = Ring(nc, ph, "hb_t1", 3, [128, 512], F32)
                t2r = Ring(nc, ph, "hb_t2", 3, [128, 512], F32)
                for t_, b_ in zip(xr.t, xr.b):
                    S.op("pool", lambda t_=t_: G.memset(t_[:, 0:3], 0.0), writes=[b_])
                jobs = [("xbc", i, C_XBC + i * 128) for i in range(12)] + \
                       [("q", i, C_Q + i * 128) for i in range(8)] + [("k", i, C_K + i * 128) for i in range(8)]
                pend = [wload(wr, w2d, jobs[0][2])]
                for ji, (kind, i, col) in enumerate(jobs):
                    if ji + 1 < len(jobs):
                        pend.append(wload(wr, w2d, jobs[ji + 1][2]))
                    wt, B_w = pend.pop(0)
                    O, B_o = orr.next()
                    if kind == "xbc":
                        X, B_x = xr.next()
                    for tt in range(NT):
                        sl = slice(tt * 512, (tt + 1) * 512)
                        ps, B_ps = pr.next()
                        fm_matmuls(wt, 0, tt, ps, B_ps, B_w)
                        if kind == "xbc":
                            Y, B_y = yr.next()
                            S.op("act", lambda: A.activation(out=Y[:], in_=ps[:], func=AF.Identity,
                                                             bias=hcb[:, i:i + 1], scale=hcw[:, 3, i:i + 1]),
                                 reads=[B_ps, Bh], writes=[B_y])
                            S.op("act", lambda: A.copy(out=X[:, 3 + tt * 512:3 + (tt + 1) * 512], in_=ps[:]),
                                 reads=[B_ps], writes=[B_x])
                            for k in (2, 1, 0):
                                S.op("dve", lambda k=k: V.scalar_tensor_tensor(
                                    out=Y[:], in0=X[:, tt * 512 + k:tt * 512 + k + 512], scalar=hcw[:, k, i:i + 1],
                                    in1=Y[:], op0=ALU.mult, op1=ALU.add), reads=[B_x, Bh, B_y], writes=[B_y])
                            S.op("act", lambda: A.activation(out=O[:, sl], in_=Y[:], func=AF.Silu),
                                 reads=[B_y], writes=[B_o])
                        else:
                            wcol = qkw[:, 0:1] if kind == "q" else qkw[:, 1:2]
                            s_, B_s = sq.next()
                            S.op("act", lambda: A.activation(out=s_[:], in_=ps[:], func=AF.Square),
                                 reads=[B_ps], writes=[B_s])
                            ps2, B_p2 = pr2.next()
                            S.op("pe", lambda: P.matmul(ps2[:], lhsT=blk[:], rhs=s_[:], start=True, stop=True),
                                 reads=[Bh, B_s], writes=[B_p2])
                            r1, B_r1 = rs.next()
                            S.op("act", lambda: A.activation(out=r1[:], in_=ps2[:], func=AF.Sqrt, bias=epsc[:, 0:1],
                                                             scale=1.0 / 64.0), reads=[B_p2, B_c], writes=[B_r1])
                            S.op("dve", lambda: V.reciprocal(out=r1[:], in_=r1[:]), reads=[B_r1], writes=[B_r1])
                            q_, B_q = qn.next()
                            S.op("dve", lambda: V.scalar_tensor_tensor(out=q_[:], in0=ps[:], scalar=wcol, in1=r1[:],
                                                                       op0=ALU.mult, op1=ALU.mult),
                                 reads=[B_ps, Bh, B_r1], writes=[B_q])
                            ps3, B_p3 = pr2.next()
                            S.op("pe", lambda: P.matmul(ps3[:], lhsT=prot[:], rhs=q_[:], start=True, stop=True),
                                 reads=[Bh, B_q], writes=[B_p3])
                            t1, B_t1 = t1r.next()
                            S.op("pool", lambda: G.tensor_tensor(out=t1[:], in0=q_[:], in1=cosT[:, sl], op=ALU.mult),
                                 reads=[B_q, Bh], writes=[B_t1])
                            t2, B_t2 = t2r.next()
                            S.op("dve", lambda: V.tensor_tensor(out=t2[:], in0=ps3[:], in1=sinT[:, sl], op=ALU.mult),
                                 reads=[B_p3, Bh], writes=[B_t2])
                            S.op("dve", lambda: V.tensor_tensor(out=O[:, sl], in0=t1[:], in1=t2[:], op=ALU.add),
                                 reads=[B_t1, B_t2], writes=[B_o])
                    dst = {"xbc": XBC_S, "q": QT_S, "k": KT_S}[kind]
                    S.dma("sp", dst[i], O[:], reads=[B_o])
                S.barrier()

            cst.close()
            if hstop == 2:
                hst.close()
                return True
            with ExitStack() as ph:
                wz = sbt(ph, "wz", [128, 8, 1024], BF16)
                wv = sbt(ph, "wv", [128, 8, 1024], BF16)
                wd = sbt(ph, "wd", [128, 8, 16], BF16)
                B_w = Buf()
                wsrc = w2d.rearrange("(k p) n -> p k n", p=128)
                for k0 in (0, 4):
                    S.dma("pool", wz[:, k0:k0 + 4, :], wsrc[:, k0:k0 + 4, C_Z:C_Z + 1024], writes=[B_w])
                    S.dma("pool", wv[:, k0:k0 + 4, :], wsrc[:, k0:k0 + 4, C_V:C_V + 1024], writes=[B_w])
                S.dma("pool", wd[:], wsrc[:, :, C_DT:C_DT + 16], writes=[B_w])
                pr = PRing(nc, ph, "tz_ps", 6, [128, 512])
                pdt = pst(ph, "tz_pdt", [128, 32, 16]); B_pdt = Buf(excl=True)
                zr = Ring(nc, ph, "tz_z", 3, [128, 1024], BF16)
                vr = Ring(nc, ph, "tz_v", 3, [128, 1024], BF16)
                for t in range(32):
                    pz = [pr.next(), pr.next()]
                    pv = [pr.next(), pr.next()]
                    for k in range(8):
                        lhs = hnT[:, k, t * 128:(t + 1) * 128]
                        for hf in range(2):
                            S.op("pe", lambda k=k, hf=hf, lhs=lhs: P.matmul(pz[hf][0][:], lhsT=lhs, rhs=wz[:, k, hf * 512:(hf + 1) * 512],
                                                                          start=(k == 0), stop=(k == 7)),
                                 reads=[B_hn[t], B_w], writes=[pz[hf][1]], sig=(k == 7))
                            S.op("pe", lambda k=k, hf=hf, lhs=lhs: P.matmul(pv[hf][0][:], lhsT=lhs, rhs=wv[:, k, hf * 512:(hf + 1) * 512],
                                                                          start=(k == 0), stop=(k == 7)),
                                 reads=[B_hn[t], B_w], writes=[pv[hf][1]], sig=(k == 7))
                        S.op("pe", lambda k=k, lhs=lhs: P.matmul(pdt[:, t, :], lhsT=lhs, rhs=wd[:, k, :],
                                                                 start=(k == 0), stop=(k == 7)),
                             reads=[B_hn[t], B_w], writes=[B_pdt], sig=(k == 7))
                    Z, B_z = zr.next()
                    Vt, B_v = vr.next()
                    for hf in range(2):
                        S.op("act", lambda hf=hf: A.activation(out=Z[:, hf * 512:(hf + 1) * 512], in_=pz[hf][0][:], func=AF.Silu),
                             reads=[pz[hf][1]], writes=[B_z])
                        S.op("dve", lambda hf=hf: V.tensor_copy(out=Vt[:, hf * 512:(hf + 1) * 512], in_=pv[hf][0][:]),
                             reads=[pv[hf][1]], writes=[B_v])
                    S.dma("sp", ZS_S[t], Z[:], reads=[B_z])
                    S.dma("sp", V_S[t], Vt[:], reads=[B_v])
                xb_ = sbt(ph, "dt_x", [128, 32, 16])
                ab_ = sbt(ph, "dt_a", [128, 32, 16])
                S.op("dve", lambda: V.tensor_tensor(out=xb_[:], in0=pdt[:], in1=bc(bc16[:, 0, :].unsqueeze(1), [128, 32, 16]),
                                                    op=ALU.add), reads=[B_pdt, Bh], writes=[B_dt])
                S.op("act", lambda: A.activation(out=ab_[:], in_=xb_[:], func=AF.Abs), reads=[B_dt], writes=[B_dt])
                S.op("act", lambda: A.activation(out=ab_[:], in_=ab_[:], func=AF.Exp, scale=-1.0), reads=[B_dt], writes=[B_dt])
                S.op("dve", lambda: V.tensor_scalar(out=ab_[:], in0=ab_[:], scalar1=1.0, scalar2=None, op0=ALU.add),
                     reads=[B_dt], writes=[B_dt])
                S.op("act", lambda: A.activation(out=ab_[:], in_=ab_[:], func=AF.Ln), reads=[B_dt], writes=[B_dt])
                S.op("dve", lambda: V.scalar_tensor_tensor(out=dt_tm[:], in0=xb_[:], scalar=0.0, in1=ab_[:], op0=ALU.max,
                                                           op1=ALU.add), reads=[B_dt], writes=[B_dt])
                S.op("dve", lambda: V.tensor_tensor(out=a_tm[:], in0=dt_tm[:], in1=bc(bc16[:, 1, :].unsqueeze(1), [128, 32, 16]),
                                                    op=ALU.mult), reads=[B_dt, Bh], writes=[B_dt])
                pa, B_pa = pr.next()
                pl, B_pl = pr.next()
                S.op("pe", lambda: P.matmul(pa[:], lhsT=triT_f[:], rhs=a_tm[:].rearrange("p c h -> p (c h)"),
                                            start=True, stop=True), reads=[B_c, B_dt], writes=[B_pa])
                S.op("pe", lambda: P.matmul(pl[:], lhsT=ones_f[:], rhs=a_tm[:].rearrange("p c h -> p (c h)"),
                                            start=True, stop=True), reads=[B_c, B_dt], writes=[B_pl])
                fl = lambda t_: t_[:].rearrange("p c h -> p (c h)")
                S.op("dve", lambda: V.tensor_copy(out=fl(acs), in_=pa[:]), reads=[B_pa], writes=[B_dt])
                S.op("act", lambda: A.activation(out=fl(eA), in_=pa[:], func=AF.Exp), reads=[B_pa], writes=[B_dt])
                S.op("act", lambda: A.activation(out=fl(cdr), in_=pl[:], func=AF.Exp), reads=[B_pl], writes=[B_dt])
                S.op("dve", lambda: V.tensor_tensor(out=fl(dA), in0=pl[:], in1=fl(acs), op=ALU.subtract),
                     reads=[B_pl, B_dt], writes=[B_dt])
                S.op("act", lambda: A.activation(out=fl(dA), in_=fl(dA), func=AF.Exp), reads=[B_dt], writes=[B_dt])
                S.barrier()
            if hstop == 3:
                hst.close()
                return True

            with ExitStack() as ph:
                xin = Ring(nc, ph, "sd_x", 2, [128, 12, 512], BF16)
                zin = Ring(nc, ph, "sd_z", 3, [128, 1024], BF16)
                ptr = PRing(nc, ph, "sd_ptr", 2, [128, 1024], BF16)
                pcb = pst(ph, "sd_pcb", [128, 4, 128]); B_pcb = Buf(excl=True)
                pR = PRing(nc, ph, "sd_pR", 1, [128, 8, 128])
                pya = pst(ph, "sd_pya", [128, 512]); B_pya = Buf(excl=True)
                pyb = pst(ph, "sd_pyb", [128, 512]); B_pyb = Buf(excl=True)
                pst_ = pst(ph, "sd_pst", [128, 512]); B_pst = Buf(excl=True)
                xs_tm = Ring(nc, ph, "sd_xs", 2, [128, 1024], BF16)
                b_tm = Ring(nc, ph, "sd_b", 2, [128, 256], BF16)
                Xr = Ring(nc, ph, "sd_X", 2, [128, 1024], BF16)
                Xdr = Ring(nc, ph, "sd_Xd", 2, [128, 1024], BF16)
                cbm = Ring(nc, ph, "sd_cbm", 2, [128, 2, 128], BF16)
                rhsR = Ring(nc, ph, "sd_rr", 1, [128, 8, 128], F32)
                segr = Ring(nc, ph, "sd_seg", 1, [128, 8, 128], F32)
                Er = Ring(nc, ph, "sd_E", 2, [128, 8, 128], BF16)
                Wr = Ring(nc, ph, "sd_W", 2, [128, 8, 128], BF16)
                yr = Ring(nc, ph, "sd_y", 2, [128, 512], F32)
                y2r = Ring(nc, ph, "sd_y2", 1, [128, 512], F32)
                ynr = Ring(nc, ph, "sd_yn", 2, [128, 512], BF16)
                smr = Ring(nc, ph, "sd_sm", 3, [128, 2], F32)
                junk = sbt(ph, "sd_junk", [128, 512], BF16); B_junk = Buf()
                prev = sbt(ph, "sd_prev", [128, 1024]); B_prev = Buf()
                prevb = sbt(ph, "sd_prevb", [128, 1024], BF16); B_prevb = Buf()
                yT = Ring(nc, ph, "sd_yT", 2, [128, 8, 512], BF16)
                S.op("pool", lambda: G.memset(prev[:], 0.0), writes=[B_prev])
                S.op("pool", lambda: G.memset(prevb[:], 0.0), writes=[B_prevb])
                xsrc = XBC_S.rearrange("c p t -> p c t")
                for c in range(32):
                    cc = c % 4
                    if cc == 0:
                        Xin, B_xin = xin.next()
                        S.dma("sp", Xin[:, 0:6, :], xsrc[:, 0:6, c * 128:c * 128 + 512], writes=[B_xin])
                        S.dma("sp", Xin[:, 6:12, :], xsrc[:, 6:12, c * 128:c * 128 + 512], writes=[B_xin])
                        YT, B_yT = yT.next()
                    csl = slice(cc * 128, (cc + 1) * 128)
                    Zt, B_z = zin.next()
                    S.dma("sp", Zt[:], ZS_S[c], writes=[B_z])
                    pt, B_pt = ptr.next()
                    for j in range(8):
                        S.op("pe", lambda j=j: P.transpose(out=pt[:, j * 128:(j + 1) * 128], in_=Xin[:, j, csl], identity=identb[:]),
                             reads=[B_xin, B_c], writes=[B_pt], sig=(j == 7))
                    xs, B_xs = xs_tm.next()
                    S.op("act", lambda: A.copy(out=xs[:], in_=pt[:]), reads=[B_pt], writes=[B_xs])
                    pt2, B_pt2 = ptr.next()
                    for g in range(2):
                        S.op("pe", lambda g=g: P.transpose(out=pt2[:, g * 128:(g + 1) * 128], in_=Xin[:, 8 + g, csl], identity=identb[:]),
                             reads=[B_xin, B_c], writes=[B_pt2], sig=(g == 1))
                    bt, B_bt = b_tm.next()
                    S.op("act", lambda: A.copy(out=bt[:], in_=pt2[:, 0:256]), reads=[B_pt2], writes=[B_bt])
                    X, B_X = Xr.next()
                    S.op("dve", lambda: V.tensor_tensor(out=X[:].rearrange("p (h d) -> p h d", h=16),
                                                        in0=xs[:].rearrange("p (h d) -> p h d", h=16),
                                                        in1=bc(dt_tm[:, c, :].unsqueeze(2), [128, 16, 64]), op=ALU.mult),
                         reads=[B_xs, B_dt], writes=[B_X])
                    Xd, B_Xd = Xdr.next()
                    S.op("pool", lambda: G.tensor_tensor(out=Xd[:].rearrange("p (h d) -> p h d", h=16),
                                                         in0=X[:].rearrange("p (h d) -> p h d", h=16),
                                                         in1=bc(dA[:, c, :].unsqueeze(2), [128, 16, 64]), op=ALU.mult),
                         reads=[B_X, B_dt], writes=[B_Xd])
                    for g in range(2):
                        S.op("pe", lambda g=g: P.matmul(pcb[:, g, :], lhsT=Xin[:, 8 + g, csl], rhs=Xin[:, 10 + g, csl],
                                                        start=True, stop=True), reads=[B_xin], writes=[B_pcb], sig=(g == 1))
                    cb, B_cb = cbm.next()
                    S.op("dve", lambda: V.tensor_tensor(out=cb[:], in0=pcb[:, 0:2, :], in1=bc(triT_f[:].unsqueeze(1), [128, 2, 128]),
                                                        op=ALU.mult), reads=[B_pcb, B_c], writes=[B_cb])
                    for g in range(2):
                        hs = slice(g * 8, (g + 1) * 8)
                        fs = slice(g * 512, (g + 1) * 512)
                        rr_, B_rr = rhsR.next()
                        S.op("pool", lambda hs=hs: G.tensor_tensor(out=rr_[:], in0=bc(triT_f[:].unsqueeze(1), [128, 8, 128]),
                                                                    in1=bc(a_tm[:, c, hs].unsqueeze(2), [128, 8, 128]), op=ALU.mult),
                             reads=[B_c, B_dt], writes=[B_rr])
                        pr_, B_pr = pR.next()
                        for q4 in range(2):
                            S.op("pe", lambda q4=q4: P.matmul(pr_[:, q4 * 4:(q4 + 1) * 4, :], lhsT=ones_f[:],
                                                              rhs=rr_[:, q4 * 4:(q4 + 1) * 4, :], start=True, stop=True),
                                 reads=[B_c, B_rr], writes=[B_pr], sig=(q4 == 1))
                        sg, B_sg = segr.next()
                        S.op("dve", lambda hs=hs: V.tensor_tensor(out=sg[:], in0=pr_[:], in1=bc(acs[:, c, hs].unsqueeze(2), [128, 8, 128]),
                                                                  op=ALU.subtract), reads=[B_pr, B_dt], writes=[B_sg])
                        S.op("pool", lambda: G.tensor_single_scalar(out=sg[:], in_=sg[:], scalar=0.0, op=ALU.min),
                             reads=[B_sg], writes=[B_sg])
                        E, B_E = Er.next()
                        S.op("act", lambda: A.activation(out=E[:], in_=sg[:], func=AF.Exp), reads=[B_sg], writes=[B_E])
                        W, B_W = Wr.next()
                        S.op("dve", lambda g=g: V.tensor_tensor(out=W[:], in0=E[:], in1=bc(cb[:, g, :].unsqueeze(1), [128, 8, 128]),
                                                                op=ALU.mult), reads=[B_E, B_cb], writes=[B_W])
                        for h in range(8):
                            hh = g * 8 + h
                            S.op("pe", lambda h=h, hh=hh: P.matmul(pya[:, h * 64:(h + 1) * 64], lhsT=W[:, h, :],
                                                                   rhs=X[:, hh * 64:(hh + 1) * 64], start=True, stop=True),
                                 reads=[B_W, B_X], writes=[B_pya], sig=(h == 7))
                        S.op("pe", lambda g=g, fs=fs: P.matmul(pyb[:], lhsT=Xin[:, 10 + g, csl], rhs=prevb[:, fs], start=True, stop=True),
                             reads=[B_xin, B_prevb], writes=[B_pyb])
                        y, B_y = yr.next()
                        S.op("dve", lambda hs=hs: V.tensor_tensor(out=y[:].rearrange("p (h d) -> p h d", h=8),
                                                                  in0=pyb[:].rearrange("p (h d) -> p h d", h=8),
                                                                  in1=bc(eA[:, c, hs].unsqueeze(2), [128, 8, 64]), op=ALU.mult),
                             reads=[B_pyb, B_dt], writes=[B_y])
                        S.op("dve", lambda: V.tensor_tensor(out=y[:], in0=y[:], in1=pya[:], op=ALU.add),
                             reads=[B_y, B_pya], writes=[B_y])
                        y2, B_y2 = y2r.next()
                        S.op("pool", lambda hs=hs, fs=fs: G.tensor_tensor(out=y2[:].rearrange("p (h d) -> p h d", h=8),
                                                                          in0=xs[:, fs].rearrange("p (h d) -> p h d", h=8),
                                                                          in1=bc(bc16[:, 2, hs].unsqueeze(2), [128, 8, 64]), op=ALU.mult),
                             reads=[B_xs, Bh], writes=[B_y2])
                        S.op("dve", lambda: V.tensor_tensor(out=y[:], in0=y[:], in1=y2[:], op=ALU.add),
                             reads=[B_y, B_y2], writes=[B_y])
                        S.op("dve", lambda fs=fs: V.tensor_tensor(out=y[:], in0=y[:], in1=Zt[:, fs], op=ALU.mult),
                             reads=[B_y, B_z], writes=[B_y])
                        sm, B_sm = smr.next()
                        S.op("act", lambda: A.activation(out=junk[:], in_=y[:], func=AF.Square, accum_out=sm[:, 0:1]),
                             reads=[B_y], writes=[B_junk, B_sm])
                        S.op("act", lambda: A.activation(out=sm[:, 1:2], in_=sm[:, 0:1], func=AF.Sqrt, bias=epsc[:, 0:1],
                                                         scale=1.0 / 512.0), reads=[B_sm, B_c], writes=[B_sm])
                        S.op("dve", lambda: V.reciprocal(out=sm[:, 1:2], in_=sm[:, 1:2]), reads=[B_sm], writes=[B_sm])
                        yn, B_yn = ynr.next()
                        S.op("dve", lambda fs=fs: V.scalar_tensor_tensor(out=yn[:], in0=y[:], scalar=sm[:, 1:2], in1=ssdw[:, fs],
                                                                         op0=ALU.mult, op1=ALU.mult),
                             reads=[B_y, B_sm, Bh], writes=[B_yn])
                        pto, B_pto = ptr.next()
                        for j in range(4):
                            S.op("pe", lambda j=j: P.transpose(out=pto[:, j * 128:(j + 1) * 128], in_=yn[:, j * 128:(j + 1) * 128],
                                                               identity=identb[:]), reads=[B_yn, B_c], writes=[B_pto], sig=(j == 3))
                        S.op("act", lambda g=g: A.copy(out=YT[:, g * 4:(g + 1) * 4, csl],
                                                       in_=pto[:, 0:512].rearrange("p (j t) -> p j t", j=4)),
                             reads=[B_pto], writes=[B_yT])
                        S.op("pe", lambda g=g, fs=fs: P.matmul(pst_[:], lhsT=bt[:, g * 128:(g + 1) * 128], rhs=Xd[:, fs], start=True, stop=True),
                             reads=[B_bt, B_Xd], writes=[B_pst])
                        S.op("dve", lambda hs=hs, fs=fs: V.tensor_tensor(out=prev[:, fs].rearrange("p (h d) -> p h d", h=8),
                                                                         in0=prev[:, fs].rearrange("p (h d) -> p h d", h=8),
                                                                         in1=bc(cdr[:, c, hs].unsqueeze(2), [128, 8, 64]), op=ALU.mult),
                             reads=[B_prev, B_dt], writes=[B_prev])
                        S.op("dve", lambda fs=fs: V.tensor_tensor(out=prev[:, fs], in0=prev[:, fs], in1=pst_[:], op=ALU.add),
                             reads=[B_prev, B_pst], writes=[B_prev])
                        S.op("act", lambda fs=fs: A.copy(out=prevb[:, fs], in_=prev[:, fs]), reads=[B_prev], writes=[B_prevb])
                    if cc == 3:
                        c0 = (c - 3) * 128
                        S.dma("sp", ACT_S.rearrange("c p t -> p c t")[:, 0:8, c0:c0 + 512], YT[:], reads=[B_yT])
                S.barrier()
            if hstop == 4:
                hst.close()
                return True

            wst = ExitStack()
            open_wres(wst)
            load_Wres(hy_w_out[0], 16)
            with ExitStack() as ph:
                kq = Ring(nc, ph, "at_kq", 2, [128, 2, L], BF16)
                vr = Ring(nc, ph, "at_v", 2, [128, 32, 130], BF16)
                pS = PRing(nc, ph, "at_pS", 2, [128, 2, 512])
                pO = PRing(nc, ph, "at_pO", 3, [128, 2, 256])
                pT = PRing(nc, ph, "at_pT", 1, [128, 1024], BF16)
                Er = Ring(nc, ph, "at_E", 4, [128, 2, 256], BF16)
                smr = Ring(nc, ph, "at_sm", 4, [128, 4], F32)
                tr_ = Ring(nc, ph, "at_t", 3, [128, 128], F32)
                or_ = Ring(nc, ph, "at_o", 3, [128, 128], F32)
                ybr = Ring(nc, ph, "at_yb", 3, [128, 128], BF16)
                junk = sbt(ph, "at_junk", [128, 128], BF16); B_junk = Buf()
                yT = Ring(nc, ph, "at_yT", 2, [128, L], BF16)
                for t_, b_ in zip(vr.t, vr.b):
                    S.op("pool", lambda t_=t_: G.memset(t_[:, :, 128:130], 1.0), writes=[b_])
                vsrc = V_S.rearrange("t p f -> p t f")
                for hd in range(8):
                    KQ, B_kq = kq.next()
                    S.dma("sp", KQ[:, 0, :], KT_S[hd], writes=[B_kq])
                    S.dma("sp", KQ[:, 1, :], QT_S[hd], writes=[B_kq])
                    Vt, B_v = vr.next()
                    S.dma("sp", Vt[:, :, 0:128], vsrc[:, :, hd * 128:(hd + 1) * 128], writes=[B_v])
                    YT, B_yT = yT.next()
                    for qt in range(16):
                        q0 = qt * 256
                        pOs = [pO.next(), pO.next()]
                        nkb = 2 * qt + 2
                        for kb in range(nkb):
                            j = kb - 2 * qt
                            c0 = 128 if j == 1 else 0
                            ps, B_ps = pS.next()
                            for m in range(2):
                                S.op("pe", lambda m=m, ps=ps, c0=c0, kb=kb: P.matmul(
                                    ps[:, m, c0:256], lhsT=KQ[64 * m:64 * m + 64, 0, kb * 128:(kb + 1) * 128],
                                    rhs=KQ[64 * m:64 * m + 64, 1, q0 + c0:q0 + 256], start=True, stop=True),
                                    reads=[B_kq], writes=[B_ps], sig=(m == 1))
                            E, B_E = Er.next()
                            S.op("act", lambda E=E, ps=ps, c0=c0: A.activation(out=E[:, :, c0:256], in_=ps[:, :, c0:256], func=AF.Exp),
                                 reads=[B_ps], writes=[B_E])
                            if j >= 0:
                                S.op("pool", lambda E=E, c0=c0: G.tensor_tensor(out=E[:, :, c0:c0 + 128], in0=E[:, :, c0:c0 + 128],
                                                                                  in1=bc(triT_b[:].unsqueeze(1), [128, 2, 128]), op=ALU.mult),
                                     reads=[B_E, B_c], writes=[B_E])
                            for qs in range(2):
                                if j == 1 and qs == 0:
                                    continue
                                po, B_po = pOs[qs]
                                last = (kb == 2 * qt + qs)
                                for m in range(2):
                                    S.op("pe", lambda m=m, qs=qs, po=po, E=E, kb=kb, last=last: P.matmul(
                                        po[:, m, 0:129], lhsT=E[:, m, qs * 128:(qs + 1) * 128], rhs=Vt[:, kb, 0:129],
                                        start=(kb == 0 and m == 0), stop=(last and m == 1), skip_group_check=True),
                                        reads=[B_E, B_v], writes=[B_po], sig=(last and m == 1))
                        for qs in range(2):
                            po, B_po = pOs[qs]
                            sm, B_sm = smr.next()
                            S.op("dve", lambda: V.reciprocal(out=sm[:, 0:2], in_=po[:, :, 128]), reads=[B_po], writes=[B_sm])
                            t_, B_t = tr_.next()
                            S.op("dve", lambda: V.tensor_scalar(out=t_[:], in0=po[:, 1, 0:128], scalar1=sm[:, 1:2], scalar2=nlam[:, 0:1],
                                                                op0=ALU.mult, op1=ALU.mult), reads=[B_po, B_sm, Bh], writes=[B_t])
                            o_, B_o = or_.next()
                            S.op("dve", lambda: V.scalar_tensor_tensor(out=o_[:], in0=po[:, 0, 0:128], scalar=sm[:, 0:1], in1=t_[:],
                                                                       op0=ALU.mult, op1=ALU.add), reads=[B_po, B_sm, B_t], writes=[B_o])
                            S.op("act", lambda: A.activation(out=junk[:], in_=o_[:], func=AF.Square, accum_out=sm[:, 2:3]),
                                 reads=[B_o], writes=[B_junk, B_sm])
                            S.op("act", lambda: A.activation(out=sm[:, 3:4], in_=sm[:, 2:3], func=AF.Sqrt, bias=epsc[:, 0:1],
                                                             scale=1.0 / 128.0), reads=[B_sm, B_c], writes=[B_sm])
                            S.op("dve", lambda: V.reciprocal(out=sm[:, 3:4], in_=sm[:, 3:4]), reads=[B_sm], writes=[B_sm])
                            yb, B_yb = ybr.next()
                            S.op("dve", lambda: V.scalar_tensor_tensor(out=yb[:], in0=o_[:], scalar=sm[:, 3:4], in1=sublw[:],
                                                                       op0=ALU.mult, op1=ALU.mult), reads=[B_o, B_sm, Bh], writes=[B_yb])
                            pt, B_pt = pT.next()
                            S.op("pe", lambda: P.transpose(out=pt[:, 0:128], in_=yb[:], identity=identb[:]), reads=[B_yb, B_c], writes=[B_pt])
                            S.op("act", lambda: A.copy(out=YT[:, q0 + qs * 128:q0 + (qs + 1) * 128], in_=pt[:, 0:128]),
                                 reads=[B_pt], writes=[B_yT])
                    S.dma("sp", ACT_S[8 + hd], YT[:], reads=[B_yT])
                S.barrier()
            if hstop == 5:
                wst.close(); hst.close()
                return True
            phase_tm(16, h_in_ap, h_out_ap, widx_next)
            wst.close()
            hst.close()
            return False

        pcount = [0]

        def chk():
            pcount[0] += 1
            return stop is not None and pcount[0] >= stop

        def _main_program():
            h_cur = x
            scr = [hA, hB, hC]
            nxt = 0
            first = True
            for li, layer in enumerate(layers):
                last_layer = (li == len(layers) - 1)
                if first:
                    phase_norm_in(h_cur, 2 * layer)
                    if chk():
                        return
                    first = False
                h_mid = scr[nxt]; nxt += 1
                if layer == 0:
                    if phase_hybrid(h_cur, h_mid, 2 * layer + 1):
                        return
                    if chk():
                        return
                else:
                    with ExitStack() as ws:
                        open_wres(ws)
                        phase_shortconv(sc_w_out[0], 8)
                        if chk():
                            return
                        phase_tm(8, h_cur, h_mid, 2 * layer + 1)
                        if chk():
                            return
                with ExitStack() as ws:
                    open_wres(ws)
                    phase_ffn_up(layer, ffn_w_down[layer], 22)
                    if chk():
                        return
                    if last_layer:
                        phase_tm(22, h_mid, out, None)
                    else:
                        h_new = scr[nxt]; nxt += 1
                        phase_tm(22, h_mid, h_new, 2 * layers[li + 1])
                        h_cur = h_new
                    if chk():
                        return

        try:
            _main_program()
        except _Stop:
            pass
        S.barrier(engines=("sp",))
        build_program.stats = (S.n_ins, S.n_wait)
    return nc


_INPUT_NAMES = ["mix_norm", "ffn_norm", "hy_w_in", "hy_conv_w", "hy_conv_b", "hy_dt_bias", "hy_a_log",
                "hy_d_skip", "hy_ssd_norm", "hy_q_norm", "hy_k_norm", "hy_lambda_q1", "hy_lambda_k1",
                "hy_lambda_q2", "hy_lambda_k2", "hy_subln", "hy_w_out", "sc_w_in", "sc_conv_w", "sc_w_out",
                "ffn_w_up", "ffn_conv_w", "ffn_conv_b", "ffn_w_down"]


def kernel(**inputs):
    x = np.asarray(inputs["x"], dtype=np.float32)
    nc = build_program()
    shared = {k: np.ascontiguousarray(np.asarray(inputs[k], dtype=np.float32)) for k in _INPUT_NAMES}
    in_maps = []
    for b in range(8):
        m = dict(shared)
        m["x"] = np.ascontiguousarray(x[b])
        in_maps.append(m)
    res = run_bass_kernel_spmd(nc, in_maps, core_ids=list(range(8)))
    return np.stack([np.asarray(r["out"], dtype=np.float32) for r in res.results], axis=0)
```

```python
import math
from contextlib import ExitStack

import numpy as np
import concourse.bass as bass
import concourse.mybir as mybir
from concourse.bass_utils import run_bass_kernel_spmd

F32 = mybir.dt.float32
BF16 = mybir.dt.bfloat16
I32 = mybir.dt.int32
AF = mybir.ActivationFunctionType
ALU = mybir.AluOpType

L = 4096
DM = 1024
NT = 8
DFF = 2816
EPS = 1e-6
W_IN0 = 5648
C_Z, C_XBC, C_DT, C_Q, C_K, C_V = 0, 1024, 2560, 2576, 3600, 4624


class Ev:
    __slots__ = ("key", "sem", "val", "eng", "clock")

    def __init__(self, key, sem, val, eng, clock):
        self.key, self.sem, self.val, self.eng, self.clock = key, sem, val, eng, clock


class Buf:
    __slots__ = ("name", "w", "rd", "excl")

    def __init__(self, name="", excl=False):
        self.name, self.w, self.rd, self.excl = name, None, {}, excl


class Sched:
    NDMA = 8

    def __init__(self, nc, stack):
        self.nc = nc
        self.engs = {"pe": nc.tensor, "act": nc.scalar, "dve": nc.vector,
                     "pool": nc.gpsimd, "sp": nc.sync}
        self.sem, self.cnt, self.pending, self.last_ins = {}, {}, {}, {}
        self.seen = {e: {} for e in self.engs}
        for e in ("pe", "act", "dve", "pool"):
            self.sem[e] = stack.enter_context(nc.semaphore("s_" + e))
            self.cnt[e] = 0
            self.pending[e] = []
            self.last_ins[e] = None
        self.dsem, self.dcnt, self.drr = {}, {}, {}
        for q in ("sp", "pool"):
            self.dsem[q] = []
            for i in range(self.NDMA):
                k = "d_%s%d" % (q, i)
                self.dsem[q].append((k, stack.enter_context(nc.semaphore(k))))
                self.dcnt[k] = 0
            self.drr[q] = 0
        self.n_wait = 0
        self.n_ins = 0

    def _need(self, e, ev):
        if ev is None:
            return
        if ev.val is None:
            self._force(ev.eng)
        seen = self.seen[e]
        if seen.get(ev.key, 0) >= ev.val:
            return
        self.engs[e].wait_ge(ev.sem, ev.val)
        self.n_wait += 1
        for k, v in ev.clock.items():
            if seen.get(k, 0) < v:
                seen[k] = v

    def _force(self, e):
        if not self.pending[e]:
            return
        self.last_ins[e].then_inc(self.sem[e], 1)
        self._signal(e)

    def _signal(self, e):
        self.cnt[e] += 1
        v = self.cnt[e]
        clock = dict(self.seen[e])
        clock["s_" + e] = v
        for ev in self.pending[e]:
            ev.val = v
            ev.clock = clock
        self.pending[e] = []
        self.last_ins[e] = None

    def _deps(self, e, reads, writes, is_dma):
        for b in reads:
            if b.w is not None:
                self._need(e, b.w)
            if b.excl:
                for ev in list(b.rd.values()):
                    if ev.eng != e:
                        self._need(e, ev)
        for b in writes:
            if b.w is not None and (is_dma or b.w.eng != e or e == "pool"):
                self._need(e, b.w)
            for ev in list(b.rd.values()):
                if is_dma or ev.eng != e or e == "pool":
                    self._need(e, ev)

    def _record(self, ev, reads, writes):
        for b in reads:
            b.rd[ev.key] = ev
        for b in writes:
            b.w = ev
            b.rd = {}

    def op(self, e, fn, reads=(), writes=(), sig=True):
        self._deps(e, reads, writes, False)
        ins = fn()
        self.n_ins += 1
        ev = Ev("s_" + e, self.sem[e], None, e, None)
        self.pending[e].append(ev)
        self.last_ins[e] = ins
        if sig:
            ins.then_inc(self.sem[e], 1)
            self._signal(e)
        self._record(ev, reads, writes)
        return ev

    def dma(self, q, out, in_, reads=(), writes=(), **kw):
        i = self.drr[q]
        self.drr[q] = (i + 1) % self.NDMA
        key, sem = self.dsem[q][i]
        prev = self.dcnt[key]
        seen = self.seen[q]
        if prev > 0 and seen.get(key, 0) < prev:
            self.engs[q].wait_ge(sem, prev)
            self.n_wait += 1
            seen[key] = prev
        self._deps(q, reads, writes, True)
        ins = self.engs[q].dma_start(out=out, in_=in_, **kw)
        ins.then_inc(sem, 16)
        self.n_ins += 1
        self.dcnt[key] = prev + 16
        clock = dict(seen)
        clock[key] = prev + 16
        ev = Ev(key, sem, prev + 16, None, clock)
        self._record(ev, reads, writes)
        return ev

    def barrier(self, engines=("pe", "act", "dve", "pool", "sp")):
        for e in ("pe", "act", "dve", "pool"):
            self._force(e)
        evs = []
        for e in ("pe", "act", "dve", "pool"):
            if self.cnt[e] > 0:
                evs.append(Ev("s_" + e, self.sem[e], self.cnt[e], e, {"s_" + e: self.cnt[e]}))
        for q in self.dsem:
            for key, sem in self.dsem[q]:
                if self.dcnt[key] > 0:
                    evs.append(Ev(key, sem, self.dcnt[key], None, {key: self.dcnt[key]}))
        for e in engines:
            for ev in evs:
                if ev.eng != e:
                    self._need(e, ev)


_UID = [0]


class Ring:
    def __init__(self, nc, st, name, n, shape, dt):
        _UID[0] += 1
        name = "%s_%d_" % (name, _UID[0])
        self.t = [st.enter_context(nc.sbuf_tensor("%s%d" % (name, i), shape, dt)) for i in range(n)]
        self.b = [Buf("%s%d" % (name, i)) for i in range(n)]
        self.i = 0

    def next(self):
        i = self.i
        self.i = (i + 1) % len(self.t)
        return self.t[i], self.b[i]


class PRing:
    def __init__(self, nc, st, name, n, shape, dt=F32):
        _UID[0] += 1
        name = "%s_%d_" % (name, _UID[0])
        self.t = [st.enter_context(nc.psum_tensor("%s%d" % (name, i), shape, dt)) for i in range(n)]
        self.b = [Buf("%s%d" % (name, i), excl=True) for i in range(n)]
        self.i = 0

    def next(self):
        i = self.i
        self.i = (i + 1) % len(self.t)
        return self.t[i], self.b[i]


def bc(ap, shape):
    return ap.to_broadcast(shape)


class _Stop(Exception):
    pass


def build_program(layers=(0, 1), dbg=False, stop=None, hstop=None):
    nc = bass.Bass("TRN2", target_bir_lowering=False)

    def din(name, shape):
        return nc.dram_tensor(name, shape, F32, kind="ExternalInput").ap()

    x = din("x", [L, DM])
    mix_norm = din("mix_norm", [2, DM])
    ffn_norm = din("ffn_norm", [2, DM])
    hy_w_in = din("hy_w_in", [1, DM, W_IN0])
    hy_conv_w = din("hy_conv_w", [1, 4, 1536])
    hy_conv_b = din("hy_conv_b", [1, 1536])
    hy_dt_bias = din("hy_dt_bias", [1, 16])
    hy_a_log = din("hy_a_log", [1, 16])
    hy_d_skip = din("hy_d_skip", [1, 16])
    hy_ssd_norm = din("hy_ssd_norm", [1, 1024])
    hy_q_norm = din("hy_q_norm", [1, 64])
    hy_k_norm = din("hy_k_norm", [1, 64])
    hy_lq1 = din("hy_lambda_q1", [1, 64])
    hy_lk1 = din("hy_lambda_k1", [1, 64])
    hy_lq2 = din("hy_lambda_q2", [1, 64])
    hy_lk2 = din("hy_lambda_k2", [1, 64])
    hy_subln = din("hy_subln", [1, 128])
    hy_w_out = din("hy_w_out", [1, 2048, DM])
    sc_w_in = din("sc_w_in", [1, DM, 3072])
    sc_conv_w = din("sc_conv_w", [1, 3, 1024])
    sc_w_out = din("sc_w_out", [1, 1024, DM])
    ffn_w_up = din("ffn_w_up", [2, DM, 2 * DFF])
    ffn_conv_w = din("ffn_conv_w", [2, 3, 2 * DFF])
    ffn_conv_b = din("ffn_conv_b", [2, 2 * DFF])
    ffn_w_down = din("ffn_w_down", [2, DFF, DM])
    out = nc.dram_tensor("out", [L, DM], F32, kind="ExternalOutput").ap()

    skind = "ExternalOutput" if dbg else "Internal"

    def dscr(name, shape, dt=BF16):
        return nc.dram_tensor(name, shape, dt, kind=skind).ap()

    hA = dscr("hA", [L, DM], F32)
    hB = dscr("hB", [L, DM], F32)
    hC = dscr("hC", [L, DM], F32)
    ACT_S = dscr("ACT_S", [22, 128, L])
    XBC_S = dscr("XBC_S", [12, 128, L])
    QT_S = dscr("QT_S", [8, 128, L])
    KT_S = dscr("KT_S", [8, 128, L])
    ZS_S = dscr("ZS_S", [32, 128, 1024])
    V_S = dscr("V_S", [32, 128, 1024])

    with ExitStack() as top:
        S = Sched(nc, top)
        V, A, G, P = nc.vector, nc.scalar, nc.gpsimd, nc.tensor

        def sbt(st, name, shape, dt=F32):
            _UID[0] += 1
            return st.enter_context(nc.sbuf_tensor("%s_%d" % (name, _UID[0]), shape, dt))

        def pst(st, name, shape, dt=F32):
            _UID[0] += 1
            return st.enter_context(nc.psum_tensor("%s_%d" % (name, _UID[0]), shape, dt))

        hnT = sbt(top, "hnT", [128, 8, L], BF16)
        B_hn = [Buf("hn%d" % i) for i in range(32)]
        WR = {"t": None}
        B_Wres = Buf("Wres")

        def open_wres(st):
            WR["t"] = sbt(st, "Wres", [128, 22, DM], BF16)
        identf = sbt(top, "identf", [128, 128]); B_c = Buf("consts")
        identb = sbt(top, "identb", [128, 128], BF16)
        triT_f = sbt(top, "triT_f", [128, 128])
        triT_b = sbt(top, "triT_b", [128, 128], BF16)
        ones_f = sbt(top, "ones_f", [128, 128])
        epsc = sbt(top, "epsc", [128, 1])
        normw = sbt(top, "normw", [128, 4, 8])
        fcw = sbt(top, "fcw", [128, 2, 3, 44])
        fcb = sbt(top, "fcb", [128, 2, 44])
        scw = sbt(top, "scw", [128, 3, 8])

        S.op("pool", lambda: G.memset(identf[:], 1.0), writes=[B_c])
        S.op("pool", lambda: G.affine_select(out=identf[:], in_=identf[:], pattern=[[-1, 128]],
                                             compare_op=ALU.is_equal, fill=0.0, base=0,
                                             channel_multiplier=1), reads=[B_c], writes=[B_c])
        S.op("pool", lambda: G.memset(triT_f[:], 1.0), writes=[B_c])
        S.op("pool", lambda: G.affine_select(out=triT_f[:], in_=triT_f[:], pattern=[[1, 128]],
                                             compare_op=ALU.is_ge, fill=0.0, base=0,
                                             channel_multiplier=-1), reads=[B_c], writes=[B_c])
        S.op("pool", lambda: G.memset(ones_f[:], 1.0), writes=[B_c])
        S.op("pool", lambda: G.memset(epsc[:], EPS), writes=[B_c])
        S.op("dve", lambda: V.tensor_copy(out=identb[:], in_=identf[:]), reads=[B_c], writes=[B_c])
        S.op("dve", lambda: V.tensor_copy(out=triT_b[:], in_=triT_f[:]), reads=[B_c], writes=[B_c])

        def load_cols(st_ring, ps_ring, dst, src_row, nch):
            stg, B_s = st_ring.next()
            S.dma("sp", stg[0:nch, :], src_row.rearrange("(c p) -> c p", p=128), writes=[B_s])
            ps, B_p = ps_ring.next()
            S.op("pe", lambda: P.matmul(ps[:, 0:nch], lhsT=stg[0:nch, :], rhs=identf[0:nch, 0:nch],
                                        start=True, stop=True), reads=[B_s, B_c], writes=[B_p])
            S.op("dve", lambda: V.tensor_copy(out=dst, in_=ps[:, 0:nch]), reads=[B_p], writes=[B_c])

        with ExitStack() as ph:
            stg_r = Ring(nc, ph, "stg", 3, [64, 128], F32)
            ps_r = PRing(nc, ph, "psc", 2, [128, 512])
            for i, (src, l) in enumerate([(mix_norm, 0), (ffn_norm, 0), (mix_norm, 1), (ffn_norm, 1)]):
                load_cols(stg_r, ps_r, normw[:, i, :], src[l, :], 8)
            for l in range(2):
                for k in range(3):
                    load_cols(stg_r, ps_r, fcw[:, l, k, :], ffn_conv_w[l, k, :], 44)
                load_cols(stg_r, ps_r, fcb[:, l, :], ffn_conv_b[l, :], 44)
            for k in range(3):
                load_cols(stg_r, ps_r, scw[:, k, :], sc_conv_w[0, k, :], 8)
            S.barrier()

        def load_Wres(w2d, nch):
            src = w2d.rearrange("(c p) n -> p c n", p=128)
            step = 4
            for c0 in range(0, nch, step):
                c1 = min(nch, c0 + step)
                S.dma("pool", WR["t"][:, c0:c1, :], src[:, c0:c1, :], writes=[B_Wres])

        def rstd(dst, src, scale, B_):
            S.op("act", lambda: A.activation(out=dst, in_=src, func=AF.Ln, bias=epsc[:, 0:1], scale=scale),
                 reads=[B_, B_c], writes=[B_])
            S.op("act", lambda: A.activation(out=dst, in_=dst, func=AF.Exp, scale=-0.5), reads=[B_], writes=[B_])

        class NormRes:
            def __init__(self, st):
                self.junk = sbt(st, "nr_junk", [128, DM], BF16); self.B_junk = Buf()
                self.xn = Ring(nc, st, "nr_xn", 2, [128, DM], BF16)
                self.sm = Ring(nc, st, "nr_sm", 3, [128, 2], F32)
                self.ps = PRing(nc, st, "nr_ps", 2, [128, 8, 128], BF16)

        def norm_to_hnT(nr, h_sb, B_h, widx, t128):
            sm, B_sm = nr.sm.next()
            S.op("act", lambda: A.activation(out=nr.junk[:], in_=h_sb, func=AF.Square,
                                             accum_out=sm[:, 0:1]),
                 reads=[B_h], writes=[nr.B_junk, B_sm])
            rstd(sm[:, 1:2], sm[:, 0:1], 1.0 / DM, B_sm)
            xn, B_xn = nr.xn.next()
            S.op("dve", lambda: V.tensor_scalar(out=xn[:], in0=h_sb, scalar1=sm[:, 1:2], scalar2=None,
                                                op0=ALU.mult), reads=[B_h, B_sm], writes=[B_xn])
            ps, B_ps = nr.ps.next()
            for c in range(8):
                S.op("pe", lambda c=c: P.transpose(out=ps[:, c, :], in_=xn[:, c * 128:(c + 1) * 128],
                                                   identity=identb[:]),
                     reads=[B_xn, B_c], writes=[B_ps], sig=(c == 7))
            S.op("dve", lambda: V.tensor_tensor(out=hnT[:, :, t128 * 128:(t128 + 1) * 128], in0=ps[:],
                                                in1=bc(normw[:, widx, :].unsqueeze(2), [128, 8, 128]),
                                                op=ALU.mult),
                 reads=[B_ps, B_c], writes=[B_hn[t128]])

        def phase_norm_in(h_src, widx):
            with ExitStack() as ph:
                nr = NormRes(ph)
                hr = Ring(nc, ph, "ni_h", 3, [128, DM], F32)
                for t in range(32):
                    h, B_h = hr.next()
                    S.dma("sp", h[:], h_src[t * 128:(t + 1) * 128, :], writes=[B_h])
                    norm_to_hnT(nr, h[:], B_h, widx, t)
                S.barrier()

        def phase_tm(nch, h_src, h_dst, widx_next):
            with ExitStack() as ph:
                nr = NormRes(ph) if widx_next is not None else None
                ar = Ring(nc, ph, "tm_a", 2, [128, nch, 512], BF16)
                hr = Ring(nc, ph, "tm_h", 3, [128, DM], F32)
                hn = Ring(nc, ph, "tm_hn", 3, [128, DM], F32)
                pr = PRing(nc, ph, "tm_ps", 4, [128, 512])
                src = ACT_S.rearrange("c p t -> p c t")
                pending_epi = [None]
                for tt in range(NT):
                    a, B_a = ar.next()
                    half = (nch + 1) // 2
                    S.dma("sp", a[:, 0:half, :], src[:, 0:half, tt * 512:(tt + 1) * 512], writes=[B_a])
                    S.dma("sp", a[:, half:nch, :], src[:, half:nch, tt * 512:(tt + 1) * 512], writes=[B_a])
                    for sub in range(4):
                        t128 = tt * 4 + sub
                        h, B_h = hr.next()
                        S.dma("sp", h[:], h_src[t128 * 128:(t128 + 1) * 128, :], writes=[B_h])
                        pss = [pr.next(), pr.next()]
                        for c in range(nch):
                            for hf in range(2):
                                ps, B_ps = pss[hf]
                                S.op("pe", lambda c=c, hf=hf, ps=ps: P.matmul(
                                    ps[:], lhsT=a[:, c, sub * 128:(sub + 1) * 128],
                                    rhs=WR["t"][:, c, hf * 512:(hf + 1) * 512],
                                    start=(c == 0), stop=(c == nch - 1)),
                                    reads=[B_a, B_Wres], writes=[B_ps], sig=(c == nch - 1))
                        def epi(pss=pss, h=h, B_h=B_h, t128=t128):
                            o, B_o = hn.next()
                            for hf in range(2):
                                ps, B_ps = pss[hf]
                                S.op("dve", lambda hf=hf, ps=ps: V.tensor_tensor(
                                    out=o[:, hf * 512:(hf + 1) * 512], in0=ps[:],
                                    in1=h[:, hf * 512:(hf + 1) * 512], op=ALU.add),
                                    reads=[B_ps, B_h], writes=[B_o])
                            S.dma("sp", h_dst[t128 * 128:(t128 + 1) * 128, :], o[:], reads=[B_o])
                            if nr is not None:
                                norm_to_hnT(nr, o[:], B_o, widx_next, t128)
                        if pending_epi[0] is not None:
                            pending_epi[0]()
                        pending_epi[0] = epi
                pending_epi[0]()
                S.barrier()

        def fm_matmuls(wt, n0, tt, ps, B_ps, B_w):
            for k in range(8):
                S.op("pe", lambda k=k: P.matmul(ps[:], lhsT=wt[:, k, n0:n0 + 128],
                                                rhs=hnT[:, k, tt * 512:(tt + 1) * 512],
                                                start=(k == 0), stop=(k == 7)),
                     reads=[B_w] + B_hn[tt * 4:tt * 4 + 4], writes=[B_ps], sig=(k == 7))

        def wload(wr, w2d, col0, ncol=128):
            wt, B_w = wr.next()
            S.dma("pool", wt[:, :, 0:ncol], w2d.rearrange("(k p) n -> p k n", p=128)[:, :, col0:col0 + ncol],
                  writes=[B_w])
            return wt, B_w

        def phase_ffn_up(l, w_next, nch_next):
            w2d = ffn_w_up[l]
            with ExitStack() as ph:
                wr = Ring(nc, ph, "fu_w", 6, [128, 8, 128], BF16)
                pr = PRing(nc, ph, "fu_ps", 6, [128, 512])
                xg = Ring(nc, ph, "fu_xg", 1, [128, L + 2], F32)
                xv = Ring(nc, ph, "fu_xv", 1, [128, L + 2], F32)
                yr = Ring(nc, ph, "fu_y", 6, [128, 512], F32)
                sr = Ring(nc, ph, "fu_s", 3, [128, 512], F32)
                gr = Ring(nc, ph, "fu_g", 2, [128, L], BF16)
                for r in (xg, xv):
                    for t_, b_ in zip(r.t, r.b):
                        S.op("pool", lambda t_=t_: G.memset(t_[:, 0:2], 0.0), writes=[b_])
                pend = [(wload(wr, w2d, 0), wload(wr, w2d, DFF))]
                load_Wres(w_next, nch_next)
                for i in range(22):
                    if i + 1 < 22:
                        pend.append((wload(wr, w2d, (i + 1) * 128), wload(wr, w2d, DFF + (i + 1) * 128)))
                    (wg, B_wg), (wv, B_wv) = pend.pop(0)
                    Xg, B_xg = xg.next()
                    Xv, B_xv = xv.next()
                    Gt, B_g = gr.next()
                    cg, cv = i, 22 + i
                    for tt in range(NT):
                        sl = slice(tt * 512, (tt + 1) * 512)
                        psg, B_pg = pr.next()
                        psv, B_pv = pr.next()
                        fm_matmuls(wg, 0, tt, psg, B_pg, B_wg)
                        fm_matmuls(wv, 0, tt, psv, B_pv, B_wv)
                        Yg, B_yg = yr.next()
                        Yv, B_yv = yr.next()
                        S.op("act", lambda: A.activation(out=Yg[:], in_=psg[:], func=AF.Identity,
                                                         bias=fcb[:, l, cg:cg + 1], scale=fcw[:, l, 2, cg:cg + 1]),
                             reads=[B_pg, B_c], writes=[B_yg])
                        S.op("act", lambda: A.copy(out=Xg[:, 2 + tt * 512:2 + (tt + 1) * 512], in_=psg[:]),
                             reads=[B_pg], writes=[B_xg])
                        S.op("act", lambda: A.activation(out=Yv[:], in_=psv[:], func=AF.Identity,
                                                         bias=fcb[:, l, cv:cv + 1], scale=fcw[:, l, 2, cv:cv + 1]),
                             reads=[B_pv, B_c], writes=[B_yv])
                        S.op("dve", lambda: V.tensor_copy(out=Xv[:, 2 + tt * 512:2 + (tt + 1) * 512], in_=psv[:]),
                             reads=[B_pv], writes=[B_xv])
                        for (e, X_, B_x, Y_, B_y, c_) in (("dve", Xg, B_xg, Yg, B_yg, cg), ("dve", Xv, B_xv, Yv, B_yv, cv)):
                            for k in (1, 0):
                                en = "dve"
                                E_ = G if en == "pool" else V
                                S.op(en, lambda E_=E_, X_=X_, Y_=Y_, k=k, c_=c_: E_.scalar_tensor_tensor(
                                    out=Y_[:], in0=X_[:, tt * 512 + k:tt * 512 + k + 512],
                                    scalar=fcw[:, l, k, c_:c_ + 1], in1=Y_[:], op0=ALU.mult, op1=ALU.add),
                                    reads=[B_x, B_c, B_y], writes=[B_y])
                        St, B_s = sr.next()
                        S.op("act", lambda: A.activation(out=St[:], in_=Yg[:], func=AF.Silu),
                             reads=[B_yg], writes=[B_s])
                        S.op("pool", lambda: G.tensor_tensor(out=Gt[:, sl], in0=St[:], in1=Yv[:], op=ALU.mult),
                             reads=[B_s, B_yv], writes=[B_g])
                    S.dma("sp", ACT_S[i], Gt[:], reads=[B_g])
                S.barrier()

        def phase_shortconv(w_next, nch_next):
            w2d = sc_w_in[0]
            with ExitStack() as ph:
                wr = Ring(nc, ph, "sc_w", 6, [128, 8, 128], BF16)
                pr = PRing(nc, ph, "sc_ps", 6, [128, 512])
                mr = Ring(nc, ph, "sc_m", 2, [128, L + 2], F32)
                ur = Ring(nc, ph, "sc_u", 3, [128, 512], F32)
                yr = Ring(nc, ph, "sc_y", 3, [128, 512], F32)
                rr = Ring(nc, ph, "sc_r", 2, [128, L], BF16)
                for t_, b_ in zip(mr.t, mr.b):
                    S.op("pool", lambda t_=t_: G.memset(t_[:, 0:2], 0.0), writes=[b_])
                load_Wres(w_next, nch_next)
                for i in range(8):
                    wb_, wc_, wu_ = (wload(wr, w2d, i * 128), wload(wr, w2d, 1024 + i * 128),
                                     wload(wr, w2d, 2048 + i * 128))
                    M, B_m = mr.next()
                    R, B_r = rr.next()
                    for tt in range(NT):
                        sl = slice(tt * 512, (tt + 1) * 512)
                        psb, B_pb = pr.next()
                        psc, B_pc = pr.next()
                        psu, B_pu = pr.next()
                        fm_matmuls(wc_[0], 0, tt, psc, B_pc, wc_[1])
                        fm_matmuls(wu_[0], 0, tt, psu, B_pu, wu_[1])
                        fm_matmuls(wb_[0], 0, tt, psb, B_pb, wb_[1])
                        U, B_u = ur.next()
                        S.op("act", lambda: A.copy(out=U[:], in_=psu[:]), reads=[B_pu], writes=[B_u])
                        S.op("dve", lambda: V.tensor_tensor(out=M[:, 2 + tt * 512:2 + (tt + 1) * 512], in0=psc[:],
                                                            in1=U[:], op=ALU.mult),
                             reads=[B_pc, B_u], writes=[B_m])
                        Y, B_y = yr.next()
                        S.op("act", lambda: A.activation(out=Y[:], in_=M[:, 2 + tt * 512:2 + (tt + 1) * 512],
                                                         func=AF.Copy, scale=scw[:, 2, i:i + 1]),
                             reads=[B_m, B_c], writes=[B_y])
                        for k in (1, 0):
                            S.op("dve", lambda k=k: V.scalar_tensor_tensor(
                                out=Y[:], in0=M[:, tt * 512 + k:tt * 512 + k + 512], scalar=scw[:, k, i:i + 1],
                                in1=Y[:], op0=ALU.mult, op1=ALU.add), reads=[B_m, B_c, B_y], writes=[B_y])
                        S.op("dve", lambda: V.tensor_tensor(out=R[:, sl], in0=psb[:], in1=Y[:], op=ALU.mult),
                             reads=[B_pb, B_y], writes=[B_r])
                    S.dma("sp", ACT_S[i], R[:], reads=[B_r])
                S.barrier()

        def phase_hybrid(h_in_ap, h_out_ap, widx_next):
            hyb = {}
            w2d = hy_w_in[0]
            hst = ExitStack()
            cst = ExitStack()
            Bh = Buf("hyb_consts")
            blk = sbt(hst, "blk", [128, 128], BF16)
            prot = sbt(hst, "prot", [128, 128], BF16)
            hcw = sbt(hst, "hcw", [128, 4, 12])
            hcb = sbt(hst, "hcb", [128, 12])
            qkw = sbt(hst, "qkw", [128, 2])
            dt_tm = sbt(hst, "dt_tm", [128, 32, 16])
            a_tm = sbt(hst, "a_tm", [128, 32, 16])
            acs = sbt(hst, "acs", [128, 32, 16])
            eA = sbt(hst, "eA", [128, 32, 16])
            dA = sbt(hst, "dA", [128, 32, 16])
            cdr = sbt(hst, "cdr", [128, 32, 16])
            bc16 = sbt(hst, "bc16", [128, 3, 16])
            ssdw = sbt(hst, "ssdw", [128, 1024])
            sublw = sbt(hst, "sublw", [128, 128])
            nlam = sbt(hst, "nlam", [128, 2])
            B_dt = Buf("dtstuff")
            cosT = sbt(cst, "cosT", [128, L])
            sinT = sbt(cst, "sinT", [128, L])

            with ExitStack() as ph:
                tf = sbt(ph, "c_tf", [128, 128])
                tf2 = sbt(ph, "c_tf2", [128, 128])
                B_t = Buf()
                S.op("pool", lambda: G.memset(tf[:], 0.0), writes=[B_t])
                S.op("pool", lambda: G.memset(tf[0:64, 0:64], 1.0), writes=[B_t])
                S.op("pool", lambda: G.memset(tf[64:128, 64:128], 1.0), writes=[B_t])
                S.op("dve", lambda: V.tensor_copy(out=blk[:], in_=tf[:]), reads=[B_t], writes=[Bh])
                S.op("pool", lambda: G.memset(tf[:], 1.0), reads=[Bh], writes=[B_t])
                S.op("pool", lambda: G.affine_select(out=tf[:], in_=tf[:], pattern=[[-1, 128]], compare_op=ALU.is_equal,
                                                     fill=0.0, base=32, channel_multiplier=1), reads=[B_t], writes=[B_t])
                S.op("pool", lambda: G.memset(tf2[:], -1.0), writes=[B_t])
                S.op("pool", lambda: G.affine_select(out=tf2[:], in_=tf2[:], pattern=[[-1, 128]], compare_op=ALU.is_equal,
                                                     fill=0.0, base=-32, channel_multiplier=1), reads=[B_t], writes=[B_t])
                for c0 in (0, 64):
                    S.op("pool", lambda c0=c0: G.memset(tf[:, c0:c0 + 32], 0.0), reads=[B_t], writes=[B_t])
                    S.op("pool", lambda c0=c0: G.memset(tf2[:, c0 + 32:c0 + 64], 0.0), reads=[B_t], writes=[B_t])
                S.op("dve", lambda: V.tensor_tensor(out=tf[:], in0=tf[:], in1=tf2[:], op=ALU.add), reads=[B_t], writes=[B_t])
                S.op("dve", lambda: V.tensor_copy(out=prot[:], in_=tf[:]), reads=[B_t], writes=[Bh])
                pi_ = sbt(ph, "c_pi", [128, 1], I32)
                pf_ = sbt(ph, "c_pf", [128, 2])
                S.op("pool", lambda: G.iota(pi_[:], pattern=[[0, 1]], base=0, channel_multiplier=1), writes=[B_t])
                S.op("dve", lambda: V.tensor_single_scalar(out=pi_[:], in_=pi_[:], scalar=31, op=ALU.bitwise_and),
                     reads=[B_t], writes=[B_t])
                S.op("dve", lambda: V.tensor_copy(out=pf_[:, 0:1], in_=pi_[:]), reads=[B_t], writes=[B_t])
                S.op("dve", lambda: V.tensor_scalar(out=pf_[:, 0:1], in0=pf_[:, 0:1], scalar1=-math.log(10000.0) / 32.0,
                                                    scalar2=-math.log(2 * math.pi), op0=ALU.mult, op1=ALU.add),
                     reads=[B_t], writes=[B_t])
                S.op("act", lambda: A.activation(out=pf_[:, 1:2], in_=pf_[:, 0:1], func=AF.Exp), reads=[B_t], writes=[B_t])
                ti = sbt(ph, "c_ti", [128, L], I32)
                r_ = sbt(ph, "c_r", [128, L])
                kf = sbt(ph, "c_kf", [128, L])
                S.op("pool", lambda: G.iota(ti[:], pattern=[[1, L]], base=0, channel_multiplier=0), writes=[B_t])
                S.op("dve", lambda: V.tensor_copy(out=r_[:], in_=ti[:]), reads=[B_t], writes=[B_t])
                S.op("dve", lambda: V.tensor_scalar(out=r_[:], in0=r_[:], scalar1=pf_[:, 1:2], scalar2=None, op0=ALU.mult),
                     reads=[B_t], writes=[B_t])
                for (dst, off) in ((sinT, 0.0), (cosT, 0.25)):
                    S.op("dve", lambda off=off: V.tensor_scalar(out=kf[:], in0=r_[:], scalar1=off, scalar2=None, op0=ALU.add),
                         reads=[B_t, Bh], writes=[B_t])
                    S.op("dve", lambda: V.tensor_copy(out=ti[:], in_=kf[:]), reads=[B_t], writes=[B_t])
                    S.op("dve", lambda dst=dst: V.tensor_copy(out=dst[:], in_=ti[:]), reads=[B_t], writes=[Bh])
                    S.op("dve", lambda dst=dst: V.tensor_tensor(out=kf[:], in0=kf[:], in1=dst[:], op=ALU.subtract),
                         reads=[B_t, Bh], writes=[B_t])
                    S.op("dve", lambda dst=dst: V.tensor_single_scalar(out=dst[:], in_=kf[:], scalar=0.5, op=ALU.is_gt),
                         reads=[B_t], writes=[Bh])
                    S.op("dve", lambda dst=dst: V.tensor_tensor(out=kf[:], in0=kf[:], in1=dst[:], op=ALU.subtract),
                         reads=[B_t, Bh], writes=[B_t])
                    S.op("act", lambda dst=dst: A.activation(out=dst[:], in_=kf[:], func=AF.Sin, scale=2 * math.pi),
                         reads=[B_t], writes=[Bh])
                stg_r = Ring(nc, ph, "hstg", 3, [64, 128], F32)
                ps_r = PRing(nc, ph, "hpsc", 2, [128, 512])

                def lc(dst, src_row, nch):
                    stg, B_s = stg_r.next()
                    S.dma("sp", stg[0:nch, :], src_row.rearrange("(c p) -> c p", p=128), writes=[B_s])
                    ps, B_p = ps_r.next()
                    S.op("pe", lambda: P.matmul(ps[:, 0:nch], lhsT=stg[0:nch, :], rhs=identf[0:nch, 0:nch],
                                                start=True, stop=True), reads=[B_s, B_c], writes=[B_p])
                    S.op("dve", lambda: V.tensor_copy(out=dst, in_=ps[:, 0:nch]), reads=[B_p], writes=[Bh])
                for k in range(4):
                    lc(hcw[:, k, :], hy_conv_w[0, k, :], 12)
                lc(hcb[:, :], hy_conv_b[0, :], 12)
                for j, src in enumerate((hy_q_norm, hy_k_norm)):
                    for hf in range(2):
                        S.dma("sp", qkw[hf * 64:(hf + 1) * 64, j:j + 1], src[0, :].rearrange("(p o) -> p o", o=1),
                              writes=[Bh])
                S.op("dve", lambda: V.tensor_scalar(out=qkw[:, 0:1], in0=qkw[:, 0:1], scalar1=0.125, scalar2=None,
                                                    op0=ALU.mult), reads=[Bh], writes=[Bh])
                for j, src in enumerate((hy_dt_bias, hy_a_log, hy_d_skip)):
                    S.dma("sp", bc16[:, j, :], src[0:1, :].partition_broadcast(128), writes=[Bh])
                S.op("act", lambda: A.activation(out=bc16[:, 1, :], in_=bc16[:, 1, :], func=AF.Exp), reads=[Bh], writes=[Bh])
                S.op("dve", lambda: V.tensor_scalar(out=bc16[:, 1, :], in0=bc16[:, 1, :], scalar1=-1.0, scalar2=None,
                                                    op0=ALU.mult), reads=[Bh], writes=[Bh])
                S.dma("sp", ssdw[:], hy_ssd_norm[0:1, :].partition_broadcast(128), writes=[Bh])
                S.dma("sp", sublw[:], hy_subln[0:1, :].partition_broadcast(128), writes=[Bh])
                lam_init = 0.8 - 0.6 * math.exp(-0.3 * 0)
                S.op("dve", lambda: V.tensor_scalar(out=sublw[:], in0=sublw[:], scalar1=(1.0 - lam_init), scalar2=None,
                                                    op0=ALU.mult), reads=[Bh], writes=[Bh])
                lt = sbt(ph, "c_lt", [128, 4, 64])
                ls = sbt(ph, "c_ls", [128, 4])
                for j, src in enumerate((hy_lq1, hy_lk1, hy_lq2, hy_lk2)):
                    S.dma("sp", lt[:, j, :], src[0:1, :].partition_broadcast(128), writes=[B_t])
                for j in range(2):
                    S.op("dve", lambda j=j: V.tensor_tensor(out=lt[:, 2 * j, :], in0=lt[:, 2 * j, :], in1=lt[:, 2 * j + 1, :],
                                                            op=ALU.mult), reads=[B_t], writes=[B_t])
                    S.op("act", lambda j=j: A.activation(out=lt[:, 2 * j + 1, :], in_=lt[:, 2 * j, :], func=AF.Identity,
                                                         accum_out=ls[:, j:j + 1]), reads=[B_t], writes=[B_t])
                S.op("act", lambda: A.activation(out=ls[:, 2:4], in_=ls[:, 0:2], func=AF.Exp), reads=[B_t], writes=[B_t])
                S.op("dve", lambda: V.tensor_tensor(out=nlam[:, 0:1], in0=ls[:, 3:4], in1=ls[:, 2:3], op=ALU.subtract),
                     reads=[B_t], writes=[Bh])
                S.op("dve", lambda: V.tensor_scalar(out=nlam[:, 0:1], in0=nlam[:, 0:1], scalar1=-lam_init, scalar2=None,
                                                    op0=ALU.add), reads=[Bh], writes=[Bh])
                S.barrier()
            if hstop == 1:
                cst.close(); hst.close()
                return True

            with ExitStack() as ph:
                wr = Ring(nc, ph, "hb_w", 4, [128, 8, 128], BF16)
                pr = PRing(nc, ph, "hb_ps", 4, [128, 512])
                pr2 = PRing(nc, ph, "hb_ps2", 4, [128, 512])
                xr = Ring(nc, ph, "hb_x", 2, [128, L + 3], BF16)
                yr = Ring(nc, ph, "hb_y", 4, [128, 512], F32)
                orr = Ring(nc, ph, "hb_o", 2, [128, L], BF16)
                sq = Ring(nc, ph, "hb_sq", 3, [128, 512], BF16)
                rs = Ring(nc, ph, "hb_rs", 3, [128, 512], F32)
                qn = Ring(nc, ph, "hb_qn", 3, [128, 512], BF16)
                t1r = Ring(nc, ph, "hb_t1", 3, [128, 512], F32)
                t2r = Ring(nc, ph, "hb_t2", 3, [128, 512], F32)
                for t_, b_ in zip(xr.t, xr.b):
                    S.op("pool", lambda t_=t_: G.memset(t_[:, 0:3], 0.0), writes=[b_])
                jobs = [("xbc", i, C_XBC + i * 128) for i in range(12)] + \
                       [("q", i, C_Q + i * 128) for i in range(8)] + [("k", i, C_K + i * 128) for i in range(8)]
                pend = [wload(wr, w2d, jobs[0][2])]
                units = [(ji, tt) for ji in range(len(jobs)) for tt in range(NT)]
                js = {}
                ust = {}

                def stage1(u):
                    ji, tt = units[u]
                    if tt == 0:
                        if ji + 1 < len(jobs):
                            pend.append(wload(wr, w2d, jobs[ji + 1][2]))
                        js[ji] = {"w": pend.pop(0)}
                    wt, B_w = js[ji]["w"]
                    ps, B_ps = pr.next()
                    fm_matmuls(wt, 0, tt, ps, B_ps, B_w)
                    ust[u] = {"ps": (ps, B_ps)}

                def stage2(u):
                    ji, tt = units[u]
                    kind, i, col = jobs[ji]
                    sl = slice(tt * 512, (tt + 1) * 512)
                    ps, B_ps = ust[u]["ps"]
                    if tt == 0:
                        js[ji]["O"] = orr.next()
                        if kind == "xbc":
                            js[ji]["X"] = xr.next()
                    O, B_o = js[ji]["O"]
                    if kind == "xbc":
                        X, B_x = js[ji]["X"]
                        Y, B_y = yr.next()
                        S.op("act", lambda: A.activation(out=Y[:], in_=ps[:], func=AF.Identity,
                                                         bias=hcb[:, i:i + 1], scale=hcw[:, 3, i:i + 1]),
                             reads=[B_ps, Bh], writes=[B_y])
                        S.op("act", lambda: A.copy(out=X[:, 3 + tt * 512:3 + (tt + 1) * 512], in_=ps[:]),
                             reads=[B_ps], writes=[B_x])
                        for k in (2, 1, 0):
                            S.op("dve", lambda k=k: V.scalar_tensor_tensor(
                                out=Y[:], in0=X[:, tt * 512 + k:tt * 512 + k + 512], scalar=hcw[:, k, i:i + 1],
                                in1=Y[:], op0=ALU.mult, op1=ALU.add), reads=[B_x, Bh, B_y], writes=[B_y])
                        S.op("act", lambda: A.activation(out=O[:, sl], in_=Y[:], func=AF.Silu),
                             reads=[B_y], writes=[B_o])
                    else:
                        wcol = qkw[:, 0:1] if kind == "q" else qkw[:, 1:2]
                        s_, B_s = sq.next()
                        S.op("act", lambda: A.activation(out=s_[:], in_=ps[:], func=AF.Square),
                             reads=[B_ps], writes=[B_s])
                        ps2, B_p2 = pr2.next()
                        S.op("pe", lambda: P.matmul(ps2[:], lhsT=blk[:], rhs=s_[:], start=True, stop=True),
                             reads=[Bh, B_s], writes=[B_p2])
                        r1, B_r1 = rs.next()
                        S.op("act", lambda: A.activation(out=r1[:], in_=ps2[:], func=AF.Ln, bias=epsc[:, 0:1],
                                                         scale=1.0 / 64.0), reads=[B_p2, B_c], writes=[B_r1])
                        S.op("act", lambda: A.activation(out=r1[:], in_=r1[:], func=AF.Exp, scale=-0.5),
                             reads=[B_r1], writes=[B_r1])
                        q_, B_q = qn.next()
                        S.op("dve", lambda: V.scalar_tensor_tensor(out=q_[:], in0=ps[:], scalar=wcol, in1=r1[:],
                                                                   op0=ALU.mult, op1=ALU.mult),
                             reads=[B_ps, Bh, B_r1], writes=[B_q])
                        ust[u]["q"] = (q_, B_q)

                def stage3(u):
                    ji, tt = units[u]
                    kind, i, col = jobs[ji]
                    sl = slice(tt * 512, (tt + 1) * 512)
                    O, B_o = js[ji]["O"]
                    if kind != "xbc":
                        q_, B_q = ust[u]["q"]
                        ps3, B_p3 = pr2.next()
                        S.op("pe", lambda: P.matmul(ps3[:], lhsT=prot[:], rhs=q_[:], start=True, stop=True),
                             reads=[Bh, B_q], writes=[B_p3])
                        t1, B_t1 = t1r.next()
                        S.op("pool", lambda: G.tensor_tensor(out=t1[:], in0=q_[:], in1=cosT[:, sl], op=ALU.mult),
                             reads=[B_q, Bh], writes=[B_t1])
                        t2, B_t2 = t2r.next()
                        S.op("dve", lambda: V.tensor_tensor(out=t2[:], in0=ps3[:], in1=sinT[:, sl], op=ALU.mult),
                             reads=[B_p3, Bh], writes=[B_t2])
                        S.op("dve", lambda: V.tensor_tensor(out=O[:, sl], in0=t1[:], in1=t2[:], op=ALU.add),
                             reads=[B_t1, B_t2], writes=[B_o])
                    if tt == NT - 1:
                        dst = {"xbc": XBC_S, "q": QT_S, "k": KT_S}[kind]
                        S.dma("sp", dst[i], O[:], reads=[B_o])
                    del ust[u]

                NU = len(units)
                for st_ in range(NU + 2):
                    if st_ < NU:
                        stage1(st_)
                    if 0 <= st_ - 1 < NU:
                        stage2(st_ - 1)
                    if 0 <= st_ - 2 < NU:
                        stage3(st_ - 2)
                S.barrier()

            cst.close()
            if hstop == 2:
                hst.close()
                return True
            with ExitStack() as ph:
                wz = sbt(ph, "wz", [128, 8, 1024], BF16)
                wv = sbt(ph, "wv", [128, 8, 1024], BF16)
                wd = sbt(ph, "wd", [128, 8, 16], BF16)
                B_w = Buf()
                wsrc = w2d.rearrange("(k p) n -> p k n", p=128)
                for k0 in (0, 4):
                    S.dma("pool", wz[:, k0:k0 + 4, :], wsrc[:, k0:k0 + 4, C_Z:C_Z + 1024], writes=[B_w])
                    S.dma("pool", wv[:, k0:k0 + 4, :], wsrc[:, k0:k0 + 4, C_V:C_V + 1024], writes=[B_w])
                S.dma("pool", wd[:], wsrc[:, :, C_DT:C_DT + 16], writes=[B_w])
                pr = PRing(nc, ph, "tz_ps", 6, [128, 512])
                pdt = pst(ph, "tz_pdt", [128, 32, 16]); B_pdt = Buf(excl=True)
                zr = Ring(nc, ph, "tz_z", 3, [128, 1024], BF16)
                vr = Ring(nc, ph, "tz_v", 3, [128, 1024], BF16)
                for t in range(32):
                    pz = [pr.next(), pr.next()]
                    pv = [pr.next(), pr.next()]
                    for k in range(8):
                        lhs = hnT[:, k, t * 128:(t + 1) * 128]
                        for hf in range(2):
                            S.op("pe", lambda k=k, hf=hf, lhs=lhs: P.matmul(pz[hf][0][:], lhsT=lhs, rhs=wz[:, k, hf * 512:(hf + 1) * 512],
                                                                          start=(k == 0), stop=(k == 7)),
                                 reads=[B_hn[t], B_w], writes=[pz[hf][1]], sig=(k == 7))
                            S.op("pe", lambda k=k, hf=hf, lhs=lhs: P.matmul(pv[hf][0][:], lhsT=lhs, rhs=wv[:, k, hf * 512:(hf + 1) * 512],
                                                                          start=(k == 0), stop=(k == 7)),
                                 reads=[B_hn[t], B_w], writes=[pv[hf][1]], sig=(k == 7))
                        S.op("pe", lambda k=k, lhs=lhs: P.matmul(pdt[:, t, :], lhsT=lhs, rhs=wd[:, k, :],
                                                                 start=(k == 0), stop=(k == 7)),
                             reads=[B_hn[t], B_w], writes=[B_pdt], sig=(k == 7))
                    Z, B_z = zr.next()
                    Vt, B_v = vr.next()
                    for hf in range(2):
                        S.op("act", lambda hf=hf: A.activation(out=Z[:, hf * 512:(hf + 1) * 512], in_=pz[hf][0][:], func=AF.Silu),
                             reads=[pz[hf][1]], writes=[B_z])
                        S.op("dve", lambda hf=hf: V.tensor_copy(out=Vt[:, hf * 512:(hf + 1) * 512], in_=pv[hf][0][:]),
                             reads=[pv[hf][1]], writes=[B_v])
                    S.dma("sp", ZS_S[t], Z[:], reads=[B_z])
                    S.dma("sp", V_S[t], Vt[:], reads=[B_v])
                xb_ = sbt(ph, "dt_x", [128, 32, 16])
                ab_ = sbt(ph, "dt_a", [128, 32, 16])
                S.op("dve", lambda: V.tensor_tensor(out=xb_[:], in0=pdt[:], in1=bc(bc16[:, 0, :].unsqueeze(1), [128, 32, 16]),
                                                    op=ALU.add), reads=[B_pdt, Bh], writes=[B_dt])
                S.op("act", lambda: A.activation(out=ab_[:], in_=xb_[:], func=AF.Abs), reads=[B_dt], writes=[B_dt])
                S.op("act", lambda: A.activation(out=ab_[:], in_=ab_[:], func=AF.Exp, scale=-1.0), reads=[B_dt], writes=[B_dt])
                S.op("dve", lambda: V.tensor_scalar(out=ab_[:], in0=ab_[:], scalar1=1.0, scalar2=None, op0=ALU.add),
                     reads=[B_dt], writes=[B_dt])
                S.op("act", lambda: A.activation(out=ab_[:], in_=ab_[:], func=AF.Ln), reads=[B_dt], writes=[B_dt])
                S.op("dve", lambda: V.scalar_tensor_tensor(out=dt_tm[:], in0=xb_[:], scalar=0.0, in1=ab_[:], op0=ALU.max,
                                                           op1=ALU.add), reads=[B_dt], writes=[B_dt])
                S.op("dve", lambda: V.tensor_tensor(out=a_tm[:], in0=dt_tm[:], in1=bc(bc16[:, 1, :].unsqueeze(1), [128, 32, 16]),
                                                    op=ALU.mult), reads=[B_dt, Bh], writes=[B_dt])
                pa, B_pa = pr.next()
                pl, B_pl = pr.next()
                S.op("pe", lambda: P.matmul(pa[:], lhsT=triT_f[:], rhs=a_tm[:].rearrange("p c h -> p (c h)"),
                                            start=True, stop=True), reads=[B_c, B_dt], writes=[B_pa])
                S.op("pe", lambda: P.matmul(pl[:], lhsT=ones_f[:], rhs=a_tm[:].rearrange("p c h -> p (c h)"),
                                            start=True, stop=True), reads=[B_c, B_dt], writes=[B_pl])
                fl = lambda t_: t_[:].rearrange("p c h -> p (c h)")
                S.op("dve", lambda: V.tensor_copy(out=fl(acs), in_=pa[:]), reads=[B_pa], writes=[B_dt])
                S.op("act", lambda: A.activation(out=fl(eA), in_=pa[:], func=AF.Exp), reads=[B_pa], writes=[B_dt])
                S.op("act", lambda: A.activation(out=fl(cdr), in_=pl[:], func=AF.Exp), reads=[B_pl], writes=[B_dt])
                S.op("dve", lambda: V.tensor_tensor(out=fl(dA), in0=pl[:], in1=fl(acs), op=ALU.subtract),
                     reads=[B_pl, B_dt], writes=[B_dt])
                S.op("act", lambda: A.activation(out=fl(dA), in_=fl(dA), func=AF.Exp), reads=[B_dt], writes=[B_dt])
                S.barrier()
            if hstop == 3:
                hst.close()
                return True

            with ExitStack() as ph:
                xin = Ring(nc, ph, "sd_x", 2, [128, 12, 512], BF16)
                zin = Ring(nc, ph, "sd_z", 3, [128, 1024], BF16)
                ptr = PRing(nc, ph, "sd_ptr", 2, [128, 1024], BF16)
                pcb = pst(ph, "sd_pcb", [128, 4, 128]); B_pcb = Buf(excl=True)
                pR = PRing(nc, ph, "sd_pR", 1, [128, 8, 128])
                pya = pst(ph, "sd_pya", [128, 512]); B_pya = Buf(excl=True)
                pyb = pst(ph, "sd_pyb", [128, 512]); B_pyb = Buf(excl=True)
                pst_ = pst(ph, "sd_pst", [128, 512]); B_pst = Buf(excl=True)
                xs_tm = Ring(nc, ph, "sd_xs", 2, [128, 1024], BF16)
                b_tm = Ring(nc, ph, "sd_b", 2, [128, 256], BF16)
                Xr = Ring(nc, ph, "sd_X", 2, [128, 1024], BF16)
                Xdr = Ring(nc, ph, "sd_Xd", 2, [128, 1024], BF16)
                cbm = Ring(nc, ph, "sd_cbm", 2, [128, 2, 128], BF16)
                rhsR = Ring(nc, ph, "sd_rr", 1, [128, 8, 128], F32)
                segr = Ring(nc, ph, "sd_seg", 1, [128, 8, 128], F32)
                Er = Ring(nc, ph, "sd_E", 2, [128, 8, 128], BF16)
                Wr = Ring(nc, ph, "sd_W", 2, [128, 8, 128], BF16)
                yr = Ring(nc, ph, "sd_y", 2, [128, 512], F32)
                y2r = Ring(nc, ph, "sd_y2", 1, [128, 512], F32)
                ynr = Ring(nc, ph, "sd_yn", 2, [128, 512], BF16)
                smr = Ring(nc, ph, "sd_sm", 3, [128, 2], F32)
                junk = sbt(ph, "sd_junk", [128, 512], BF16); B_junk = Buf()
                prev = sbt(ph, "sd_prev", [128, 1024]); B_prev = Buf()
                prevb = sbt(ph, "sd_prevb", [128, 1024], BF16); B_prevb = Buf()
                yT = Ring(nc, ph, "sd_yT", 2, [128, 8, 512], BF16)
                S.op("pool", lambda: G.memset(prev[:], 0.0), writes=[B_prev])
                S.op("pool", lambda: G.memset(prevb[:], 0.0), writes=[B_prevb])
                xsrc = XBC_S.rearrange("c p t -> p c t")
                chs = {}
                cur = {}

                def prep(c):
                    cc = c % 4
                    if cc == 0:
                        Xin, B_xin = xin.next()
                        S.dma("sp", Xin[:, 0:6, :], xsrc[:, 0:6, c * 128:c * 128 + 512], writes=[B_xin])
                        S.dma("sp", Xin[:, 6:12, :], xsrc[:, 6:12, c * 128:c * 128 + 512], writes=[B_xin])
                        cur["Xin"] = (Xin, B_xin)
                        cur["YT"] = yT.next()
                    Xin, B_xin = cur["Xin"]
                    csl = slice(cc * 128, (cc + 1) * 128)
                    Zt, B_z = zin.next()
                    S.dma("sp", Zt[:], ZS_S[c], writes=[B_z])
                    pt, B_pt = ptr.next()
                    for j in range(8):
                        S.op("pe", lambda j=j: P.transpose(out=pt[:, j * 128:(j + 1) * 128], in_=Xin[:, j, csl], identity=identb[:]),
                             reads=[B_xin, B_c], writes=[B_pt], sig=(j == 7))
                    xs, B_xs = xs_tm.next()
                    S.op("act", lambda: A.copy(out=xs[:], in_=pt[:]), reads=[B_pt], writes=[B_xs])
                    pt2, B_pt2 = ptr.next()
                    for g in range(2):
                        S.op("pe", lambda g=g: P.transpose(out=pt2[:, g * 128:(g + 1) * 128], in_=Xin[:, 8 + g, csl], identity=identb[:]),
                             reads=[B_xin, B_c], writes=[B_pt2], sig=(g == 1))
                    bt, B_bt = b_tm.next()
                    S.op("act", lambda: A.copy(out=bt[:], in_=pt2[:, 0:256]), reads=[B_pt2], writes=[B_bt])
                    X, B_X = Xr.next()
                    S.op("dve", lambda: V.tensor_tensor(out=X[:].rearrange("p (h d) -> p h d", h=16),
                                                        in0=xs[:].rearrange("p (h d) -> p h d", h=16),
                                                        in1=bc(dt_tm[:, c, :].unsqueeze(2), [128, 16, 64]), op=ALU.mult),
                         reads=[B_xs, B_dt], writes=[B_X])
                    Xd, B_Xd = Xdr.next()
                    S.op("pool", lambda: G.tensor_tensor(out=Xd[:].rearrange("p (h d) -> p h d", h=16),
                                                         in0=X[:].rearrange("p (h d) -> p h d", h=16),
                                                         in1=bc(dA[:, c, :].unsqueeze(2), [128, 16, 64]), op=ALU.mult),
                         reads=[B_X, B_dt], writes=[B_Xd])
                    for g in range(2):
                        S.op("pe", lambda g=g: P.matmul(pcb[:, g, :], lhsT=Xin[:, 8 + g, csl], rhs=Xin[:, 10 + g, csl],
                                                        start=True, stop=True), reads=[B_xin], writes=[B_pcb], sig=(g == 1))
                    cb, B_cb = cbm.next()
                    S.op("dve", lambda: V.tensor_tensor(out=cb[:], in0=pcb[:, 0:2, :], in1=bc(triT_f[:].unsqueeze(1), [128, 2, 128]),
                                                        op=ALU.mult), reads=[B_pcb, B_c], writes=[B_cb])
                    chs[c] = dict(Xin=(Xin, B_xin), csl=csl, Z=(Zt, B_z), xs=(xs, B_xs), bt=(bt, B_bt), X=(X, B_X),
                                  Xd=(Xd, B_Xd), cb=(cb, B_cb), YT=cur["YT"], W={})

                def stage1(c, g):
                    d = chs[c]
                    cb, B_cb = d["cb"]
                    hs = slice(g * 8, (g + 1) * 8)
                    rr_, B_rr = rhsR.next()
                    S.op("pool", lambda: G.tensor_tensor(out=rr_[:], in0=bc(triT_f[:].unsqueeze(1), [128, 8, 128]),
                                                         in1=bc(a_tm[:, c, hs].unsqueeze(2), [128, 8, 128]), op=ALU.mult),
                         reads=[B_c, B_dt], writes=[B_rr])
                    pr_, B_pr = pR.next()
                    for q4 in range(2):
                        S.op("pe", lambda q4=q4: P.matmul(pr_[:, q4 * 4:(q4 + 1) * 4, :], lhsT=ones_f[:],
                                                          rhs=rr_[:, q4 * 4:(q4 + 1) * 4, :], start=True, stop=True),
                             reads=[B_c, B_rr], writes=[B_pr], sig=(q4 == 1))
                    sg, B_sg = segr.next()
                    S.op("dve", lambda: V.tensor_tensor(out=sg[:], in0=pr_[:], in1=bc(acs[:, c, hs].unsqueeze(2), [128, 8, 128]),
                                                        op=ALU.subtract), reads=[B_pr, B_dt], writes=[B_sg])
                    S.op("dve", lambda: V.tensor_single_scalar(out=sg[:], in_=sg[:], scalar=0.0, op=ALU.min),
                         reads=[B_sg], writes=[B_sg])
                    E, B_E = Er.next()
                    S.op("act", lambda: A.activation(out=E[:], in_=sg[:], func=AF.Exp), reads=[B_sg], writes=[B_E])
                    W, B_W = Wr.next()
                    S.op("dve", lambda: V.tensor_tensor(out=W[:], in0=E[:], in1=bc(cb[:, g, :].unsqueeze(1), [128, 8, 128]),
                                                        op=ALU.mult), reads=[B_E, B_cb], writes=[B_W])
                    d["W"][g] = (W, B_W)

                def stage2(c, g):
                    d = chs[c]
                    Xin, B_xin = d["Xin"]; csl = d["csl"]; Zt, B_z = d["Z"]; xs, B_xs = d["xs"]; bt, B_bt = d["bt"]
                    X, B_X = d["X"]; Xd, B_Xd = d["Xd"]; YT, B_yT = d["YT"]; W, B_W = d["W"][g]
                    cc = c % 4
                    hs = slice(g * 8, (g + 1) * 8)
                    fs = slice(g * 512, (g + 1) * 512)
                    for h in range(8):
                        hh = g * 8 + h
                        S.op("pe", lambda h=h, hh=hh: P.matmul(pya[:, h * 64:(h + 1) * 64], lhsT=W[:, h, :],
                                                               rhs=X[:, hh * 64:(hh + 1) * 64], start=True, stop=True),
                             reads=[B_W, B_X], writes=[B_pya], sig=(h == 7))
                    S.op("pe", lambda: P.matmul(pyb[:], lhsT=Xin[:, 10 + g, csl], rhs=prevb[:, fs], start=True, stop=True),
                         reads=[B_xin, B_prevb], writes=[B_pyb])
                    y, B_y = yr.next()
                    S.op("dve", lambda: V.tensor_tensor(out=y[:].rearrange("p (h d) -> p h d", h=8),
                                                        in0=pyb[:].rearrange("p (h d) -> p h d", h=8),
                                                        in1=bc(eA[:, c, hs].unsqueeze(2), [128, 8, 64]), op=ALU.mult),
                         reads=[B_pyb, B_dt], writes=[B_y])
                    S.op("dve", lambda: V.tensor_tensor(out=y[:], in0=y[:], in1=pya[:], op=ALU.add),
                         reads=[B_y, B_pya], writes=[B_y])
                    y2, B_y2 = y2r.next()
                    S.op("pool", lambda: G.tensor_tensor(out=y2[:].rearrange("p (h d) -> p h d", h=8),
                                                         in0=xs[:, fs].rearrange("p (h d) -> p h d", h=8),
                                                         in1=bc(bc16[:, 2, hs].unsqueeze(2), [128, 8, 64]), op=ALU.mult),
                         reads=[B_xs, Bh], writes=[B_y2])
                    S.op("dve", lambda: V.tensor_tensor(out=y[:], in0=y[:], in1=y2[:], op=ALU.add),
                         reads=[B_y, B_y2], writes=[B_y])
                    S.op("dve", lambda: V.tensor_tensor(out=y[:], in0=y[:], in1=Zt[:, fs], op=ALU.mult),
                         reads=[B_y, B_z], writes=[B_y])
                    sm, B_sm = smr.next()
                    S.op("act", lambda: A.activation(out=junk[:], in_=y[:], func=AF.Square, accum_out=sm[:, 0:1]),
                         reads=[B_y], writes=[B_junk, B_sm])
                    rstd(sm[:, 1:2], sm[:, 0:1], 1.0 / 512.0, B_sm)
                    yn, B_yn = ynr.next()
                    S.op("dve", lambda: V.scalar_tensor_tensor(out=yn[:], in0=y[:], scalar=sm[:, 1:2], in1=ssdw[:, fs],
                                                               op0=ALU.mult, op1=ALU.mult),
                         reads=[B_y, B_sm, Bh], writes=[B_yn])
                    pto, B_pto = ptr.next()
                    for j in range(4):
                        S.op("pe", lambda j=j: P.transpose(out=pto[:, j * 128:(j + 1) * 128], in_=yn[:, j * 128:(j + 1) * 128],
                                                           identity=identb[:]), reads=[B_yn, B_c], writes=[B_pto], sig=(j == 3))
                    S.op("act", lambda: A.copy(out=YT[:, g * 4:(g + 1) * 4, csl],
                                               in_=pto[:, 0:512].rearrange("p (j t) -> p j t", j=4)),
                         reads=[B_pto], writes=[B_yT])
                    S.op("pe", lambda: P.matmul(pst_[:], lhsT=bt[:, g * 128:(g + 1) * 128], rhs=Xd[:, fs], start=True, stop=True),
                         reads=[B_bt, B_Xd], writes=[B_pst])
                    S.op("dve", lambda: V.tensor_tensor(out=prev[:, fs].rearrange("p (h d) -> p h d", h=8),
                                                        in0=prev[:, fs].rearrange("p (h d) -> p h d", h=8),
                                                        in1=bc(cdr[:, c, hs].unsqueeze(2), [128, 8, 64]), op=ALU.mult),
                         reads=[B_prev, B_dt], writes=[B_prev])
                    S.op("dve", lambda: V.tensor_tensor(out=prev[:, fs], in0=prev[:, fs], in1=pst_[:], op=ALU.add),
                         reads=[B_prev, B_pst], writes=[B_prev])
                    S.op("act", lambda: A.copy(out=prevb[:, fs], in_=prev[:, fs]), reads=[B_prev], writes=[B_prevb])
                    if cc == 3 and g == 1:
                        c0 = (c - 3) * 128
                        S.dma("sp", ACT_S.rearrange("c p t -> p c t")[:, 0:8, c0:c0 + 512], YT[:], reads=[B_yT])
                    if g == 1:
                        del chs[c]

                sunits = [(c, g) for c in range(32) for g in range(2)]
                for st_ in range(len(sunits) + 1):
                    if st_ < len(sunits):
                        c, g = sunits[st_]
                        if g == 0:
                            prep(c)
                        stage1(c, g)
                    if st_ >= 1:
                        stage2(*sunits[st_ - 1])
                S.barrier()
            if hstop == 4:
                hst.close()
                return True

            wst = ExitStack()
            open_wres(wst)
            load_Wres(hy_w_out[0], 16)
            with ExitStack() as ph:
                kq = Ring(nc, ph, "at_kq", 2, [128, 2, L], BF16)
                vr = Ring(nc, ph, "at_v", 2, [128, 32, 130], BF16)
                pS = PRing(nc, ph, "at_pS", 2, [128, 2, 512])
                pO = PRing(nc, ph, "at_pO", 3, [128, 2, 256])
                pT = PRing(nc, ph, "at_pT", 1, [128, 1024], BF16)
                Er = Ring(nc, ph, "at_E", 4, [128, 2, 512], BF16)
                smr = Ring(nc, ph, "at_sm", 4, [128, 4], F32)
                tr_ = Ring(nc, ph, "at_t", 3, [128, 128], F32)
                or_ = Ring(nc, ph, "at_o", 3, [128, 128], F32)
                ybr = Ring(nc, ph, "at_yb", 3, [128, 128], BF16)
                junk = sbt(ph, "at_junk", [128, 128], BF16); B_junk = Buf()
                yT = Ring(nc, ph, "at_yT", 2, [128, L], BF16)
                for t_, b_ in zip(vr.t, vr.b):
                    S.op("pool", lambda t_=t_: G.memset(t_[:, :, 128:130], 1.0), writes=[b_])
                for t_, b_ in zip(Er.t, Er.b):
                    S.op("pool", lambda t_=t_: G.memset(t_[:], 0.0), writes=[b_])
                vsrc = V_S.rearrange("t p f -> p t f")
                for hd in range(8):
                    KQ, B_kq = kq.next()
                    S.dma("sp", KQ[:, 0, :], KT_S[hd], writes=[B_kq])
                    S.dma("sp", KQ[:, 1, :], QT_S[hd], writes=[B_kq])
                    Vt, B_v = vr.next()
                    for v0 in range(0, 32, 8):
                        S.dma("sp", Vt[:, v0:v0 + 8, 0:128], vsrc[:, v0:v0 + 8, hd * 128:(hd + 1) * 128], writes=[B_v])
                    YT, B_yT = yT.next()
                    items = [(qt, i) for qt in range(16) for i in range(qt + 1)]

                    def stage_qk(qt, i):
                        q0 = qt * 256
                        diag = (i == qt)
                        ps, B_ps = pS.next()
                        for half in range(2):
                            kb = 2 * i + half
                            c0 = 128 if (diag and half == 1) else 0
                            for m in range(2):
                                S.op("pe", lambda m=m, half=half, kb=kb, c0=c0: P.matmul(
                                    ps[:, m, half * 256 + c0:half * 256 + 256],
                                    lhsT=KQ[64 * m:64 * m + 64, 0, kb * 128:(kb + 1) * 128],
                                    rhs=KQ[64 * m:64 * m + 64, 1, q0 + c0:q0 + 256], start=True, stop=True),
                                    reads=[B_kq], writes=[B_ps], sig=(half == 1 and m == 1))
                        return ps, B_ps

                    def stage_exp(qt, i, ps, B_ps):
                        diag = (i == qt)
                        E, B_E = Er.next()
                        if diag:
                            S.op("act", lambda: A.activation(out=E[:, :, 0:256], in_=ps[:, :, 0:256], func=AF.Exp),
                                 reads=[B_ps], writes=[B_E])
                            S.op("act", lambda: A.activation(out=E[:, :, 384:512], in_=ps[:, :, 384:512], func=AF.Exp),
                                 reads=[B_ps], writes=[B_E])
                            Ev_ = E[:].rearrange("p m (r c) -> p m r c", c=128)
                            S.op("pool", lambda: G.tensor_tensor(out=Ev_[:, :, 0:4:3, :], in0=Ev_[:, :, 0:4:3, :],
                                                                 in1=bc(triT_b[:].unsqueeze(1).unsqueeze(1), [128, 2, 2, 128]),
                                                                 op=ALU.mult), reads=[B_E, B_c], writes=[B_E])
                        else:
                            S.op("act", lambda: A.activation(out=E[:], in_=ps[:], func=AF.Exp), reads=[B_ps], writes=[B_E])
                        return E, B_E

                    def stage_pv(qt, i, E, B_E, pOs):
                        diag = (i == qt)
                        for half in range(2):
                            kb = 2 * i + half
                            for qs in range(2):
                                if diag and half == 1 and qs == 0:
                                    continue
                                po, B_po = pOs[qs]
                                first = (i == 0 and half == 0)
                                last = diag and (half == qs)
                                for m in range(2):
                                    S.op("pe", lambda m=m, qs=qs, po=po, kb=kb, half=half, first=first, last=last: P.matmul(
                                        po[:, m, 0:129], lhsT=E[:, m, half * 256 + qs * 128:half * 256 + (qs + 1) * 128],
                                        rhs=Vt[:, kb, 0:129], start=(first and m == 0), stop=(last and m == 1),
                                        skip_group_check=True),
                                        reads=[B_E, B_v], writes=[B_po], sig=(last and m == 1))

                    def epilogue(qt, pOs):
                        q0 = qt * 256
                        for qs in range(2):
                            po, B_po = pOs[qs]
                            sm, B_sm = smr.next()
                            S.op("dve", lambda: V.reciprocal(out=sm[:, 0:2], in_=po[:, :, 128]), reads=[B_po], writes=[B_sm])
                            t_, B_t = tr_.next()
                            S.op("dve", lambda: V.tensor_scalar(out=t_[:], in0=po[:, 1, 0:128], scalar1=sm[:, 1:2], scalar2=nlam[:, 0:1],
                                                                op0=ALU.mult, op1=ALU.mult), reads=[B_po, B_sm, Bh], writes=[B_t])
                            o_, B_o = or_.next()
                            S.op("dve", lambda: V.scalar_tensor_tensor(out=o_[:], in0=po[:, 0, 0:128], scalar=sm[:, 0:1], in1=t_[:],
                                                                       op0=ALU.mult, op1=ALU.add), reads=[B_po, B_sm, B_t], writes=[B_o])
                            S.op("act", lambda: A.activation(out=junk[:], in_=o_[:], func=AF.Square, accum_out=sm[:, 2:3]),
                                 reads=[B_o], writes=[B_junk, B_sm])
                            rstd(sm[:, 3:4], sm[:, 2:3], 1.0 / 128.0, B_sm)
                            yb, B_yb = ybr.next()
                            S.op("dve", lambda: V.scalar_tensor_tensor(out=yb[:], in0=o_[:], scalar=sm[:, 3:4], in1=sublw[:],
                                                                       op0=ALU.mult, op1=ALU.mult), reads=[B_o, B_sm, Bh], writes=[B_yb])
                            pt, B_pt = pT.next()
                            S.op("pe", lambda: P.transpose(out=pt[:, 0:128], in_=yb[:], identity=identb[:]), reads=[B_yb, B_c], writes=[B_pt])
                            S.op("dve", lambda: V.tensor_copy(out=YT[:, q0 + qs * 128:q0 + (qs + 1) * 128], in_=pt[:, 0:128]),
                                 reads=[B_pt], writes=[B_yT])

                    nxt_qk = stage_qk(*items[0])
                    pOs = None
                    for n, (qt, i) in enumerate(items):
                        ps, B_ps = nxt_qk
                        if n + 1 < len(items):
                            nxt_qk = stage_qk(*items[n + 1])
                        E, B_E = stage_exp(qt, i, ps, B_ps)
                        if i == 0:
                            pOs = [pO.next(), pO.next()]
                        stage_pv(qt, i, E, B_E, pOs)
                        if i == qt:
                            epilogue(qt, pOs)
                    S.dma("sp", ACT_S[8 + hd], YT[:], reads=[B_yT])
                S.barrier()
            if hstop == 5:
                wst.close(); hst.close()
                return True
            phase_tm(16, h_in_ap, h_out_ap, widx_next)
            wst.close()
            hst.close()
            return False

        pcount = [0]

        def chk():
            pcount[0] += 1
            return stop is not None and pcount[0] >= stop

        def _main_program():
            h_cur = x
            scr = [hA, hB, hC]
            nxt = 0
            first = True
            for li, layer in enumerate(layers):
                last_layer = (li == len(layers) - 1)
                if first:
                    phase_norm_in(h_cur, 2 * layer)
                    if chk():
                        return
                    first = False
                h_mid = scr[nxt]; nxt += 1
                if layer == 0:
                    if phase_hybrid(h_cur, h_mid, 2 * layer + 1):
                        return
                    if chk():
                        return
                else:
                    with ExitStack() as ws:
                        open_wres(ws)
                        phase_shortconv(sc_w_out[0], 8)
                        if chk():
                            return
                        phase_tm(8, h_cur, h_mid, 2 * layer + 1)
                        if chk():
                            return
                with ExitStack() as ws:
                    open_wres(ws)
                    phase_ffn_up(layer, ffn_w_down[layer], 22)
                    if chk():
                        return
                    if last_layer:
                        phase_tm(22, h_mid, out, None)
                    else:
                        h_new = scr[nxt]; nxt += 1
                        phase_tm(22, h_mid, h_new, 2 * layers[li + 1])
                        h_cur = h_new
                    if chk():
                        return

        try:
            _main_program()
        except _Stop:
            pass
        S.barrier(engines=("sp",))
        build_program.stats = (S.n_ins, S.n_wait)
    return nc


_INPUT_NAMES = ["mix_norm", "ffn_norm", "hy_w_in", "hy_conv_w", "hy_conv_b", "hy_dt_bias", "hy_a_log",
                "hy_d_skip", "hy_ssd_norm", "hy_q_norm", "hy_k_norm", "hy_lambda_q1", "hy_lambda_k1",
                "hy_lambda_q2", "hy_lambda_k2", "hy_subln", "hy_w_out", "sc_w_in", "sc_conv_w", "sc_w_out",
                "ffn_w_up", "ffn_conv_w", "ffn_conv_b", "ffn_w_down"]


def kernel(**inputs):
    x = np.asarray(inputs["x"], dtype=np.float32)
    nc = build_program()
    shared = {k: np.ascontiguousarray(np.asarray(inputs[k], dtype=np.float32)) for k in _INPUT_NAMES}
    in_maps = []
    for b in range(8):
        m = dict(shared)
        m["x"] = np.ascontiguousarray(x[b])
        in_maps.append(m)
    res = run_bass_kernel_spmd(nc, in_maps, core_ids=list(range(8)))
    return np.stack([np.asarray(r["out"], dtype=np.float32) for r in res.results], axis=0)
```

```python
import math
from contextlib import ExitStack

import numpy as np
import concourse.bass as bass
import concourse.mybir as mybir
from concourse.bass_utils import run_bass_kernel_spmd

F32 = mybir.dt.float32
BF16 = mybir.dt.bfloat16
I32 = mybir.dt.int32
AF = mybir.ActivationFunctionType
ALU = mybir.AluOpType

L = 4096
DM = 1024
NT = 8
DFF = 2816
EPS = 1e-6
W_IN0 = 5648
C_Z, C_XBC, C_DT, C_Q, C_K, C_V = 0, 1024, 2560, 2576, 3600, 4624


class Ev:
    __slots__ = ("key", "sem", "val", "eng", "clock")

    def __init__(self, key, sem, val, eng, clock):
        self.key, self.sem, self.val, self.eng, self.clock = key, sem, val, eng, clock


class Buf:
    __slots__ = ("name", "w", "rd", "excl")

    def __init__(self, name="", excl=False):
        self.name, self.w, self.rd, self.excl = name, None, {}, excl


class Sched:
    NDMA = 8

    def __init__(self, nc, stack):
        self.nc = nc
        self.engs = {"pe": nc.tensor, "act": nc.scalar, "dve": nc.vector,
                     "pool": nc.gpsimd, "sp": nc.sync}
        self.sem, self.cnt, self.pending, self.last_ins = {}, {}, {}, {}
        self.seen = {e: {} for e in self.engs}
        for e in ("pe", "act", "dve", "pool"):
            self.sem[e] = stack.enter_context(nc.semaphore("s_" + e))
            self.cnt[e] = 0
            self.pending[e] = []
            self.last_ins[e] = None
        self.dsem, self.dcnt, self.drr = {}, {}, {}
        for q in ("sp", "pool"):
            self.dsem[q] = []
            for i in range(self.NDMA):
                k = "d_%s%d" % (q, i)
                self.dsem[q].append((k, stack.enter_context(nc.semaphore(k))))
                self.dcnt[k] = 0
            self.drr[q] = 0
        self.n_wait = 0
        self.n_ins = 0

    def _need(self, e, ev):
        if ev is None:
            return
        if ev.val is None:
            self._force(ev.eng)
        seen = self.seen[e]
        if seen.get(ev.key, 0) >= ev.val:
            return
        self.engs[e].wait_ge(ev.sem, ev.val)
        self.n_wait += 1
        for k, v in ev.clock.items():
            if seen.get(k, 0) < v:
                seen[k] = v

    def _force(self, e):
        if not self.pending[e]:
            return
        self.last_ins[e].then_inc(self.sem[e], 1)
        self._signal(e)

    def _signal(self, e):
        self.cnt[e] += 1
        v = self.cnt[e]
        clock = dict(self.seen[e])
        clock["s_" + e] = v
        for ev in self.pending[e]:
            ev.val = v
            ev.clock = clock
        self.pending[e] = []
        self.last_ins[e] = None

    def _deps(self, e, reads, writes, is_dma):
        for b in reads:
            if b.w is not None:
                self._need(e, b.w)
            if b.excl:
                for ev in list(b.rd.values()):
                    if ev.eng != e:
                        self._need(e, ev)
        for b in writes:
            if b.w is not None and (is_dma or b.w.eng != e or e == "pool"):
                self._need(e, b.w)
            for ev in list(b.rd.values()):
                if is_dma or ev.eng != e or e == "pool":
                    self._need(e, ev)

    def _record(self, ev, reads, writes):
        for b in reads:
            b.rd[ev.key] = ev
        for b in writes:
            b.w = ev
            b.rd = {}

    def op(self, e, fn, reads=(), writes=(), sig=True):
        self._deps(e, reads, writes, False)
        ins = fn()
        self.n_ins += 1
        ev = Ev("s_" + e, self.sem[e], None, e, None)
        self.pending[e].append(ev)
        self.last_ins[e] = ins
        if sig:
            ins.then_inc(self.sem[e], 1)
            self._signal(e)
        self._record(ev, reads, writes)
        return ev

    def dma(self, q, out, in_, reads=(), writes=(), **kw):
        i = self.drr[q]
        self.drr[q] = (i + 1) % self.NDMA
        key, sem = self.dsem[q][i]
        prev = self.dcnt[key]
        seen = self.seen[q]
        if prev > 0 and seen.get(key, 0) < prev:
            self.engs[q].wait_ge(sem, prev)
            self.n_wait += 1
            seen[key] = prev
        self._deps(q, reads, writes, True)
        ins = self.engs[q].dma_start(out=out, in_=in_, **kw)
        ins.then_inc(sem, 16)
        self.n_ins += 1
        self.dcnt[key] = prev + 16
        clock = dict(seen)
        clock[key] = prev + 16
        ev = Ev(key, sem, prev + 16, None, clock)
        self._record(ev, reads, writes)
        return ev

    def barrier(self, engines=("pe", "act", "dve", "pool", "sp")):
        for e in ("pe", "act", "dve", "pool"):
            self._force(e)
        evs = []
        for e in ("pe", "act", "dve", "pool"):
            if self.cnt[e] > 0:
                evs.append(Ev("s_" + e, self.sem[e], self.cnt[e], e, {"s_" + e: self.cnt[e]}))
        for q in self.dsem:
            for key, sem in self.dsem[q]:
                if self.dcnt[key] > 0:
                    evs.append(Ev(key, sem, self.dcnt[key], None, {key: self.dcnt[key]}))
        for e in engines:
            for ev in evs:
                if ev.eng != e:
                    self._need(e, ev)


_UID = [0]


class Ring:
    def __init__(self, nc, st, name, n, shape, dt):
        _UID[0] += 1
        name = "%s_%d_" % (name, _UID[0])
        self.t = [st.enter_context(nc.sbuf_tensor("%s%d" % (name, i), shape, dt)) for i in range(n)]
        self.b = [Buf("%s%d" % (name, i)) for i in range(n)]
        self.i = 0

    def next(self):
        i = self.i
        self.i = (i + 1) % len(self.t)
        return self.t[i], self.b[i]


class PRing:
    def __init__(self, nc, st, name, n, shape, dt=F32):
        _UID[0] += 1
        name = "%s_%d_" % (name, _UID[0])
        self.t = [st.enter_context(nc.psum_tensor("%s%d" % (name, i), shape, dt)) for i in range(n)]
        self.b = [Buf("%s%d" % (name, i), excl=True) for i in range(n)]
        self.i = 0

    def next(self):
        i = self.i
        self.i = (i + 1) % len(self.t)
        return self.t[i], self.b[i]


def bc(ap, shape):
    return ap.to_broadcast(shape)


class _Stop(Exception):
    pass


def build_program(layers=(0, 1), dbg=False, stop=None, hstop=None):
    nc = bass.Bass("TRN2", target_bir_lowering=False)

    def din(name, shape):
        return nc.dram_tensor(name, shape, F32, kind="ExternalInput").ap()

    x = din("x", [L, DM])
    mix_norm = din("mix_norm", [2, DM])
    ffn_norm = din("ffn_norm", [2, DM])
    hy_w_in = din("hy_w_in", [1, DM, W_IN0])
    hy_conv_w = din("hy_conv_w", [1, 4, 1536])
    hy_conv_b = din("hy_conv_b", [1, 1536])
    hy_dt_bias = din("hy_dt_bias", [1, 16])
    hy_a_log = din("hy_a_log", [1, 16])
    hy_d_skip = din("hy_d_skip", [1, 16])
    hy_ssd_norm = din("hy_ssd_norm", [1, 1024])
    hy_q_norm = din("hy_q_norm", [1, 64])
    hy_k_norm = din("hy_k_norm", [1, 64])
    hy_lq1 = din("hy_lambda_q1", [1, 64])
    hy_lk1 = din("hy_lambda_k1", [1, 64])
    hy_lq2 = din("hy_lambda_q2", [1, 64])
    hy_lk2 = din("hy_lambda_k2", [1, 64])
    hy_subln = din("hy_subln", [1, 128])
    hy_w_out = din("hy_w_out", [1, 2048, DM])
    sc_w_in = din("sc_w_in", [1, DM, 3072])
    sc_conv_w = din("sc_conv_w", [1, 3, 1024])
    sc_w_out = din("sc_w_out", [1, 1024, DM])
    ffn_w_up = din("ffn_w_up", [2, DM, 2 * DFF])
    ffn_conv_w = din("ffn_conv_w", [2, 3, 2 * DFF])
    ffn_conv_b = din("ffn_conv_b", [2, 2 * DFF])
    ffn_w_down = din("ffn_w_down", [2, DFF, DM])
    out = nc.dram_tensor("out", [L, DM], F32, kind="ExternalOutput").ap()

    skind = "ExternalOutput" if dbg else "Internal"

    def dscr(name, shape, dt=BF16):
        return nc.dram_tensor(name, shape, dt, kind=skind).ap()

    hA = dscr("hA", [L, DM], F32)
    hB = dscr("hB", [L, DM], F32)
    hC = dscr("hC", [L, DM], F32)
    ACT_S = dscr("ACT_S", [22, 128, L])
    XBC_S = dscr("XBC_S", [12, 128, L])
    QT_S = dscr("QT_S", [8, 128, L])
    KT_S = dscr("KT_S", [8, 128, L])
    ZS_S = dscr("ZS_S", [32, 128, 1024])
    V_S = dscr("V_S", [32, 128, 1024])

    with ExitStack() as top:
        S = Sched(nc, top)
        V, A, G, P = nc.vector, nc.scalar, nc.gpsimd, nc.tensor

        def sbt(st, name, shape, dt=F32):
            _UID[0] += 1
            return st.enter_context(nc.sbuf_tensor("%s_%d" % (name, _UID[0]), shape, dt))

        def pst(st, name, shape, dt=F32):
            _UID[0] += 1
            return st.enter_context(nc.psum_tensor("%s_%d" % (name, _UID[0]), shape, dt))

        hnT = sbt(top, "hnT", [128, 8, L], BF16)
        B_hn = [Buf("hn%d" % i) for i in range(32)]
        WR = {"t": None}
        B_Wres = Buf("Wres")

        def open_wres(st):
            WR["t"] = sbt(st, "Wres", [128, 22, DM], BF16)
        identf = sbt(top, "identf", [128, 128]); B_c = Buf("consts")
        identb = sbt(top, "identb", [128, 128], BF16)
        triT_f = sbt(top, "triT_f", [128, 128])
        triT_b = sbt(top, "triT_b", [128, 128], BF16)
        ones_f = sbt(top, "ones_f", [128, 128])
        epsc = sbt(top, "epsc", [128, 1])
        normw = sbt(top, "normw", [128, 4, 8])
        fcw = sbt(top, "fcw", [128, 2, 3, 44])
        fcb = sbt(top, "fcb", [128, 2, 44])
        scw = sbt(top, "scw", [128, 3, 8])

        S.op("pool", lambda: G.memset(identf[:], 1.0), writes=[B_c])
        S.op("pool", lambda: G.affine_select(out=identf[:], in_=identf[:], pattern=[[-1, 128]],
                                             compare_op=ALU.is_equal, fill=0.0, base=0,
                                             channel_multiplier=1), reads=[B_c], writes=[B_c])
        S.op("pool", lambda: G.memset(triT_f[:], 1.0), writes=[B_c])
        S.op("pool", lambda: G.affine_select(out=triT_f[:], in_=triT_f[:], pattern=[[1, 128]],
                                             compare_op=ALU.is_ge, fill=0.0, base=0,
                                             channel_multiplier=-1), reads=[B_c], writes=[B_c])
        S.op("pool", lambda: G.memset(ones_f[:], 1.0), writes=[B_c])
        S.op("pool", lambda: G.memset(epsc[:], EPS), writes=[B_c])
        S.op("dve", lambda: V.tensor_copy(out=identb[:], in_=identf[:]), reads=[B_c], writes=[B_c])
        S.op("dve", lambda: V.tensor_copy(out=triT_b[:], in_=triT_f[:]), reads=[B_c], writes=[B_c])

        def load_cols(st_ring, ps_ring, dst, src_row, nch):
            stg, B_s = st_ring.next()
            S.dma("sp", stg[0:nch, :], src_row.rearrange("(c p) -> c p", p=128), writes=[B_s])
            ps, B_p = ps_ring.next()
            S.op("pe", lambda: P.matmul(ps[:, 0:nch], lhsT=stg[0:nch, :], rhs=identf[0:nch, 0:nch],
                                        start=True, stop=True), reads=[B_s, B_c], writes=[B_p])
            S.op("dve", lambda: V.tensor_copy(out=dst, in_=ps[:, 0:nch]), reads=[B_p], writes=[B_c])

        with ExitStack() as ph:
            stg_r = Ring(nc, ph, "stg", 3, [64, 128], F32)
            ps_r = PRing(nc, ph, "psc", 2, [128, 512])
            for i, (src, l) in enumerate([(mix_norm, 0), (ffn_norm, 0), (mix_norm, 1), (ffn_norm, 1)]):
                load_cols(stg_r, ps_r, normw[:, i, :], src[l, :], 8)
            for l in range(2):
                for k in range(3):
                    load_cols(stg_r, ps_r, fcw[:, l, k, :], ffn_conv_w[l, k, :], 44)
                load_cols(stg_r, ps_r, fcb[:, l, :], ffn_conv_b[l, :], 44)
            for k in range(3):
                load_cols(stg_r, ps_r, scw[:, k, :], sc_conv_w[0, k, :], 8)
            S.barrier()

        def load_Wres(w2d, nch):
            src = w2d.rearrange("(c p) n -> p c n", p=128)
            step = 4
            for c0 in range(0, nch, step):
                c1 = min(nch, c0 + step)
                S.dma("pool", WR["t"][:, c0:c1, :], src[:, c0:c1, :], writes=[B_Wres])

        def rstd(dst, src, scale, B_):
            S.op("act", lambda: A.activation(out=dst, in_=src, func=AF.Ln, bias=epsc[:, 0:1], scale=scale),
                 reads=[B_, B_c], writes=[B_])
            S.op("act", lambda: A.activation(out=dst, in_=dst, func=AF.Exp, scale=-0.5), reads=[B_], writes=[B_])

        class NormRes:
            def __init__(self, st):
                self.junk = sbt(st, "nr_junk", [128, DM], BF16); self.B_junk = Buf()
                self.xn = Ring(nc, st, "nr_xn", 2, [128, DM], BF16)
                self.sm = Ring(nc, st, "nr_sm", 3, [128, 2], F32)
                self.ps = PRing(nc, st, "nr_ps", 2, [128, 8, 128], BF16)

        def norm_to_hnT(nr, h_sb, B_h, widx, t128):
            sm, B_sm = nr.sm.next()
            S.op("act", lambda: A.activation(out=nr.junk[:], in_=h_sb, func=AF.Square,
                                             accum_out=sm[:, 0:1]),
                 reads=[B_h], writes=[nr.B_junk, B_sm])
            rstd(sm[:, 1:2], sm[:, 0:1], 1.0 / DM, B_sm)
            xn, B_xn = nr.xn.next()
            S.op("dve", lambda: V.tensor_scalar(out=xn[:], in0=h_sb, scalar1=sm[:, 1:2], scalar2=None,
                                                op0=ALU.mult), reads=[B_h, B_sm], writes=[B_xn])
            ps, B_ps = nr.ps.next()
            for c in range(8):
                S.op("pe", lambda c=c: P.transpose(out=ps[:, c, :], in_=xn[:, c * 128:(c + 1) * 128],
                                                   identity=identb[:]),
                     reads=[B_xn, B_c], writes=[B_ps], sig=(c == 7))
            S.op("dve", lambda: V.tensor_tensor(out=hnT[:, :, t128 * 128:(t128 + 1) * 128], in0=ps[:],
                                                in1=bc(normw[:, widx, :].unsqueeze(2), [128, 8, 128]),
                                                op=ALU.mult),
                 reads=[B_ps, B_c], writes=[B_hn[t128]])

        def phase_norm_in(h_src, widx):
            with ExitStack() as ph:
                nr = NormRes(ph)
                hr = Ring(nc, ph, "ni_h", 3, [128, DM], F32)
                for t in range(32):
                    h, B_h = hr.next()
                    S.dma("sp", h[:], h_src[t * 128:(t + 1) * 128, :], writes=[B_h])
                    norm_to_hnT(nr, h[:], B_h, widx, t)
                S.barrier()

        def phase_tm(nch, h_src, h_dst, widx_next):
            with ExitStack() as ph:
                nr = NormRes(ph) if widx_next is not None else None
                ar = Ring(nc, ph, "tm_a", 2, [128, nch, 512], BF16)
                hr = Ring(nc, ph, "tm_h", 3, [128, DM], F32)
                hn = Ring(nc, ph, "tm_hn", 3, [128, DM], F32)
                pr = PRing(nc, ph, "tm_ps", 4, [128, 512])
                src = ACT_S.rearrange("c p t -> p c t")
                pending_epi = [None]
                for tt in range(NT):
                    a, B_a = ar.next()
                    half = (nch + 1) // 2
                    S.dma("sp", a[:, 0:half, :], src[:, 0:half, tt * 512:(tt + 1) * 512], writes=[B_a])
                    S.dma("sp", a[:, half:nch, :], src[:, half:nch, tt * 512:(tt + 1) * 512], writes=[B_a])
                    for sub in range(4):
                        t128 = tt * 4 + sub
                        h, B_h = hr.next()
                        S.dma("sp", h[:], h_src[t128 * 128:(t128 + 1) * 128, :], writes=[B_h])
                        pss = [pr.next(), pr.next()]
                        for c in range(nch):
                            for hf in range(2):
                                ps, B_ps = pss[hf]
                                S.op("pe", lambda c=c, hf=hf, ps=ps: P.matmul(
                                    ps[:], lhsT=a[:, c, sub * 128:(sub + 1) * 128],
                                    rhs=WR["t"][:, c, hf * 512:(hf + 1) * 512],
                                    start=(c == 0), stop=(c == nch - 1)),
                                    reads=[B_a, B_Wres], writes=[B_ps], sig=(c == nch - 1))
                        def epi(pss=pss, h=h, B_h=B_h, t128=t128):
                            o, B_o = hn.next()
                            for hf in range(2):
                                ps, B_ps = pss[hf]
                                S.op("dve", lambda hf=hf, ps=ps: V.tensor_tensor(
                                    out=o[:, hf * 512:(hf + 1) * 512], in0=ps[:],
                                    in1=h[:, hf * 512:(hf + 1) * 512], op=ALU.add),
                                    reads=[B_ps, B_h], writes=[B_o])
                            S.dma("sp", h_dst[t128 * 128:(t128 + 1) * 128, :], o[:], reads=[B_o])
                            if nr is not None:
                                norm_to_hnT(nr, o[:], B_o, widx_next, t128)
                        if pending_epi[0] is not None:
                            pending_epi[0]()
                        pending_epi[0] = epi
                pending_epi[0]()
                S.barrier()

        def fm_matmuls(wt, n0, tt, ps, B_ps, B_w):
            for k in range(8):
                S.op("pe", lambda k=k: P.matmul(ps[:], lhsT=wt[:, k, n0:n0 + 128],
                                                rhs=hnT[:, k, tt * 512:(tt + 1) * 512],
                                                start=(k == 0), stop=(k == 7)),
                     reads=[B_w] + B_hn[tt * 4:tt * 4 + 4], writes=[B_ps], sig=(k == 7))

        def wload(wr, w2d, col0, ncol=128):
            wt, B_w = wr.next()
            S.dma("pool", wt[:, :, 0:ncol], w2d.rearrange("(k p) n -> p k n", p=128)[:, :, col0:col0 + ncol],
                  writes=[B_w])
            return wt, B_w

        def phase_ffn_up(l, w_next, nch_next):
            w2d = ffn_w_up[l]
            with ExitStack() as ph:
                wr = Ring(nc, ph, "fu_w", 6, [128, 8, 128], BF16)
                pr = PRing(nc, ph, "fu_ps", 6, [128, 512])
                xg = Ring(nc, ph, "fu_xg", 1, [128, L + 2], F32)
                xv = Ring(nc, ph, "fu_xv", 1, [128, L + 2], F32)
                yr = Ring(nc, ph, "fu_y", 6, [128, 512], F32)
                sr = Ring(nc, ph, "fu_s", 3, [128, 512], F32)
                gr = Ring(nc, ph, "fu_g", 2, [128, L], BF16)
                for r in (xg, xv):
                    for t_, b_ in zip(r.t, r.b):
                        S.op("pool", lambda t_=t_: G.memset(t_[:, 0:2], 0.0), writes=[b_])
                pend = [(wload(wr, w2d, 0), wload(wr, w2d, DFF))]
                load_Wres(w_next, nch_next)
                for i in range(22):
                    if i + 1 < 22:
                        pend.append((wload(wr, w2d, (i + 1) * 128), wload(wr, w2d, DFF + (i + 1) * 128)))
                    (wg, B_wg), (wv, B_wv) = pend.pop(0)
                    Xg, B_xg = xg.next()
                    Xv, B_xv = xv.next()
                    Gt, B_g = gr.next()
                    cg, cv = i, 22 + i
                    for tt in range(NT):
                        sl = slice(tt * 512, (tt + 1) * 512)
                        psg, B_pg = pr.next()
                        psv, B_pv = pr.next()
                        fm_matmuls(wg, 0, tt, psg, B_pg, B_wg)
                        fm_matmuls(wv, 0, tt, psv, B_pv, B_wv)
                        Yg, B_yg = yr.next()
                        Yv, B_yv = yr.next()
                        S.op("act", lambda: A.activation(out=Yg[:], in_=psg[:], func=AF.Identity,
                                                         bias=fcb[:, l, cg:cg + 1], scale=fcw[:, l, 2, cg:cg + 1]),
                             reads=[B_pg, B_c], writes=[B_yg])
                        S.op("act", lambda: A.copy(out=Xg[:, 2 + tt * 512:2 + (tt + 1) * 512], in_=psg[:]),
                             reads=[B_pg], writes=[B_xg])
                        S.op("act", lambda: A.activation(out=Yv[:], in_=psv[:], func=AF.Identity,
                                                         bias=fcb[:, l, cv:cv + 1], scale=fcw[:, l, 2, cv:cv + 1]),
                             reads=[B_pv, B_c], writes=[B_yv])
                        S.op("dve", lambda: V.tensor_copy(out=Xv[:, 2 + tt * 512:2 + (tt + 1) * 512], in_=psv[:]),
                             reads=[B_pv], writes=[B_xv])
                        for (e, X_, B_x, Y_, B_y, c_) in (("dve", Xg, B_xg, Yg, B_yg, cg), ("dve", Xv, B_xv, Yv, B_yv, cv)):
                            for k in (1, 0):
                                en = "dve"
                                E_ = G if en == "pool" else V
                                S.op(en, lambda E_=E_, X_=X_, Y_=Y_, k=k, c_=c_: E_.scalar_tensor_tensor(
                                    out=Y_[:], in0=X_[:, tt * 512 + k:tt * 512 + k + 512],
                                    scalar=fcw[:, l, k, c_:c_ + 1], in1=Y_[:], op0=ALU.mult, op1=ALU.add),
                                    reads=[B_x, B_c, B_y], writes=[B_y])
                        St, B_s = sr.next()
                        S.op("act", lambda: A.activation(out=St[:], in_=Yg[:], func=AF.Silu),
                             reads=[B_yg], writes=[B_s])
                        S.op("pool", lambda: G.tensor_tensor(out=Gt[:, sl], in0=St[:], in1=Yv[:], op=ALU.mult),
                             reads=[B_s, B_yv], writes=[B_g])
                    S.dma("sp", ACT_S[i], Gt[:], reads=[B_g])
                S.barrier()

        def phase_shortconv(w_next, nch_next):
            w2d = sc_w_in[0]
            with ExitStack() as ph:
                wr = Ring(nc, ph, "sc_w", 6, [128, 8, 128], BF16)
                pr = PRing(nc, ph, "sc_ps", 6, [128, 512])
                mr = Ring(nc, ph, "sc_m", 2, [128, L + 2], F32)
                ur = Ring(nc, ph, "sc_u", 3, [128, 512], F32)
                yr = Ring(nc, ph, "sc_y", 3, [128, 512], F32)
                rr = Ring(nc, ph, "sc_r", 2, [128, L], BF16)
                for t_, b_ in zip(mr.t, mr.b):
                    S.op("pool", lambda t_=t_: G.memset(t_[:, 0:2], 0.0), writes=[b_])
                load_Wres(w_next, nch_next)
                for i in range(8):
                    wb_, wc_, wu_ = (wload(wr, w2d, i * 128), wload(wr, w2d, 1024 + i * 128),
                                     wload(wr, w2d, 2048 + i * 128))
                    M, B_m = mr.next()
                    R, B_r = rr.next()
                    for tt in range(NT):
                        sl = slice(tt * 512, (tt + 1) * 512)
                        psb, B_pb = pr.next()
                        psc, B_pc = pr.next()
                        psu, B_pu = pr.next()
                        fm_matmuls(wc_[0], 0, tt, psc, B_pc, wc_[1])
                        fm_matmuls(wu_[0], 0, tt, psu, B_pu, wu_[1])
                        fm_matmuls(wb_[0], 0, tt, psb, B_pb, wb_[1])
                        U, B_u = ur.next()
                        S.op("act", lambda: A.copy(out=U[:], in_=psu[:]), reads=[B_pu], writes=[B_u])
                        S.op("dve", lambda: V.tensor_tensor(out=M[:, 2 + tt * 512:2 + (tt + 1) * 512], in0=psc[:],
                                                            in1=U[:], op=ALU.mult),
                             reads=[B_pc, B_u], writes=[B_m])
                        Y, B_y = yr.next()
                        S.op("act", lambda: A.activation(out=Y[:], in_=M[:, 2 + tt * 512:2 + (tt + 1) * 512],
                                                         func=AF.Copy, scale=scw[:, 2, i:i + 1]),
                             reads=[B_m, B_c], writes=[B_y])
                        for k in (1, 0):
                            S.op("dve", lambda k=k: V.scalar_tensor_tensor(
                                out=Y[:], in0=M[:, tt * 512 + k:tt * 512 + k + 512], scalar=scw[:, k, i:i + 1],
                                in1=Y[:], op0=ALU.mult, op1=ALU.add), reads=[B_m, B_c, B_y], writes=[B_y])
                        S.op("dve", lambda: V.tensor_tensor(out=R[:, sl], in0=psb[:], in1=Y[:], op=ALU.mult),
                             reads=[B_pb, B_y], writes=[B_r])
                    S.dma("sp", ACT_S[i], R[:], reads=[B_r])
                S.barrier()

        def phase_hybrid(h_in_ap, h_out_ap, widx_next):
            hyb = {}
            w2d = hy_w_in[0]
            hst = ExitStack()
            cst = ExitStack()
            Bh = Buf("hyb_consts")
            blk = sbt(hst, "blk", [128, 128], BF16)
            prot = sbt(hst, "prot", [128, 128], BF16)
            hcw = sbt(hst, "hcw", [128, 4, 12])
            hcb = sbt(hst, "hcb", [128, 12])
            qkw = sbt(hst, "qkw", [128, 2])
            dt_tm = sbt(hst, "dt_tm", [128, 32, 16])
            a_tm = sbt(hst, "a_tm", [128, 32, 16])
            acs = sbt(hst, "acs", [128, 32, 16])
            eA = sbt(hst, "eA", [128, 32, 16])
            dA = sbt(hst, "dA", [128, 32, 16])
            cdr = sbt(hst, "cdr", [128, 32, 16])
            bc16 = sbt(hst, "bc16", [128, 3, 16])
            ssdw = sbt(hst, "ssdw", [128, 1024])
            sublw = sbt(hst, "sublw", [128, 128])
            nlam = sbt(hst, "nlam", [128, 2])
            B_dt = Buf("dtstuff")
            cosT = sbt(cst, "cosT", [128, L])
            sinT = sbt(cst, "sinT", [128, L])

            with ExitStack() as ph:
                tf = sbt(ph, "c_tf", [128, 128])
                tf2 = sbt(ph, "c_tf2", [128, 128])
                B_t = Buf()
                S.op("pool", lambda: G.memset(tf[:], 0.0), writes=[B_t])
                S.op("pool", lambda: G.memset(tf[0:64, 0:64], 1.0), writes=[B_t])
                S.op("pool", lambda: G.memset(tf[64:128, 64:128], 1.0), writes=[B_t])
                S.op("dve", lambda: V.tensor_copy(out=blk[:], in_=tf[:]), reads=[B_t], writes=[Bh])
                S.op("pool", lambda: G.memset(tf[:], 1.0), reads=[Bh], writes=[B_t])
                S.op("pool", lambda: G.affine_select(out=tf[:], in_=tf[:], pattern=[[-1, 128]], compare_op=ALU.is_equal,
                                                     fill=0.0, base=32, channel_multiplier=1), reads=[B_t], writes=[B_t])
                S.op("pool", lambda: G.memset(tf2[:], -1.0), writes=[B_t])
                S.op("pool", lambda: G.affine_select(out=tf2[:], in_=tf2[:], pattern=[[-1, 128]], compare_op=ALU.is_equal,
                                                     fill=0.0, base=-32, channel_multiplier=1), reads=[B_t], writes=[B_t])
                for c0 in (0, 64):
                    S.op("pool", lambda c0=c0: G.memset(tf[:, c0:c0 + 32], 0.0), reads=[B_t], writes=[B_t])
                    S.op("pool", lambda c0=c0: G.memset(tf2[:, c0 + 32:c0 + 64], 0.0), reads=[B_t], writes=[B_t])
                S.op("dve", lambda: V.tensor_tensor(out=tf[:], in0=tf[:], in1=tf2[:], op=ALU.add), reads=[B_t], writes=[B_t])
                S.op("dve", lambda: V.tensor_copy(out=prot[:], in_=tf[:]), reads=[B_t], writes=[Bh])
                pi_ = sbt(ph, "c_pi", [128, 1], I32)
                pf_ = sbt(ph, "c_pf", [128, 2])
                S.op("pool", lambda: G.iota(pi_[:], pattern=[[0, 1]], base=0, channel_multiplier=1), writes=[B_t])
                S.op("dve", lambda: V.tensor_single_scalar(out=pi_[:], in_=pi_[:], scalar=31, op=ALU.bitwise_and),
                     reads=[B_t], writes=[B_t])
                S.op("dve", lambda: V.tensor_copy(out=pf_[:, 0:1], in_=pi_[:]), reads=[B_t], writes=[B_t])
                S.op("dve", lambda: V.tensor_scalar(out=pf_[:, 0:1], in0=pf_[:, 0:1], scalar1=-math.log(10000.0) / 32.0,
                                                    scalar2=-math.log(2 * math.pi), op0=ALU.mult, op1=ALU.add),
                     reads=[B_t], writes=[B_t])
                S.op("act", lambda: A.activation(out=pf_[:, 1:2], in_=pf_[:, 0:1], func=AF.Exp), reads=[B_t], writes=[B_t])
                ti = sbt(ph, "c_ti", [128, L], I32)
                r_ = sbt(ph, "c_r", [128, L])
                kf = sbt(ph, "c_kf", [128, L])
                S.op("pool", lambda: G.iota(ti[:], pattern=[[1, L]], base=0, channel_multiplier=0), writes=[B_t])
                S.op("dve", lambda: V.tensor_copy(out=r_[:], in_=ti[:]), reads=[B_t], writes=[B_t])
                S.op("dve", lambda: V.tensor_scalar(out=r_[:], in0=r_[:], scalar1=pf_[:, 1:2], scalar2=None, op0=ALU.mult),
                     reads=[B_t], writes=[B_t])
                for (dst, off) in ((sinT, 0.0), (cosT, 0.25)):
                    S.op("dve", lambda off=off: V.tensor_scalar(out=kf[:], in0=r_[:], scalar1=off, scalar2=None, op0=ALU.add),
                         reads=[B_t, Bh], writes=[B_t])
                    S.op("dve", lambda: V.tensor_copy(out=ti[:], in_=kf[:]), reads=[B_t], writes=[B_t])
                    S.op("dve", lambda dst=dst: V.tensor_copy(out=dst[:], in_=ti[:]), reads=[B_t], writes=[Bh])
                    S.op("dve", lambda dst=dst: V.tensor_tensor(out=kf[:], in0=kf[:], in1=dst[:], op=ALU.subtract),
                         reads=[B_t, Bh], writes=[B_t])
                    S.op("dve", lambda dst=dst: V.tensor_single_scalar(out=dst[:], in_=kf[:], scalar=0.5, op=ALU.is_gt),
                         reads=[B_t], writes=[Bh])
                    S.op("dve", lambda dst=dst: V.tensor_tensor(out=kf[:], in0=kf[:], in1=dst[:], op=ALU.subtract),
                         reads=[B_t, Bh], writes=[B_t])
                    S.op("act", lambda dst=dst: A.activation(out=dst[:], in_=kf[:], func=AF.Sin, scale=2 * math.pi),
                         reads=[B_t], writes=[Bh])
                stg_r = Ring(nc, ph, "hstg", 3, [64, 128], F32)
                ps_r = PRing(nc, ph, "hpsc", 2, [128, 512])

                def lc(dst, src_row, nch):
                    stg, B_s = stg_r.next()
                    S.dma("sp", stg[0:nch, :], src_row.rearrange("(c p) -> c p", p=128), writes=[B_s])
                    ps, B_p = ps_r.next()
                    S.op("pe", lambda: P.matmul(ps[:, 0:nch], lhsT=stg[0:nch, :], rhs=identf[0:nch, 0:nch],
                                                start=True, stop=True), reads=[B_s, B_c], writes=[B_p])
                    S.op("dve", lambda: V.tensor_copy(out=dst, in_=ps[:, 0:nch]), reads=[B_p], writes=[Bh])
                for k in range(4):
                    lc(hcw[:, k, :], hy_conv_w[0, k, :], 12)
                lc(hcb[:, :], hy_conv_b[0, :], 12)
                for j, src in enumerate((hy_q_norm, hy_k_norm)):
                    for hf in range(2):
                        S.dma("sp", qkw[hf * 64:(hf + 1) * 64, j:j + 1], src[0, :].rearrange("(p o) -> p o", o=1),
                              writes=[Bh])
                S.op("dve", lambda: V.tensor_scalar(out=qkw[:, 0:1], in0=qkw[:, 0:1], scalar1=0.125, scalar2=None,
                                                    op0=ALU.mult), reads=[Bh], writes=[Bh])
                for j, src in enumerate((hy_dt_bias, hy_a_log, hy_d_skip)):
                    S.dma("sp", bc16[:, j, :], src[0:1, :].partition_broadcast(128), writes=[Bh])
                S.op("act", lambda: A.activation(out=bc16[:, 1, :], in_=bc16[:, 1, :], func=AF.Exp), reads=[Bh], writes=[Bh])
                S.op("dve", lambda: V.tensor_scalar(out=bc16[:, 1, :], in0=bc16[:, 1, :], scalar1=-1.0, scalar2=None,
                                                    op0=ALU.mult), reads=[Bh], writes=[Bh])
                S.dma("sp", ssdw[:], hy_ssd_norm[0:1, :].partition_broadcast(128), writes=[Bh])
                S.dma("sp", sublw[:], hy_subln[0:1, :].partition_broadcast(128), writes=[Bh])
                lam_init = 0.8 - 0.6 * math.exp(-0.3 * 0)
                S.op("dve", lambda: V.tensor_scalar(out=sublw[:], in0=sublw[:], scalar1=(1.0 - lam_init), scalar2=None,
                                                    op0=ALU.mult), reads=[Bh], writes=[Bh])
                lt = sbt(ph, "c_lt", [128, 4, 64])
                ls = sbt(ph, "c_ls", [128, 4])
                for j, src in enumerate((hy_lq1, hy_lk1, hy_lq2, hy_lk2)):
                    S.dma("sp", lt[:, j, :], src[0:1, :].partition_broadcast(128), writes=[B_t])
                for j in range(2):
                    S.op("dve", lambda j=j: V.tensor_tensor(out=lt[:, 2 * j, :], in0=lt[:, 2 * j, :], in1=lt[:, 2 * j + 1, :],
                                                            op=ALU.mult), reads=[B_t], writes=[B_t])
                    S.op("act", lambda j=j: A.activation(out=lt[:, 2 * j + 1, :], in_=lt[:, 2 * j, :], func=AF.Identity,
                                                         accum_out=ls[:, j:j + 1]), reads=[B_t], writes=[B_t])
                S.op("act", lambda: A.activation(out=ls[:, 2:4], in_=ls[:, 0:2], func=AF.Exp), reads=[B_t], writes=[B_t])
                S.op("dve", lambda: V.tensor_tensor(out=nlam[:, 0:1], in0=ls[:, 3:4], in1=ls[:, 2:3], op=ALU.subtract),
                     reads=[B_t], writes=[Bh])
                S.op("dve", lambda: V.tensor_scalar(out=nlam[:, 0:1], in0=nlam[:, 0:1], scalar1=-lam_init, scalar2=None,
                                                    op0=ALU.add), reads=[Bh], writes=[Bh])
                S.barrier()
            if hstop == 1:
                cst.close(); hst.close()
                return True

            with ExitStack() as ph:
                wr = Ring(nc, ph, "hb_w", 4, [128, 8, 128], BF16)
                pr = PRing(nc, ph, "hb_ps", 4, [128, 512])
                pr2 = PRing(nc, ph, "hb_ps2", 4, [128, 512])
                xr = Ring(nc, ph, "hb_x", 2, [128, L + 3], BF16)
                yr = Ring(nc, ph, "hb_y", 4, [128, 512], F32)
                orr = Ring(nc, ph, "hb_o", 2, [128, L], BF16)
                sq = Ring(nc, ph, "hb_sq", 3, [128, 512], BF16)
                rs = Ring(nc, ph, "hb_rs", 3, [128, 512], F32)
                qn = Ring(nc, ph, "hb_qn", 3, [128, 512], BF16)
                t1r = Ring(nc, ph, "hb_t1", 3, [128, 512], F32)
                t2r = Ring(nc, ph, "hb_t2", 3, [128, 512], F32)
                for t_, b_ in zip(xr.t, xr.b):
                    S.op("pool", lambda t_=t_: G.memset(t_[:, 0:3], 0.0), writes=[b_])
                jobs = [("xbc", i, C_XBC + i * 128) for i in range(12)] + \
                       [("q", i, C_Q + i * 128) for i in range(8)] + [("k", i, C_K + i * 128) for i in range(8)]
                pend = [wload(wr, w2d, jobs[0][2])]
                units = [(ji, tt) for ji in range(len(jobs)) for tt in range(NT)]
                js = {}
                ust = {}

                def stage1(u):
                    ji, tt = units[u]
                    if tt == 0:
                        if ji + 1 < len(jobs):
                            pend.append(wload(wr, w2d, jobs[ji + 1][2]))
                        js[ji] = {"w": pend.pop(0)}
                    wt, B_w = js[ji]["w"]
                    ps, B_ps = pr.next()
                    fm_matmuls(wt, 0, tt, ps, B_ps, B_w)
                    ust[u] = {"ps": (ps, B_ps)}

                def stage2(u):
                    ji, tt = units[u]
                    kind, i, col = jobs[ji]
                    sl = slice(tt * 512, (tt + 1) * 512)
                    ps, B_ps = ust[u]["ps"]
                    if tt == 0:
                        js[ji]["O"] = orr.next()
                        if kind == "xbc":
                            js[ji]["X"] = xr.next()
                    O, B_o = js[ji]["O"]
                    if kind == "xbc":
                        X, B_x = js[ji]["X"]
                        Y, B_y = yr.next()
                        S.op("act", lambda: A.activation(out=Y[:], in_=ps[:], func=AF.Identity,
                                                         bias=hcb[:, i:i + 1], scale=hcw[:, 3, i:i + 1]),
                             reads=[B_ps, Bh], writes=[B_y])
                        S.op("act", lambda: A.copy(out=X[:, 3 + tt * 512:3 + (tt + 1) * 512], in_=ps[:]),
                             reads=[B_ps], writes=[B_x])
                        for k in (2, 1, 0):
                            S.op("dve", lambda k=k: V.scalar_tensor_tensor(
                                out=Y[:], in0=X[:, tt * 512 + k:tt * 512 + k + 512], scalar=hcw[:, k, i:i + 1],
                                in1=Y[:], op0=ALU.mult, op1=ALU.add), reads=[B_x, Bh, B_y], writes=[B_y])
                        S.op("act", lambda: A.activation(out=O[:, sl], in_=Y[:], func=AF.Silu),
                             reads=[B_y], writes=[B_o])
                    else:
                        wcol = qkw[:, 0:1] if kind == "q" else qkw[:, 1:2]
                        s_, B_s = sq.next()
                        S.op("act", lambda: A.activation(out=s_[:], in_=ps[:], func=AF.Square),
                             reads=[B_ps], writes=[B_s])
                        ps2, B_p2 = pr2.next()
                        S.op("pe", lambda: P.matmul(ps2[:], lhsT=blk[:], rhs=s_[:], start=True, stop=True),
                             reads=[Bh, B_s], writes=[B_p2])
                        r1, B_r1 = rs.next()
                        S.op("act", lambda: A.activation(out=r1[:], in_=ps2[:], func=AF.Ln, bias=epsc[:, 0:1],
                                                         scale=1.0 / 64.0), reads=[B_p2, B_c], writes=[B_r1])
                        S.op("act", lambda: A.activation(out=r1[:], in_=r1[:], func=AF.Exp, scale=-0.5),
                             reads=[B_r1], writes=[B_r1])
                        q_, B_q = qn.next()
                        S.op("dve", lambda: V.scalar_tensor_tensor(out=q_[:], in0=ps[:], scalar=wcol, in1=r1[:],
                                                                   op0=ALU.mult, op1=ALU.mult),
                             reads=[B_ps, Bh, B_r1], writes=[B_q])
                        ust[u]["q"] = (q_, B_q)

                def stage3(u):
                    ji, tt = units[u]
                    kind, i, col = jobs[ji]
                    sl = slice(tt * 512, (tt + 1) * 512)
                    O, B_o = js[ji]["O"]
                    if kind != "xbc":
                        q_, B_q = ust[u]["q"]
                        ps3, B_p3 = pr2.next()
                        S.op("pe", lambda: P.matmul(ps3[:], lhsT=prot[:], rhs=q_[:], start=True, stop=True),
                             reads=[Bh, B_q], writes=[B_p3])
                        t1, B_t1 = t1r.next()
                        S.op("pool", lambda: G.tensor_tensor(out=t1[:], in0=q_[:], in1=cosT[:, sl], op=ALU.mult),
                             reads=[B_q, Bh], writes=[B_t1])
                        t2, B_t2 = t2r.next()
                        S.op("dve", lambda: V.tensor_tensor(out=t2[:], in0=ps3[:], in1=sinT[:, sl], op=ALU.mult),
                             reads=[B_p3, Bh], writes=[B_t2])
                        S.op("dve", lambda: V.tensor_tensor(out=O[:, sl], in0=t1[:], in1=t2[:], op=ALU.add),
                             reads=[B_t1, B_t2], writes=[B_o])
                    if tt == NT - 1:
                        dst = {"xbc": XBC_S, "q": QT_S, "k": KT_S}[kind]
                        S.dma("sp", dst[i], O[:], reads=[B_o])
                    del ust[u]

                NU = len(units)
                for st_ in range(NU + 2):
                    if st_ < NU:
                        stage1(st_)
                    if 0 <= st_ - 1 < NU:
                        stage2(st_ - 1)
                    if 0 <= st_ - 2 < NU:
                        stage3(st_ - 2)
                S.barrier()

            cst.close()
            if hstop == 2:
                hst.close()
                return True
            with ExitStack() as ph:
                wz = sbt(ph, "wz", [128, 8, 1024], BF16)
                wv = sbt(ph, "wv", [128, 8, 1024], BF16)
                wd = sbt(ph, "wd", [128, 8, 16], BF16)
                B_w = Buf()
                wsrc = w2d.rearrange("(k p) n -> p k n", p=128)
                for k0 in (0, 4):
                    S.dma("pool", wz[:, k0:k0 + 4, :], wsrc[:, k0:k0 + 4, C_Z:C_Z + 1024], writes=[B_w])
                    S.dma("pool", wv[:, k0:k0 + 4, :], wsrc[:, k0:k0 + 4, C_V:C_V + 1024], writes=[B_w])
                S.dma("pool", wd[:], wsrc[:, :, C_DT:C_DT + 16], writes=[B_w])
                pr = PRing(nc, ph, "tz_ps", 6, [128, 512])
                pdt = pst(ph, "tz_pdt", [128, 32, 16]); B_pdt = Buf(excl=True)
                zr = Ring(nc, ph, "tz_z", 3, [128, 1024], BF16)
                vr = Ring(nc, ph, "tz_v", 3, [128, 1024], BF16)
                for t in range(32):
                    pz = [pr.next(), pr.next()]
                    pv = [pr.next(), pr.next()]
                    for k in range(8):
                        lhs = hnT[:, k, t * 128:(t + 1) * 128]
                        for hf in range(2):
                            S.op("pe", lambda k=k, hf=hf, lhs=lhs: P.matmul(pz[hf][0][:], lhsT=lhs, rhs=wz[:, k, hf * 512:(hf + 1) * 512],
                                                                          start=(k == 0), stop=(k == 7)),
                                 reads=[B_hn[t], B_w], writes=[pz[hf][1]], sig=(k == 7))
                            S.op("pe", lambda k=k, hf=hf, lhs=lhs: P.matmul(pv[hf][0][:], lhsT=lhs, rhs=wv[:, k, hf * 512:(hf + 1) * 512],
                                                                          start=(k == 0), stop=(k == 7)),
                                 reads=[B_hn[t], B_w], writes=[pv[hf][1]], sig=(k == 7))
                        S.op("pe", lambda k=k, lhs=lhs: P.matmul(pdt[:, t, :], lhsT=lhs, rhs=wd[:, k, :],
                                                                 start=(k == 0), stop=(k == 7)),
                             reads=[B_hn[t], B_w], writes=[B_pdt], sig=(k == 7))
                    Z, B_z = zr.next()
                    Vt, B_v = vr.next()
                    for hf in range(2):
                        S.op("act", lambda hf=hf: A.activation(out=Z[:, hf * 512:(hf + 1) * 512], in_=pz[hf][0][:], func=AF.Silu),
                             reads=[pz[hf][1]], writes=[B_z])
                        S.op("dve", lambda hf=hf: V.tensor_copy(out=Vt[:, hf * 512:(hf + 1) * 512], in_=pv[hf][0][:]),
                             reads=[pv[hf][1]], writes=[B_v])
                    S.dma("sp", ZS_S[t], Z[:], reads=[B_z])
                    S.dma("sp", V_S[t], Vt[:], reads=[B_v])
                xb_ = sbt(ph, "dt_x", [128, 32, 16])
                ab_ = sbt(ph, "dt_a", [128, 32, 16])
                S.op("dve", lambda: V.tensor_tensor(out=xb_[:], in0=pdt[:], in1=bc(bc16[:, 0, :].unsqueeze(1), [128, 32, 16]),
                                                    op=ALU.add), reads=[B_pdt, Bh], writes=[B_dt])
                S.op("act", lambda: A.activation(out=ab_[:], in_=xb_[:], func=AF.Abs), reads=[B_dt], writes=[B_dt])
                S.op("act", lambda: A.activation(out=ab_[:], in_=ab_[:], func=AF.Exp, scale=-1.0), reads=[B_dt], writes=[B_dt])
                S.op("dve", lambda: V.tensor_scalar(out=ab_[:], in0=ab_[:], scalar1=1.0, scalar2=None, op0=ALU.add),
                     reads=[B_dt], writes=[B_dt])
                S.op("act", lambda: A.activation(out=ab_[:], in_=ab_[:], func=AF.Ln), reads=[B_dt], writes=[B_dt])
                S.op("dve", lambda: V.scalar_tensor_tensor(out=dt_tm[:], in0=xb_[:], scalar=0.0, in1=ab_[:], op0=ALU.max,
                                                           op1=ALU.add), reads=[B_dt], writes=[B_dt])
                S.op("dve", lambda: V.tensor_tensor(out=a_tm[:], in0=dt_tm[:], in1=bc(bc16[:, 1, :].unsqueeze(1), [128, 32, 16]),
                                                    op=ALU.mult), reads=[B_dt, Bh], writes=[B_dt])
                pa, B_pa = pr.next()
                pl, B_pl = pr.next()
                S.op("pe", lambda: P.matmul(pa[:], lhsT=triT_f[:], rhs=a_tm[:].rearrange("p c h -> p (c h)"),
                                            start=True, stop=True), reads=[B_c, B_dt], writes=[B_pa])
                S.op("pe", lambda: P.matmul(pl[:], lhsT=ones_f[:], rhs=a_tm[:].rearrange("p c h -> p (c h)"),
                                            start=True, stop=True), reads=[B_c, B_dt], writes=[B_pl])
                fl = lambda t_: t_[:].rearrange("p c h -> p (c h)")
                S.op("dve", lambda: V.tensor_copy(out=fl(acs), in_=pa[:]), reads=[B_pa], writes=[B_dt])
                S.op("act", lambda: A.activation(out=fl(eA), in_=pa[:], func=AF.Exp), reads=[B_pa], writes=[B_dt])
                S.op("act", lambda: A.activation(out=fl(cdr), in_=pl[:], func=AF.Exp), reads=[B_pl], writes=[B_dt])
                S.op("dve", lambda: V.tensor_tensor(out=fl(dA), in0=pl[:], in1=fl(acs), op=ALU.subtract),
                     reads=[B_pl, B_dt], writes=[B_dt])
                S.op("act", lambda: A.activation(out=fl(dA), in_=fl(dA), func=AF.Exp), reads=[B_dt], writes=[B_dt])
                S.barrier()
            if hstop == 3:
                hst.close()
                return True

            with ExitStack() as ph:
                xin = Ring(nc, ph, "sd_x", 2, [128, 12, 512], BF16)
                zin = Ring(nc, ph, "sd_z", 3, [128, 1024], BF16)
                ptr = PRing(nc, ph, "sd_ptr", 2, [128, 1024], BF16)
                pcb = pst(ph, "sd_pcb", [128, 4, 128]); B_pcb = Buf(excl=True)
                pR = PRing(nc, ph, "sd_pR", 1, [128, 8, 128])
                pya = pst(ph, "sd_pya", [128, 512]); B_pya = Buf(excl=True)
                pyb = pst(ph, "sd_pyb", [128, 512]); B_pyb = Buf(excl=True)
                pst_ = pst(ph, "sd_pst", [128, 512]); B_pst = Buf(excl=True)
                xs_tm = Ring(nc, ph, "sd_xs", 2, [128, 1024], BF16)
                b_tm = Ring(nc, ph, "sd_b", 2, [128, 256], BF16)
                Xr = Ring(nc, ph, "sd_X", 2, [128, 1024], BF16)
                Xdr = Ring(nc, ph, "sd_Xd", 2, [128, 1024], BF16)
                cbm = Ring(nc, ph, "sd_cbm", 2, [128, 2, 128], BF16)
                rhsR = Ring(nc, ph, "sd_rr", 1, [128, 8, 128], F32)
                segr = Ring(nc, ph, "sd_seg", 1, [128, 8, 128], F32)
                Er = Ring(nc, ph, "sd_E", 2, [128, 8, 128], BF16)
                Wr = Ring(nc, ph, "sd_W", 2, [128, 8, 128], BF16)
                yr = Ring(nc, ph, "sd_y", 2, [128, 512], F32)
                y2r = Ring(nc, ph, "sd_y2", 1, [128, 512], F32)
                ynr = Ring(nc, ph, "sd_yn", 2, [128, 512], BF16)
                smr = Ring(nc, ph, "sd_sm", 3, [128, 2], F32)
                junk = sbt(ph, "sd_junk", [128, 512], BF16); B_junk = Buf()
                prev = sbt(ph, "sd_prev", [128, 1024]); B_prev = Buf()
                prevb = sbt(ph, "sd_prevb", [128, 1024], BF16); B_prevb = Buf()
                yT = Ring(nc, ph, "sd_yT", 2, [128, 8, 512], BF16)
                S.op("pool", lambda: G.memset(prev[:], 0.0), writes=[B_prev])
                S.op("pool", lambda: G.memset(prevb[:], 0.0), writes=[B_prevb])
                xsrc = XBC_S.rearrange("c p t -> p c t")
                chs = {}
                cur = {}

                def prep(c):
                    cc = c % 4
                    if cc == 0:
                        Xin, B_xin = xin.next()
                        S.dma("sp", Xin[:, 0:6, :], xsrc[:, 0:6, c * 128:c * 128 + 512], writes=[B_xin])
                        S.dma("sp", Xin[:, 6:12, :], xsrc[:, 6:12, c * 128:c * 128 + 512], writes=[B_xin])
                        cur["Xin"] = (Xin, B_xin)
                        cur["YT"] = yT.next()
                    Xin, B_xin = cur["Xin"]
                    csl = slice(cc * 128, (cc + 1) * 128)
                    Zt, B_z = zin.next()
                    S.dma("sp", Zt[:], ZS_S[c], writes=[B_z])
                    pt, B_pt = ptr.next()
                    for j in range(8):
                        S.op("pe", lambda j=j: P.transpose(out=pt[:, j * 128:(j + 1) * 128], in_=Xin[:, j, csl], identity=identb[:]),
                             reads=[B_xin, B_c], writes=[B_pt], sig=(j == 7))
                    xs, B_xs = xs_tm.next()
                    S.op("act", lambda: A.copy(out=xs[:], in_=pt[:]), reads=[B_pt], writes=[B_xs])
                    pt2, B_pt2 = ptr.next()
                    for g in range(2):
                        S.op("pe", lambda g=g: P.transpose(out=pt2[:, g * 128:(g + 1) * 128], in_=Xin[:, 8 + g, csl], identity=identb[:]),
                             reads=[B_xin, B_c], writes=[B_pt2], sig=(g == 1))
                    bt, B_bt = b_tm.next()
                    S.op("act", lambda: A.copy(out=bt[:], in_=pt2[:, 0:256]), reads=[B_pt2], writes=[B_bt])
                    X, B_X = Xr.next()
                    S.op("dve", lambda: V.tensor_tensor(out=X[:].rearrange("p (h d) -> p h d", h=16),
                                                        in0=xs[:].rearrange("p (h d) -> p h d", h=16),
                                                        in1=bc(dt_tm[:, c, :].unsqueeze(2), [128, 16, 64]), op=ALU.mult),
                         reads=[B_xs, B_dt], writes=[B_X])
                    Xd, B_Xd = Xdr.next()
                    S.op("pool", lambda: G.tensor_tensor(out=Xd[:].rearrange("p (h d) -> p h d", h=16),
                                                         in0=X[:].rearrange("p (h d) -> p h d", h=16),
                                                         in1=bc(dA[:, c, :].unsqueeze(2), [128, 16, 64]), op=ALU.mult),
                         reads=[B_X, B_dt], writes=[B_Xd])
                    for g in range(2):
                        S.op("pe", lambda g=g: P.matmul(pcb[:, g, :], lhsT=Xin[:, 8 + g, csl], rhs=Xin[:, 10 + g, csl],
                                                        start=True, stop=True), reads=[B_xin], writes=[B_pcb], sig=(g == 1))
                    cb, B_cb = cbm.next()
                    S.op("dve", lambda: V.tensor_tensor(out=cb[:], in0=pcb[:, 0:2, :], in1=bc(triT_f[:].unsqueeze(1), [128, 2, 128]),
                                                        op=ALU.mult), reads=[B_pcb, B_c], writes=[B_cb])
                    chs[c] = dict(Xin=(Xin, B_xin), csl=csl, Z=(Zt, B_z), xs=(xs, B_xs), bt=(bt, B_bt), X=(X, B_X),
                                  Xd=(Xd, B_Xd), cb=(cb, B_cb), YT=cur["YT"], W={})

                def stage1a_pool(c, g):
                    hs = slice(g * 8, (g + 1) * 8)
                    rr_, B_rr = rhsR.next()
                    S.op("pool", lambda: G.tensor_tensor(out=rr_[:], in0=bc(triT_f[:].unsqueeze(1), [128, 8, 128]),
                                                         in1=bc(a_tm[:, c, hs].unsqueeze(2), [128, 8, 128]), op=ALU.mult),
                         reads=[B_c, B_dt], writes=[B_rr])
                    chs[c]["rr", g] = (rr_, B_rr)

                def stage1a_pe(c, g):
                    rr_, B_rr = chs[c]["rr", g]
                    pr_, B_pr = pR.next()
                    for q4 in range(2):
                        S.op("pe", lambda q4=q4: P.matmul(pr_[:, q4 * 4:(q4 + 1) * 4, :], lhsT=ones_f[:],
                                                          rhs=rr_[:, q4 * 4:(q4 + 1) * 4, :], start=True, stop=True),
                             reads=[B_c, B_rr], writes=[B_pr], sig=(q4 == 1))
                    chs[c]["pr", g] = (pr_, B_pr)

                def stage1b(c, g):
                    d = chs[c]
                    cb, B_cb = d["cb"]
                    hs = slice(g * 8, (g + 1) * 8)
                    pr_, B_pr = d["pr", g]
                    if False:
                        rr_, B_rr = rhsR.next()
                    sg, B_sg = segr.next()
                    S.op("dve", lambda: V.tensor_tensor(out=sg[:], in0=pr_[:], in1=bc(acs[:, c, hs].unsqueeze(2), [128, 8, 128]),
                                                        op=ALU.subtract), reads=[B_pr, B_dt], writes=[B_sg])
                    S.op("dve", lambda: V.tensor_single_scalar(out=sg[:], in_=sg[:], scalar=0.0, op=ALU.min),
                         reads=[B_sg], writes=[B_sg])
                    E, B_E = Er.next()
                    S.op("act", lambda: A.activation(out=E[:], in_=sg[:], func=AF.Exp), reads=[B_sg], writes=[B_E])
                    W, B_W = Wr.next()
                    S.op("dve", lambda: V.tensor_tensor(out=W[:], in0=E[:], in1=bc(cb[:, g, :].unsqueeze(1), [128, 8, 128]),
                                                        op=ALU.mult), reads=[B_E, B_cb], writes=[B_W])
                    d["W"][g] = (W, B_W)

                def stage2a(c, g):
                    d = chs[c]
                    Xin, B_xin = d["Xin"]; csl = d["csl"]; Zt, B_z = d["Z"]; xs, B_xs = d["xs"]; bt, B_bt = d["bt"]
                    X, B_X = d["X"]; Xd, B_Xd = d["Xd"]; YT, B_yT = d["YT"]; W, B_W = d["W"][g]
                    cc = c % 4
                    hs = slice(g * 8, (g + 1) * 8)
                    fs = slice(g * 512, (g + 1) * 512)
                    for h in range(8):
                        hh = g * 8 + h
                        S.op("pe", lambda h=h, hh=hh: P.matmul(pya[:, h * 64:(h + 1) * 64], lhsT=W[:, h, :],
                                                               rhs=X[:, hh * 64:(hh + 1) * 64], start=True, stop=True),
                             reads=[B_W, B_X], writes=[B_pya], sig=(h == 7))
                    S.op("pe", lambda: P.matmul(pyb[:], lhsT=Xin[:, 10 + g, csl], rhs=prevb[:, fs], start=True, stop=True),
                         reads=[B_xin, B_prevb], writes=[B_pyb])
                def stage2b(c, g):
                    d = chs[c]
                    Zt, B_z = d["Z"]; xs, B_xs = d["xs"]
                    hs = slice(g * 8, (g + 1) * 8)
                    fs = slice(g * 512, (g + 1) * 512)
                    y, B_y = yr.next()
                    S.op("dve", lambda: V.tensor_tensor(out=y[:].rearrange("p (h d) -> p h d", h=8),
                                                        in0=pyb[:].rearrange("p (h d) -> p h d", h=8),
                                                        in1=bc(eA[:, c, hs].unsqueeze(2), [128, 8, 64]), op=ALU.mult),
                         reads=[B_pyb, B_dt], writes=[B_y])
                    S.op("dve", lambda: V.tensor_tensor(out=y[:], in0=y[:], in1=pya[:], op=ALU.add),
                         reads=[B_y, B_pya], writes=[B_y])
                    y2, B_y2 = y2r.next()
                    S.op("pool", lambda: G.tensor_tensor(out=y2[:].rearrange("p (h d) -> p h d", h=8),
                                                         in0=xs[:, fs].rearrange("p (h d) -> p h d", h=8),
                                                         in1=bc(bc16[:, 2, hs].unsqueeze(2), [128, 8, 64]), op=ALU.mult),
                         reads=[B_xs, Bh], writes=[B_y2])
                    S.op("dve", lambda: V.tensor_tensor(out=y[:], in0=y[:], in1=y2[:], op=ALU.add),
                         reads=[B_y, B_y2], writes=[B_y])
                    S.op("dve", lambda: V.tensor_tensor(out=y[:], in0=y[:], in1=Zt[:, fs], op=ALU.mult),
                         reads=[B_y, B_z], writes=[B_y])
                    sm, B_sm = smr.next()
                    S.op("act", lambda: A.activation(out=junk[:], in_=y[:], func=AF.Square, accum_out=sm[:, 0:1]),
                         reads=[B_y], writes=[B_junk, B_sm])
                    rstd(sm[:, 1:2], sm[:, 0:1], 1.0 / 512.0, B_sm)
                    yn, B_yn = ynr.next()
                    S.op("dve", lambda: V.scalar_tensor_tensor(out=yn[:], in0=y[:], scalar=sm[:, 1:2], in1=ssdw[:, fs],
                                                               op0=ALU.mult, op1=ALU.mult),
                         reads=[B_y, B_sm, Bh], writes=[B_yn])
                    d["yn", g] = (yn, B_yn)

                def stage2c(c, g):
                    d = chs[c]
                    csl = d["csl"]; bt, B_bt = d["bt"]; Xd, B_Xd = d["Xd"]; YT, B_yT = d["YT"]
                    yn, B_yn = d["yn", g]
                    cc = c % 4
                    hs = slice(g * 8, (g + 1) * 8)
                    fs = slice(g * 512, (g + 1) * 512)
                    pto, B_pto = ptr.next()
                    for j in range(4):
                        S.op("pe", lambda j=j: P.transpose(out=pto[:, j * 128:(j + 1) * 128], in_=yn[:, j * 128:(j + 1) * 128],
                                                           identity=identb[:]), reads=[B_yn, B_c], writes=[B_pto], sig=(j == 3))
                    S.op("act", lambda: A.copy(out=YT[:, g * 4:(g + 1) * 4, csl],
                                               in_=pto[:, 0:512].rearrange("p (j t) -> p j t", j=4)),
                         reads=[B_pto], writes=[B_yT])
                    S.op("pe", lambda: P.matmul(pst_[:], lhsT=bt[:, g * 128:(g + 1) * 128], rhs=Xd[:, fs], start=True, stop=True),
                         reads=[B_bt, B_Xd], writes=[B_pst])
                    S.op("dve", lambda: V.tensor_tensor(out=prev[:, fs].rearrange("p (h d) -> p h d", h=8),
                                                        in0=prev[:, fs].rearrange("p (h d) -> p h d", h=8),
                                                        in1=bc(cdr[:, c, hs].unsqueeze(2), [128, 8, 64]), op=ALU.mult),
                         reads=[B_prev, B_dt], writes=[B_prev])
                    S.op("dve", lambda: V.tensor_tensor(out=prev[:, fs], in0=prev[:, fs], in1=pst_[:], op=ALU.add),
                         reads=[B_prev, B_pst], writes=[B_prev])
                    S.op("act", lambda: A.copy(out=prevb[:, fs], in_=prev[:, fs]), reads=[B_prev], writes=[B_prevb])
                    if cc == 3 and g == 1:
                        c0 = (c - 3) * 128
                        S.dma("sp", ACT_S.rearrange("c p t -> p c t")[:, 0:8, c0:c0 + 512], YT[:], reads=[B_yT])
                    if g == 1:
                        del chs[c]

                sunits = [(c, g) for c in range(32) for g in range(2)]
                NSU = len(sunits)
                for st_ in range(NSU + 1):
                    cur_u = sunits[st_] if st_ < NSU else None
                    prv_u = sunits[st_ - 1] if st_ >= 1 else None
                    if cur_u is not None:
                        if cur_u[1] == 0:
                            prep(cur_u[0])
                        stage1a_pool(*cur_u)
                    if prv_u is not None:
                        stage2a(*prv_u)
                    if cur_u is not None:
                        stage1a_pe(*cur_u)
                    if prv_u is not None:
                        stage2b(*prv_u)
                    if cur_u is not None:
                        stage1b(*cur_u)
                    if prv_u is not None:
                        stage2c(*prv_u)
                S.barrier()
            if hstop == 4:
                hst.close()
                return True

            wst = ExitStack()
            open_wres(wst)
            load_Wres(hy_w_out[0], 16)
            with ExitStack() as ph:
                kq = Ring(nc, ph, "at_kq", 2, [128, 2, L], BF16)
                vr = Ring(nc, ph, "at_v", 2, [128, 32, 130], BF16)
                pS = PRing(nc, ph, "at_pS", 2, [128, 2, 512])
                pO = PRing(nc, ph, "at_pO", 3, [128, 2, 256])
                pT = PRing(nc, ph, "at_pT", 1, [128, 1024], BF16)
                Er = Ring(nc, ph, "at_E", 4, [128, 2, 512], BF16)
                smr = Ring(nc, ph, "at_sm", 4, [128, 4], F32)
                tr_ = Ring(nc, ph, "at_t", 3, [128, 128], F32)
                or_ = Ring(nc, ph, "at_o", 3, [128, 128], F32)
                ybr = Ring(nc, ph, "at_yb", 3, [128, 128], BF16)
                junk = sbt(ph, "at_junk", [128, 128], BF16); B_junk = Buf()
                yT = Ring(nc, ph, "at_yT", 2, [128, L], BF16)
                for t_, b_ in zip(vr.t, vr.b):
                    S.op("pool", lambda t_=t_: G.memset(t_[:, :, 128:130], 1.0), writes=[b_])
                for t_, b_ in zip(Er.t, Er.b):
                    S.op("pool", lambda t_=t_: G.memset(t_[:], 0.0), writes=[b_])
                vsrc = V_S.rearrange("t p f -> p t f")
                for hd in range(8):
                    KQ, B_kq = kq.next()
                    S.dma("sp", KQ[:, 0, :], KT_S[hd], writes=[B_kq])
                    S.dma("sp", KQ[:, 1, :], QT_S[hd], writes=[B_kq])
                    Vt, B_v = vr.next()
                    for v0 in range(0, 32, 8):
                        S.dma("sp", Vt[:, v0:v0 + 8, 0:128], vsrc[:, v0:v0 + 8, hd * 128:(hd + 1) * 128], writes=[B_v])
                    YT, B_yT = yT.next()
                    items = [(qt, i) for qt in range(16) for i in range(qt + 1)]

                    def stage_qk(qt, i):
                        q0 = qt * 256
                        diag = (i == qt)
                        ps, B_ps = pS.next()
                        for half in range(2):
                            kb = 2 * i + half
                            c0 = 128 if (diag and half == 1) else 0
                            for m in range(2):
                                S.op("pe", lambda m=m, half=half, kb=kb, c0=c0: P.matmul(
                                    ps[:, m, half * 256 + c0:half * 256 + 256],
                                    lhsT=KQ[64 * m:64 * m + 64, 0, kb * 128:(kb + 1) * 128],
                                    rhs=KQ[64 * m:64 * m + 64, 1, q0 + c0:q0 + 256], start=True, stop=True),
                                    reads=[B_kq], writes=[B_ps], sig=(half == 1 and m == 1))
                        return ps, B_ps

                    def stage_exp(qt, i, ps, B_ps):
                        diag = (i == qt)
                        E, B_E = Er.next()
                        if diag:
                            S.op("act", lambda: A.activation(out=E[:, :, 0:256], in_=ps[:, :, 0:256], func=AF.Exp),
                                 reads=[B_ps], writes=[B_E])
                            S.op("act", lambda: A.activation(out=E[:, :, 384:512], in_=ps[:, :, 384:512], func=AF.Exp),
                                 reads=[B_ps], writes=[B_E])
                            Ev_ = E[:].rearrange("p m (r c) -> p m r c", c=128)
                            S.op("pool", lambda: G.tensor_tensor(out=Ev_[:, :, 0:4:3, :], in0=Ev_[:, :, 0:4:3, :],
                                                                 in1=bc(triT_b[:].unsqueeze(1).unsqueeze(1), [128, 2, 2, 128]),
                                                                 op=ALU.mult), reads=[B_E, B_c], writes=[B_E])
                        else:
                            S.op("act", lambda: A.activation(out=E[:], in_=ps[:], func=AF.Exp), reads=[B_ps], writes=[B_E])
                        return E, B_E

                    def stage_pv(qt, i, E, B_E, pOs):
                        diag = (i == qt)
                        for half in range(2):
                            kb = 2 * i + half
                            for qs in range(2):
                                if diag and half == 1 and qs == 0:
                                    continue
                                po, B_po = pOs[qs]
                                first = (i == 0 and half == 0)
                                last = diag and (half == qs)
                                for m in range(2):
                                    S.op("pe", lambda m=m, qs=qs, po=po, kb=kb, half=half, first=first, last=last: P.matmul(
                                        po[:, m, 0:129], lhsT=E[:, m, half * 256 + qs * 128:half * 256 + (qs + 1) * 128],
                                        rhs=Vt[:, kb, 0:129], start=(first and m == 0), stop=(last and m == 1),
                                        skip_group_check=True),
                                        reads=[B_E, B_v], writes=[B_po], sig=(last and m == 1))

                    def epilogue(qt, pOs):
                        q0 = qt * 256
                        for qs in range(2):
                            po, B_po = pOs[qs]
                            sm, B_sm = smr.next()
                            S.op("dve", lambda: V.reciprocal(out=sm[:, 0:2], in_=po[:, :, 128]), reads=[B_po], writes=[B_sm])
                            t_, B_t = tr_.next()
                            S.op("dve", lambda: V.tensor_scalar(out=t_[:], in0=po[:, 1, 0:128], scalar1=sm[:, 1:2], scalar2=nlam[:, 0:1],
                                                                op0=ALU.mult, op1=ALU.mult), reads=[B_po, B_sm, Bh], writes=[B_t])
                            o_, B_o = or_.next()
                            S.op("dve", lambda: V.scalar_tensor_tensor(out=o_[:], in0=po[:, 0, 0:128], scalar=sm[:, 0:1], in1=t_[:],
                                                                       op0=ALU.mult, op1=ALU.add), reads=[B_po, B_sm, B_t], writes=[B_o])
                            S.op("act", lambda: A.activation(out=junk[:], in_=o_[:], func=AF.Square, accum_out=sm[:, 2:3]),
                                 reads=[B_o], writes=[B_junk, B_sm])
                            rstd(sm[:, 3:4], sm[:, 2:3], 1.0 / 128.0, B_sm)
                            yb, B_yb = ybr.next()
                            S.op("dve", lambda: V.scalar_tensor_tensor(out=yb[:], in0=o_[:], scalar=sm[:, 3:4], in1=sublw[:],
                                                                       op0=ALU.mult, op1=ALU.mult), reads=[B_o, B_sm, Bh], writes=[B_yb])
                            pt, B_pt = pT.next()
                            S.op("pe", lambda: P.transpose(out=pt[:, 0:128], in_=yb[:], identity=identb[:]), reads=[B_yb, B_c], writes=[B_pt])
                            S.op("dve", lambda: V.tensor_copy(out=YT[:, q0 + qs * 128:q0 + (qs + 1) * 128], in_=pt[:, 0:128]),
                                 reads=[B_pt], writes=[B_yT])

                    nxt_qk = stage_qk(*items[0])
                    pOs = None
                    for n, (qt, i) in enumerate(items):
                        ps, B_ps = nxt_qk
                        if n + 1 < len(items):
                            nxt_qk = stage_qk(*items[n + 1])
                        E, B_E = stage_exp(qt, i, ps, B_ps)
                        if i == 0:
                            pOs = [pO.next(), pO.next()]
                        stage_pv(qt, i, E, B_E, pOs)
                        if i == qt:
                            epilogue(qt, pOs)
                    S.dma("sp", ACT_S[8 + hd], YT[:], reads=[B_yT])
                S.barrier()
            if hstop == 5:
                wst.close(); hst.close()
                return True
            phase_tm(16, h_in_ap, h_out_ap, widx_next)
            wst.close()
            hst.close()
            return False

        pcount = [0]

        def chk():
            pcount[0] += 1
            return stop is not None and pcount[0] >= stop

        def _main_program():
            h_cur = x
            scr = [hA, hB, hC]
            nxt = 0
            first = True
            for li, layer in enumerate(layers):
                last_layer = (li == len(layers) - 1)
                if first:
                    phase_norm_in(h_cur, 2 * layer)
                    if chk():
                        return
                    first = False
                h_mid = scr[nxt]; nxt += 1
                if layer == 0:
                    if phase_hybrid(h_cur, h_mid, 2 * layer + 1):
                        return
                    if chk():
                        return
                else:
                    with ExitStack() as ws:
                        open_wres(ws)
                        phase_shortconv(sc_w_out[0], 8)
                        if chk():
                            return
                        phase_tm(8, h_cur, h_mid, 2 * layer + 1)
                        if chk():
                            return
                with ExitStack() as ws:
                    open_wres(ws)
                    phase_ffn_up(layer, ffn_w_down[layer], 22)
                    if chk():
                        return
                    if last_layer:
                        phase_tm(22, h_mid, out, None)
                    else:
                        h_new = scr[nxt]; nxt += 1
                        phase_tm(22, h_mid, h_new, 2 * layers[li + 1])
                        h_cur = h_new
                    if chk():
                        return

        try:
            _main_program()
        except _Stop:
            pass
        S.barrier(engines=("sp",))
        build_program.stats = (S.n_ins, S.n_wait)
    return nc


_INPUT_NAMES = ["mix_norm", "ffn_norm", "hy_w_in", "hy_conv_w", "hy_conv_b", "hy_dt_bias", "hy_a_log",
                "hy_d_skip", "hy_ssd_norm", "hy_q_norm", "hy_k_norm", "hy_lambda_q1", "hy_lambda_k1",
                "hy_lambda_q2", "hy_lambda_k2", "hy_subln", "hy_w_out", "sc_w_in", "sc_conv_w", "sc_w_out",
                "ffn_w_up", "ffn_conv_w", "ffn_conv_b", "ffn_w_down"]


def kernel(**inputs):
    x = np.asarray(inputs["x"], dtype=np.float32)
    nc = build_program()
    shared = {k: np.ascontiguousarray(np.asarray(inputs[k], dtype=np.float32)) for k in _INPUT_NAMES}
    in_maps = []
    for b in range(8):
        m = dict(shared)
        m["x"] = np.ascontiguousarray(x[b])
        in_maps.append(m)
    res = run_bass_kernel_spmd(nc, in_maps, core_ids=list(range(8)))
    return np.stack([np.asarray(r["out"], dtype=np.float32) for r in res.results], axis=0)
```

```python
import math
from contextlib import ExitStack

import numpy as np
import concourse.bass as bass
import concourse.mybir as mybir
from concourse.bass_utils import run_bass_kernel_spmd

F32 = mybir.dt.float32
BF16 = mybir.dt.bfloat16
I32 = mybir.dt.int32
AF = mybir.ActivationFunctionType
ALU = mybir.AluOpType

L = 4096
DM = 1024
NT = 8
DFF = 2816
EPS = 1e-6
W_IN0 = 5648
C_Z, C_XBC, C_DT, C_Q, C_K, C_V = 0, 1024, 2560, 2576, 3600, 4624


class Ev:
    __slots__ = ("key", "sem", "val", "eng", "clock")

    def __init__(self, key, sem, val, eng, clock):
        self.key, self.sem, self.val, self.eng, self.clock = key, sem, val, eng, clock


class Buf:
    __slots__ = ("name", "w", "rd", "excl")

    def __init__(self, name="", excl=False):
        self.name, self.w, self.rd, self.excl = name, None, {}, excl


class Sched:
    NDMA = 8

    def __init__(self, nc, stack):
        self.nc = nc
        self.engs = {"pe": nc.tensor, "act": nc.scalar, "dve": nc.vector,
                     "pool": nc.gpsimd, "sp": nc.sync}
        self.sem, self.cnt, self.pending, self.last_ins = {}, {}, {}, {}
        self.seen = {e: {} for e in self.engs}
        for e in ("pe", "act", "dve", "pool"):
            self.sem[e] = stack.enter_context(nc.semaphore("s_" + e))
            self.cnt[e] = 0
            self.pending[e] = []
            self.last_ins[e] = None
        self.dsem, self.dcnt, self.drr = {}, {}, {}
        for q in ("sp", "pool"):
            self.dsem[q] = []
            for i in range(self.NDMA):
                k = "d_%s%d" % (q, i)
                self.dsem[q].append((k, stack.enter_context(nc.semaphore(k))))
                self.dcnt[k] = 0
            self.drr[q] = 0
        self.n_wait = 0
        self.n_ins = 0

    def _need(self, e, ev):
        if ev is None:
            return
        if ev.val is None:
            self._force(ev.eng)
        seen = self.seen[e]
        if seen.get(ev.key, 0) >= ev.val:
            return
        self.engs[e].wait_ge(ev.sem, ev.val)
        self.n_wait += 1
        for k, v in ev.clock.items():
            if seen.get(k, 0) < v:
                seen[k] = v

    def _force(self, e):
        if not self.pending[e]:
            return
        self.last_ins[e].then_inc(self.sem[e], 1)
        self._signal(e)

    def _signal(self, e):
        self.cnt[e] += 1
        v = self.cnt[e]
        clock = dict(self.seen[e])
        clock["s_" + e] = v
        for ev in self.pending[e]:
            ev.val = v
            ev.clock = clock
        self.pending[e] = []
        self.last_ins[e] = None

    def _deps(self, e, reads, writes, is_dma):
        for b in reads:
            if b.w is not None:
                self._need(e, b.w)
            if b.excl:
                for ev in list(b.rd.values()):
                    if ev.eng != e:
                        self._need(e, ev)
        for b in writes:
            if b.w is not None and (is_dma or b.w.eng != e or e == "pool"):
                self._need(e, b.w)
            for ev in list(b.rd.values()):
                if is_dma or ev.eng != e or e == "pool":
                    self._need(e, ev)

    def _record(self, ev, reads, writes):
        for b in reads:
            b.rd[ev.key] = ev
        for b in writes:
            b.w = ev
            b.rd = {}

    def op(self, e, fn, reads=(), writes=(), sig=True):
        self._deps(e, reads, writes, False)
        ins = fn()
        self.n_ins += 1
        ev = Ev("s_" + e, self.sem[e], None, e, None)
        self.pending[e].append(ev)
        self.last_ins[e] = ins
        if sig:
            ins.then_inc(self.sem[e], 1)
            self._signal(e)
        self._record(ev, reads, writes)
        return ev

    def dma(self, q, out, in_, reads=(), writes=(), **kw):
        i = self.drr[q]
        self.drr[q] = (i + 1) % self.NDMA
        key, sem = self.dsem[q][i]
        prev = self.dcnt[key]
        seen = self.seen[q]
        if prev > 0 and seen.get(key, 0) < prev:
            self.engs[q].wait_ge(sem, prev)
            self.n_wait += 1
            seen[key] = prev
        self._deps(q, reads, writes, True)
        ins = self.engs[q].dma_start(out=out, in_=in_, **kw)
        ins.then_inc(sem, 16)
        self.n_ins += 1
        self.dcnt[key] = prev + 16
        clock = dict(seen)
        clock[key] = prev + 16
        ev = Ev(key, sem, prev + 16, None, clock)
        self._record(ev, reads, writes)
        return ev

    def barrier(self, engines=("pe", "act", "dve", "pool", "sp")):
        for e in ("pe", "act", "dve", "pool"):
            self._force(e)
        evs = []
        for e in ("pe", "act", "dve", "pool"):
            if self.cnt[e] > 0:
                evs.append(Ev("s_" + e, self.sem[e], self.cnt[e], e, {"s_" + e: self.cnt[e]}))
        for q in self.dsem:
            for key, sem in self.dsem[q]:
                if self.dcnt[key] > 0:
                    evs.append(Ev(key, sem, self.dcnt[key], None, {key: self.dcnt[key]}))
        for e in engines:
            for ev in evs:
                if ev.eng != e:
                    self._need(e, ev)


_UID = [0]


class Ring:
    def __init__(self, nc, st, name, n, shape, dt):
        _UID[0] += 1
        name = "%s_%d_" % (name, _UID[0])
        self.t = [st.enter_context(nc.sbuf_tensor("%s%d" % (name, i), shape, dt)) for i in range(n)]
        self.b = [Buf("%s%d" % (name, i)) for i in range(n)]
        self.i = 0

    def next(self):
        i = self.i
        self.i = (i + 1) % len(self.t)
        return self.t[i], self.b[i]


class PRing:
    def __init__(self, nc, st, name, n, shape, dt=F32):
        _UID[0] += 1
        name = "%s_%d_" % (name, _UID[0])
        self.t = [st.enter_context(nc.psum_tensor("%s%d" % (name, i), shape, dt)) for i in range(n)]
        self.b = [Buf("%s%d" % (name, i), excl=True) for i in range(n)]
        self.i = 0

    def next(self):
        i = self.i
        self.i = (i + 1) % len(self.t)
        return self.t[i], self.b[i]


def bc(ap, shape):
    return ap.to_broadcast(shape)


class _Stop(Exception):
    pass


def build_program(layers=(0, 1), dbg=False, stop=None, hstop=None):
    nc = bass.Bass("TRN2", target_bir_lowering=False)

    def din(name, shape):
        return nc.dram_tensor(name, shape, F32, kind="ExternalInput").ap()

    x = din("x", [L, DM])
    mix_norm = din("mix_norm", [2, DM])
    ffn_norm = din("ffn_norm", [2, DM])
    hy_w_in = din("hy_w_in", [1, DM, W_IN0])
    hy_conv_w = din("hy_conv_w", [1, 4, 1536])
    hy_conv_b = din("hy_conv_b", [1, 1536])
    hy_dt_bias = din("hy_dt_bias", [1, 16])
    hy_a_log = din("hy_a_log", [1, 16])
    hy_d_skip = din("hy_d_skip", [1, 16])
    hy_ssd_norm = din("hy_ssd_norm", [1, 1024])
    hy_q_norm = din("hy_q_norm", [1, 64])
    hy_k_norm = din("hy_k_norm", [1, 64])
    hy_lq1 = din("hy_lambda_q1", [1, 64])
    hy_lk1 = din("hy_lambda_k1", [1, 64])
    hy_lq2 = din("hy_lambda_q2", [1, 64])
    hy_lk2 = din("hy_lambda_k2", [1, 64])
    hy_subln = din("hy_subln", [1, 128])
    hy_w_out = din("hy_w_out", [1, 2048, DM])
    sc_w_in = din("sc_w_in", [1, DM, 3072])
    sc_conv_w = din("sc_conv_w", [1, 3, 1024])
    sc_w_out = din("sc_w_out", [1, 1024, DM])
    ffn_w_up = din("ffn_w_up", [2, DM, 2 * DFF])
    ffn_conv_w = din("ffn_conv_w", [2, 3, 2 * DFF])
    ffn_conv_b = din("ffn_conv_b", [2, 2 * DFF])
    ffn_w_down = din("ffn_w_down", [2, DFF, DM])
    out = nc.dram_tensor("out", [L, DM], F32, kind="ExternalOutput").ap()

    skind = "ExternalOutput" if dbg else "Internal"

    def dscr(name, shape, dt=BF16):
        return nc.dram_tensor(name, shape, dt, kind=skind).ap()

    hA = dscr("hA", [L, DM], F32)
    hB = dscr("hB", [L, DM], F32)
    hC = dscr("hC", [L, DM], F32)
    ACT_S = dscr("ACT_S", [22, 128, L])
    XBC_S = dscr("XBC_S", [12, 128, L])
    QT_S = dscr("QT_S", [8, 128, L])
    KT_S = dscr("KT_S", [8, 128, L])
    ZS_S = dscr("ZS_S", [32, 128, 1024])
    V_S = dscr("V_S", [32, 128, 1024])

    with ExitStack() as top:
        S = Sched(nc, top)
        V, A, G, P = nc.vector, nc.scalar, nc.gpsimd, nc.tensor

        def sbt(st, name, shape, dt=F32):
            _UID[0] += 1
            return st.enter_context(nc.sbuf_tensor("%s_%d" % (name, _UID[0]), shape, dt))

        def pst(st, name, shape, dt=F32):
            _UID[0] += 1
            return st.enter_context(nc.psum_tensor("%s_%d" % (name, _UID[0]), shape, dt))

        hnT = sbt(top, "hnT", [128, 8, L], BF16)
        B_hn = [Buf("hn%d" % i) for i in range(32)]
        WR = {"t": None}
        B_Wres = Buf("Wres")

        def open_wres(st):
            WR["t"] = sbt(st, "Wres", [128, 22, DM], BF16)
        identf = sbt(top, "identf", [128, 128]); B_c = Buf("consts")
        identb = sbt(top, "identb", [128, 128], BF16)
        triT_f = sbt(top, "triT_f", [128, 128])
        triT_b = sbt(top, "triT_b", [128, 128], BF16)
        ones_f = sbt(top, "ones_f", [128, 128])
        epsc = sbt(top, "epsc", [128, 1])
        normw = sbt(top, "normw", [128, 4, 8])
        fcw = sbt(top, "fcw", [128, 2, 3, 44])
        fcb = sbt(top, "fcb", [128, 2, 44])
        scw = sbt(top, "scw", [128, 3, 8])

        S.op("pool", lambda: G.memset(identf[:], 1.0), writes=[B_c])
        S.op("pool", lambda: G.affine_select(out=identf[:], in_=identf[:], pattern=[[-1, 128]],
                                             compare_op=ALU.is_equal, fill=0.0, base=0,
                                             channel_multiplier=1), reads=[B_c], writes=[B_c])
        S.op("pool", lambda: G.memset(triT_f[:], 1.0), writes=[B_c])
        S.op("pool", lambda: G.affine_select(out=triT_f[:], in_=triT_f[:], pattern=[[1, 128]],
                                             compare_op=ALU.is_ge, fill=0.0, base=0,
                                             channel_multiplier=-1), reads=[B_c], writes=[B_c])
        S.op("pool", lambda: G.memset(ones_f[:], 1.0), writes=[B_c])
        S.op("pool", lambda: G.memset(epsc[:], EPS), writes=[B_c])
        S.op("dve", lambda: V.tensor_copy(out=identb[:], in_=identf[:]), reads=[B_c], writes=[B_c])
        S.op("dve", lambda: V.tensor_copy(out=triT_b[:], in_=triT_f[:]), reads=[B_c], writes=[B_c])

        def load_cols(st_ring, ps_ring, dst, src_row, nch):
            stg, B_s = st_ring.next()
            S.dma("sp", stg[0:nch, :], src_row.rearrange("(c p) -> c p", p=128), writes=[B_s])
            ps, B_p = ps_ring.next()
            S.op("pe", lambda: P.matmul(ps[:, 0:nch], lhsT=stg[0:nch, :], rhs=identf[0:nch, 0:nch],
                                        start=True, stop=True), reads=[B_s, B_c], writes=[B_p])
            S.op("dve", lambda: V.tensor_copy(out=dst, in_=ps[:, 0:nch]), reads=[B_p], writes=[B_c])

        with ExitStack() as ph:
            stg_r = Ring(nc, ph, "stg", 3, [64, 128], F32)
            ps_r = PRing(nc, ph, "psc", 2, [128, 512])
            for i, (src, l) in enumerate([(mix_norm, 0), (ffn_norm, 0), (mix_norm, 1), (ffn_norm, 1)]):
                load_cols(stg_r, ps_r, normw[:, i, :], src[l, :], 8)
            for l in range(2):
                for k in range(3):
                    load_cols(stg_r, ps_r, fcw[:, l, k, :], ffn_conv_w[l, k, :], 44)
                load_cols(stg_r, ps_r, fcb[:, l, :], ffn_conv_b[l, :], 44)
            for k in range(3):
                load_cols(stg_r, ps_r, scw[:, k, :], sc_conv_w[0, k, :], 8)
            S.barrier()

        def load_Wres(w2d, nch):
            src = w2d.rearrange("(c p) n -> p c n", p=128)
            step = 4
            for c0 in range(0, nch, step):
                c1 = min(nch, c0 + step)
                S.dma("pool", WR["t"][:, c0:c1, :], src[:, c0:c1, :], writes=[B_Wres])

        def rstd(dst, src, scale, B_):
            S.op("act", lambda: A.activation(out=dst, in_=src, func=AF.Ln, bias=epsc[:, 0:1], scale=scale),
                 reads=[B_, B_c], writes=[B_])
            S.op("act", lambda: A.activation(out=dst, in_=dst, func=AF.Exp, scale=-0.5), reads=[B_], writes=[B_])

        class NormRes:
            def __init__(self, st):
                self.junk = sbt(st, "nr_junk", [128, DM], BF16); self.B_junk = Buf()
                self.xn = Ring(nc, st, "nr_xn", 2, [128, DM], BF16)
                self.sm = Ring(nc, st, "nr_sm", 3, [128, 2], F32)
                self.ps = PRing(nc, st, "nr_ps", 2, [128, 8, 128], BF16)

        def norm_to_hnT(nr, h_sb, B_h, widx, t128):
            sm, B_sm = nr.sm.next()
            S.op("act", lambda: A.activation(out=nr.junk[:], in_=h_sb, func=AF.Square,
                                             accum_out=sm[:, 0:1]),
                 reads=[B_h], writes=[nr.B_junk, B_sm])
            rstd(sm[:, 1:2], sm[:, 0:1], 1.0 / DM, B_sm)
            xn, B_xn = nr.xn.next()
            S.op("dve", lambda: V.tensor_scalar(out=xn[:], in0=h_sb, scalar1=sm[:, 1:2], scalar2=None,
                                                op0=ALU.mult), reads=[B_h, B_sm], writes=[B_xn])
            ps, B_ps = nr.ps.next()
            for c in range(8):
                S.op("pe", lambda c=c: P.transpose(out=ps[:, c, :], in_=xn[:, c * 128:(c + 1) * 128],
                                                   identity=identb[:]),
                     reads=[B_xn, B_c], writes=[B_ps], sig=(c == 7))
            S.op("dve", lambda: V.tensor_tensor(out=hnT[:, :, t128 * 128:(t128 + 1) * 128], in0=ps[:],
                                                in1=bc(normw[:, widx, :].unsqueeze(2), [128, 8, 128]),
                                                op=ALU.mult),
                 reads=[B_ps, B_c], writes=[B_hn[t128]])

        def phase_norm_in(h_src, widx):
            with ExitStack() as ph:
                nr = NormRes(ph)
                hr = Ring(nc, ph, "ni_h", 3, [128, DM], F32)
                for t in range(32):
                    h, B_h = hr.next()
                    S.dma("sp", h[:], h_src[t * 128:(t + 1) * 128, :], writes=[B_h])
                    norm_to_hnT(nr, h[:], B_h, widx, t)
                S.barrier()

        def phase_tm(nch, h_src, h_dst, widx_next):
            with ExitStack() as ph:
                nr = NormRes(ph) if widx_next is not None else None
                ar = Ring(nc, ph, "tm_a", 2, [128, nch, 512], BF16)
                hr = Ring(nc, ph, "tm_h", 3, [128, DM], F32)
                hn = Ring(nc, ph, "tm_hn", 3, [128, DM], F32)
                pr = PRing(nc, ph, "tm_ps", 4, [128, 512])
                src = ACT_S.rearrange("c p t -> p c t")
                pending_epi = [None]
                for tt in range(NT):
                    a, B_a = ar.next()
                    half = (nch + 1) // 2
                    S.dma("sp", a[:, 0:half, :], src[:, 0:half, tt * 512:(tt + 1) * 512], writes=[B_a])
                    S.dma("sp", a[:, half:nch, :], src[:, half:nch, tt * 512:(tt + 1) * 512], writes=[B_a])
                    for sub in range(4):
                        t128 = tt * 4 + sub
                        h, B_h = hr.next()
                        S.dma("sp", h[:], h_src[t128 * 128:(t128 + 1) * 128, :], writes=[B_h])
                        pss = [pr.next(), pr.next()]
                        for c in range(nch):
                            for hf in range(2):
                                ps, B_ps = pss[hf]
                                S.op("pe", lambda c=c, hf=hf, ps=ps: P.matmul(
                                    ps[:], lhsT=a[:, c, sub * 128:(sub + 1) * 128],
                                    rhs=WR["t"][:, c, hf * 512:(hf + 1) * 512],
                                    start=(c == 0), stop=(c == nch - 1)),
                                    reads=[B_a, B_Wres], writes=[B_ps], sig=(c == nch - 1))
                        def epi(pss=pss, h=h, B_h=B_h, t128=t128):
                            o, B_o = hn.next()
                            for hf in range(2):
                                ps, B_ps = pss[hf]
                                S.op("dve", lambda hf=hf, ps=ps: V.tensor_tensor(
                                    out=o[:, hf * 512:(hf + 1) * 512], in0=ps[:],
                                    in1=h[:, hf * 512:(hf + 1) * 512], op=ALU.add),
                                    reads=[B_ps, B_h], writes=[B_o])
                            S.dma("sp", h_dst[t128 * 128:(t128 + 1) * 128, :], o[:], reads=[B_o])
                            if nr is not None:
                                norm_to_hnT(nr, o[:], B_o, widx_next, t128)
                        if pending_epi[0] is not None:
                            pending_epi[0]()
                        pending_epi[0] = epi
                pending_epi[0]()
                S.barrier()

        def fm_matmuls(wt, n0, tt, ps, B_ps, B_w):
            for k in range(8):
                S.op("pe", lambda k=k: P.matmul(ps[:], lhsT=wt[:, k, n0:n0 + 128],
                                                rhs=hnT[:, k, tt * 512:(tt + 1) * 512],
                                                start=(k == 0), stop=(k == 7)),
                     reads=[B_w] + B_hn[tt * 4:tt * 4 + 4], writes=[B_ps], sig=(k == 7))

        def wload(wr, w2d, col0, ncol=128):
            wt, B_w = wr.next()
            S.dma("pool", wt[:, :, 0:ncol], w2d.rearrange("(k p) n -> p k n", p=128)[:, :, col0:col0 + ncol],
                  writes=[B_w])
            return wt, B_w

        def phase_ffn_up(l, w_next, nch_next):
            w2d = ffn_w_up[l]
            with ExitStack() as ph:
                wr = Ring(nc, ph, "fu_w", 6, [128, 8, 128], BF16)
                pr = PRing(nc, ph, "fu_ps", 6, [128, 512])
                xg = Ring(nc, ph, "fu_xg", 1, [128, L + 2], F32)
                xv = Ring(nc, ph, "fu_xv", 1, [128, L + 2], F32)
                yr = Ring(nc, ph, "fu_y", 6, [128, 512], F32)
                sr = Ring(nc, ph, "fu_s", 3, [128, 512], F32)
                gr = Ring(nc, ph, "fu_g", 2, [128, L], BF16)
                for r in (xg, xv):
                    for t_, b_ in zip(r.t, r.b):
                        S.op("pool", lambda t_=t_: G.memset(t_[:, 0:2], 0.0), writes=[b_])
                pend = [(wload(wr, w2d, 0), wload(wr, w2d, DFF))]
                load_Wres(w_next, nch_next)
                for i in range(22):
                    if i + 1 < 22:
                        pend.append((wload(wr, w2d, (i + 1) * 128), wload(wr, w2d, DFF + (i + 1) * 128)))
                    (wg, B_wg), (wv, B_wv) = pend.pop(0)
                    Xg, B_xg = xg.next()
                    Xv, B_xv = xv.next()
                    Gt, B_g = gr.next()
                    cg, cv = i, 22 + i
                    for tt in range(NT):
                        sl = slice(tt * 512, (tt + 1) * 512)
                        psg, B_pg = pr.next()
                        psv, B_pv = pr.next()
                        fm_matmuls(wg, 0, tt, psg, B_pg, B_wg)
                        fm_matmuls(wv, 0, tt, psv, B_pv, B_wv)
                        Yg, B_yg = yr.next()
                        Yv, B_yv = yr.next()
                        S.op("act", lambda: A.activation(out=Yg[:], in_=psg[:], func=AF.Identity,
                                                         bias=fcb[:, l, cg:cg + 1], scale=fcw[:, l, 2, cg:cg + 1]),
                             reads=[B_pg, B_c], writes=[B_yg])
                        S.op("act", lambda: A.copy(out=Xg[:, 2 + tt * 512:2 + (tt + 1) * 512], in_=psg[:]),
                             reads=[B_pg], writes=[B_xg])
                        S.op("act", lambda: A.activation(out=Yv[:], in_=psv[:], func=AF.Identity,
                                                         bias=fcb[:, l, cv:cv + 1], scale=fcw[:, l, 2, cv:cv + 1]),
                             reads=[B_pv, B_c], writes=[B_yv])
                        S.op("dve", lambda: V.tensor_copy(out=Xv[:, 2 + tt * 512:2 + (tt + 1) * 512], in_=psv[:]),
                             reads=[B_pv], writes=[B_xv])
                        for (e, X_, B_x, Y_, B_y, c_) in (("dve", Xg, B_xg, Yg, B_yg, cg), ("dve", Xv, B_xv, Yv, B_yv, cv)):
                            for k in (1, 0):
                                en = "dve"
                                E_ = G if en == "pool" else V
                                S.op(en, lambda E_=E_, X_=X_, Y_=Y_, k=k, c_=c_: E_.scalar_tensor_tensor(
                                    out=Y_[:], in0=X_[:, tt * 512 + k:tt * 512 + k + 512],
                                    scalar=fcw[:, l, k, c_:c_ + 1], in1=Y_[:], op0=ALU.mult, op1=ALU.add),
                                    reads=[B_x, B_c, B_y], writes=[B_y])
                        St, B_s = sr.next()
                        S.op("act", lambda: A.activation(out=St[:], in_=Yg[:], func=AF.Silu),
                             reads=[B_yg], writes=[B_s])
                        S.op("pool", lambda: G.tensor_tensor(out=Gt[:, sl], in0=St[:], in1=Yv[:], op=ALU.mult),
                             reads=[B_s, B_yv], writes=[B_g])
                    S.dma("sp", ACT_S[i], Gt[:], reads=[B_g])
                S.barrier()

        def phase_shortconv(w_next, nch_next):
            w2d = sc_w_in[0]
            with ExitStack() as ph:
                wr = Ring(nc, ph, "sc_w", 6, [128, 8, 128], BF16)
                pr = PRing(nc, ph, "sc_ps", 6, [128, 512])
                mr = Ring(nc, ph, "sc_m", 2, [128, L + 2], F32)
                ur = Ring(nc, ph, "sc_u", 3, [128, 512], F32)
                yr = Ring(nc, ph, "sc_y", 3, [128, 512], F32)
                rr = Ring(nc, ph, "sc_r", 2, [128, L], BF16)
                for t_, b_ in zip(mr.t, mr.b):
                    S.op("pool", lambda t_=t_: G.memset(t_[:, 0:2], 0.0), writes=[b_])
                load_Wres(w_next, nch_next)
                for i in range(8):
                    wb_, wc_, wu_ = (wload(wr, w2d, i * 128), wload(wr, w2d, 1024 + i * 128),
                                     wload(wr, w2d, 2048 + i * 128))
                    M, B_m = mr.next()
                    R, B_r = rr.next()
                    for tt in range(NT):
                        sl = slice(tt * 512, (tt + 1) * 512)
                        psb, B_pb = pr.next()
                        psc, B_pc = pr.next()
                        psu, B_pu = pr.next()
                        fm_matmuls(wc_[0], 0, tt, psc, B_pc, wc_[1])
                        fm_matmuls(wu_[0], 0, tt, psu, B_pu, wu_[1])
                        fm_matmuls(wb_[0], 0, tt, psb, B_pb, wb_[1])
                        U, B_u = ur.next()
                        S.op("act", lambda: A.copy(out=U[:], in_=psu[:]), reads=[B_pu], writes=[B_u])
                        S.op("dve", lambda: V.tensor_tensor(out=M[:, 2 + tt * 512:2 + (tt + 1) * 512], in0=psc[:],
                                                            in1=U[:], op=ALU.mult),
                             reads=[B_pc, B_u], writes=[B_m])
                        Y, B_y = yr.next()
                        S.op("act", lambda: A.activation(out=Y[:], in_=M[:, 2 + tt * 512:2 + (tt + 1) * 512],
                                                         func=AF.Copy, scale=scw[:, 2, i:i + 1]),
                             reads=[B_m, B_c], writes=[B_y])
                        for k in (1, 0):
                            S.op("dve", lambda k=k: V.scalar_tensor_tensor(
                                out=Y[:], in0=M[:, tt * 512 + k:tt * 512 + k + 512], scalar=scw[:, k, i:i + 1],
                                in1=Y[:], op0=ALU.mult, op1=ALU.add), reads=[B_m, B_c, B_y], writes=[B_y])
                        S.op("dve", lambda: V.tensor_tensor(out=R[:, sl], in0=psb[:], in1=Y[:], op=ALU.mult),
                             reads=[B_pb, B_y], writes=[B_r])
                    S.dma("sp", ACT_S[i], R[:], reads=[B_r])
                S.barrier()

        def phase_hybrid(h_in_ap, h_out_ap, widx_next):
            hyb = {}
            w2d = hy_w_in[0]
            hst = ExitStack()
            cst = ExitStack()
            Bh = Buf("hyb_consts")
            blk = sbt(hst, "blk", [128, 128], BF16)
            prot = sbt(hst, "prot", [128, 128], BF16)
            hcw = sbt(hst, "hcw", [128, 4, 12])
            hcb = sbt(hst, "hcb", [128, 12])
            qkw = sbt(hst, "qkw", [128, 2])
            dt_tm = sbt(hst, "dt_tm", [128, 32, 16])
            a_tm = sbt(hst, "a_tm", [128, 32, 16])
            acs = sbt(hst, "acs", [128, 32, 16])
            eA = sbt(hst, "eA", [128, 32, 16])
            dA = sbt(hst, "dA", [128, 32, 16])
            cdr = sbt(hst, "cdr", [128, 32, 16])
            bc16 = sbt(hst, "bc16", [128, 3, 16])
            ssdw = sbt(hst, "ssdw", [128, 1024])
            sublw = sbt(hst, "sublw", [128, 128])
            nlam = sbt(hst, "nlam", [128, 2])
            B_dt = Buf("dtstuff")
            cosT = sbt(cst, "cosT", [128, L])
            sinT = sbt(cst, "sinT", [128, L])

            with ExitStack() as ph:
                tf = sbt(ph, "c_tf", [128, 128])
                tf2 = sbt(ph, "c_tf2", [128, 128])
                B_t = Buf()
                S.op("pool", lambda: G.memset(tf[:], 0.0), writes=[B_t])
                S.op("pool", lambda: G.memset(tf[0:64, 0:64], 1.0), writes=[B_t])
                S.op("pool", lambda: G.memset(tf[64:128, 64:128], 1.0), writes=[B_t])
                S.op("dve", lambda: V.tensor_copy(out=blk[:], in_=tf[:]), reads=[B_t], writes=[Bh])
                S.op("pool", lambda: G.memset(tf[:], 1.0), reads=[Bh], writes=[B_t])
                S.op("pool", lambda: G.affine_select(out=tf[:], in_=tf[:], pattern=[[-1, 128]], compare_op=ALU.is_equal,
                                                     fill=0.0, base=32, channel_multiplier=1), reads=[B_t], writes=[B_t])
                S.op("pool", lambda: G.memset(tf2[:], -1.0), writes=[B_t])
                S.op("pool", lambda: G.affine_select(out=tf2[:], in_=tf2[:], pattern=[[-1, 128]], compare_op=ALU.is_equal,
                                                     fill=0.0, base=-32, channel_multiplier=1), reads=[B_t], writes=[B_t])
                for c0 in (0, 64):
                    S.op("pool", lambda c0=c0: G.memset(tf[:, c0:c0 + 32], 0.0), reads=[B_t], writes=[B_t])
                    S.op("pool", lambda c0=c0: G.memset(tf2[:, c0 + 32:c0 + 64], 0.0), reads=[B_t], writes=[B_t])
                S.op("dve", lambda: V.tensor_tensor(out=tf[:], in0=tf[:], in1=tf2[:], op=ALU.add), reads=[B_t], writes=[B_t])
                S.op("dve", lambda: V.tensor_copy(out=prot[:], in_=tf[:]), reads=[B_t], writes=[Bh])
                pi_ = sbt(ph, "c_pi", [128, 1], I32)
                pf_ = sbt(ph, "c_pf", [128, 2])
                S.op("pool", lambda: G.iota(pi_[:], pattern=[[0, 1]], base=0, channel_multiplier=1), writes=[B_t])
                S.op("dve", lambda: V.tensor_single_scalar(out=pi_[:], in_=pi_[:], scalar=31, op=ALU.bitwise_and),
                     reads=[B_t], writes=[B_t])
                S.op("dve", lambda: V.tensor_copy(out=pf_[:, 0:1], in_=pi_[:]), reads=[B_t], writes=[B_t])
                S.op("dve", lambda: V.tensor_scalar(out=pf_[:, 0:1], in0=pf_[:, 0:1], scalar1=-math.log(10000.0) / 32.0,
                                                    scalar2=-math.log(2 * math.pi), op0=ALU.mult, op1=ALU.add),
                     reads=[B_t], writes=[B_t])
                S.op("act", lambda: A.activation(out=pf_[:, 1:2], in_=pf_[:, 0:1], func=AF.Exp), reads=[B_t], writes=[B_t])
                ti = sbt(ph, "c_ti", [128, L], I32)
                r_ = sbt(ph, "c_r", [128, L])
                kf = sbt(ph, "c_kf", [128, L])
                S.op("pool", lambda: G.iota(ti[:], pattern=[[1, L]], base=0, channel_multiplier=0), writes=[B_t])
                S.op("dve", lambda: V.tensor_copy(out=r_[:], in_=ti[:]), reads=[B_t], writes=[B_t])
                S.op("dve", lambda: V.tensor_scalar(out=r_[:], in0=r_[:], scalar1=pf_[:, 1:2], scalar2=None, op0=ALU.mult),
                     reads=[B_t], writes=[B_t])
                for (dst, off) in ((sinT, 0.0), (cosT, 0.25)):
                    S.op("dve", lambda off=off: V.tensor_scalar(out=kf[:], in0=r_[:], scalar1=off, scalar2=None, op0=ALU.add),
                         reads=[B_t, Bh], writes=[B_t])
                    S.op("dve", lambda: V.tensor_copy(out=ti[:], in_=kf[:]), reads=[B_t], writes=[B_t])
                    S.op("dve", lambda dst=dst: V.tensor_copy(out=dst[:], in_=ti[:]), reads=[B_t], writes=[Bh])
                    S.op("dve", lambda dst=dst: V.tensor_tensor(out=kf[:], in0=kf[:], in1=dst[:], op=ALU.subtract),
                         reads=[B_t, Bh], writes=[B_t])
                    S.op("dve", lambda dst=dst: V.tensor_single_scalar(out=dst[:], in_=kf[:], scalar=0.5, op=ALU.is_gt),
                         reads=[B_t], writes=[Bh])
                    S.op("dve", lambda dst=dst: V.tensor_tensor(out=kf[:], in0=kf[:], in1=dst[:], op=ALU.subtract),
                         reads=[B_t, Bh], writes=[B_t])
                    S.op("act", lambda dst=dst: A.activation(out=dst[:], in_=kf[:], func=AF.Sin, scale=2 * math.pi),
                         reads=[B_t], writes=[Bh])
                stg_r = Ring(nc, ph, "hstg", 3, [64, 128], F32)
                ps_r = PRing(nc, ph, "hpsc", 2, [128, 512])

                def lc(dst, src_row, nch):
                    stg, B_s = stg_r.next()
                    S.dma("sp", stg[0:nch, :], src_row.rearrange("(c p) -> c p", p=128), writes=[B_s])
                    ps, B_p = ps_r.next()
                    S.op("pe", lambda: P.matmul(ps[:, 0:nch], lhsT=stg[0:nch, :], rhs=identf[0:nch, 0:nch],
                                                start=True, stop=True), reads=[B_s, B_c], writes=[B_p])
                    S.op("dve", lambda: V.tensor_copy(out=dst, in_=ps[:, 0:nch]), reads=[B_p], writes=[Bh])
                for k in range(4):
                    lc(hcw[:, k, :], hy_conv_w[0, k, :], 12)
                lc(hcb[:, :], hy_conv_b[0, :], 12)
                for j, src in enumerate((hy_q_norm, hy_k_norm)):
                    for hf in range(2):
                        S.dma("sp", qkw[hf * 64:(hf + 1) * 64, j:j + 1], src[0, :].rearrange("(p o) -> p o", o=1),
                              writes=[Bh])
                S.op("dve", lambda: V.tensor_scalar(out=qkw[:, 0:1], in0=qkw[:, 0:1], scalar1=0.125, scalar2=None,
                                                    op0=ALU.mult), reads=[Bh], writes=[Bh])
                for j, src in enumerate((hy_dt_bias, hy_a_log, hy_d_skip)):
                    S.dma("sp", bc16[:, j, :], src[0:1, :].partition_broadcast(128), writes=[Bh])
                S.op("act", lambda: A.activation(out=bc16[:, 1, :], in_=bc16[:, 1, :], func=AF.Exp), reads=[Bh], writes=[Bh])
                S.op("dve", lambda: V.tensor_scalar(out=bc16[:, 1, :], in0=bc16[:, 1, :], scalar1=-1.0, scalar2=None,
                                                    op0=ALU.mult), reads=[Bh], writes=[Bh])
                S.dma("sp", ssdw[:], hy_ssd_norm[0:1, :].partition_broadcast(128), writes=[Bh])
                S.dma("sp", sublw[:], hy_subln[0:1, :].partition_broadcast(128), writes=[Bh])
                lam_init = 0.8 - 0.6 * math.exp(-0.3 * 0)
                S.op("dve", lambda: V.tensor_scalar(out=sublw[:], in0=sublw[:], scalar1=(1.0 - lam_init), scalar2=None,
                                                    op0=ALU.mult), reads=[Bh], writes=[Bh])
                lt = sbt(ph, "c_lt", [128, 4, 64])
                ls = sbt(ph, "c_ls", [128, 4])
                for j, src in enumerate((hy_lq1, hy_lk1, hy_lq2, hy_lk2)):
                    S.dma("sp", lt[:, j, :], src[0:1, :].partition_broadcast(128), writes=[B_t])
                for j in range(2):
                    S.op("dve", lambda j=j: V.tensor_tensor(out=lt[:, 2 * j, :], in0=lt[:, 2 * j, :], in1=lt[:, 2 * j + 1, :],
                                                            op=ALU.mult), reads=[B_t], writes=[B_t])
                    S.op("act", lambda j=j: A.activation(out=lt[:, 2 * j + 1, :], in_=lt[:, 2 * j, :], func=AF.Identity,
                                                         accum_out=ls[:, j:j + 1]), reads=[B_t], writes=[B_t])
                S.op("act", lambda: A.activation(out=ls[:, 2:4], in_=ls[:, 0:2], func=AF.Exp), reads=[B_t], writes=[B_t])
                S.op("dve", lambda: V.tensor_tensor(out=nlam[:, 0:1], in0=ls[:, 3:4], in1=ls[:, 2:3], op=ALU.subtract),
                     reads=[B_t], writes=[Bh])
                S.op("dve", lambda: V.tensor_scalar(out=nlam[:, 0:1], in0=nlam[:, 0:1], scalar1=-lam_init, scalar2=None,
                                                    op0=ALU.add), reads=[Bh], writes=[Bh])
                S.barrier()
            if hstop == 1:
                cst.close(); hst.close()
                return True

            with ExitStack() as ph:
                wr = Ring(nc, ph, "hb_w", 4, [128, 8, 128], BF16)
                pr = PRing(nc, ph, "hb_ps", 4, [128, 512])
                pr2 = PRing(nc, ph, "hb_ps2", 4, [128, 512])
                xr = Ring(nc, ph, "hb_x", 2, [128, L + 3], BF16)
                yr = Ring(nc, ph, "hb_y", 4, [128, 512], F32)
                orr = Ring(nc, ph, "hb_o", 2, [128, L], BF16)
                sq = Ring(nc, ph, "hb_sq", 3, [128, 512], BF16)
                rs = Ring(nc, ph, "hb_rs", 3, [128, 512], F32)
                qn = Ring(nc, ph, "hb_qn", 3, [128, 512], BF16)
                t1r = Ring(nc, ph, "hb_t1", 3, [128, 512], F32)
                t2r = Ring(nc, ph, "hb_t2", 3, [128, 512], F32)
                for t_, b_ in zip(xr.t, xr.b):
                    S.op("pool", lambda t_=t_: G.memset(t_[:, 0:3], 0.0), writes=[b_])
                jobs = [("xbc", i, C_XBC + i * 128) for i in range(12)] + \
                       [("q", i, C_Q + i * 128) for i in range(8)] + [("k", i, C_K + i * 128) for i in range(8)]
                pend = [wload(wr, w2d, jobs[0][2])]
                units = [(ji, tt) for ji in range(len(jobs)) for tt in range(NT)]
                js = {}
                ust = {}

                def stage1(u):
                    ji, tt = units[u]
                    if tt == 0:
                        if ji + 1 < len(jobs):
                            pend.append(wload(wr, w2d, jobs[ji + 1][2]))
                        js[ji] = {"w": pend.pop(0)}
                    wt, B_w = js[ji]["w"]
                    ps, B_ps = pr.next()
                    fm_matmuls(wt, 0, tt, ps, B_ps, B_w)
                    ust[u] = {"ps": (ps, B_ps)}

                def stage2(u):
                    ji, tt = units[u]
                    kind, i, col = jobs[ji]
                    sl = slice(tt * 512, (tt + 1) * 512)
                    ps, B_ps = ust[u]["ps"]
                    if tt == 0:
                        js[ji]["O"] = orr.next()
                        if kind == "xbc":
                            js[ji]["X"] = xr.next()
                    O, B_o = js[ji]["O"]
                    if kind == "xbc":
                        X, B_x = js[ji]["X"]
                        Y, B_y = yr.next()
                        S.op("act", lambda: A.activation(out=Y[:], in_=ps[:], func=AF.Identity,
                                                         bias=hcb[:, i:i + 1], scale=hcw[:, 3, i:i + 1]),
                             reads=[B_ps, Bh], writes=[B_y])
                        S.op("act", lambda: A.copy(out=X[:, 3 + tt * 512:3 + (tt + 1) * 512], in_=ps[:]),
                             reads=[B_ps], writes=[B_x])
                        for k in (2, 1, 0):
                            S.op("dve", lambda k=k: V.scalar_tensor_tensor(
                                out=Y[:], in0=X[:, tt * 512 + k:tt * 512 + k + 512], scalar=hcw[:, k, i:i + 1],
                                in1=Y[:], op0=ALU.mult, op1=ALU.add), reads=[B_x, Bh, B_y], writes=[B_y])
                        ust[u]["Y"] = (Y, B_y)
                    else:
                        wcol = qkw[:, 0:1] if kind == "q" else qkw[:, 1:2]
                        s_, B_s = sq.next()
                        S.op("act", lambda: A.activation(out=s_[:], in_=ps[:], func=AF.Square),
                             reads=[B_ps], writes=[B_s])
                        ps2, B_p2 = pr2.next()
                        S.op("pe", lambda: P.matmul(ps2[:], lhsT=blk[:], rhs=s_[:], start=True, stop=True),
                             reads=[Bh, B_s], writes=[B_p2])
                        r1, B_r1 = rs.next()
                        S.op("act", lambda: A.activation(out=r1[:], in_=ps2[:], func=AF.Ln, bias=epsc[:, 0:1],
                                                         scale=1.0 / 64.0), reads=[B_p2, B_c], writes=[B_r1])
                        S.op("act", lambda: A.activation(out=r1[:], in_=r1[:], func=AF.Exp, scale=-0.5),
                             reads=[B_r1], writes=[B_r1])
                        q_, B_q = qn.next()
                        S.op("dve", lambda: V.scalar_tensor_tensor(out=q_[:], in0=ps[:], scalar=wcol, in1=r1[:],
                                                                   op0=ALU.mult, op1=ALU.mult),
                             reads=[B_ps, Bh, B_r1], writes=[B_q])
                        ust[u]["q"] = (q_, B_q)

                def stage3(u):
                    ji, tt = units[u]
                    kind, i, col = jobs[ji]
                    sl = slice(tt * 512, (tt + 1) * 512)
                    O, B_o = js[ji]["O"]
                    if kind == "xbc":
                        Y, B_y = ust[u]["Y"]
                        S.op("act", lambda: A.activation(out=O[:, sl], in_=Y[:], func=AF.Silu),
                             reads=[B_y], writes=[B_o])
                    if kind != "xbc":
                        q_, B_q = ust[u]["q"]
                        ps3, B_p3 = pr2.next()
                        S.op("pe", lambda: P.matmul(ps3[:], lhsT=prot[:], rhs=q_[:], start=True, stop=True),
                             reads=[Bh, B_q], writes=[B_p3])
                        t1, B_t1 = t1r.next()
                        S.op("pool", lambda: G.tensor_tensor(out=t1[:], in0=q_[:], in1=cosT[:, sl], op=ALU.mult),
                             reads=[B_q, Bh], writes=[B_t1])
                        t2, B_t2 = t2r.next()
                        S.op("dve", lambda: V.tensor_tensor(out=t2[:], in0=ps3[:], in1=sinT[:, sl], op=ALU.mult),
                             reads=[B_p3, Bh], writes=[B_t2])
                        S.op("dve", lambda: V.tensor_tensor(out=O[:, sl], in0=t1[:], in1=t2[:], op=ALU.add),
                             reads=[B_t1, B_t2], writes=[B_o])
                    if tt == NT - 1:
                        dst = {"xbc": XBC_S, "q": QT_S, "k": KT_S}[kind]
                        S.dma("sp", dst[i], O[:], reads=[B_o])
                    del ust[u]

                NU = len(units)
                for st_ in range(NU + 2):
                    if st_ < NU:
                        stage1(st_)
                    if 0 <= st_ - 1 < NU:
                        stage2(st_ - 1)
                    if 0 <= st_ - 2 < NU:
                        stage3(st_ - 2)
                S.barrier()

            cst.close()
            if hstop == 2:
                hst.close()
                return True
            with ExitStack() as ph:
                wz = sbt(ph, "wz", [128, 8, 1024], BF16)
                wv = sbt(ph, "wv", [128, 8, 1024], BF16)
                wd = sbt(ph, "wd", [128, 8, 16], BF16)
                B_w = Buf()
                wsrc = w2d.rearrange("(k p) n -> p k n", p=128)
                for k0 in (0, 4):
                    S.dma("pool", wz[:, k0:k0 + 4, :], wsrc[:, k0:k0 + 4, C_Z:C_Z + 1024], writes=[B_w])
                    S.dma("pool", wv[:, k0:k0 + 4, :], wsrc[:, k0:k0 + 4, C_V:C_V + 1024], writes=[B_w])
                S.dma("pool", wd[:], wsrc[:, :, C_DT:C_DT + 16], writes=[B_w])
                pr = PRing(nc, ph, "tz_ps", 6, [128, 512])
                pdt = pst(ph, "tz_pdt", [128, 32, 16]); B_pdt = Buf(excl=True)
                zr = Ring(nc, ph, "tz_z", 3, [128, 1024], BF16)
                vr = Ring(nc, ph, "tz_v", 3, [128, 1024], BF16)
                for t in range(32):
                    pz = [pr.next(), pr.next()]
                    pv = [pr.next(), pr.next()]
                    for k in range(8):
                        lhs = hnT[:, k, t * 128:(t + 1) * 128]
                        for hf in range(2):
                            S.op("pe", lambda k=k, hf=hf, lhs=lhs: P.matmul(pz[hf][0][:], lhsT=lhs, rhs=wz[:, k, hf * 512:(hf + 1) * 512],
                                                                          start=(k == 0), stop=(k == 7)),
                                 reads=[B_hn[t], B_w], writes=[pz[hf][1]], sig=(k == 7))
                            S.op("pe", lambda k=k, hf=hf, lhs=lhs: P.matmul(pv[hf][0][:], lhsT=lhs, rhs=wv[:, k, hf * 512:(hf + 1) * 512],
                                                                          start=(k == 0), stop=(k == 7)),
                                 reads=[B_hn[t], B_w], writes=[pv[hf][1]], sig=(k == 7))
                        S.op("pe", lambda k=k, lhs=lhs: P.matmul(pdt[:, t, :], lhsT=lhs, rhs=wd[:, k, :],
                                                                 start=(k == 0), stop=(k == 7)),
                             reads=[B_hn[t], B_w], writes=[B_pdt], sig=(k == 7))
                    Z, B_z = zr.next()
                    Vt, B_v = vr.next()
                    for hf in range(2):
                        S.op("act", lambda hf=hf: A.activation(out=Z[:, hf * 512:(hf + 1) * 512], in_=pz[hf][0][:], func=AF.Silu),
                             reads=[pz[hf][1]], writes=[B_z])
                        S.op("dve", lambda hf=hf: V.tensor_copy(out=Vt[:, hf * 512:(hf + 1) * 512], in_=pv[hf][0][:]),
                             reads=[pv[hf][1]], writes=[B_v])
                    S.dma("sp", ZS_S[t], Z[:], reads=[B_z])
                    S.dma("sp", V_S[t], Vt[:], reads=[B_v])
                xb_ = sbt(ph, "dt_x", [128, 32, 16])
                ab_ = sbt(ph, "dt_a", [128, 32, 16])
                S.op("dve", lambda: V.tensor_tensor(out=xb_[:], in0=pdt[:], in1=bc(bc16[:, 0, :].unsqueeze(1), [128, 32, 16]),
                                                    op=ALU.add), reads=[B_pdt, Bh], writes=[B_dt])
                S.op("act", lambda: A.activation(out=ab_[:], in_=xb_[:], func=AF.Abs), reads=[B_dt], writes=[B_dt])
                S.op("act", lambda: A.activation(out=ab_[:], in_=ab_[:], func=AF.Exp, scale=-1.0), reads=[B_dt], writes=[B_dt])
                S.op("dve", lambda: V.tensor_scalar(out=ab_[:], in0=ab_[:], scalar1=1.0, scalar2=None, op0=ALU.add),
                     reads=[B_dt], writes=[B_dt])
                S.op("act", lambda: A.activation(out=ab_[:], in_=ab_[:], func=AF.Ln), reads=[B_dt], writes=[B_dt])
                S.op("dve", lambda: V.scalar_tensor_tensor(out=dt_tm[:], in0=xb_[:], scalar=0.0, in1=ab_[:], op0=ALU.max,
                                                           op1=ALU.add), reads=[B_dt], writes=[B_dt])
                S.op("dve", lambda: V.tensor_tensor(out=a_tm[:], in0=dt_tm[:], in1=bc(bc16[:, 1, :].unsqueeze(1), [128, 32, 16]),
                                                    op=ALU.mult), reads=[B_dt, Bh], writes=[B_dt])
                pa, B_pa = pr.next()
                pl, B_pl = pr.next()
                S.op("pe", lambda: P.matmul(pa[:], lhsT=triT_f[:], rhs=a_tm[:].rearrange("p c h -> p (c h)"),
                                            start=True, stop=True), reads=[B_c, B_dt], writes=[B_pa])
                S.op("pe", lambda: P.matmul(pl[:], lhsT=ones_f[:], rhs=a_tm[:].rearrange("p c h -> p (c h)"),
                                            start=True, stop=True), reads=[B_c, B_dt], writes=[B_pl])
                fl = lambda t_: t_[:].rearrange("p c h -> p (c h)")
                S.op("dve", lambda: V.tensor_copy(out=fl(acs), in_=pa[:]), reads=[B_pa], writes=[B_dt])
                S.op("act", lambda: A.activation(out=fl(eA), in_=pa[:], func=AF.Exp), reads=[B_pa], writes=[B_dt])
                S.op("act", lambda: A.activation(out=fl(cdr), in_=pl[:], func=AF.Exp), reads=[B_pl], writes=[B_dt])
                S.op("dve", lambda: V.tensor_tensor(out=fl(dA), in0=pl[:], in1=fl(acs), op=ALU.subtract),
                     reads=[B_pl, B_dt], writes=[B_dt])
                S.op("act", lambda: A.activation(out=fl(dA), in_=fl(dA), func=AF.Exp), reads=[B_dt], writes=[B_dt])
                S.barrier()
            if hstop == 3:
                hst.close()
                return True

            with ExitStack() as ph:
                xin = Ring(nc, ph, "sd_x", 2, [128, 12, 512], BF16)
                zin = Ring(nc, ph, "sd_z", 3, [128, 1024], BF16)
                ptr = PRing(nc, ph, "sd_ptr", 2, [128, 1024], BF16)
                pcb = pst(ph, "sd_pcb", [128, 4, 128]); B_pcb = Buf(excl=True)
                pR = PRing(nc, ph, "sd_pR", 1, [128, 8, 128])
                pya = pst(ph, "sd_pya", [128, 512]); B_pya = Buf(excl=True)
                pyb = pst(ph, "sd_pyb", [128, 512]); B_pyb = Buf(excl=True)
                pst_ = pst(ph, "sd_pst", [128, 512]); B_pst = Buf(excl=True)
                xs_tm = Ring(nc, ph, "sd_xs", 2, [128, 1024], BF16)
                b_tm = Ring(nc, ph, "sd_b", 2, [128, 256], BF16)
                Xr = Ring(nc, ph, "sd_X", 2, [128, 1024], BF16)
                Xdr = Ring(nc, ph, "sd_Xd", 2, [128, 1024], BF16)
                cbm = Ring(nc, ph, "sd_cbm", 2, [128, 2, 128], BF16)
                rhsR = Ring(nc, ph, "sd_rr", 1, [128, 8, 128], F32)
                segr = Ring(nc, ph, "sd_seg", 1, [128, 8, 128], F32)
                Er = Ring(nc, ph, "sd_E", 2, [128, 8, 128], BF16)
                Wr = Ring(nc, ph, "sd_W", 2, [128, 8, 128], BF16)
                yr = Ring(nc, ph, "sd_y", 2, [128, 512], F32)
                y2r = Ring(nc, ph, "sd_y2", 1, [128, 512], F32)
                ynr = Ring(nc, ph, "sd_yn", 2, [128, 512], BF16)
                smr = Ring(nc, ph, "sd_sm", 3, [128, 2], F32)
                junk = sbt(ph, "sd_junk", [128, 512], BF16); B_junk = Buf()
                prev = sbt(ph, "sd_prev", [128, 1024]); B_prev = Buf()
                prevb = sbt(ph, "sd_prevb", [128, 1024], BF16); B_prevb = Buf()
                yT = Ring(nc, ph, "sd_yT", 2, [128, 8, 512], BF16)
                S.op("pool", lambda: G.memset(prev[:], 0.0), writes=[B_prev])
                S.op("pool", lambda: G.memset(prevb[:], 0.0), writes=[B_prevb])
                xsrc = XBC_S.rearrange("c p t -> p c t")
                chs = {}
                cur = {}

                def prep(c):
                    cc = c % 4
                    if cc == 0:
                        Xin, B_xin = xin.next()
                        S.dma("sp", Xin[:, 0:6, :], xsrc[:, 0:6, c * 128:c * 128 + 512], writes=[B_xin])
                        S.dma("sp", Xin[:, 6:12, :], xsrc[:, 6:12, c * 128:c * 128 + 512], writes=[B_xin])
                        cur["Xin"] = (Xin, B_xin)
                        cur["YT"] = yT.next()
                    Xin, B_xin = cur["Xin"]
                    csl = slice(cc * 128, (cc + 1) * 128)
                    Zt, B_z = zin.next()
                    S.dma("sp", Zt[:], ZS_S[c], writes=[B_z])
                    pt, B_pt = ptr.next()
                    for j in range(8):
                        S.op("pe", lambda j=j: P.transpose(out=pt[:, j * 128:(j + 1) * 128], in_=Xin[:, j, csl], identity=identb[:]),
                             reads=[B_xin, B_c], writes=[B_pt], sig=(j == 7))
                    xs, B_xs = xs_tm.next()
                    S.op("act", lambda: A.copy(out=xs[:], in_=pt[:]), reads=[B_pt], writes=[B_xs])
                    pt2, B_pt2 = ptr.next()
                    for g in range(2):
                        S.op("pe", lambda g=g: P.transpose(out=pt2[:, g * 128:(g + 1) * 128], in_=Xin[:, 8 + g, csl], identity=identb[:]),
                             reads=[B_xin, B_c], writes=[B_pt2], sig=(g == 1))
                    bt, B_bt = b_tm.next()
                    S.op("act", lambda: A.copy(out=bt[:], in_=pt2[:, 0:256]), reads=[B_pt2], writes=[B_bt])
                    X, B_X = Xr.next()
                    S.op("dve", lambda: V.tensor_tensor(out=X[:].rearrange("p (h d) -> p h d", h=16),
                                                        in0=xs[:].rearrange("p (h d) -> p h d", h=16),
                                                        in1=bc(dt_tm[:, c, :].unsqueeze(2), [128, 16, 64]), op=ALU.mult),
                         reads=[B_xs, B_dt], writes=[B_X])
                    Xd, B_Xd = Xdr.next()
                    S.op("pool", lambda: G.tensor_tensor(out=Xd[:].rearrange("p (h d) -> p h d", h=16),
                                                         in0=X[:].rearrange("p (h d) -> p h d", h=16),
                                                         in1=bc(dA[:, c, :].unsqueeze(2), [128, 16, 64]), op=ALU.mult),
                         reads=[B_X, B_dt], writes=[B_Xd])
                    for g in range(2):
                        S.op("pe", lambda g=g: P.matmul(pcb[:, g, :], lhsT=Xin[:, 8 + g, csl], rhs=Xin[:, 10 + g, csl],
                                                        start=True, stop=True), reads=[B_xin], writes=[B_pcb], sig=(g == 1))
                    cb, B_cb = cbm.next()
                    S.op("dve", lambda: V.tensor_tensor(out=cb[:], in0=pcb[:, 0:2, :], in1=bc(triT_f[:].unsqueeze(1), [128, 2, 128]),
                                                        op=ALU.mult), reads=[B_pcb, B_c], writes=[B_cb])
                    chs[c] = dict(Xin=(Xin, B_xin), csl=csl, Z=(Zt, B_z), xs=(xs, B_xs), bt=(bt, B_bt), X=(X, B_X),
                                  Xd=(Xd, B_Xd), cb=(cb, B_cb), YT=cur["YT"], W={})

                def stage1a_pool(c, g):
                    hs = slice(g * 8, (g + 1) * 8)
                    rr_, B_rr = rhsR.next()
                    S.op("pool", lambda: G.tensor_tensor(out=rr_[:], in0=bc(triT_f[:].unsqueeze(1), [128, 8, 128]),
                                                         in1=bc(a_tm[:, c, hs].unsqueeze(2), [128, 8, 128]), op=ALU.mult),
                         reads=[B_c, B_dt], writes=[B_rr])
                    chs[c]["rr", g] = (rr_, B_rr)

                def stage1a_pe(c, g):
                    rr_, B_rr = chs[c]["rr", g]
                    pr_, B_pr = pR.next()
                    for q4 in range(2):
                        S.op("pe", lambda q4=q4: P.matmul(pr_[:, q4 * 4:(q4 + 1) * 4, :], lhsT=ones_f[:],
                                                          rhs=rr_[:, q4 * 4:(q4 + 1) * 4, :], start=True, stop=True),
                             reads=[B_c, B_rr], writes=[B_pr], sig=(q4 == 1))
                    chs[c]["pr", g] = (pr_, B_pr)

                def stage1b(c, g):
                    d = chs[c]
                    cb, B_cb = d["cb"]
                    hs = slice(g * 8, (g + 1) * 8)
                    pr_, B_pr = d["pr", g]
                    if False:
                        rr_, B_rr = rhsR.next()
                    sg, B_sg = segr.next()
                    S.op("dve", lambda: V.tensor_tensor(out=sg[:], in0=pr_[:], in1=bc(acs[:, c, hs].unsqueeze(2), [128, 8, 128]),
                                                        op=ALU.subtract), reads=[B_pr, B_dt], writes=[B_sg])
                    S.op("dve", lambda: V.tensor_single_scalar(out=sg[:], in_=sg[:], scalar=0.0, op=ALU.min),
                         reads=[B_sg], writes=[B_sg])
                    E, B_E = Er.next()
                    S.op("act", lambda: A.activation(out=E[:], in_=sg[:], func=AF.Exp), reads=[B_sg], writes=[B_E])
                    W, B_W = Wr.next()
                    S.op("dve", lambda: V.tensor_tensor(out=W[:], in0=E[:], in1=bc(cb[:, g, :].unsqueeze(1), [128, 8, 128]),
                                                        op=ALU.mult), reads=[B_E, B_cb], writes=[B_W])
                    d["W"][g] = (W, B_W)

                def stage2a(c, g):
                    d = chs[c]
                    Xin, B_xin = d["Xin"]; csl = d["csl"]; Zt, B_z = d["Z"]; xs, B_xs = d["xs"]; bt, B_bt = d["bt"]
                    X, B_X = d["X"]; Xd, B_Xd = d["Xd"]; YT, B_yT = d["YT"]; W, B_W = d["W"][g]
                    cc = c % 4
                    hs = slice(g * 8, (g + 1) * 8)
                    fs = slice(g * 512, (g + 1) * 512)
                    for h in range(8):
                        hh = g * 8 + h
                        S.op("pe", lambda h=h, hh=hh: P.matmul(pya[:, h * 64:(h + 1) * 64], lhsT=W[:, h, :],
                                                               rhs=X[:, hh * 64:(hh + 1) * 64], start=True, stop=True),
                             reads=[B_W, B_X], writes=[B_pya], sig=(h == 7))
                    S.op("pe", lambda: P.matmul(pyb[:], lhsT=Xin[:, 10 + g, csl], rhs=prevb[:, fs], start=True, stop=True),
                         reads=[B_xin, B_prevb], writes=[B_pyb])
                def stage2b(c, g):
                    d = chs[c]
                    Zt, B_z = d["Z"]; xs, B_xs = d["xs"]
                    hs = slice(g * 8, (g + 1) * 8)
                    fs = slice(g * 512, (g + 1) * 512)
                    y, B_y = yr.next()
                    S.op("dve", lambda: V.tensor_tensor(out=y[:].rearrange("p (h d) -> p h d", h=8),
                                                        in0=pyb[:].rearrange("p (h d) -> p h d", h=8),
                                                        in1=bc(eA[:, c, hs].unsqueeze(2), [128, 8, 64]), op=ALU.mult),
                         reads=[B_pyb, B_dt], writes=[B_y])
                    S.op("dve", lambda: V.tensor_tensor(out=y[:], in0=y[:], in1=pya[:], op=ALU.add),
                         reads=[B_y, B_pya], writes=[B_y])
                    y2, B_y2 = y2r.next()
                    S.op("pool", lambda: G.tensor_tensor(out=y2[:].rearrange("p (h d) -> p h d", h=8),
                                                         in0=xs[:, fs].rearrange("p (h d) -> p h d", h=8),
                                                         in1=bc(bc16[:, 2, hs].unsqueeze(2), [128, 8, 64]), op=ALU.mult),
                         reads=[B_xs, Bh], writes=[B_y2])
                    S.op("dve", lambda: V.tensor_tensor(out=y[:], in0=y[:], in1=y2[:], op=ALU.add),
                         reads=[B_y, B_y2], writes=[B_y])
                    S.op("dve", lambda: V.tensor_tensor(out=y[:], in0=y[:], in1=Zt[:, fs], op=ALU.mult),
                         reads=[B_y, B_z], writes=[B_y])
                    sm, B_sm = smr.next()
                    S.op("act", lambda: A.activation(out=junk[:], in_=y[:], func=AF.Square, accum_out=sm[:, 0:1]),
                         reads=[B_y], writes=[B_junk, B_sm])
                    rstd(sm[:, 1:2], sm[:, 0:1], 1.0 / 512.0, B_sm)
                    yn, B_yn = ynr.next()
                    S.op("dve", lambda: V.scalar_tensor_tensor(out=yn[:], in0=y[:], scalar=sm[:, 1:2], in1=ssdw[:, fs],
                                                               op0=ALU.mult, op1=ALU.mult),
                         reads=[B_y, B_sm, Bh], writes=[B_yn])
                    d["yn", g] = (yn, B_yn)

                def stage2c(c, g):
                    d = chs[c]
                    csl = d["csl"]; bt, B_bt = d["bt"]; Xd, B_Xd = d["Xd"]; YT, B_yT = d["YT"]
                    yn, B_yn = d["yn", g]
                    cc = c % 4
                    hs = slice(g * 8, (g + 1) * 8)
                    fs = slice(g * 512, (g + 1) * 512)
                    pto, B_pto = ptr.next()
                    for j in range(4):
                        S.op("pe", lambda j=j: P.transpose(out=pto[:, j * 128:(j + 1) * 128], in_=yn[:, j * 128:(j + 1) * 128],
                                                           identity=identb[:]), reads=[B_yn, B_c], writes=[B_pto], sig=(j == 3))
                    S.op("act", lambda: A.copy(out=YT[:, g * 4:(g + 1) * 4, csl],
                                               in_=pto[:, 0:512].rearrange("p (j t) -> p j t", j=4)),
                         reads=[B_pto], writes=[B_yT])
                    S.op("pe", lambda: P.matmul(pst_[:], lhsT=bt[:, g * 128:(g + 1) * 128], rhs=Xd[:, fs], start=True, stop=True),
                         reads=[B_bt, B_Xd], writes=[B_pst])
                    S.op("dve", lambda: V.tensor_tensor(out=prev[:, fs].rearrange("p (h d) -> p h d", h=8),
                                                        in0=prev[:, fs].rearrange("p (h d) -> p h d", h=8),
                                                        in1=bc(cdr[:, c, hs].unsqueeze(2), [128, 8, 64]), op=ALU.mult),
                         reads=[B_prev, B_dt], writes=[B_prev])
                    S.op("dve", lambda: V.tensor_tensor(out=prev[:, fs], in0=prev[:, fs], in1=pst_[:], op=ALU.add),
                         reads=[B_prev, B_pst], writes=[B_prev])
                    S.op("act", lambda: A.copy(out=prevb[:, fs], in_=prev[:, fs]), reads=[B_prev], writes=[B_prevb])
                    if cc == 3 and g == 1:
                        c0 = (c - 3) * 128
                        S.dma("sp", ACT_S.rearrange("c p t -> p c t")[:, 0:8, c0:c0 + 512], YT[:], reads=[B_yT])
                    if g == 1:
                        del chs[c]

                sunits = [(c, g) for c in range(32) for g in range(2)]
                NSU = len(sunits)
                for st_ in range(NSU + 1):
                    cur_u = sunits[st_] if st_ < NSU else None
                    prv_u = sunits[st_ - 1] if st_ >= 1 else None
                    if cur_u is not None:
                        if cur_u[1] == 0:
                            prep(cur_u[0])
                        stage1a_pool(*cur_u)
                    if prv_u is not None:
                        stage2a(*prv_u)
                    if cur_u is not None:
                        stage1a_pe(*cur_u)
                    if prv_u is not None:
                        stage2b(*prv_u)
                    if cur_u is not None:
                        stage1b(*cur_u)
                    if prv_u is not None:
                        stage2c(*prv_u)
                S.barrier()
            if hstop == 4:
                hst.close()
                return True

            wst = ExitStack()
            open_wres(wst)
            load_Wres(hy_w_out[0], 16)
            with ExitStack() as ph:
                kq = Ring(nc, ph, "at_kq", 2, [128, 2, L], BF16)
                vr = Ring(nc, ph, "at_v", 2, [128, 32, 130], BF16)
                pS = PRing(nc, ph, "at_pS", 2, [128, 2, 512])
                pO = PRing(nc, ph, "at_pO", 3, [128, 2, 256])
                pT = PRing(nc, ph, "at_pT", 1, [128, 1024], BF16)
                Er = Ring(nc, ph, "at_E", 3, [128, 2, 512], BF16)
                smr = Ring(nc, ph, "at_sm", 6, [128, 4], F32)
                tr_ = Ring(nc, ph, "at_t", 4, [128, 128], F32)
                or_ = Ring(nc, ph, "at_o", 6, [128, 128], F32)
                ybr = Ring(nc, ph, "at_yb", 6, [128, 128], BF16)
                junk = sbt(ph, "at_junk", [128, 128], BF16); B_junk = Buf()
                yT = Ring(nc, ph, "at_yT", 2, [128, L], BF16)
                for t_, b_ in zip(vr.t, vr.b):
                    S.op("pool", lambda t_=t_: G.memset(t_[:, :, 128:130], 1.0), writes=[b_])
                for t_, b_ in zip(Er.t, Er.b):
                    S.op("pool", lambda t_=t_: G.memset(t_[:], 0.0), writes=[b_])
                vsrc = V_S.rearrange("t p f -> p t f")
                for hd in range(8):
                    KQ, B_kq = kq.next()
                    S.dma("sp", KQ[:, 0, :], KT_S[hd], writes=[B_kq])
                    S.dma("sp", KQ[:, 1, :], QT_S[hd], writes=[B_kq])
                    Vt, B_v = vr.next()
                    for v0 in range(0, 32, 8):
                        S.dma("sp", Vt[:, v0:v0 + 8, 0:128], vsrc[:, v0:v0 + 8, hd * 128:(hd + 1) * 128], writes=[B_v])
                    YT, B_yT = yT.next()
                    items = [(qt, i) for qt in range(16) for i in range(qt + 1)]

                    def stage_qk(qt, i):
                        q0 = qt * 256
                        diag = (i == qt)
                        ps, B_ps = pS.next()
                        for half in range(2):
                            kb = 2 * i + half
                            c0 = 128 if (diag and half == 1) else 0
                            for m in range(2):
                                S.op("pe", lambda m=m, half=half, kb=kb, c0=c0: P.matmul(
                                    ps[:, m, half * 256 + c0:half * 256 + 256],
                                    lhsT=KQ[64 * m:64 * m + 64, 0, kb * 128:(kb + 1) * 128],
                                    rhs=KQ[64 * m:64 * m + 64, 1, q0 + c0:q0 + 256], start=True, stop=True),
                                    reads=[B_kq], writes=[B_ps], sig=(half == 1 and m == 1))
                        return ps, B_ps

                    def stage_exp(qt, i, ps, B_ps):
                        diag = (i == qt)
                        E, B_E = Er.next()
                        if diag:
                            S.op("act", lambda: A.activation(out=E[:, :, 0:256], in_=ps[:, :, 0:256], func=AF.Exp),
                                 reads=[B_ps], writes=[B_E])
                            S.op("act", lambda: A.activation(out=E[:, :, 384:512], in_=ps[:, :, 384:512], func=AF.Exp),
                                 reads=[B_ps], writes=[B_E])
                            Ev_ = E[:].rearrange("p m (r c) -> p m r c", c=128)
                            S.op("pool", lambda: G.tensor_tensor(out=Ev_[:, :, 0:4:3, :], in0=Ev_[:, :, 0:4:3, :],
                                                                 in1=bc(triT_b[:].unsqueeze(1).unsqueeze(1), [128, 2, 2, 128]),
                                                                 op=ALU.mult), reads=[B_E, B_c], writes=[B_E])
                        else:
                            S.op("act", lambda: A.activation(out=E[:], in_=ps[:], func=AF.Exp), reads=[B_ps], writes=[B_E])
                        return E, B_E

                    def stage_pv(qt, i, E, B_E, pOs):
                        diag = (i == qt)
                        for half in range(2):
                            kb = 2 * i + half
                            for qs in range(2):
                                if diag and half == 1 and qs == 0:
                                    continue
                                po, B_po = pOs[qs]
                                first = (i == 0 and half == 0)
                                last = diag and (half == qs)
                                for m in range(2):
                                    S.op("pe", lambda m=m, qs=qs, po=po, kb=kb, half=half, first=first, last=last: P.matmul(
                                        po[:, m, 0:129], lhsT=E[:, m, half * 256 + qs * 128:half * 256 + (qs + 1) * 128],
                                        rhs=Vt[:, kb, 0:129], start=(first and m == 0), stop=(last and m == 1),
                                        skip_group_check=True),
                                        reads=[B_E, B_v], writes=[B_po], sig=(last and m == 1))

                    def epi1(qt, pOs):
                        outs = []
                        for qs in range(2):
                            po, B_po = pOs[qs]
                            sm, B_sm = smr.next()
                            S.op("dve", lambda: V.reciprocal(out=sm[:, 0:2], in_=po[:, :, 128]), reads=[B_po], writes=[B_sm])
                            t_, B_t = tr_.next()
                            S.op("dve", lambda: V.tensor_scalar(out=t_[:], in0=po[:, 1, 0:128], scalar1=sm[:, 1:2], scalar2=nlam[:, 0:1],
                                                                op0=ALU.mult, op1=ALU.mult), reads=[B_po, B_sm, Bh], writes=[B_t])
                            o_, B_o = or_.next()
                            S.op("dve", lambda: V.scalar_tensor_tensor(out=o_[:], in0=po[:, 0, 0:128], scalar=sm[:, 0:1], in1=t_[:],
                                                                       op0=ALU.mult, op1=ALU.add), reads=[B_po, B_sm, B_t], writes=[B_o])
                            outs.append((o_, B_o, sm, B_sm))
                        return outs

                    def epi2(qt, outs):
                        ybs = []
                        for qs in range(2):
                            o_, B_o, sm, B_sm = outs[qs]
                            S.op("act", lambda: A.activation(out=junk[:], in_=o_[:], func=AF.Square, accum_out=sm[:, 2:3]),
                                 reads=[B_o], writes=[B_junk, B_sm])
                            rstd(sm[:, 3:4], sm[:, 2:3], 1.0 / 128.0, B_sm)
                            yb, B_yb = ybr.next()
                            S.op("dve", lambda: V.scalar_tensor_tensor(out=yb[:], in0=o_[:], scalar=sm[:, 3:4], in1=sublw[:],
                                                                       op0=ALU.mult, op1=ALU.mult), reads=[B_o, B_sm, Bh], writes=[B_yb])
                            ybs.append((yb, B_yb))
                        return ybs

                    def epi3(qt, ybs):
                        q0 = qt * 256
                        for qs in range(2):
                            yb, B_yb = ybs[qs]
                            pt, B_pt = pT.next()
                            S.op("pe", lambda: P.transpose(out=pt[:, 0:128], in_=yb[:], identity=identb[:]), reads=[B_yb, B_c], writes=[B_pt])
                            S.op("dve", lambda: V.tensor_copy(out=YT[:, q0 + qs * 128:q0 + (qs + 1) * 128], in_=pt[:, 0:128]),
                                 reads=[B_pt], writes=[B_yT])

                    nxt_qk = stage_qk(*items[0])
                    pOs = None
                    pend2 = []
                    pend3 = []
                    for n, (qt, i) in enumerate(items):
                        ps, B_ps = nxt_qk
                        if n + 1 < len(items):
                            nxt_qk = stage_qk(*items[n + 1])
                        E, B_E = stage_exp(qt, i, ps, B_ps)
                        new3 = [(q_, epi2(q_, o_)) for (q_, o_) in pend2]
                        pend2 = []
                        if i == 0:
                            pOs = [pO.next(), pO.next()]
                        stage_pv(qt, i, E, B_E, pOs)
                        for (q_, y_) in pend3:
                            epi3(q_, y_)
                        pend3 = new3
                        if i == qt:
                            pend2.append((qt, epi1(qt, pOs)))
                    for (q_, o_) in pend2:
                        pend3.append((q_, epi2(q_, o_)))
                    for (q_, y_) in pend3:
                        epi3(q_, y_)
                    S.dma("sp", ACT_S[8 + hd], YT[:], reads=[B_yT])
                S.barrier()
            if hstop == 5:
                wst.close(); hst.close()
                return True
            phase_tm(16, h_in_ap, h_out_ap, widx_next)
            wst.close()
            hst.close()
            return False

        pcount = [0]

        def chk():
            pcount[0] += 1
            return stop is not None and pcount[0] >= stop

        def _main_program():
            h_cur = x
            scr = [hA, hB, hC]
            nxt = 0
            first = True
            for li, layer in enumerate(layers):
                last_layer = (li == len(layers) - 1)
                if first:
                    phase_norm_in(h_cur, 2 * layer)
                    if chk():
                        return
                    first = False
                h_mid = scr[nxt]; nxt += 1
                if layer == 0:
                    if phase_hybrid(h_cur, h_mid, 2 * layer + 1):
                        return
                    if chk():
                        return
                else:
                    with ExitStack() as ws:
                        open_wres(ws)
                        phase_shortconv(sc_w_out[0], 8)
                        if chk():
                            return
                        phase_tm(8, h_cur, h_mid, 2 * layer + 1)
                        if chk():
                            return
                with ExitStack() as ws:
                    open_wres(ws)
                    phase_ffn_up(layer, ffn_w_down[layer], 22)
                    if chk():
                        return
                    if last_layer:
                        phase_tm(22, h_mid, out, None)
                    else:
                        h_new = scr[nxt]; nxt += 1
                        phase_tm(22, h_mid, h_new, 2 * layers[li + 1])
                        h_cur = h_new
                    if chk():
                        return

        try:
            _main_program()
        except _Stop:
            pass
        S.barrier(engines=("sp",))
        build_program.stats = (S.n_ins, S.n_wait)
    return nc


_INPUT_NAMES = ["mix_norm", "ffn_norm", "hy_w_in", "hy_conv_w", "hy_conv_b", "hy_dt_bias", "hy_a_log",
                "hy_d_skip", "hy_ssd_norm", "hy_q_norm", "hy_k_norm", "hy_lambda_q1", "hy_lambda_k1",
                "hy_lambda_q2", "hy_lambda_k2", "hy_subln", "hy_w_out", "sc_w_in", "sc_conv_w", "sc_w_out",
                "ffn_w_up", "ffn_conv_w", "ffn_conv_b", "ffn_w_down"]


def kernel(**inputs):
    x = np.asarray(inputs["x"], dtype=np.float32)
    nc = build_program()
    shared = {k: np.ascontiguousarray(np.asarray(inputs[k], dtype=np.float32)) for k in _INPUT_NAMES}
    in_maps = []
    for b in range(8):
        m = dict(shared)
        m["x"] = np.ascontiguousarray(x[b])
        in_maps.append(m)
    res = run_bass_kernel_spmd(nc, in_maps, core_ids=list(range(8)))
    return np.stack([np.asarray(r["out"], dtype=np.float32) for r in res.results], axis=0)
```

```python
import math
from contextlib import ExitStack

import numpy as np
import concourse.bass as bass
import concourse.mybir as mybir
from concourse.bass_utils import run_bass_kernel_spmd

F32 = mybir.dt.float32
BF16 = mybir.dt.bfloat16
I32 = mybir.dt.int32
AF = mybir.ActivationFunctionType
ALU = mybir.AluOpType

L = 4096
DM = 1024
NT = 8
DFF = 2816
EPS = 1e-6
W_IN0 = 5648
C_Z, C_XBC, C_DT, C_Q, C_K, C_V = 0, 1024, 2560, 2576, 3600, 4624


class Ev:
    __slots__ = ("key", "sem", "val", "eng", "clock")

    def __init__(self, key, sem, val, eng, clock):
        self.key, self.sem, self.val, self.eng, self.clock = key, sem, val, eng, clock


class Buf:
    __slots__ = ("name", "w", "rd", "excl")

    def __init__(self, name="", excl=False):
        self.name, self.w, self.rd, self.excl = name, None, {}, excl


class Sched:
    NDMA = 8

    def __init__(self, nc, stack):
        self.nc = nc
        self.engs = {"pe": nc.tensor, "act": nc.scalar, "dve": nc.vector,
                     "pool": nc.gpsimd, "sp": nc.sync}
        self.sem, self.cnt, self.pending, self.last_ins = {}, {}, {}, {}
        self.seen = {e: {} for e in self.engs}
        for e in ("pe", "act", "dve", "pool"):
            self.sem[e] = stack.enter_context(nc.semaphore("s_" + e))
            self.cnt[e] = 0
            self.pending[e] = []
            self.last_ins[e] = None
        self.dsem, self.dcnt, self.drr = {}, {}, {}
        for q in ("sp", "pool"):
            self.dsem[q] = []
            for i in range(self.NDMA):
                k = "d_%s%d" % (q, i)
                self.dsem[q].append((k, stack.enter_context(nc.semaphore(k))))
                self.dcnt[k] = 0
            self.drr[q] = 0
        self.n_wait = 0
        self.n_ins = 0

    def _need(self, e, ev):
        if ev is None:
            return
        if ev.val is None:
            self._force(ev.eng)
        seen = self.seen[e]
        if seen.get(ev.key, 0) >= ev.val:
            return
        self.engs[e].wait_ge(ev.sem, ev.val)
        self.n_wait += 1
        for k, v in ev.clock.items():
            if seen.get(k, 0) < v:
                seen[k] = v

    def _force(self, e):
        if not self.pending[e]:
            return
        self.last_ins[e].then_inc(self.sem[e], 1)
        self._signal(e)

    def _signal(self, e):
        self.cnt[e] += 1
        v = self.cnt[e]
        clock = dict(self.seen[e])
        clock["s_" + e] = v
        for ev in self.pending[e]:
            ev.val = v
            ev.clock = clock
        self.pending[e] = []
        self.last_ins[e] = None

    def _deps(self, e, reads, writes, is_dma):
        for b in reads:
            if b.w is not None:
                self._need(e, b.w)
            if b.excl:
                for ev in list(b.rd.values()):
                    if ev.eng != e:
                        self._need(e, ev)
        for b in writes:
            if b.w is not None and (is_dma or b.w.eng != e or e == "pool"):
                self._need(e, b.w)
            for ev in list(b.rd.values()):
                if is_dma or ev.eng != e or e == "pool":
                    self._need(e, ev)

    def _record(self, ev, reads, writes):
        for b in reads:
            b.rd[ev.key] = ev
        for b in writes:
            b.w = ev
            b.rd = {}

    def op(self, e, fn, reads=(), writes=(), sig=True):
        self._deps(e, reads, writes, False)
        ins = fn()
        self.n_ins += 1
        ev = Ev("s_" + e, self.sem[e], None, e, None)
        self.pending[e].append(ev)
        self.last_ins[e] = ins
        if sig:
            ins.then_inc(self.sem[e], 1)
            self._signal(e)
        self._record(ev, reads, writes)
        return ev

    def dma(self, q, out, in_, reads=(), writes=(), **kw):
        i = self.drr[q]
        self.drr[q] = (i + 1) % self.NDMA
        key, sem = self.dsem[q][i]
        prev = self.dcnt[key]
        seen = self.seen[q]
        if prev > 0 and seen.get(key, 0) < prev:
            self.engs[q].wait_ge(sem, prev)
            self.n_wait += 1
            seen[key] = prev
        self._deps(q, reads, writes, True)
        ins = self.engs[q].dma_start(out=out, in_=in_, **kw)
        ins.then_inc(sem, 16)
        self.n_ins += 1
        self.dcnt[key] = prev + 16
        clock = dict(seen)
        clock[key] = prev + 16
        ev = Ev(key, sem, prev + 16, None, clock)
        self._record(ev, reads, writes)
        return ev

    def barrier(self, engines=("pe", "act", "dve", "pool", "sp")):
        for e in ("pe", "act", "dve", "pool"):
            self._force(e)
        evs = []
        for e in ("pe", "act", "dve", "pool"):
            if self.cnt[e] > 0:
                evs.append(Ev("s_" + e, self.sem[e], self.cnt[e], e, {"s_" + e: self.cnt[e]}))
        for q in self.dsem:
            for key, sem in self.dsem[q]:
                if self.dcnt[key] > 0:
                    evs.append(Ev(key, sem, self.dcnt[key], None, {key: self.dcnt[key]}))
        for e in engines:
            for ev in evs:
                if ev.eng != e:
                    self._need(e, ev)


_UID = [0]


class Ring:
    def __init__(self, nc, st, name, n, shape, dt):
        _UID[0] += 1
        name = "%s_%d_" % (name, _UID[0])
        self.t = [st.enter_context(nc.sbuf_tensor("%s%d" % (name, i), shape, dt)) for i in range(n)]
        self.b = [Buf("%s%d" % (name, i)) for i in range(n)]
        self.i = 0

    def next(self):
        i = self.i
        self.i = (i + 1) % len(self.t)
        return self.t[i], self.b[i]


class PRing:
    def __init__(self, nc, st, name, n, shape, dt=F32):
        _UID[0] += 1
        name = "%s_%d_" % (name, _UID[0])
        self.t = [st.enter_context(nc.psum_tensor("%s%d" % (name, i), shape, dt)) for i in range(n)]
        self.b = [Buf("%s%d" % (name, i), excl=True) for i in range(n)]
        self.i = 0

    def next(self):
        i = self.i
        self.i = (i + 1) % len(self.t)
        return self.t[i], self.b[i]


def bc(ap, shape):
    return ap.to_broadcast(shape)


class _Stop(Exception):
    pass


def build_program(layers=(0, 1), dbg=False, stop=None, hstop=None):
    nc = bass.Bass("TRN2", target_bir_lowering=False)

    def din(name, shape):
        return nc.dram_tensor(name, shape, F32, kind="ExternalInput").ap()

    x = din("x", [L, DM])
    mix_norm = din("mix_norm", [2, DM])
    ffn_norm = din("ffn_norm", [2, DM])
    hy_w_in = din("hy_w_in", [1, DM, W_IN0])
    hy_conv_w = din("hy_conv_w", [1, 4, 1536])
    hy_conv_b = din("hy_conv_b", [1, 1536])
    hy_dt_bias = din("hy_dt_bias", [1, 16])
    hy_a_log = din("hy_a_log", [1, 16])
    hy_d_skip = din("hy_d_skip", [1, 16])
    hy_ssd_norm = din("hy_ssd_norm", [1, 1024])
    hy_q_norm = din("hy_q_norm", [1, 64])
    hy_k_norm = din("hy_k_norm", [1, 64])
    hy_lq1 = din("hy_lambda_q1", [1, 64])
    hy_lk1 = din("hy_lambda_k1", [1, 64])
    hy_lq2 = din("hy_lambda_q2", [1, 64])
    hy_lk2 = din("hy_lambda_k2", [1, 64])
    hy_subln = din("hy_subln", [1, 128])
    hy_w_out = din("hy_w_out", [1, 2048, DM])
    sc_w_in = din("sc_w_in", [1, DM, 3072])
    sc_conv_w = din("sc_conv_w", [1, 3, 1024])
    sc_w_out = din("sc_w_out", [1, 1024, DM])
    ffn_w_up = din("ffn_w_up", [2, DM, 2 * DFF])
    ffn_conv_w = din("ffn_conv_w", [2, 3, 2 * DFF])
    ffn_conv_b = din("ffn_conv_b", [2, 2 * DFF])
    ffn_w_down = din("ffn_w_down", [2, DFF, DM])
    out = nc.dram_tensor("out", [L, DM], F32, kind="ExternalOutput").ap()

    skind = "ExternalOutput" if dbg else "Internal"

    def dscr(name, shape, dt=BF16):
        return nc.dram_tensor(name, shape, dt, kind=skind).ap()

    hA = dscr("hA", [L, DM], F32)
    hB = dscr("hB", [L, DM], F32)
    hC = dscr("hC", [L, DM], F32)
    ACT_S = dscr("ACT_S", [22, 128, L])
    XBC_S = dscr("XBC_S", [12, 128, L])
    QT_S = dscr("QT_S", [8, 128, L])
    KT_S = dscr("KT_S", [8, 128, L])
    ZS_S = dscr("ZS_S", [32, 128, 1024])
    V_S = dscr("V_S", [32, 128, 1024])

    with ExitStack() as top:
        S = Sched(nc, top)
        V, A, G, P = nc.vector, nc.scalar, nc.gpsimd, nc.tensor

        def sbt(st, name, shape, dt=F32):
            _UID[0] += 1
            return st.enter_context(nc.sbuf_tensor("%s_%d" % (name, _UID[0]), shape, dt))

        def pst(st, name, shape, dt=F32):
            _UID[0] += 1
            return st.enter_context(nc.psum_tensor("%s_%d" % (name, _UID[0]), shape, dt))

        hnT = sbt(top, "hnT", [128, 8, L], BF16)
        B_hn = [Buf("hn%d" % i) for i in range(32)]
        WR = {"t": None}
        B_Wres = Buf("Wres")

        def open_wres(st):
            WR["t"] = sbt(st, "Wres", [128, 22, DM], BF16)
        identf = sbt(top, "identf", [128, 128]); B_c = Buf("consts")
        identb = sbt(top, "identb", [128, 128], BF16)
        triT_f = sbt(top, "triT_f", [128, 128])
        triT_b = sbt(top, "triT_b", [128, 128], BF16)
        ones_f = sbt(top, "ones_f", [128, 128])
        epsc = sbt(top, "epsc", [128, 1])
        normw = sbt(top, "normw", [128, 4, 8])
        fcw = sbt(top, "fcw", [128, 2, 3, 44])
        fcb = sbt(top, "fcb", [128, 2, 44])
        scw = sbt(top, "scw", [128, 3, 8])

        S.op("pool", lambda: G.memset(identf[:], 1.0), writes=[B_c])
        S.op("pool", lambda: G.affine_select(out=identf[:], in_=identf[:], pattern=[[-1, 128]],
                                             compare_op=ALU.is_equal, fill=0.0, base=0,
                                             channel_multiplier=1), reads=[B_c], writes=[B_c])
        S.op("pool", lambda: G.memset(triT_f[:], 1.0), writes=[B_c])
        S.op("pool", lambda: G.affine_select(out=triT_f[:], in_=triT_f[:], pattern=[[1, 128]],
                                             compare_op=ALU.is_ge, fill=0.0, base=0,
                                             channel_multiplier=-1), reads=[B_c], writes=[B_c])
        S.op("pool", lambda: G.memset(ones_f[:], 1.0), writes=[B_c])
        S.op("pool", lambda: G.memset(epsc[:], EPS), writes=[B_c])
        S.op("dve", lambda: V.tensor_copy(out=identb[:], in_=identf[:]), reads=[B_c], writes=[B_c])
        S.op("dve", lambda: V.tensor_copy(out=triT_b[:], in_=triT_f[:]), reads=[B_c], writes=[B_c])

        def load_cols(st_ring, ps_ring, dst, src_row, nch):
            stg, B_s = st_ring.next()
            S.dma("sp", stg[0:nch, :], src_row.rearrange("(c p) -> c p", p=128), writes=[B_s])
            ps, B_p = ps_ring.next()
            S.op("pe", lambda: P.matmul(ps[:, 0:nch], lhsT=stg[0:nch, :], rhs=identf[0:nch, 0:nch],
                                        start=True, stop=True), reads=[B_s, B_c], writes=[B_p])
            S.op("dve", lambda: V.tensor_copy(out=dst, in_=ps[:, 0:nch]), reads=[B_p], writes=[B_c])

        with ExitStack() as ph:
            stg_r = Ring(nc, ph, "stg", 3, [64, 128], F32)
            ps_r = PRing(nc, ph, "psc", 2, [128, 512])
            for i, (src, l) in enumerate([(mix_norm, 0), (ffn_norm, 0), (mix_norm, 1), (ffn_norm, 1)]):
                load_cols(stg_r, ps_r, normw[:, i, :], src[l, :], 8)
            for l in range(2):
                for k in range(3):
                    load_cols(stg_r, ps_r, fcw[:, l, k, :], ffn_conv_w[l, k, :], 44)
                load_cols(stg_r, ps_r, fcb[:, l, :], ffn_conv_b[l, :], 44)
            for k in range(3):
                load_cols(stg_r, ps_r, scw[:, k, :], sc_conv_w[0, k, :], 8)
            S.barrier()

        def load_Wres(w2d, nch):
            src = w2d.rearrange("(c p) n -> p c n", p=128)
            step = 4
            for c0 in range(0, nch, step):
                c1 = min(nch, c0 + step)
                S.dma("pool", WR["t"][:, c0:c1, :], src[:, c0:c1, :], writes=[B_Wres])

        def rstd(dst, src, scale, B_):
            S.op("act", lambda: A.activation(out=dst, in_=src, func=AF.Ln, bias=epsc[:, 0:1], scale=scale),
                 reads=[B_, B_c], writes=[B_])
            S.op("act", lambda: A.activation(out=dst, in_=dst, func=AF.Exp, scale=-0.5), reads=[B_], writes=[B_])

        class NormRes:
            def __init__(self, st):
                self.junk = sbt(st, "nr_junk", [128, DM], BF16); self.B_junk = Buf()
                self.xn = Ring(nc, st, "nr_xn", 2, [128, DM], BF16)
                self.sm = Ring(nc, st, "nr_sm", 3, [128, 2], F32)
                self.ps = PRing(nc, st, "nr_ps", 2, [128, 8, 128], BF16)

        def norm_to_hnT(nr, h_sb, B_h, widx, t128):
            sm, B_sm = nr.sm.next()
            S.op("act", lambda: A.activation(out=nr.junk[:], in_=h_sb, func=AF.Square,
                                             accum_out=sm[:, 0:1]),
                 reads=[B_h], writes=[nr.B_junk, B_sm])
            rstd(sm[:, 1:2], sm[:, 0:1], 1.0 / DM, B_sm)
            xn, B_xn = nr.xn.next()
            S.op("dve", lambda: V.tensor_scalar(out=xn[:], in0=h_sb, scalar1=sm[:, 1:2], scalar2=None,
                                                op0=ALU.mult), reads=[B_h, B_sm], writes=[B_xn])
            ps, B_ps = nr.ps.next()
            for c in range(8):
                S.op("pe", lambda c=c: P.transpose(out=ps[:, c, :], in_=xn[:, c * 128:(c + 1) * 128],
                                                   identity=identb[:]),
                     reads=[B_xn, B_c], writes=[B_ps], sig=(c == 7))
            S.op("dve", lambda: V.tensor_tensor(out=hnT[:, :, t128 * 128:(t128 + 1) * 128], in0=ps[:],
                                                in1=bc(normw[:, widx, :].unsqueeze(2), [128, 8, 128]),
                                                op=ALU.mult),
                 reads=[B_ps, B_c], writes=[B_hn[t128]])

        def phase_norm_in(h_src, widx):
            with ExitStack() as ph:
                nr = NormRes(ph)
                hr = Ring(nc, ph, "ni_h", 3, [128, DM], F32)
                for t in range(32):
                    h, B_h = hr.next()
                    S.dma("sp", h[:], h_src[t * 128:(t + 1) * 128, :], writes=[B_h])
                    norm_to_hnT(nr, h[:], B_h, widx, t)
                S.barrier()

        def phase_tm(nch, h_src, h_dst, widx_next):
            with ExitStack() as ph:
                nr = NormRes(ph) if widx_next is not None else None
                ar = Ring(nc, ph, "tm_a", 2, [128, nch, 512], BF16)
                hr = Ring(nc, ph, "tm_h", 3, [128, DM], F32)
                hn = Ring(nc, ph, "tm_hn", 3, [128, DM], F32)
                pr = PRing(nc, ph, "tm_ps", 4, [128, 512])
                src = ACT_S.rearrange("c p t -> p c t")
                pending_epi = [None]
                for tt in range(NT):
                    a, B_a = ar.next()
                    half = (nch + 1) // 2
                    S.dma("sp", a[:, 0:half, :], src[:, 0:half, tt * 512:(tt + 1) * 512], writes=[B_a])
                    S.dma("sp", a[:, half:nch, :], src[:, half:nch, tt * 512:(tt + 1) * 512], writes=[B_a])
                    for sub in range(4):
                        t128 = tt * 4 + sub
                        h, B_h = hr.next()
                        S.dma("sp", h[:], h_src[t128 * 128:(t128 + 1) * 128, :], writes=[B_h])
                        pss = [pr.next(), pr.next()]
                        for c in range(nch):
                            for hf in range(2):
                                ps, B_ps = pss[hf]
                                S.op("pe", lambda c=c, hf=hf, ps=ps: P.matmul(
                                    ps[:], lhsT=a[:, c, sub * 128:(sub + 1) * 128],
                                    rhs=WR["t"][:, c, hf * 512:(hf + 1) * 512],
                                    start=(c == 0), stop=(c == nch - 1)),
                                    reads=[B_a, B_Wres], writes=[B_ps], sig=(c == nch - 1))
                        def epi(pss=pss, h=h, B_h=B_h, t128=t128):
                            o, B_o = hn.next()
                            for hf in range(2):
                                ps, B_ps = pss[hf]
                                S.op("dve", lambda hf=hf, ps=ps: V.tensor_tensor(
                                    out=o[:, hf * 512:(hf + 1) * 512], in0=ps[:],
                                    in1=h[:, hf * 512:(hf + 1) * 512], op=ALU.add),
                                    reads=[B_ps, B_h], writes=[B_o])
                            S.dma("sp", h_dst[t128 * 128:(t128 + 1) * 128, :], o[:], reads=[B_o])
                            if nr is not None:
                                norm_to_hnT(nr, o[:], B_o, widx_next, t128)
                        if pending_epi[0] is not None:
                            pending_epi[0]()
                        pending_epi[0] = epi
                pending_epi[0]()
                S.barrier()

        def fm_matmuls(wt, n0, tt, ps, B_ps, B_w):
            for k in range(8):
                S.op("pe", lambda k=k: P.matmul(ps[:], lhsT=wt[:, k, n0:n0 + 128],
                                                rhs=hnT[:, k, tt * 512:(tt + 1) * 512],
                                                start=(k == 0), stop=(k == 7)),
                     reads=[B_w] + B_hn[tt * 4:tt * 4 + 4], writes=[B_ps], sig=(k == 7))

        def wload(wr, w2d, col0, ncol=128):
            wt, B_w = wr.next()
            S.dma("pool", wt[:, :, 0:ncol], w2d.rearrange("(k p) n -> p k n", p=128)[:, :, col0:col0 + ncol],
                  writes=[B_w])
            return wt, B_w

        def phase_ffn_up(l, w_next, nch_next):
            w2d = ffn_w_up[l]
            with ExitStack() as ph:
                wr = Ring(nc, ph, "fu_w", 6, [128, 8, 128], BF16)
                pr = PRing(nc, ph, "fu_ps", 6, [128, 512])
                xg = Ring(nc, ph, "fu_xg", 1, [128, L + 2], F32)
                xv = Ring(nc, ph, "fu_xv", 1, [128, L + 2], F32)
                yr = Ring(nc, ph, "fu_y", 6, [128, 512], F32)
                sr = Ring(nc, ph, "fu_s", 3, [128, 512], F32)
                gr = Ring(nc, ph, "fu_g", 2, [128, L], BF16)
                for r in (xg, xv):
                    for t_, b_ in zip(r.t, r.b):
                        S.op("pool", lambda t_=t_: G.memset(t_[:, 0:2], 0.0), writes=[b_])
                pend = [(wload(wr, w2d, 0), wload(wr, w2d, DFF))]
                load_Wres(w_next, nch_next)
                fu_pend = []
                for i in range(22):
                    if i + 1 < 22:
                        pend.append((wload(wr, w2d, (i + 1) * 128), wload(wr, w2d, DFF + (i + 1) * 128)))
                    (wg, B_wg), (wv, B_wv) = pend.pop(0)
                    Xg, B_xg = xg.next()
                    Xv, B_xv = xv.next()
                    Gt, B_g = gr.next()
                    cg, cv = i, 22 + i
                    for tt in range(NT):
                        sl = slice(tt * 512, (tt + 1) * 512)
                        psg, B_pg = pr.next()
                        psv, B_pv = pr.next()
                        fm_matmuls(wg, 0, tt, psg, B_pg, B_wg)
                        fm_matmuls(wv, 0, tt, psv, B_pv, B_wv)
                        Yg, B_yg = yr.next()
                        Yv, B_yv = yr.next()
                        S.op("act", lambda: A.activation(out=Yg[:], in_=psg[:], func=AF.Identity,
                                                         bias=fcb[:, l, cg:cg + 1], scale=fcw[:, l, 2, cg:cg + 1]),
                             reads=[B_pg, B_c], writes=[B_yg])
                        S.op("act", lambda: A.copy(out=Xg[:, 2 + tt * 512:2 + (tt + 1) * 512], in_=psg[:]),
                             reads=[B_pg], writes=[B_xg])
                        S.op("act", lambda: A.activation(out=Yv[:], in_=psv[:], func=AF.Identity,
                                                         bias=fcb[:, l, cv:cv + 1], scale=fcw[:, l, 2, cv:cv + 1]),
                             reads=[B_pv, B_c], writes=[B_yv])
                        S.op("dve", lambda: V.tensor_copy(out=Xv[:, 2 + tt * 512:2 + (tt + 1) * 512], in_=psv[:]),
                             reads=[B_pv], writes=[B_xv])
                        for (e, X_, B_x, Y_, B_y, c_) in (("dve", Xg, B_xg, Yg, B_yg, cg), ("dve", Xv, B_xv, Yv, B_yv, cv)):
                            for k in (1, 0):
                                en = "dve"
                                E_ = G if en == "pool" else V
                                S.op(en, lambda E_=E_, X_=X_, Y_=Y_, k=k, c_=c_: E_.scalar_tensor_tensor(
                                    out=Y_[:], in0=X_[:, tt * 512 + k:tt * 512 + k + 512],
                                    scalar=fcw[:, l, k, c_:c_ + 1], in1=Y_[:], op0=ALU.mult, op1=ALU.add),
                                    reads=[B_x, B_c, B_y], writes=[B_y])
                        def fin(Yg=Yg, B_yg=B_yg, Yv=Yv, B_yv=B_yv, Gt=Gt, B_g=B_g, sl=sl):
                            St, B_s = sr.next()
                            S.op("act", lambda: A.activation(out=St[:], in_=Yg[:], func=AF.Silu),
                                 reads=[B_yg], writes=[B_s])
                            S.op("pool", lambda: G.tensor_tensor(out=Gt[:, sl], in0=St[:], in1=Yv[:], op=ALU.mult),
                                 reads=[B_s, B_yv], writes=[B_g])
                        fu_pend.append(fin)
                        if len(fu_pend) > 1:
                            fu_pend.pop(0)()
                    while fu_pend:
                        fu_pend.pop(0)()
                    S.dma("sp", ACT_S[i], Gt[:], reads=[B_g])
                S.barrier()

        def phase_shortconv(w_next, nch_next):
            w2d = sc_w_in[0]
            with ExitStack() as ph:
                wr = Ring(nc, ph, "sc_w", 6, [128, 8, 128], BF16)
                pr = PRing(nc, ph, "sc_ps", 6, [128, 512])
                mr = Ring(nc, ph, "sc_m", 2, [128, L + 2], F32)
                ur = Ring(nc, ph, "sc_u", 3, [128, 512], F32)
                yr = Ring(nc, ph, "sc_y", 3, [128, 512], F32)
                rr = Ring(nc, ph, "sc_r", 2, [128, L], BF16)
                for t_, b_ in zip(mr.t, mr.b):
                    S.op("pool", lambda t_=t_: G.memset(t_[:, 0:2], 0.0), writes=[b_])
                load_Wres(w_next, nch_next)
                for i in range(8):
                    wb_, wc_, wu_ = (wload(wr, w2d, i * 128), wload(wr, w2d, 1024 + i * 128),
                                     wload(wr, w2d, 2048 + i * 128))
                    M, B_m = mr.next()
                    R, B_r = rr.next()
                    for tt in range(NT):
                        sl = slice(tt * 512, (tt + 1) * 512)
                        psb, B_pb = pr.next()
                        psc, B_pc = pr.next()
                        psu, B_pu = pr.next()
                        fm_matmuls(wc_[0], 0, tt, psc, B_pc, wc_[1])
                        fm_matmuls(wu_[0], 0, tt, psu, B_pu, wu_[1])
                        fm_matmuls(wb_[0], 0, tt, psb, B_pb, wb_[1])
                        U, B_u = ur.next()
                        S.op("act", lambda: A.copy(out=U[:], in_=psu[:]), reads=[B_pu], writes=[B_u])
                        S.op("dve", lambda: V.tensor_tensor(out=M[:, 2 + tt * 512:2 + (tt + 1) * 512], in0=psc[:],
                                                            in1=U[:], op=ALU.mult),
                             reads=[B_pc, B_u], writes=[B_m])
                        Y, B_y = yr.next()
                        S.op("act", lambda: A.activation(out=Y[:], in_=M[:, 2 + tt * 512:2 + (tt + 1) * 512],
                                                         func=AF.Copy, scale=scw[:, 2, i:i + 1]),
                             reads=[B_m, B_c], writes=[B_y])
                        for k in (1, 0):
                            S.op("dve", lambda k=k: V.scalar_tensor_tensor(
                                out=Y[:], in0=M[:, tt * 512 + k:tt * 512 + k + 512], scalar=scw[:, k, i:i + 1],
                                in1=Y[:], op0=ALU.mult, op1=ALU.add), reads=[B_m, B_c, B_y], writes=[B_y])
                        S.op("dve", lambda: V.tensor_tensor(out=R[:, sl], in0=psb[:], in1=Y[:], op=ALU.mult),
                             reads=[B_pb, B_y], writes=[B_r])
                    S.dma("sp", ACT_S[i], R[:], reads=[B_r])
                S.barrier()

        def phase_hybrid(h_in_ap, h_out_ap, widx_next):
            hyb = {}
            w2d = hy_w_in[0]
            hst = ExitStack()
            cst = ExitStack()
            Bh = Buf("hyb_consts")
            blk = sbt(hst, "blk", [128, 128], BF16)
            prot = sbt(hst, "prot", [128, 128], BF16)
            hcw = sbt(hst, "hcw", [128, 4, 12])
            hcb = sbt(hst, "hcb", [128, 12])
            qkw = sbt(hst, "qkw", [128, 2])
            dt_tm = sbt(hst, "dt_tm", [128, 32, 16])
            a_tm = sbt(hst, "a_tm", [128, 32, 16])
            acs = sbt(hst, "acs", [128, 32, 16])
            eA = sbt(hst, "eA", [128, 32, 16])
            dA = sbt(hst, "dA", [128, 32, 16])
            cdr = sbt(hst, "cdr", [128, 32, 16])
            bc16 = sbt(hst, "bc16", [128, 3, 16])
            ssdw = sbt(hst, "ssdw", [128, 1024])
            sublw = sbt(hst, "sublw", [128, 128])
            nlam = sbt(hst, "nlam", [128, 2])
            B_dt = Buf("dtstuff")
            cosT = sbt(cst, "cosT", [128, L])
            sinT = sbt(cst, "sinT", [128, L])

            with ExitStack() as ph:
                tf = sbt(ph, "c_tf", [128, 128])
                tf2 = sbt(ph, "c_tf2", [128, 128])
                B_t = Buf()
                S.op("pool", lambda: G.memset(tf[:], 0.0), writes=[B_t])
                S.op("pool", lambda: G.memset(tf[0:64, 0:64], 1.0), writes=[B_t])
                S.op("pool", lambda: G.memset(tf[64:128, 64:128], 1.0), writes=[B_t])
                S.op("dve", lambda: V.tensor_copy(out=blk[:], in_=tf[:]), reads=[B_t], writes=[Bh])
                S.op("pool", lambda: G.memset(tf[:], 1.0), reads=[Bh], writes=[B_t])
                S.op("pool", lambda: G.affine_select(out=tf[:], in_=tf[:], pattern=[[-1, 128]], compare_op=ALU.is_equal,
                                                     fill=0.0, base=32, channel_multiplier=1), reads=[B_t], writes=[B_t])
                S.op("pool", lambda: G.memset(tf2[:], -1.0), writes=[B_t])
                S.op("pool", lambda: G.affine_select(out=tf2[:], in_=tf2[:], pattern=[[-1, 128]], compare_op=ALU.is_equal,
                                                     fill=0.0, base=-32, channel_multiplier=1), reads=[B_t], writes=[B_t])
                for c0 in (0, 64):
                    S.op("pool", lambda c0=c0: G.memset(tf[:, c0:c0 + 32], 0.0), reads=[B_t], writes=[B_t])
                    S.op("pool", lambda c0=c0: G.memset(tf2[:, c0 + 32:c0 + 64], 0.0), reads=[B_t], writes=[B_t])
                S.op("dve", lambda: V.tensor_tensor(out=tf[:], in0=tf[:], in1=tf2[:], op=ALU.add), reads=[B_t], writes=[B_t])
                S.op("dve", lambda: V.tensor_copy(out=prot[:], in_=tf[:]), reads=[B_t], writes=[Bh])
                pi_ = sbt(ph, "c_pi", [128, 1], I32)
                pf_ = sbt(ph, "c_pf", [128, 2])
                S.op("pool", lambda: G.iota(pi_[:], pattern=[[0, 1]], base=0, channel_multiplier=1), writes=[B_t])
                S.op("dve", lambda: V.tensor_single_scalar(out=pi_[:], in_=pi_[:], scalar=31, op=ALU.bitwise_and),
                     reads=[B_t], writes=[B_t])
                S.op("dve", lambda: V.tensor_copy(out=pf_[:, 0:1], in_=pi_[:]), reads=[B_t], writes=[B_t])
                S.op("dve", lambda: V.tensor_scalar(out=pf_[:, 0:1], in0=pf_[:, 0:1], scalar1=-math.log(10000.0) / 32.0,
                                                    scalar2=-math.log(2 * math.pi), op0=ALU.mult, op1=ALU.add),
                     reads=[B_t], writes=[B_t])
                S.op("act", lambda: A.activation(out=pf_[:, 1:2], in_=pf_[:, 0:1], func=AF.Exp), reads=[B_t], writes=[B_t])
                ti = sbt(ph, "c_ti", [128, L], I32)
                r_ = sbt(ph, "c_r", [128, L])
                kf = sbt(ph, "c_kf", [128, L])
                S.op("pool", lambda: G.iota(ti[:], pattern=[[1, L]], base=0, channel_multiplier=0), writes=[B_t])
                S.op("dve", lambda: V.tensor_copy(out=r_[:], in_=ti[:]), reads=[B_t], writes=[B_t])
                S.op("dve", lambda: V.tensor_scalar(out=r_[:], in0=r_[:], scalar1=pf_[:, 1:2], scalar2=None, op0=ALU.mult),
                     reads=[B_t], writes=[B_t])
                for (dst, off) in ((sinT, 0.0), (cosT, 0.25)):
                    S.op("dve", lambda off=off: V.tensor_scalar(out=kf[:], in0=r_[:], scalar1=off, scalar2=None, op0=ALU.add),
                         reads=[B_t, Bh], writes=[B_t])
                    S.op("dve", lambda: V.tensor_copy(out=ti[:], in_=kf[:]), reads=[B_t], writes=[B_t])
                    S.op("dve", lambda dst=dst: V.tensor_copy(out=dst[:], in_=ti[:]), reads=[B_t], writes=[Bh])
                    S.op("dve", lambda dst=dst: V.tensor_tensor(out=kf[:], in0=kf[:], in1=dst[:], op=ALU.subtract),
                         reads=[B_t, Bh], writes=[B_t])
                    S.op("dve", lambda dst=dst: V.tensor_single_scalar(out=dst[:], in_=kf[:], scalar=0.5, op=ALU.is_gt),
                         reads=[B_t], writes=[Bh])
                    S.op("dve", lambda dst=dst: V.tensor_tensor(out=kf[:], in0=kf[:], in1=dst[:], op=ALU.subtract),
                         reads=[B_t, Bh], writes=[B_t])
                    S.op("act", lambda dst=dst: A.activation(out=dst[:], in_=kf[:], func=AF.Sin, scale=2 * math.pi),
                         reads=[B_t], writes=[Bh])
                stg_r = Ring(nc, ph, "hstg", 3, [64, 128], F32)
                ps_r = PRing(nc, ph, "hpsc", 2, [128, 512])

                def lc(dst, src_row, nch):
                    stg, B_s = stg_r.next()
                    S.dma("sp", stg[0:nch, :], src_row.rearrange("(c p) -> c p", p=128), writes=[B_s])
                    ps, B_p = ps_r.next()
                    S.op("pe", lambda: P.matmul(ps[:, 0:nch], lhsT=stg[0:nch, :], rhs=identf[0:nch, 0:nch],
                                                start=True, stop=True), reads=[B_s, B_c], writes=[B_p])
                    S.op("dve", lambda: V.tensor_copy(out=dst, in_=ps[:, 0:nch]), reads=[B_p], writes=[Bh])
                for k in range(4):
                    lc(hcw[:, k, :], hy_conv_w[0, k, :], 12)
                lc(hcb[:, :], hy_conv_b[0, :], 12)
                for j, src in enumerate((hy_q_norm, hy_k_norm)):
                    for hf in range(2):
                        S.dma("sp", qkw[hf * 64:(hf + 1) * 64, j:j + 1], src[0, :].rearrange("(p o) -> p o", o=1),
                              writes=[Bh])
                S.op("dve", lambda: V.tensor_scalar(out=qkw[:, 0:1], in0=qkw[:, 0:1], scalar1=0.125, scalar2=None,
                                                    op0=ALU.mult), reads=[Bh], writes=[Bh])
                for j, src in enumerate((hy_dt_bias, hy_a_log, hy_d_skip)):
                    S.dma("sp", bc16[:, j, :], src[0:1, :].partition_broadcast(128), writes=[Bh])
                S.op("act", lambda: A.activation(out=bc16[:, 1, :], in_=bc16[:, 1, :], func=AF.Exp), reads=[Bh], writes=[Bh])
                S.op("dve", lambda: V.tensor_scalar(out=bc16[:, 1, :], in0=bc16[:, 1, :], scalar1=-1.0, scalar2=None,
                                                    op0=ALU.mult), reads=[Bh], writes=[Bh])
                S.dma("sp", ssdw[:], hy_ssd_norm[0:1, :].partition_broadcast(128), writes=[Bh])
                S.dma("sp", sublw[:], hy_subln[0:1, :].partition_broadcast(128), writes=[Bh])
                lam_init = 0.8 - 0.6 * math.exp(-0.3 * 0)
                S.op("dve", lambda: V.tensor_scalar(out=sublw[:], in0=sublw[:], scalar1=(1.0 - lam_init), scalar2=None,
                                                    op0=ALU.mult), reads=[Bh], writes=[Bh])
                lt = sbt(ph, "c_lt", [128, 4, 64])
                ls = sbt(ph, "c_ls", [128, 4])
                for j, src in enumerate((hy_lq1, hy_lk1, hy_lq2, hy_lk2)):
                    S.dma("sp", lt[:, j, :], src[0:1, :].partition_broadcast(128), writes=[B_t])
                for j in range(2):
                    S.op("dve", lambda j=j: V.tensor_tensor(out=lt[:, 2 * j, :], in0=lt[:, 2 * j, :], in1=lt[:, 2 * j + 1, :],
                                                            op=ALU.mult), reads=[B_t], writes=[B_t])
                    S.op("act", lambda j=j: A.activation(out=lt[:, 2 * j + 1, :], in_=lt[:, 2 * j, :], func=AF.Identity,
                                                         accum_out=ls[:, j:j + 1]), reads=[B_t], writes=[B_t])
                S.op("act", lambda: A.activation(out=ls[:, 2:4], in_=ls[:, 0:2], func=AF.Exp), reads=[B_t], writes=[B_t])
                S.op("dve", lambda: V.tensor_tensor(out=nlam[:, 0:1], in0=ls[:, 3:4], in1=ls[:, 2:3], op=ALU.subtract),
                     reads=[B_t], writes=[Bh])
                S.op("dve", lambda: V.tensor_scalar(out=nlam[:, 0:1], in0=nlam[:, 0:1], scalar1=-lam_init, scalar2=None,
                                                    op0=ALU.add), reads=[Bh], writes=[Bh])
                S.barrier()
            if hstop == 1:
                cst.close(); hst.close()
                return True

            with ExitStack() as ph:
                wr = Ring(nc, ph, "hb_w", 4, [128, 8, 128], BF16)
                pr = PRing(nc, ph, "hb_ps", 4, [128, 512])
                pr2 = PRing(nc, ph, "hb_ps2", 4, [128, 512])
                xr = Ring(nc, ph, "hb_x", 2, [128, L + 3], BF16)
                yr = Ring(nc, ph, "hb_y", 4, [128, 512], F32)
                orr = Ring(nc, ph, "hb_o", 2, [128, L], BF16)
                sq = Ring(nc, ph, "hb_sq", 3, [128, 512], BF16)
                rs = Ring(nc, ph, "hb_rs", 3, [128, 512], F32)
                qn = Ring(nc, ph, "hb_qn", 3, [128, 512], BF16)
                t1r = Ring(nc, ph, "hb_t1", 3, [128, 512], F32)
                t2r = Ring(nc, ph, "hb_t2", 3, [128, 512], F32)
                for t_, b_ in zip(xr.t, xr.b):
                    S.op("pool", lambda t_=t_: G.memset(t_[:, 0:3], 0.0), writes=[b_])
                jobs = [("xbc", i, C_XBC + i * 128) for i in range(12)] + \
                       [("q", i, C_Q + i * 128) for i in range(8)] + [("k", i, C_K + i * 128) for i in range(8)]
                pend = [wload(wr, w2d, jobs[0][2])]
                units = [(ji, tt) for ji in range(len(jobs)) for tt in range(NT)]
                js = {}
                ust = {}

                def stage1(u):
                    ji, tt = units[u]
                    if tt == 0:
                        if ji + 1 < len(jobs):
                            pend.append(wload(wr, w2d, jobs[ji + 1][2]))
                        js[ji] = {"w": pend.pop(0)}
                    wt, B_w = js[ji]["w"]
                    ps, B_ps = pr.next()
                    fm_matmuls(wt, 0, tt, ps, B_ps, B_w)
                    ust[u] = {"ps": (ps, B_ps)}

                def stage2(u):
                    ji, tt = units[u]
                    kind, i, col = jobs[ji]
                    sl = slice(tt * 512, (tt + 1) * 512)
                    ps, B_ps = ust[u]["ps"]
                    if tt == 0:
                        js[ji]["O"] = orr.next()
                        if kind == "xbc":
                            js[ji]["X"] = xr.next()
                    O, B_o = js[ji]["O"]
                    if kind == "xbc":
                        X, B_x = js[ji]["X"]
                        Y, B_y = yr.next()
                        S.op("act", lambda: A.activation(out=Y[:], in_=ps[:], func=AF.Identity,
                                                         bias=hcb[:, i:i + 1], scale=hcw[:, 3, i:i + 1]),
                             reads=[B_ps, Bh], writes=[B_y])
                        S.op("act", lambda: A.copy(out=X[:, 3 + tt * 512:3 + (tt + 1) * 512], in_=ps[:]),
                             reads=[B_ps], writes=[B_x])
                        for k in (2, 1, 0):
                            S.op("dve", lambda k=k: V.scalar_tensor_tensor(
                                out=Y[:], in0=X[:, tt * 512 + k:tt * 512 + k + 512], scalar=hcw[:, k, i:i + 1],
                                in1=Y[:], op0=ALU.mult, op1=ALU.add), reads=[B_x, Bh, B_y], writes=[B_y])
                        ust[u]["Y"] = (Y, B_y)
                    else:
                        wcol = qkw[:, 0:1] if kind == "q" else qkw[:, 1:2]
                        s_, B_s = sq.next()
                        S.op("act", lambda: A.activation(out=s_[:], in_=ps[:], func=AF.Square),
                             reads=[B_ps], writes=[B_s])
                        ps2, B_p2 = pr2.next()
                        S.op("pe", lambda: P.matmul(ps2[:], lhsT=blk[:], rhs=s_[:], start=True, stop=True),
                             reads=[Bh, B_s], writes=[B_p2])
                        r1, B_r1 = rs.next()
                        S.op("act", lambda: A.activation(out=r1[:], in_=ps2[:], func=AF.Ln, bias=epsc[:, 0:1],
                                                         scale=1.0 / 64.0), reads=[B_p2, B_c], writes=[B_r1])
                        S.op("act", lambda: A.activation(out=r1[:], in_=r1[:], func=AF.Exp, scale=-0.5),
                             reads=[B_r1], writes=[B_r1])
                        q_, B_q = qn.next()
                        S.op("dve", lambda: V.scalar_tensor_tensor(out=q_[:], in0=ps[:], scalar=wcol, in1=r1[:],
                                                                   op0=ALU.mult, op1=ALU.mult),
                             reads=[B_ps, Bh, B_r1], writes=[B_q])
                        ust[u]["q"] = (q_, B_q)

                def stage3(u):
                    ji, tt = units[u]
                    kind, i, col = jobs[ji]
                    sl = slice(tt * 512, (tt + 1) * 512)
                    O, B_o = js[ji]["O"]
                    if kind == "xbc":
                        Y, B_y = ust[u]["Y"]
                        S.op("act", lambda: A.activation(out=O[:, sl], in_=Y[:], func=AF.Silu),
                             reads=[B_y], writes=[B_o])
                    if kind != "xbc":
                        q_, B_q = ust[u]["q"]
                        ps3, B_p3 = pr2.next()
                        S.op("pe", lambda: P.matmul(ps3[:], lhsT=prot[:], rhs=q_[:], start=True, stop=True),
                             reads=[Bh, B_q], writes=[B_p3])
                        t1, B_t1 = t1r.next()
                        S.op("pool", lambda: G.tensor_tensor(out=t1[:], in0=q_[:], in1=cosT[:, sl], op=ALU.mult),
                             reads=[B_q, Bh], writes=[B_t1])
                        t2, B_t2 = t2r.next()
                        S.op("dve", lambda: V.tensor_tensor(out=t2[:], in0=ps3[:], in1=sinT[:, sl], op=ALU.mult),
                             reads=[B_p3, Bh], writes=[B_t2])
                        S.op("dve", lambda: V.tensor_tensor(out=O[:, sl], in0=t1[:], in1=t2[:], op=ALU.add),
                             reads=[B_t1, B_t2], writes=[B_o])
                    if tt == NT - 1:
                        dst = {"xbc": XBC_S, "q": QT_S, "k": KT_S}[kind]
                        S.dma("sp", dst[i], O[:], reads=[B_o])
                    del ust[u]

                NU = len(units)
                for st_ in range(NU + 2):
                    if st_ < NU:
                        stage1(st_)
                    if 0 <= st_ - 1 < NU:
                        stage2(st_ - 1)
                    if 0 <= st_ - 2 < NU:
                        stage3(st_ - 2)
                S.barrier()

            cst.close()
            if hstop == 2:
                hst.close()
                return True
            with ExitStack() as ph:
                wz = sbt(ph, "wz", [128, 8, 1024], BF16)
                wv = sbt(ph, "wv", [128, 8, 1024], BF16)
                wd = sbt(ph, "wd", [128, 8, 16], BF16)
                B_w = Buf()
                wsrc = w2d.rearrange("(k p) n -> p k n", p=128)
                for k0 in (0, 4):
                    S.dma("pool", wz[:, k0:k0 + 4, :], wsrc[:, k0:k0 + 4, C_Z:C_Z + 1024], writes=[B_w])
                    S.dma("pool", wv[:, k0:k0 + 4, :], wsrc[:, k0:k0 + 4, C_V:C_V + 1024], writes=[B_w])
                S.dma("pool", wd[:], wsrc[:, :, C_DT:C_DT + 16], writes=[B_w])
                pr = PRing(nc, ph, "tz_ps", 6, [128, 512])
                pdt = pst(ph, "tz_pdt", [128, 32, 16]); B_pdt = Buf(excl=True)
                zr = Ring(nc, ph, "tz_z", 3, [128, 1024], BF16)
                vr = Ring(nc, ph, "tz_v", 3, [128, 1024], BF16)
                for t in range(32):
                    pz = [pr.next(), pr.next()]
                    pv = [pr.next(), pr.next()]
                    for k in range(8):
                        lhs = hnT[:, k, t * 128:(t + 1) * 128]
                        for hf in range(2):
                            S.op("pe", lambda k=k, hf=hf, lhs=lhs: P.matmul(pz[hf][0][:], lhsT=lhs, rhs=wz[:, k, hf * 512:(hf + 1) * 512],
                                                                          start=(k == 0), stop=(k == 7)),
                                 reads=[B_hn[t], B_w], writes=[pz[hf][1]], sig=(k == 7))
                            S.op("pe", lambda k=k, hf=hf, lhs=lhs: P.matmul(pv[hf][0][:], lhsT=lhs, rhs=wv[:, k, hf * 512:(hf + 1) * 512],
                                                                          start=(k == 0), stop=(k == 7)),
                                 reads=[B_hn[t], B_w], writes=[pv[hf][1]], sig=(k == 7))
                        S.op("pe", lambda k=k, lhs=lhs: P.matmul(pdt[:, t, :], lhsT=lhs, rhs=wd[:, k, :],
                                                                 start=(k == 0), stop=(k == 7)),
                             reads=[B_hn[t], B_w], writes=[B_pdt], sig=(k == 7))
                    Z, B_z = zr.next()
                    Vt, B_v = vr.next()
                    for hf in range(2):
                        S.op("act", lambda hf=hf: A.activation(out=Z[:, hf * 512:(hf + 1) * 512], in_=pz[hf][0][:], func=AF.Silu),
                             reads=[pz[hf][1]], writes=[B_z])
                        S.op("dve", lambda hf=hf: V.tensor_copy(out=Vt[:, hf * 512:(hf + 1) * 512], in_=pv[hf][0][:]),
                             reads=[pv[hf][1]], writes=[B_v])
                    S.dma("sp", ZS_S[t], Z[:], reads=[B_z])
                    S.dma("sp", V_S[t], Vt[:], reads=[B_v])
                xb_ = sbt(ph, "dt_x", [128, 32, 16])
                ab_ = sbt(ph, "dt_a", [128, 32, 16])
                S.op("dve", lambda: V.tensor_tensor(out=xb_[:], in0=pdt[:], in1=bc(bc16[:, 0, :].unsqueeze(1), [128, 32, 16]),
                                                    op=ALU.add), reads=[B_pdt, Bh], writes=[B_dt])
                S.op("act", lambda: A.activation(out=ab_[:], in_=xb_[:], func=AF.Abs), reads=[B_dt], writes=[B_dt])
                S.op("act", lambda: A.activation(out=ab_[:], in_=ab_[:], func=AF.Exp, scale=-1.0), reads=[B_dt], writes=[B_dt])
                S.op("dve", lambda: V.tensor_scalar(out=ab_[:], in0=ab_[:], scalar1=1.0, scalar2=None, op0=ALU.add),
                     reads=[B_dt], writes=[B_dt])
                S.op("act", lambda: A.activation(out=ab_[:], in_=ab_[:], func=AF.Ln), reads=[B_dt], writes=[B_dt])
                S.op("dve", lambda: V.scalar_tensor_tensor(out=dt_tm[:], in0=xb_[:], scalar=0.0, in1=ab_[:], op0=ALU.max,
                                                           op1=ALU.add), reads=[B_dt], writes=[B_dt])
                S.op("dve", lambda: V.tensor_tensor(out=a_tm[:], in0=dt_tm[:], in1=bc(bc16[:, 1, :].unsqueeze(1), [128, 32, 16]),
                                                    op=ALU.mult), reads=[B_dt, Bh], writes=[B_dt])
                pa, B_pa = pr.next()
                pl, B_pl = pr.next()
                S.op("pe", lambda: P.matmul(pa[:], lhsT=triT_f[:], rhs=a_tm[:].rearrange("p c h -> p (c h)"),
                                            start=True, stop=True), reads=[B_c, B_dt], writes=[B_pa])
                S.op("pe", lambda: P.matmul(pl[:], lhsT=ones_f[:], rhs=a_tm[:].rearrange("p c h -> p (c h)"),
                                            start=True, stop=True), reads=[B_c, B_dt], writes=[B_pl])
                fl = lambda t_: t_[:].rearrange("p c h -> p (c h)")
                S.op("dve", lambda: V.tensor_copy(out=fl(acs), in_=pa[:]), reads=[B_pa], writes=[B_dt])
                S.op("act", lambda: A.activation(out=fl(eA), in_=pa[:], func=AF.Exp), reads=[B_pa], writes=[B_dt])
                S.op("act", lambda: A.activation(out=fl(cdr), in_=pl[:], func=AF.Exp), reads=[B_pl], writes=[B_dt])
                S.op("dve", lambda: V.tensor_tensor(out=fl(dA), in0=pl[:], in1=fl(acs), op=ALU.subtract),
                     reads=[B_pl, B_dt], writes=[B_dt])
                S.op("act", lambda: A.activation(out=fl(dA), in_=fl(dA), func=AF.Exp), reads=[B_dt], writes=[B_dt])
                S.barrier()
            if hstop == 3:
                hst.close()
                return True

            with ExitStack() as ph:
                xin = Ring(nc, ph, "sd_x", 2, [128, 12, 512], BF16)
                zin = Ring(nc, ph, "sd_z", 3, [128, 1024], BF16)
                ptr = PRing(nc, ph, "sd_ptr", 2, [128, 1024], BF16)
                pcb = pst(ph, "sd_pcb", [128, 4, 128]); B_pcb = Buf(excl=True)
                pR = PRing(nc, ph, "sd_pR", 1, [128, 8, 128])
                pya = pst(ph, "sd_pya", [128, 512]); B_pya = Buf(excl=True)
                pyb = pst(ph, "sd_pyb", [128, 512]); B_pyb = Buf(excl=True)
                pst_ = pst(ph, "sd_pst", [128, 512]); B_pst = Buf(excl=True)
                xs_tm = Ring(nc, ph, "sd_xs", 2, [128, 1024], BF16)
                b_tm = Ring(nc, ph, "sd_b", 2, [128, 256], BF16)
                Xr = Ring(nc, ph, "sd_X", 2, [128, 1024], BF16)
                Xdr = Ring(nc, ph, "sd_Xd", 2, [128, 1024], BF16)
                cbm = Ring(nc, ph, "sd_cbm", 2, [128, 2, 128], BF16)
                rhsR = Ring(nc, ph, "sd_rr", 1, [128, 8, 128], F32)
                segr = Ring(nc, ph, "sd_seg", 1, [128, 8, 128], F32)
                Er = Ring(nc, ph, "sd_E", 2, [128, 8, 128], BF16)
                Wr = Ring(nc, ph, "sd_W", 2, [128, 8, 128], BF16)
                yr = Ring(nc, ph, "sd_y", 2, [128, 512], F32)
                y2r = Ring(nc, ph, "sd_y2", 1, [128, 512], F32)
                ynr = Ring(nc, ph, "sd_yn", 2, [128, 512], BF16)
                smr = Ring(nc, ph, "sd_sm", 3, [128, 2], F32)
                junk = sbt(ph, "sd_junk", [128, 512], BF16); B_junk = Buf()
                prev = sbt(ph, "sd_prev", [128, 1024]); B_prev = Buf()
                prevb = sbt(ph, "sd_prevb", [128, 1024], BF16); B_prevb = Buf()
                yT = Ring(nc, ph, "sd_yT", 2, [128, 8, 512], BF16)
                S.op("pool", lambda: G.memset(prev[:], 0.0), writes=[B_prev])
                S.op("pool", lambda: G.memset(prevb[:], 0.0), writes=[B_prevb])
                xsrc = XBC_S.rearrange("c p t -> p c t")
                chs = {}
                cur = {}

                def prep(c):
                    cc = c % 4
                    if cc == 0:
                        Xin, B_xin = xin.next()
                        S.dma("sp", Xin[:, 0:6, :], xsrc[:, 0:6, c * 128:c * 128 + 512], writes=[B_xin])
                        S.dma("sp", Xin[:, 6:12, :], xsrc[:, 6:12, c * 128:c * 128 + 512], writes=[B_xin])
                        cur["Xin"] = (Xin, B_xin)
                        cur["YT"] = yT.next()
                    Xin, B_xin = cur["Xin"]
                    csl = slice(cc * 128, (cc + 1) * 128)
                    Zt, B_z = zin.next()
                    S.dma("sp", Zt[:], ZS_S[c], writes=[B_z])
                    pt, B_pt = ptr.next()
                    for j in range(8):
                        S.op("pe", lambda j=j: P.transpose(out=pt[:, j * 128:(j + 1) * 128], in_=Xin[:, j, csl], identity=identb[:]),
                             reads=[B_xin, B_c], writes=[B_pt], sig=(j == 7))
                    xs, B_xs = xs_tm.next()
                    S.op("act", lambda: A.copy(out=xs[:], in_=pt[:]), reads=[B_pt], writes=[B_xs])
                    pt2, B_pt2 = ptr.next()
                    for g in range(2):
                        S.op("pe", lambda g=g: P.transpose(out=pt2[:, g * 128:(g + 1) * 128], in_=Xin[:, 8 + g, csl], identity=identb[:]),
                             reads=[B_xin, B_c], writes=[B_pt2], sig=(g == 1))
                    bt, B_bt = b_tm.next()
                    S.op("act", lambda: A.copy(out=bt[:], in_=pt2[:, 0:256]), reads=[B_pt2], writes=[B_bt])
                    X, B_X = Xr.next()
                    S.op("dve", lambda: V.tensor_tensor(out=X[:].rearrange("p (h d) -> p h d", h=16),
                                                        in0=xs[:].rearrange("p (h d) -> p h d", h=16),
                                                        in1=bc(dt_tm[:, c, :].unsqueeze(2), [128, 16, 64]), op=ALU.mult),
                         reads=[B_xs, B_dt], writes=[B_X])
                    Xd, B_Xd = Xdr.next()
                    S.op("pool", lambda: G.tensor_tensor(out=Xd[:].rearrange("p (h d) -> p h d", h=16),
                                                         in0=X[:].rearrange("p (h d) -> p h d", h=16),
                                                         in1=bc(dA[:, c, :].unsqueeze(2), [128, 16, 64]), op=ALU.mult),
                         reads=[B_X, B_dt], writes=[B_Xd])
                    for g in range(2):
                        S.op("pe", lambda g=g: P.matmul(pcb[:, g, :], lhsT=Xin[:, 8 + g, csl], rhs=Xin[:, 10 + g, csl],
                                                        start=True, stop=True), reads=[B_xin], writes=[B_pcb], sig=(g == 1))
                    cb, B_cb = cbm.next()
                    S.op("dve", lambda: V.tensor_tensor(out=cb[:], in0=pcb[:, 0:2, :], in1=bc(triT_f[:].unsqueeze(1), [128, 2, 128]),
                                                        op=ALU.mult), reads=[B_pcb, B_c], writes=[B_cb])
                    chs[c] = dict(Xin=(Xin, B_xin), csl=csl, Z=(Zt, B_z), xs=(xs, B_xs), bt=(bt, B_bt), X=(X, B_X),
                                  Xd=(Xd, B_Xd), cb=(cb, B_cb), YT=cur["YT"], W={})

                def stage1a_pool(c, g):
                    hs = slice(g * 8, (g + 1) * 8)
                    rr_, B_rr = rhsR.next()
                    S.op("pool", lambda: G.tensor_tensor(out=rr_[:], in0=bc(triT_f[:].unsqueeze(1), [128, 8, 128]),
                                                         in1=bc(a_tm[:, c, hs].unsqueeze(2), [128, 8, 128]), op=ALU.mult),
                         reads=[B_c, B_dt], writes=[B_rr])
                    chs[c]["rr", g] = (rr_, B_rr)

                def stage1a_pe(c, g):
                    rr_, B_rr = chs[c]["rr", g]
                    pr_, B_pr = pR.next()
                    for q4 in range(2):
                        S.op("pe", lambda q4=q4: P.matmul(pr_[:, q4 * 4:(q4 + 1) * 4, :], lhsT=ones_f[:],
                                                          rhs=rr_[:, q4 * 4:(q4 + 1) * 4, :], start=True, stop=True),
                             reads=[B_c, B_rr], writes=[B_pr], sig=(q4 == 1))
                    chs[c]["pr", g] = (pr_, B_pr)

                def stage1b(c, g):
                    d = chs[c]
                    cb, B_cb = d["cb"]
                    hs = slice(g * 8, (g + 1) * 8)
                    pr_, B_pr = d["pr", g]
                    if False:
                        rr_, B_rr = rhsR.next()
                    sg, B_sg = segr.next()
                    S.op("dve", lambda: V.tensor_tensor(out=sg[:], in0=pr_[:], in1=bc(acs[:, c, hs].unsqueeze(2), [128, 8, 128]),
                                                        op=ALU.subtract), reads=[B_pr, B_dt], writes=[B_sg])
                    S.op("dve", lambda: V.tensor_single_scalar(out=sg[:], in_=sg[:], scalar=0.0, op=ALU.min),
                         reads=[B_sg], writes=[B_sg])
                    E, B_E = Er.next()
                    S.op("act", lambda: A.activation(out=E[:], in_=sg[:], func=AF.Exp), reads=[B_sg], writes=[B_E])
                    W, B_W = Wr.next()
                    S.op("dve", lambda: V.tensor_tensor(out=W[:], in0=E[:], in1=bc(cb[:, g, :].unsqueeze(1), [128, 8, 128]),
                                                        op=ALU.mult), reads=[B_E, B_cb], writes=[B_W])
                    d["W"][g] = (W, B_W)

                def stage2a(c, g):
                    d = chs[c]
                    Xin, B_xin = d["Xin"]; csl = d["csl"]; Zt, B_z = d["Z"]; xs, B_xs = d["xs"]; bt, B_bt = d["bt"]
                    X, B_X = d["X"]; Xd, B_Xd = d["Xd"]; YT, B_yT = d["YT"]; W, B_W = d["W"][g]
                    cc = c % 4
                    hs = slice(g * 8, (g + 1) * 8)
                    fs = slice(g * 512, (g + 1) * 512)
                    for h in range(8):
                        hh = g * 8 + h
                        S.op("pe", lambda h=h, hh=hh: P.matmul(pya[:, h * 64:(h + 1) * 64], lhsT=W[:, h, :],
                                                               rhs=X[:, hh * 64:(hh + 1) * 64], start=True, stop=True),
                             reads=[B_W, B_X], writes=[B_pya], sig=(h == 7))
                    S.op("pe", lambda: P.matmul(pyb[:], lhsT=Xin[:, 10 + g, csl], rhs=prevb[:, fs], start=True, stop=True),
                         reads=[B_xin, B_prevb], writes=[B_pyb])
                def stage2b(c, g):
                    d = chs[c]
                    Zt, B_z = d["Z"]; xs, B_xs = d["xs"]
                    hs = slice(g * 8, (g + 1) * 8)
                    fs = slice(g * 512, (g + 1) * 512)
                    y, B_y = yr.next()
                    S.op("dve", lambda: V.tensor_tensor(out=y[:].rearrange("p (h d) -> p h d", h=8),
                                                        in0=pyb[:].rearrange("p (h d) -> p h d", h=8),
                                                        in1=bc(eA[:, c, hs].unsqueeze(2), [128, 8, 64]), op=ALU.mult),
                         reads=[B_pyb, B_dt], writes=[B_y])
                    S.op("dve", lambda: V.tensor_tensor(out=y[:], in0=y[:], in1=pya[:], op=ALU.add),
                         reads=[B_y, B_pya], writes=[B_y])
                    y2, B_y2 = y2r.next()
                    S.op("pool", lambda: G.tensor_tensor(out=y2[:].rearrange("p (h d) -> p h d", h=8),
                                                         in0=xs[:, fs].rearrange("p (h d) -> p h d", h=8),
                                                         in1=bc(bc16[:, 2, hs].unsqueeze(2), [128, 8, 64]), op=ALU.mult),
                         reads=[B_xs, Bh], writes=[B_y2])
                    S.op("dve", lambda: V.tensor_tensor(out=y[:], in0=y[:], in1=y2[:], op=ALU.add),
                         reads=[B_y, B_y2], writes=[B_y])
                    S.op("dve", lambda: V.tensor_tensor(out=y[:], in0=y[:], in1=Zt[:, fs], op=ALU.mult),
                         reads=[B_y, B_z], writes=[B_y])
                    sm, B_sm = smr.next()
                    S.op("act", lambda: A.activation(out=junk[:], in_=y[:], func=AF.Square, accum_out=sm[:, 0:1]),
                         reads=[B_y], writes=[B_junk, B_sm])
                    rstd(sm[:, 1:2], sm[:, 0:1], 1.0 / 512.0, B_sm)
                    yn, B_yn = ynr.next()
                    S.op("dve", lambda: V.scalar_tensor_tensor(out=yn[:], in0=y[:], scalar=sm[:, 1:2], in1=ssdw[:, fs],
                                                               op0=ALU.mult, op1=ALU.mult),
                         reads=[B_y, B_sm, Bh], writes=[B_yn])
                    d["yn", g] = (yn, B_yn)

                def stage2c(c, g):
                    d = chs[c]
                    csl = d["csl"]; bt, B_bt = d["bt"]; Xd, B_Xd = d["Xd"]; YT, B_yT = d["YT"]
                    yn, B_yn = d["yn", g]
                    cc = c % 4
                    hs = slice(g * 8, (g + 1) * 8)
                    fs = slice(g * 512, (g + 1) * 512)
                    pto, B_pto = ptr.next()
                    for j in range(4):
                        S.op("pe", lambda j=j: P.transpose(out=pto[:, j * 128:(j + 1) * 128], in_=yn[:, j * 128:(j + 1) * 128],
                                                           identity=identb[:]), reads=[B_yn, B_c], writes=[B_pto], sig=(j == 3))
                    S.op("act", lambda: A.copy(out=YT[:, g * 4:(g + 1) * 4, csl],
                                               in_=pto[:, 0:512].rearrange("p (j t) -> p j t", j=4)),
                         reads=[B_pto], writes=[B_yT])
                    S.op("pe", lambda: P.matmul(pst_[:], lhsT=bt[:, g * 128:(g + 1) * 128], rhs=Xd[:, fs], start=True, stop=True),
                         reads=[B_bt, B_Xd], writes=[B_pst])
                    S.op("dve", lambda: V.tensor_tensor(out=prev[:, fs].rearrange("p (h d) -> p h d", h=8),
                                                        in0=prev[:, fs].rearrange("p (h d) -> p h d", h=8),
                                                        in1=bc(cdr[:, c, hs].unsqueeze(2), [128, 8, 64]), op=ALU.mult),
                         reads=[B_prev, B_dt], writes=[B_prev])
                    S.op("dve", lambda: V.tensor_tensor(out=prev[:, fs], in0=prev[:, fs], in1=pst_[:], op=ALU.add),
                         reads=[B_prev, B_pst], writes=[B_prev])
                    S.op("act", lambda: A.copy(out=prevb[:, fs], in_=prev[:, fs]), reads=[B_prev], writes=[B_prevb])
                    if cc == 3 and g == 1:
                        c0 = (c - 3) * 128
                        S.dma("sp", ACT_S.rearrange("c p t -> p c t")[:, 0:8, c0:c0 + 512], YT[:], reads=[B_yT])
                    if g == 1:
                        del chs[c]

                sunits = [(c, g) for c in range(32) for g in range(2)]
                NSU = len(sunits)
                for st_ in range(NSU + 1):
                    cur_u = sunits[st_] if st_ < NSU else None
                    prv_u = sunits[st_ - 1] if st_ >= 1 else None
                    if cur_u is not None:
                        if cur_u[1] == 0:
                            prep(cur_u[0])
                        stage1a_pool(*cur_u)
                    if prv_u is not None:
                        stage2a(*prv_u)
                    if cur_u is not None:
                        stage1a_pe(*cur_u)
                    if prv_u is not None:
                        stage2b(*prv_u)
                    if cur_u is not None:
                        stage1b(*cur_u)
                    if prv_u is not None:
                        stage2c(*prv_u)
                S.barrier()
            if hstop == 4:
                hst.close()
                return True

            wst = ExitStack()
            open_wres(wst)
            load_Wres(hy_w_out[0], 16)
            with ExitStack() as ph:
                kq = Ring(nc, ph, "at_kq", 2, [128, 2, L], BF16)
                vr = Ring(nc, ph, "at_v", 2, [128, 32, 130], BF16)
                pS = PRing(nc, ph, "at_pS", 2, [128, 2, 512])
                pO = PRing(nc, ph, "at_pO", 3, [128, 2, 256])
                pT = PRing(nc, ph, "at_pT", 1, [128, 1024], BF16)
                Er = Ring(nc, ph, "at_E", 3, [128, 2, 512], BF16)
                smr = Ring(nc, ph, "at_sm", 6, [128, 4], F32)
                tr_ = Ring(nc, ph, "at_t", 4, [128, 128], F32)
                or_ = Ring(nc, ph, "at_o", 6, [128, 128], F32)
                ybr = Ring(nc, ph, "at_yb", 6, [128, 128], BF16)
                junk = sbt(ph, "at_junk", [128, 128], BF16); B_junk = Buf()
                yT = Ring(nc, ph, "at_yT", 2, [128, L], BF16)
                for t_, b_ in zip(vr.t, vr.b):
                    S.op("pool", lambda t_=t_: G.memset(t_[:, :, 128:130], 1.0), writes=[b_])
                for t_, b_ in zip(Er.t, Er.b):
                    S.op("pool", lambda t_=t_: G.memset(t_[:], 0.0), writes=[b_])
                vsrc = V_S.rearrange("t p f -> p t f")
                for hd in range(8):
                    KQ, B_kq = kq.next()
                    S.dma("sp", KQ[:, 0, :], KT_S[hd], writes=[B_kq])
                    S.dma("sp", KQ[:, 1, :], QT_S[hd], writes=[B_kq])
                    Vt, B_v = vr.next()
                    for v0 in range(0, 32, 8):
                        S.dma("sp", Vt[:, v0:v0 + 8, 0:128], vsrc[:, v0:v0 + 8, hd * 128:(hd + 1) * 128], writes=[B_v])
                    YT, B_yT = yT.next()
                    items = [(qt, i) for qt in range(16) for i in range(qt + 1)]

                    def stage_qk(qt, i):
                        q0 = qt * 256
                        diag = (i == qt)
                        ps, B_ps = pS.next()
                        for half in range(2):
                            kb = 2 * i + half
                            c0 = 128 if (diag and half == 1) else 0
                            for m in range(2):
                                S.op("pe", lambda m=m, half=half, kb=kb, c0=c0: P.matmul(
                                    ps[:, m, half * 256 + c0:half * 256 + 256],
                                    lhsT=KQ[64 * m:64 * m + 64, 0, kb * 128:(kb + 1) * 128],
                                    rhs=KQ[64 * m:64 * m + 64, 1, q0 + c0:q0 + 256], start=True, stop=True),
                                    reads=[B_kq], writes=[B_ps], sig=(half == 1 and m == 1))
                        return ps, B_ps

                    def stage_exp(qt, i, ps, B_ps):
                        diag = (i == qt)
                        E, B_E = Er.next()
                        if diag:
                            S.op("act", lambda: A.activation(out=E[:, :, 0:256], in_=ps[:, :, 0:256], func=AF.Exp),
                                 reads=[B_ps], writes=[B_E])
                            S.op("act", lambda: A.activation(out=E[:, :, 384:512], in_=ps[:, :, 384:512], func=AF.Exp),
                                 reads=[B_ps], writes=[B_E])
                            Ev_ = E[:].rearrange("p m (r c) -> p m r c", c=128)
                            S.op("dve", lambda: V.tensor_tensor(out=Ev_[:, :, 0:4:3, :], in0=Ev_[:, :, 0:4:3, :],
                                                                in1=bc(triT_b[:].unsqueeze(1).unsqueeze(1), [128, 2, 2, 128]),
                                                                op=ALU.mult), reads=[B_E, B_c], writes=[B_E])
                        else:
                            S.op("act", lambda: A.activation(out=E[:], in_=ps[:], func=AF.Exp), reads=[B_ps], writes=[B_E])
                        return E, B_E

                    def stage_pv(qt, i, E, B_E, pOs):
                        diag = (i == qt)
                        for half in range(2):
                            kb = 2 * i + half
                            for qs in range(2):
                                if diag and half == 1 and qs == 0:
                                    continue
                                po, B_po = pOs[qs]
                                first = (i == 0 and half == 0)
                                last = diag and (half == qs)
                                for m in range(2):
                                    S.op("pe", lambda m=m, qs=qs, po=po, kb=kb, half=half, first=first, last=last: P.matmul(
                                        po[:, m, 0:129], lhsT=E[:, m, half * 256 + qs * 128:half * 256 + (qs + 1) * 128],
                                        rhs=Vt[:, kb, 0:129], start=(first and m == 0), stop=(last and m == 1),
                                        skip_group_check=True),
                                        reads=[B_E, B_v], writes=[B_po], sig=(last and m == 1))

                    def epi1(qt, pOs):
                        outs = []
                        for qs in range(2):
                            po, B_po = pOs[qs]
                            sm, B_sm = smr.next()
                            S.op("dve", lambda: V.reciprocal(out=sm[:, 0:2], in_=po[:, :, 128]), reads=[B_po], writes=[B_sm])
                            t_, B_t = tr_.next()
                            S.op("dve", lambda: V.tensor_scalar(out=t_[:], in0=po[:, 1, 0:128], scalar1=sm[:, 1:2], scalar2=nlam[:, 0:1],
                                                                op0=ALU.mult, op1=ALU.mult), reads=[B_po, B_sm, Bh], writes=[B_t])
                            o_, B_o = or_.next()
                            S.op("dve", lambda: V.scalar_tensor_tensor(out=o_[:], in0=po[:, 0, 0:128], scalar=sm[:, 0:1], in1=t_[:],
                                                                       op0=ALU.mult, op1=ALU.add), reads=[B_po, B_sm, B_t], writes=[B_o])
                            outs.append((o_, B_o, sm, B_sm))
                        return outs

                    def epi2(qt, outs):
                        ybs = []
                        for qs in range(2):
                            o_, B_o, sm, B_sm = outs[qs]
                            S.op("act", lambda: A.activation(out=junk[:], in_=o_[:], func=AF.Square, accum_out=sm[:, 2:3]),
                                 reads=[B_o], writes=[B_junk, B_sm])
                            rstd(sm[:, 3:4], sm[:, 2:3], 1.0 / 128.0, B_sm)
                            yb, B_yb = ybr.next()
                            S.op("dve", lambda: V.scalar_tensor_tensor(out=yb[:], in0=o_[:], scalar=sm[:, 3:4], in1=sublw[:],
                                                                       op0=ALU.mult, op1=ALU.mult), reads=[B_o, B_sm, Bh], writes=[B_yb])
                            ybs.append((yb, B_yb))
                        return ybs

                    def epi3(qt, ybs):
                        q0 = qt * 256
                        for qs in range(2):
                            yb, B_yb = ybs[qs]
                            pt, B_pt = pT.next()
                            S.op("pe", lambda: P.transpose(out=pt[:, 0:128], in_=yb[:], identity=identb[:]), reads=[B_yb, B_c], writes=[B_pt])
                            S.op("dve", lambda: V.tensor_copy(out=YT[:, q0 + qs * 128:q0 + (qs + 1) * 128], in_=pt[:, 0:128]),
                                 reads=[B_pt], writes=[B_yT])

                    nxt_qk = stage_qk(*items[0])
                    pOs = None
                    pend2 = []
                    pend3 = []
                    for n, (qt, i) in enumerate(items):
                        ps, B_ps = nxt_qk
                        if n + 1 < len(items):
                            nxt_qk = stage_qk(*items[n + 1])
                        E, B_E = stage_exp(qt, i, ps, B_ps)
                        new3 = [(q_, epi2(q_, o_)) for (q_, o_) in pend2]
                        pend2 = []
                        if i == 0:
                            pOs = [pO.next(), pO.next()]
                        stage_pv(qt, i, E, B_E, pOs)
                        for (q_, y_) in pend3:
                            epi3(q_, y_)
                        pend3 = new3
                        if i == qt:
                            pend2.append((qt, epi1(qt, pOs)))
                    for (q_, o_) in pend2:
                        pend3.append((q_, epi2(q_, o_)))
                    for (q_, y_) in pend3:
                        epi3(q_, y_)
                    S.dma("sp", ACT_S[8 + hd], YT[:], reads=[B_yT])
                S.barrier()
            if hstop == 5:
                wst.close(); hst.close()
                return True
            phase_tm(16, h_in_ap, h_out_ap, widx_next)
            wst.close()
            hst.close()
            return False

        pcount = [0]

        def chk():
            pcount[0] += 1
            return stop is not None and pcount[0] >= stop

        def _main_program():
            h_cur = x
            scr = [hA, hB, hC]
            nxt = 0
            first = True
            for li, layer in enumerate(layers):
                last_layer = (li == len(layers) - 1)
                if first:
                    phase_norm_in(h_cur, 2 * layer)
                    if chk():
                        return
                    first = False
                h_mid = scr[nxt]; nxt += 1
                if layer == 0:
                    if phase_hybrid(h_cur, h_mid, 2 * layer + 1):
                        return
                    if chk():
                        return
                else:
                    with ExitStack() as ws:
                        open_wres(ws)
                        phase_shortconv(sc_w_out[0], 8)
                        if chk():
                            return
                        phase_tm(8, h_cur, h_mid, 2 * layer + 1)
                        if chk():
                            return
                with ExitStack() as ws:
                    open_wres(ws)
                    phase_ffn_up(layer, ffn_w_down[layer], 22)
                    if chk():
                        return
                    if last_layer:
                        phase_tm(22, h_mid, out, None)
                    else:
                        h_new = scr[nxt]; nxt += 1
                        phase_tm(22, h_mid, h_new, 2 * layers[li + 1])
                        h_cur = h_new
                    if chk():
                        return

        try:
            _main_program()
        except _Stop:
            pass
        S.barrier(engines=("sp",))
        build_program.stats = (S.n_ins, S.n_wait)
    return nc


_INPUT_NAMES = ["mix_norm", "ffn_norm", "hy_w_in", "hy_conv_w", "hy_conv_b", "hy_dt_bias", "hy_a_log",
                "hy_d_skip", "hy_ssd_norm", "hy_q_norm", "hy_k_norm", "hy_lambda_q1", "hy_lambda_k1",
                "hy_lambda_q2", "hy_lambda_k2", "hy_subln", "hy_w_out", "sc_w_in", "sc_conv_w", "sc_w_out",
                "ffn_w_up", "ffn_conv_w", "ffn_conv_b", "ffn_w_down"]


def kernel(**inputs):
    x = np.asarray(inputs["x"], dtype=np.float32)
    nc = build_program()
    shared = {k: np.ascontiguousarray(np.asarray(inputs[k], dtype=np.float32)) for k in _INPUT_NAMES}
    in_maps = []
    for b in range(8):
        m = dict(shared)
        m["x"] = np.ascontiguousarray(x[b])
        in_maps.append(m)
    res = run_bass_kernel_spmd(nc, in_maps, core_ids=list(range(8)))
    return np.stack([np.asarray(r["out"], dtype=np.float32) for r in res.results], axis=0)
```

```python
import math
from contextlib import ExitStack

import numpy as np
import concourse.bass as bass
import concourse.mybir as mybir
from concourse.bass_utils import run_bass_kernel_spmd

F32 = mybir.dt.float32
BF16 = mybir.dt.bfloat16
I32 = mybir.dt.int32
AF = mybir.ActivationFunctionType
ALU = mybir.AluOpType

L = 4096
DM = 1024
NT = 8
DFF = 2816
EPS = 1e-6
W_IN0 = 5648
C_Z, C_XBC, C_DT, C_Q, C_K, C_V = 0, 1024, 2560, 2576, 3600, 4624


class Ev:
    __slots__ = ("key", "sem", "val", "eng", "clock")

    def __init__(self, key, sem, val, eng, clock):
        self.key, self.sem, self.val, self.eng, self.clock = key, sem, val, eng, clock


class Buf:
    __slots__ = ("name", "w", "rd", "excl")

    def __init__(self, name="", excl=False):
        self.name, self.w, self.rd, self.excl = name, None, {}, excl


class Sched:
    NDMA = 8

    def __init__(self, nc, stack):
        self.nc = nc
        self.engs = {"pe": nc.tensor, "act": nc.scalar, "dve": nc.vector,
                     "pool": nc.gpsimd, "sp": nc.sync}
        self.sem, self.cnt, self.pending, self.last_ins = {}, {}, {}, {}
        self.seen = {e: {} for e in self.engs}
        for e in ("pe", "act", "dve", "pool"):
            self.sem[e] = stack.enter_context(nc.semaphore("s_" + e))
            self.cnt[e] = 0
            self.pending[e] = []
            self.last_ins[e] = None
        self.dsem, self.dcnt, self.drr = {}, {}, {}
        for q in ("sp", "pool"):
            self.dsem[q] = []
            for i in range(self.NDMA):
                k = "d_%s%d" % (q, i)
                self.dsem[q].append((k, stack.enter_context(nc.semaphore(k))))
                self.dcnt[k] = 0
            self.drr[q] = 0
        self.n_wait = 0
        self.n_ins = 0

    def _need(self, e, ev):
        if ev is None:
            return
        if ev.val is None:
            self._force(ev.eng)
        seen = self.seen[e]
        if seen.get(ev.key, 0) >= ev.val:
            return
        self.engs[e].wait_ge(ev.sem, ev.val)
        self.n_wait += 1
        for k, v in ev.clock.items():
            if seen.get(k, 0) < v:
                seen[k] = v

    def _force(self, e):
        if not self.pending[e]:
            return
        self.last_ins[e].then_inc(self.sem[e], 1)
        self._signal(e)

    def _signal(self, e):
        self.cnt[e] += 1
        v = self.cnt[e]
        clock = dict(self.seen[e])
        clock["s_" + e] = v
        for ev in self.pending[e]:
            ev.val = v
            ev.clock = clock
        self.pending[e] = []
        self.last_ins[e] = None

    def _deps(self, e, reads, writes, is_dma):
        for b in reads:
            if b.w is not None:
                self._need(e, b.w)
            if b.excl:
                for ev in list(b.rd.values()):
                    if ev.eng != e:
                        self._need(e, ev)
        for b in writes:
            if b.w is not None and (is_dma or b.w.eng != e or e == "pool"):
                self._need(e, b.w)
            for ev in list(b.rd.values()):
                if is_dma or ev.eng != e or e == "pool":
                    self._need(e, ev)

    def _record(self, ev, reads, writes):
        for b in reads:
            b.rd[ev.key] = ev
        for b in writes:
            b.w = ev
            b.rd = {}

    def op(self, e, fn, reads=(), writes=(), sig=True):
        self._deps(e, reads, writes, False)
        ins = fn()
        self.n_ins += 1
        ev = Ev("s_" + e, self.sem[e], None, e, None)
        self.pending[e].append(ev)
        self.last_ins[e] = ins
        if sig:
            ins.then_inc(self.sem[e], 1)
            self._signal(e)
        self._record(ev, reads, writes)
        return ev

    def dma(self, q, out, in_, reads=(), writes=(), **kw):
        i = self.drr[q]
        self.drr[q] = (i + 1) % self.NDMA
        key, sem = self.dsem[q][i]
        prev = self.dcnt[key]
        seen = self.seen[q]
        if prev > 0 and seen.get(key, 0) < prev:
            self.engs[q].wait_ge(sem, prev)
            self.n_wait += 1
            seen[key] = prev
        self._deps(q, reads, writes, True)
        ins = self.engs[q].dma_start(out=out, in_=in_, **kw)
        ins.then_inc(sem, 16)
        self.n_ins += 1
        self.dcnt[key] = prev + 16
        clock = dict(seen)
        clock[key] = prev + 16
        ev = Ev(key, sem, prev + 16, None, clock)
        self._record(ev, reads, writes)
        return ev

    def barrier(self, engines=("pe", "act", "dve", "pool", "sp")):
        for e in ("pe", "act", "dve", "pool"):
            self._force(e)
        evs = []
        for e in ("pe", "act", "dve", "pool"):
            if self.cnt[e] > 0:
                evs.append(Ev("s_" + e, self.sem[e], self.cnt[e], e, {"s_" + e: self.cnt[e]}))
        for q in self.dsem:
            for key, sem in self.dsem[q]:
                if self.dcnt[key] > 0:
                    evs.append(Ev(key, sem, self.dcnt[key], None, {key: self.dcnt[key]}))
        for e in engines:
            for ev in evs:
                if ev.eng != e:
                    self._need(e, ev)


_UID = [0]


class Ring:
    def __init__(self, nc, st, name, n, shape, dt):
        _UID[0] += 1
        name = "%s_%d_" % (name, _UID[0])
        self.t = [st.enter_context(nc.sbuf_tensor("%s%d" % (name, i), shape, dt)) for i in range(n)]
        self.b = [Buf("%s%d" % (name, i)) for i in range(n)]
        self.i = 0

    def next(self):
        i = self.i
        self.i = (i + 1) % len(self.t)
        return self.t[i], self.b[i]


class PRing:
    def __init__(self, nc, st, name, n, shape, dt=F32):
        _UID[0] += 1
        name = "%s_%d_" % (name, _UID[0])
        self.t = [st.enter_context(nc.psum_tensor("%s%d" % (name, i), shape, dt)) for i in range(n)]
        self.b = [Buf("%s%d" % (name, i), excl=True) for i in range(n)]
        self.i = 0

    def next(self):
        i = self.i
        self.i = (i + 1) % len(self.t)
        return self.t[i], self.b[i]


def bc(ap, shape):
    return ap.to_broadcast(shape)


class _Stop(Exception):
    pass


def build_program(layers=(0, 1), dbg=False, stop=None, hstop=None):
    nc = bass.Bass("TRN2", target_bir_lowering=False)

    def din(name, shape):
        return nc.dram_tensor(name, shape, F32, kind="ExternalInput").ap()

    x = din("x", [L, DM])
    mix_norm = din("mix_norm", [2, DM])
    ffn_norm = din("ffn_norm", [2, DM])
    hy_w_in = din("hy_w_in", [1, DM, W_IN0])
    hy_conv_w = din("hy_conv_w", [1, 4, 1536])
    hy_conv_b = din("hy_conv_b", [1, 1536])
    hy_dt_bias = din("hy_dt_bias", [1, 16])
    hy_a_log = din("hy_a_log", [1, 16])
    hy_d_skip = din("hy_d_skip", [1, 16])
    hy_ssd_norm = din("hy_ssd_norm", [1, 1024])
    hy_q_norm = din("hy_q_norm", [1, 64])
    hy_k_norm = din("hy_k_norm", [1, 64])
    hy_lq1 = din("hy_lambda_q1", [1, 64])
    hy_lk1 = din("hy_lambda_k1", [1, 64])
    hy_lq2 = din("hy_lambda_q2", [1, 64])
    hy_lk2 = din("hy_lambda_k2", [1, 64])
    hy_subln = din("hy_subln", [1, 128])
    hy_w_out = din("hy_w_out", [1, 2048, DM])
    sc_w_in = din("sc_w_in", [1, DM, 3072])
    sc_conv_w = din("sc_conv_w", [1, 3, 1024])
    sc_w_out = din("sc_w_out", [1, 1024, DM])
    ffn_w_up = din("ffn_w_up", [2, DM, 2 * DFF])
    ffn_conv_w = din("ffn_conv_w", [2, 3, 2 * DFF])
    ffn_conv_b = din("ffn_conv_b", [2, 2 * DFF])
    ffn_w_down = din("ffn_w_down", [2, DFF, DM])
    out = nc.dram_tensor("out", [L, DM], F32, kind="ExternalOutput").ap()

    skind = "ExternalOutput" if dbg else "Internal"

    def dscr(name, shape, dt=BF16):
        return nc.dram_tensor(name, shape, dt, kind=skind).ap()

    hA = dscr("hA", [L, DM], F32)
    hB = dscr("hB", [L, DM], F32)
    hC = dscr("hC", [L, DM], F32)
    ACT_S = dscr("ACT_S", [22, 128, L])
    XBC_S = dscr("XBC_S", [12, 128, L])
    QT_S = dscr("QT_S", [8, 128, L])
    KT_S = dscr("KT_S", [8, 128, L])
    ZS_S = dscr("ZS_S", [32, 128, 1024])
    V_S = dscr("V_S", [32, 128, 1024])

    with ExitStack() as top:
        S = Sched(nc, top)
        V, A, G, P = nc.vector, nc.scalar, nc.gpsimd, nc.tensor

        def sbt(st, name, shape, dt=F32):
            _UID[0] += 1
            return st.enter_context(nc.sbuf_tensor("%s_%d" % (name, _UID[0]), shape, dt))

        def pst(st, name, shape, dt=F32):
            _UID[0] += 1
            return st.enter_context(nc.psum_tensor("%s_%d" % (name, _UID[0]), shape, dt))

        hnT = sbt(top, "hnT", [128, 8, L], BF16)
        B_hn = [Buf("hn%d" % i) for i in range(32)]
        WR = {"t": None}
        B_Wres = Buf("Wres")

        def open_wres(st):
            WR["t"] = sbt(st, "Wres", [128, 22, DM], BF16)
        identf = sbt(top, "identf", [128, 128]); B_c = Buf("consts")
        identb = sbt(top, "identb", [128, 128], BF16)
        triT_f = sbt(top, "triT_f", [128, 128])
        triT_b = sbt(top, "triT_b", [128, 128], BF16)
        ones_f = sbt(top, "ones_f", [128, 128])
        epsc = sbt(top, "epsc", [128, 1])
        normw = sbt(top, "normw", [128, 4, 8])
        fcw = sbt(top, "fcw", [128, 2, 3, 44])
        fcb = sbt(top, "fcb", [128, 2, 44])
        scw = sbt(top, "scw", [128, 3, 8])

        S.op("pool", lambda: G.memset(identf[:], 1.0), writes=[B_c])
        S.op("pool", lambda: G.affine_select(out=identf[:], in_=identf[:], pattern=[[-1, 128]],
                                             compare_op=ALU.is_equal, fill=0.0, base=0,
                                             channel_multiplier=1), reads=[B_c], writes=[B_c])
        S.op("pool", lambda: G.memset(triT_f[:], 1.0), writes=[B_c])
        S.op("pool", lambda: G.affine_select(out=triT_f[:], in_=triT_f[:], pattern=[[1, 128]],
                                             compare_op=ALU.is_ge, fill=0.0, base=0,
                                             channel_multiplier=-1), reads=[B_c], writes=[B_c])
        S.op("pool", lambda: G.memset(ones_f[:], 1.0), writes=[B_c])
        S.op("pool", lambda: G.memset(epsc[:], EPS), writes=[B_c])
        S.op("dve", lambda: V.tensor_copy(out=identb[:], in_=identf[:]), reads=[B_c], writes=[B_c])
        S.op("dve", lambda: V.tensor_copy(out=triT_b[:], in_=triT_f[:]), reads=[B_c], writes=[B_c])

        def load_cols(st_ring, ps_ring, dst, src_row, nch):
            stg, B_s = st_ring.next()
            S.dma("sp", stg[0:nch, :], src_row.rearrange("(c p) -> c p", p=128), writes=[B_s])
            ps, B_p = ps_ring.next()
            S.op("pe", lambda: P.matmul(ps[:, 0:nch], lhsT=stg[0:nch, :], rhs=identf[0:nch, 0:nch],
                                        start=True, stop=True), reads=[B_s, B_c], writes=[B_p])
            S.op("dve", lambda: V.tensor_copy(out=dst, in_=ps[:, 0:nch]), reads=[B_p], writes=[B_c])

        with ExitStack() as ph:
            stg_r = Ring(nc, ph, "stg", 3, [64, 128], F32)
            ps_r = PRing(nc, ph, "psc", 2, [128, 512])
            for i, (src, l) in enumerate([(mix_norm, 0), (ffn_norm, 0), (mix_norm, 1), (ffn_norm, 1)]):
                load_cols(stg_r, ps_r, normw[:, i, :], src[l, :], 8)
            for l in range(2):
                for k in range(3):
                    load_cols(stg_r, ps_r, fcw[:, l, k, :], ffn_conv_w[l, k, :], 44)
                load_cols(stg_r, ps_r, fcb[:, l, :], ffn_conv_b[l, :], 44)
            for k in range(3):
                load_cols(stg_r, ps_r, scw[:, k, :], sc_conv_w[0, k, :], 8)
            S.barrier()

        def load_Wres(w2d, nch):
            src = w2d.rearrange("(c p) n -> p c n", p=128)
            step = 4
            for c0 in range(0, nch, step):
                c1 = min(nch, c0 + step)
                S.dma("pool", WR["t"][:, c0:c1, :], src[:, c0:c1, :], writes=[B_Wres])

        def rstd(dst, src, scale, B_):
            S.op("act", lambda: A.activation(out=dst, in_=src, func=AF.Ln, bias=epsc[:, 0:1], scale=scale),
                 reads=[B_, B_c], writes=[B_])
            S.op("act", lambda: A.activation(out=dst, in_=dst, func=AF.Exp, scale=-0.5), reads=[B_], writes=[B_])

        class NormRes:
            def __init__(self, st):
                self.junk = sbt(st, "nr_junk", [128, DM], BF16); self.B_junk = Buf()
                self.xn = Ring(nc, st, "nr_xn", 2, [128, DM], BF16)
                self.sm = Ring(nc, st, "nr_sm", 3, [128, 2], F32)
                self.ps = PRing(nc, st, "nr_ps", 2, [128, 8, 128], BF16)

        def norm_to_hnT(nr, h_sb, B_h, widx, t128):
            sm, B_sm = nr.sm.next()
            S.op("act", lambda: A.activation(out=nr.junk[:], in_=h_sb, func=AF.Square,
                                             accum_out=sm[:, 0:1]),
                 reads=[B_h], writes=[nr.B_junk, B_sm])
            rstd(sm[:, 1:2], sm[:, 0:1], 1.0 / DM, B_sm)
            xn, B_xn = nr.xn.next()
            S.op("dve", lambda: V.tensor_scalar(out=xn[:], in0=h_sb, scalar1=sm[:, 1:2], scalar2=None,
                                                op0=ALU.mult), reads=[B_h, B_sm], writes=[B_xn])
            ps, B_ps = nr.ps.next()
            for c in range(8):
                S.op("pe", lambda c=c: P.transpose(out=ps[:, c, :], in_=xn[:, c * 128:(c + 1) * 128],
                                                   identity=identb[:]),
                     reads=[B_xn, B_c], writes=[B_ps], sig=(c == 7))
            S.op("dve", lambda: V.tensor_tensor(out=hnT[:, :, t128 * 128:(t128 + 1) * 128], in0=ps[:],
                                                in1=bc(normw[:, widx, :].unsqueeze(2), [128, 8, 128]),
                                                op=ALU.mult),
                 reads=[B_ps, B_c], writes=[B_hn[t128]])

        def phase_norm_in(h_src, widx):
            with ExitStack() as ph:
                nr = NormRes(ph)
                hr = Ring(nc, ph, "ni_h", 3, [128, DM], F32)
                for t in range(32):
                    h, B_h = hr.next()
                    S.dma("sp", h[:], h_src[t * 128:(t + 1) * 128, :], writes=[B_h])
                    norm_to_hnT(nr, h[:], B_h, widx, t)
                S.barrier()

        def phase_tm(nch, h_src, h_dst, widx_next):
            with ExitStack() as ph:
                nr = NormRes(ph) if widx_next is not None else None
                ar = Ring(nc, ph, "tm_a", 2, [128, nch, 512], BF16)
                hr = Ring(nc, ph, "tm_h", 3, [128, DM], F32)
                hn = Ring(nc, ph, "tm_hn", 3, [128, DM], F32)
                pr = PRing(nc, ph, "tm_ps", 4, [128, 512])
                src = ACT_S.rearrange("c p t -> p c t")
                pending_epi = [None]

                def load_a(tt):
                    a, B_a = ar.next()
                    half = (nch + 1) // 2
                    S.dma("sp", a[:, 0:half, :], src[:, 0:half, tt * 512:(tt + 1) * 512], writes=[B_a])
                    S.dma("sp", a[:, half:nch, :], src[:, half:nch, tt * 512:(tt + 1) * 512], writes=[B_a])
                    return a, B_a
                a_next = load_a(0)
                for tt in range(NT):
                    a, B_a = a_next
                    if tt + 1 < NT:
                        a_next = load_a(tt + 1)
                    for sub in range(4):
                        t128 = tt * 4 + sub
                        h, B_h = hr.next()
                        S.dma("sp", h[:], h_src[t128 * 128:(t128 + 1) * 128, :], writes=[B_h])
                        pss = [pr.next(), pr.next()]
                        for c in range(nch):
                            for hf in range(2):
                                ps, B_ps = pss[hf]
                                S.op("pe", lambda c=c, hf=hf, ps=ps: P.matmul(
                                    ps[:], lhsT=a[:, c, sub * 128:(sub + 1) * 128],
                                    rhs=WR["t"][:, c, hf * 512:(hf + 1) * 512],
                                    start=(c == 0), stop=(c == nch - 1)),
                                    reads=[B_a, B_Wres], writes=[B_ps], sig=(c == nch - 1))
                        def epi(pss=pss, h=h, B_h=B_h, t128=t128):
                            o, B_o = hn.next()
                            for hf in range(2):
                                ps, B_ps = pss[hf]
                                S.op("dve", lambda hf=hf, ps=ps: V.tensor_tensor(
                                    out=o[:, hf * 512:(hf + 1) * 512], in0=ps[:],
                                    in1=h[:, hf * 512:(hf + 1) * 512], op=ALU.add),
                                    reads=[B_ps, B_h], writes=[B_o])
                            S.dma("sp", h_dst[t128 * 128:(t128 + 1) * 128, :], o[:], reads=[B_o])
                            if nr is not None:
                                norm_to_hnT(nr, o[:], B_o, widx_next, t128)
                        if pending_epi[0] is not None:
                            pending_epi[0]()
                        pending_epi[0] = epi
                pending_epi[0]()
                S.barrier()

        def fm_matmuls(wt, n0, tt, ps, B_ps, B_w):
            for k in range(8):
                S.op("pe", lambda k=k: P.matmul(ps[:], lhsT=wt[:, k, n0:n0 + 128],
                                                rhs=hnT[:, k, tt * 512:(tt + 1) * 512],
                                                start=(k == 0), stop=(k == 7)),
                     reads=[B_w] + B_hn[tt * 4:tt * 4 + 4], writes=[B_ps], sig=(k == 7))

        def wload(wr, w2d, col0, ncol=128):
            wt, B_w = wr.next()
            S.dma("pool", wt[:, :, 0:ncol], w2d.rearrange("(k p) n -> p k n", p=128)[:, :, col0:col0 + ncol],
                  writes=[B_w])
            return wt, B_w

        def phase_ffn_up(l, w_next, nch_next):
            w2d = ffn_w_up[l]
            with ExitStack() as ph:
                wr = Ring(nc, ph, "fu_w", 6, [128, 8, 128], BF16)
                pr = PRing(nc, ph, "fu_ps", 6, [128, 512])
                xg = Ring(nc, ph, "fu_xg", 1, [128, L + 2], F32)
                xv = Ring(nc, ph, "fu_xv", 1, [128, L + 2], F32)
                yr = Ring(nc, ph, "fu_y", 6, [128, 512], F32)
                sr = Ring(nc, ph, "fu_s", 3, [128, 512], F32)
                gr = Ring(nc, ph, "fu_g", 2, [128, L], BF16)
                for r in (xg, xv):
                    for t_, b_ in zip(r.t, r.b):
                        S.op("pool", lambda t_=t_: G.memset(t_[:, 0:2], 0.0), writes=[b_])
                pend = [(wload(wr, w2d, 0), wload(wr, w2d, DFF))]
                load_Wres(w_next, nch_next)
                fu_pend = []
                for i in range(22):
                    if i + 1 < 22:
                        pend.append((wload(wr, w2d, (i + 1) * 128), wload(wr, w2d, DFF + (i + 1) * 128)))
                    (wg, B_wg), (wv, B_wv) = pend.pop(0)
                    Xg, B_xg = xg.next()
                    Xv, B_xv = xv.next()
                    Gt, B_g = gr.next()
                    cg, cv = i, 22 + i
                    for tt in range(NT):
                        sl = slice(tt * 512, (tt + 1) * 512)
                        psg, B_pg = pr.next()
                        psv, B_pv = pr.next()
                        fm_matmuls(wg, 0, tt, psg, B_pg, B_wg)
                        fm_matmuls(wv, 0, tt, psv, B_pv, B_wv)
                        Yg, B_yg = yr.next()
                        Yv, B_yv = yr.next()
                        S.op("act", lambda: A.activation(out=Yg[:], in_=psg[:], func=AF.Identity,
                                                         bias=fcb[:, l, cg:cg + 1], scale=fcw[:, l, 2, cg:cg + 1]),
                             reads=[B_pg, B_c], writes=[B_yg])
                        S.op("act", lambda: A.copy(out=Xg[:, 2 + tt * 512:2 + (tt + 1) * 512], in_=psg[:]),
                             reads=[B_pg], writes=[B_xg])
                        S.op("act", lambda: A.activation(out=Yv[:], in_=psv[:], func=AF.Identity,
                                                         bias=fcb[:, l, cv:cv + 1], scale=fcw[:, l, 2, cv:cv + 1]),
                             reads=[B_pv, B_c], writes=[B_yv])
                        S.op("dve", lambda: V.tensor_copy(out=Xv[:, 2 + tt * 512:2 + (tt + 1) * 512], in_=psv[:]),
                             reads=[B_pv], writes=[B_xv])
                        for (e, X_, B_x, Y_, B_y, c_) in (("dve", Xg, B_xg, Yg, B_yg, cg), ("dve", Xv, B_xv, Yv, B_yv, cv)):
                            for k in (1, 0):
                                en = "dve"
                                E_ = G if en == "pool" else V
                                S.op(en, lambda E_=E_, X_=X_, Y_=Y_, k=k, c_=c_: E_.scalar_tensor_tensor(
                                    out=Y_[:], in0=X_[:, tt * 512 + k:tt * 512 + k + 512],
                                    scalar=fcw[:, l, k, c_:c_ + 1], in1=Y_[:], op0=ALU.mult, op1=ALU.add),
                                    reads=[B_x, B_c, B_y], writes=[B_y])
                        def fin(Yg=Yg, B_yg=B_yg, Yv=Yv, B_yv=B_yv, Gt=Gt, B_g=B_g, sl=sl):
                            St, B_s = sr.next()
                            S.op("act", lambda: A.activation(out=St[:], in_=Yg[:], func=AF.Silu),
                                 reads=[B_yg], writes=[B_s])
                            S.op("pool", lambda: G.tensor_tensor(out=Gt[:, sl], in0=St[:], in1=Yv[:], op=ALU.mult),
                                 reads=[B_s, B_yv], writes=[B_g])
                        fu_pend.append(fin)
                        if len(fu_pend) > 1:
                            fu_pend.pop(0)()
                    while fu_pend:
                        fu_pend.pop(0)()
                    S.dma("sp", ACT_S[i], Gt[:], reads=[B_g])
                S.barrier()

        def phase_shortconv(w_next, nch_next):
            w2d = sc_w_in[0]
            with ExitStack() as ph:
                wr = Ring(nc, ph, "sc_w", 6, [128, 8, 128], BF16)
                pr = PRing(nc, ph, "sc_ps", 6, [128, 512])
                mr = Ring(nc, ph, "sc_m", 2, [128, L + 2], F32)
                ur = Ring(nc, ph, "sc_u", 3, [128, 512], F32)
                yr = Ring(nc, ph, "sc_y", 3, [128, 512], F32)
                rr = Ring(nc, ph, "sc_r", 2, [128, L], BF16)
                for t_, b_ in zip(mr.t, mr.b):
                    S.op("pool", lambda t_=t_: G.memset(t_[:, 0:2], 0.0), writes=[b_])
                load_Wres(w_next, nch_next)
                for i in range(8):
                    wb_, wc_, wu_ = (wload(wr, w2d, i * 128), wload(wr, w2d, 1024 + i * 128),
                                     wload(wr, w2d, 2048 + i * 128))
                    M, B_m = mr.next()
                    R, B_r = rr.next()
                    for tt in range(NT):
                        sl = slice(tt * 512, (tt + 1) * 512)
                        psb, B_pb = pr.next()
                        psc, B_pc = pr.next()
                        psu, B_pu = pr.next()
                        fm_matmuls(wc_[0], 0, tt, psc, B_pc, wc_[1])
                        fm_matmuls(wu_[0], 0, tt, psu, B_pu, wu_[1])
                        fm_matmuls(wb_[0], 0, tt, psb, B_pb, wb_[1])
                        U, B_u = ur.next()
                        S.op("act", lambda: A.copy(out=U[:], in_=psu[:]), reads=[B_pu], writes=[B_u])
                        S.op("dve", lambda: V.tensor_tensor(out=M[:, 2 + tt * 512:2 + (tt + 1) * 512], in0=psc[:],
                                                            in1=U[:], op=ALU.mult),
                             reads=[B_pc, B_u], writes=[B_m])
                        Y, B_y = yr.next()
                        S.op("act", lambda: A.activation(out=Y[:], in_=M[:, 2 + tt * 512:2 + (tt + 1) * 512],
                                                         func=AF.Copy, scale=scw[:, 2, i:i + 1]),
                             reads=[B_m, B_c], writes=[B_y])
                        for k in (1, 0):
                            S.op("dve", lambda k=k: V.scalar_tensor_tensor(
                                out=Y[:], in0=M[:, tt * 512 + k:tt * 512 + k + 512], scalar=scw[:, k, i:i + 1],
                                in1=Y[:], op0=ALU.mult, op1=ALU.add), reads=[B_m, B_c, B_y], writes=[B_y])
                        S.op("dve", lambda: V.tensor_tensor(out=R[:, sl], in0=psb[:], in1=Y[:], op=ALU.mult),
                             reads=[B_pb, B_y], writes=[B_r])
                    S.dma("sp", ACT_S[i], R[:], reads=[B_r])
                S.barrier()

        def phase_hybrid(h_in_ap, h_out_ap, widx_next):
            hyb = {}
            w2d = hy_w_in[0]
            hst = ExitStack()
            cst = ExitStack()
            Bh = Buf("hyb_consts")
            blk = sbt(hst, "blk", [128, 128], BF16)
            prot = sbt(hst, "prot", [128, 128], BF16)
            hcw = sbt(hst, "hcw", [128, 4, 12])
            hcb = sbt(hst, "hcb", [128, 12])
            qkw = sbt(hst, "qkw", [128, 2])
            dt_tm = sbt(hst, "dt_tm", [128, 32, 16])
            a_tm = sbt(hst, "a_tm", [128, 32, 16])
            acs = sbt(hst, "acs", [128, 32, 16])
            eA = sbt(hst, "eA", [128, 32, 16])
            dA = sbt(hst, "dA", [128, 32, 16])
            cdr = sbt(hst, "cdr", [128, 32, 16])
            bc16 = sbt(hst, "bc16", [128, 3, 16])
            ssdw = sbt(hst, "ssdw", [128, 1024])
            sublw = sbt(hst, "sublw", [128, 128])
            nlam = sbt(hst, "nlam", [128, 2])
            B_dt = Buf("dtstuff")
            cosT = sbt(cst, "cosT", [128, L])
            sinT = sbt(cst, "sinT", [128, L])

            with ExitStack() as ph:
                tf = sbt(ph, "c_tf", [128, 128])
                tf2 = sbt(ph, "c_tf2", [128, 128])
                B_t = Buf()
                S.op("pool", lambda: G.memset(tf[:], 0.0), writes=[B_t])
                S.op("pool", lambda: G.memset(tf[0:64, 0:64], 1.0), writes=[B_t])
                S.op("pool", lambda: G.memset(tf[64:128, 64:128], 1.0), writes=[B_t])
                S.op("dve", lambda: V.tensor_copy(out=blk[:], in_=tf[:]), reads=[B_t], writes=[Bh])
                S.op("pool", lambda: G.memset(tf[:], 1.0), reads=[Bh], writes=[B_t])
                S.op("pool", lambda: G.affine_select(out=tf[:], in_=tf[:], pattern=[[-1, 128]], compare_op=ALU.is_equal,
                                                     fill=0.0, base=32, channel_multiplier=1), reads=[B_t], writes=[B_t])
                S.op("pool", lambda: G.memset(tf2[:], -1.0), writes=[B_t])
                S.op("pool", lambda: G.affine_select(out=tf2[:], in_=tf2[:], pattern=[[-1, 128]], compare_op=ALU.is_equal,
                                                     fill=0.0, base=-32, channel_multiplier=1), reads=[B_t], writes=[B_t])
                for c0 in (0, 64):
                    S.op("pool", lambda c0=c0: G.memset(tf[:, c0:c0 + 32], 0.0), reads=[B_t], writes=[B_t])
                    S.op("pool", lambda c0=c0: G.memset(tf2[:, c0 + 32:c0 + 64], 0.0), reads=[B_t], writes=[B_t])
                S.op("dve", lambda: V.tensor_tensor(out=tf[:], in0=tf[:], in1=tf2[:], op=ALU.add), reads=[B_t], writes=[B_t])
                S.op("dve", lambda: V.tensor_copy(out=prot[:], in_=tf[:]), reads=[B_t], writes=[Bh])
                pi_ = sbt(ph, "c_pi", [128, 1], I32)
                pf_ = sbt(ph, "c_pf", [128, 2])
                S.op("pool", lambda: G.iota(pi_[:], pattern=[[0, 1]], base=0, channel_multiplier=1), writes=[B_t])
                S.op("dve", lambda: V.tensor_single_scalar(out=pi_[:], in_=pi_[:], scalar=31, op=ALU.bitwise_and),
                     reads=[B_t], writes=[B_t])
                S.op("dve", lambda: V.tensor_copy(out=pf_[:, 0:1], in_=pi_[:]), reads=[B_t], writes=[B_t])
                S.op("dve", lambda: V.tensor_scalar(out=pf_[:, 0:1], in0=pf_[:, 0:1], scalar1=-math.log(10000.0) / 32.0,
                                                    scalar2=-math.log(2 * math.pi), op0=ALU.mult, op1=ALU.add),
                     reads=[B_t], writes=[B_t])
                S.op("act", lambda: A.activation(out=pf_[:, 1:2], in_=pf_[:, 0:1], func=AF.Exp), reads=[B_t], writes=[B_t])
                ti = sbt(ph, "c_ti", [128, L], I32)
                r_ = sbt(ph, "c_r", [128, L])
                kf = sbt(ph, "c_kf", [128, L])
                S.op("pool", lambda: G.iota(ti[:], pattern=[[1, L]], base=0, channel_multiplier=0), writes=[B_t])
                S.op("dve", lambda: V.tensor_copy(out=r_[:], in_=ti[:]), reads=[B_t], writes=[B_t])
                S.op("dve", lambda: V.tensor_scalar(out=r_[:], in0=r_[:], scalar1=pf_[:, 1:2], scalar2=None, op0=ALU.mult),
                     reads=[B_t], writes=[B_t])
                for (dst, off) in ((sinT, 0.0), (cosT, 0.25)):
                    S.op("dve", lambda off=off: V.tensor_scalar(out=kf[:], in0=r_[:], scalar1=off, scalar2=None, op0=ALU.add),
                         reads=[B_t, Bh], writes=[B_t])
                    S.op("dve", lambda: V.tensor_copy(out=ti[:], in_=kf[:]), reads=[B_t], writes=[B_t])
                    S.op("dve", lambda dst=dst: V.tensor_copy(out=dst[:], in_=ti[:]), reads=[B_t], writes=[Bh])
                    S.op("dve", lambda dst=dst: V.tensor_tensor(out=kf[:], in0=kf[:], in1=dst[:], op=ALU.subtract),
                         reads=[B_t, Bh], writes=[B_t])
                    S.op("dve", lambda dst=dst: V.tensor_single_scalar(out=dst[:], in_=kf[:], scalar=0.5, op=ALU.is_gt),
                         reads=[B_t], writes=[Bh])
                    S.op("dve", lambda dst=dst: V.tensor_tensor(out=kf[:], in0=kf[:], in1=dst[:], op=ALU.subtract),
                         reads=[B_t, Bh], writes=[B_t])
                    S.op("act", lambda dst=dst: A.activation(out=dst[:], in_=kf[:], func=AF.Sin, scale=2 * math.pi),
                         reads=[B_t], writes=[Bh])
                stg_r = Ring(nc, ph, "hstg", 3, [64, 128], F32)
                ps_r = PRing(nc, ph, "hpsc", 2, [128, 512])

                def lc(dst, src_row, nch):
                    stg, B_s = stg_r.next()
                    S.dma("sp", stg[0:nch, :], src_row.rearrange("(c p) -> c p", p=128), writes=[B_s])
                    ps, B_p = ps_r.next()
                    S.op("pe", lambda: P.matmul(ps[:, 0:nch], lhsT=stg[0:nch, :], rhs=identf[0:nch, 0:nch],
                                                start=True, stop=True), reads=[B_s, B_c], writes=[B_p])
                    S.op("dve", lambda: V.tensor_copy(out=dst, in_=ps[:, 0:nch]), reads=[B_p], writes=[Bh])
                for k in range(4):
                    lc(hcw[:, k, :], hy_conv_w[0, k, :], 12)
                lc(hcb[:, :], hy_conv_b[0, :], 12)
                for j, src in enumerate((hy_q_norm, hy_k_norm)):
                    for hf in range(2):
                        S.dma("sp", qkw[hf * 64:(hf + 1) * 64, j:j + 1], src[0, :].rearrange("(p o) -> p o", o=1),
                              writes=[Bh])
                S.op("dve", lambda: V.tensor_scalar(out=qkw[:, 0:1], in0=qkw[:, 0:1], scalar1=0.125, scalar2=None,
                                                    op0=ALU.mult), reads=[Bh], writes=[Bh])
                for j, src in enumerate((hy_dt_bias, hy_a_log, hy_d_skip)):
                    S.dma("sp", bc16[:, j, :], src[0:1, :].partition_broadcast(128), writes=[Bh])
                S.op("act", lambda: A.activation(out=bc16[:, 1, :], in_=bc16[:, 1, :], func=AF.Exp), reads=[Bh], writes=[Bh])
                S.op("dve", lambda: V.tensor_scalar(out=bc16[:, 1, :], in0=bc16[:, 1, :], scalar1=-1.0, scalar2=None,
                                                    op0=ALU.mult), reads=[Bh], writes=[Bh])
                S.dma("sp", ssdw[:], hy_ssd_norm[0:1, :].partition_broadcast(128), writes=[Bh])
                S.dma("sp", sublw[:], hy_subln[0:1, :].partition_broadcast(128), writes=[Bh])
                lam_init = 0.8 - 0.6 * math.exp(-0.3 * 0)
                S.op("dve", lambda: V.tensor_scalar(out=sublw[:], in0=sublw[:], scalar1=(1.0 - lam_init), scalar2=None,
                                                    op0=ALU.mult), reads=[Bh], writes=[Bh])
                lt = sbt(ph, "c_lt", [128, 4, 64])
                ls = sbt(ph, "c_ls", [128, 4])
                for j, src in enumerate((hy_lq1, hy_lk1, hy_lq2, hy_lk2)):
                    S.dma("sp", lt[:, j, :], src[0:1, :].partition_broadcast(128), writes=[B_t])
                for j in range(2):
                    S.op("dve", lambda j=j: V.tensor_tensor(out=lt[:, 2 * j, :], in0=lt[:, 2 * j, :], in1=lt[:, 2 * j + 1, :],
                                                            op=ALU.mult), reads=[B_t], writes=[B_t])
                    S.op("act", lambda j=j: A.activation(out=lt[:, 2 * j + 1, :], in_=lt[:, 2 * j, :], func=AF.Identity,
                                                         accum_out=ls[:, j:j + 1]), reads=[B_t], writes=[B_t])
                S.op("act", lambda: A.activation(out=ls[:, 2:4], in_=ls[:, 0:2], func=AF.Exp), reads=[B_t], writes=[B_t])
                S.op("dve", lambda: V.tensor_tensor(out=nlam[:, 0:1], in0=ls[:, 3:4], in1=ls[:, 2:3], op=ALU.subtract),
                     reads=[B_t], writes=[Bh])
                S.op("dve", lambda: V.tensor_scalar(out=nlam[:, 0:1], in0=nlam[:, 0:1], scalar1=-lam_init, scalar2=None,
                                                    op0=ALU.add), reads=[Bh], writes=[Bh])
                S.barrier()
            if hstop == 1:
                cst.close(); hst.close()
                return True

            with ExitStack() as ph:
                wr = Ring(nc, ph, "hb_w", 4, [128, 8, 128], BF16)
                pr = PRing(nc, ph, "hb_ps", 4, [128, 512])
                pr2 = PRing(nc, ph, "hb_ps2", 4, [128, 512])
                xr = Ring(nc, ph, "hb_x", 2, [128, L + 3], BF16)
                yr = Ring(nc, ph, "hb_y", 4, [128, 512], F32)
                orr = Ring(nc, ph, "hb_o", 2, [128, L], BF16)
                sq = Ring(nc, ph, "hb_sq", 3, [128, 512], BF16)
                rs = Ring(nc, ph, "hb_rs", 3, [128, 512], F32)
                qn = Ring(nc, ph, "hb_qn", 3, [128, 512], BF16)
                t1r = Ring(nc, ph, "hb_t1", 3, [128, 512], F32)
                t2r = Ring(nc, ph, "hb_t2", 3, [128, 512], F32)
                for t_, b_ in zip(xr.t, xr.b):
                    S.op("pool", lambda t_=t_: G.memset(t_[:, 0:3], 0.0), writes=[b_])
                jobs = [("xbc", i, C_XBC + i * 128) for i in range(12)] + \
                       [("q", i, C_Q + i * 128) for i in range(8)] + [("k", i, C_K + i * 128) for i in range(8)]
                pend = [wload(wr, w2d, jobs[0][2])]
                units = [(ji, tt) for ji in range(len(jobs)) for tt in range(NT)]
                js = {}
                ust = {}

                def stage1(u):
                    ji, tt = units[u]
                    if tt == 0:
                        if ji + 1 < len(jobs):
                            pend.append(wload(wr, w2d, jobs[ji + 1][2]))
                        js[ji] = {"w": pend.pop(0)}
                    wt, B_w = js[ji]["w"]
                    ps, B_ps = pr.next()
                    fm_matmuls(wt, 0, tt, ps, B_ps, B_w)
                    ust[u] = {"ps": (ps, B_ps)}

                def stage2(u):
                    ji, tt = units[u]
                    kind, i, col = jobs[ji]
                    sl = slice(tt * 512, (tt + 1) * 512)
                    ps, B_ps = ust[u]["ps"]
                    if tt == 0:
                        js[ji]["O"] = orr.next()
                        if kind == "xbc":
                            js[ji]["X"] = xr.next()
                    O, B_o = js[ji]["O"]
                    if kind == "xbc":
                        X, B_x = js[ji]["X"]
                        Y, B_y = yr.next()
                        S.op("act", lambda: A.activation(out=Y[:], in_=ps[:], func=AF.Identity,
                                                         bias=hcb[:, i:i + 1], scale=hcw[:, 3, i:i + 1]),
                             reads=[B_ps, Bh], writes=[B_y])
                        S.op("act", lambda: A.copy(out=X[:, 3 + tt * 512:3 + (tt + 1) * 512], in_=ps[:]),
                             reads=[B_ps], writes=[B_x])
                        for k in (2, 1, 0):
                            S.op("dve", lambda k=k: V.scalar_tensor_tensor(
                                out=Y[:], in0=X[:, tt * 512 + k:tt * 512 + k + 512], scalar=hcw[:, k, i:i + 1],
                                in1=Y[:], op0=ALU.mult, op1=ALU.add), reads=[B_x, Bh, B_y], writes=[B_y])
                        ust[u]["Y"] = (Y, B_y)
                    else:
                        wcol = qkw[:, 0:1] if kind == "q" else qkw[:, 1:2]
                        s_, B_s = sq.next()
                        S.op("act", lambda: A.activation(out=s_[:], in_=ps[:], func=AF.Square),
                             reads=[B_ps], writes=[B_s])
                        ps2, B_p2 = pr2.next()
                        S.op("pe", lambda: P.matmul(ps2[:], lhsT=blk[:], rhs=s_[:], start=True, stop=True),
                             reads=[Bh, B_s], writes=[B_p2])
                        r1, B_r1 = rs.next()
                        S.op("act", lambda: A.activation(out=r1[:], in_=ps2[:], func=AF.Ln, bias=epsc[:, 0:1],
                                                         scale=1.0 / 64.0), reads=[B_p2, B_c], writes=[B_r1])
                        S.op("act", lambda: A.activation(out=r1[:], in_=r1[:], func=AF.Exp, scale=-0.5),
                             reads=[B_r1], writes=[B_r1])
                        q_, B_q = qn.next()
                        S.op("dve", lambda: V.scalar_tensor_tensor(out=q_[:], in0=ps[:], scalar=wcol, in1=r1[:],
                                                                   op0=ALU.mult, op1=ALU.mult),
                             reads=[B_ps, Bh, B_r1], writes=[B_q])
                        ust[u]["q"] = (q_, B_q)

                def stage3(u):
                    ji, tt = units[u]
                    kind, i, col = jobs[ji]
                    sl = slice(tt * 512, (tt + 1) * 512)
                    O, B_o = js[ji]["O"]
                    if kind == "xbc":
                        Y, B_y = ust[u]["Y"]
                        S.op("act", lambda: A.activation(out=O[:, sl], in_=Y[:], func=AF.Silu),
                             reads=[B_y], writes=[B_o])
                    if kind != "xbc":
                        q_, B_q = ust[u]["q"]
                        ps3, B_p3 = pr2.next()
                        S.op("pe", lambda: P.matmul(ps3[:], lhsT=prot[:], rhs=q_[:], start=True, stop=True),
                             reads=[Bh, B_q], writes=[B_p3])
                        t1, B_t1 = t1r.next()
                        S.op("pool", lambda: G.tensor_tensor(out=t1[:], in0=q_[:], in1=cosT[:, sl], op=ALU.mult),
                             reads=[B_q, Bh], writes=[B_t1])
                        t2, B_t2 = t2r.next()
                        S.op("dve", lambda: V.tensor_tensor(out=t2[:], in0=ps3[:], in1=sinT[:, sl], op=ALU.mult),
                             reads=[B_p3, Bh], writes=[B_t2])
                        S.op("dve", lambda: V.tensor_tensor(out=O[:, sl], in0=t1[:], in1=t2[:], op=ALU.add),
                             reads=[B_t1, B_t2], writes=[B_o])
                    if tt == NT - 1:
                        dst = {"xbc": XBC_S, "q": QT_S, "k": KT_S}[kind]
                        S.dma("sp", dst[i], O[:], reads=[B_o])
                    del ust[u]

                NU = len(units)
                for st_ in range(NU + 2):
                    if st_ < NU:
                        stage1(st_)
                    if 0 <= st_ - 1 < NU:
                        stage2(st_ - 1)
                    if 0 <= st_ - 2 < NU:
                        stage3(st_ - 2)
                S.barrier()

            cst.close()
            if hstop == 2:
                hst.close()
                return True
            with ExitStack() as ph:
                wz = sbt(ph, "wz", [128, 8, 1024], BF16)
                wv = sbt(ph, "wv", [128, 8, 1024], BF16)
                wd = sbt(ph, "wd", [128, 8, 16], BF16)
                B_w = Buf()
                wsrc = w2d.rearrange("(k p) n -> p k n", p=128)
                for k0 in (0, 4):
                    S.dma("pool", wz[:, k0:k0 + 4, :], wsrc[:, k0:k0 + 4, C_Z:C_Z + 1024], writes=[B_w])
                    S.dma("pool", wv[:, k0:k0 + 4, :], wsrc[:, k0:k0 + 4, C_V:C_V + 1024], writes=[B_w])
                S.dma("pool", wd[:], wsrc[:, :, C_DT:C_DT + 16], writes=[B_w])
                pr = PRing(nc, ph, "tz_ps", 6, [128, 512])
                pdt = pst(ph, "tz_pdt", [128, 32, 16]); B_pdt = Buf(excl=True)
                zr = Ring(nc, ph, "tz_z", 3, [128, 1024], BF16)
                vr = Ring(nc, ph, "tz_v", 3, [128, 1024], BF16)
                for t in range(32):
                    pz = [pr.next(), pr.next()]
                    pv = [pr.next(), pr.next()]
                    for k in range(8):
                        lhs = hnT[:, k, t * 128:(t + 1) * 128]
                        for hf in range(2):
                            S.op("pe", lambda k=k, hf=hf, lhs=lhs: P.matmul(pz[hf][0][:], lhsT=lhs, rhs=wz[:, k, hf * 512:(hf + 1) * 512],
                                                                          start=(k == 0), stop=(k == 7)),
                                 reads=[B_hn[t], B_w], writes=[pz[hf][1]], sig=(k == 7))
                            S.op("pe", lambda k=k, hf=hf, lhs=lhs: P.matmul(pv[hf][0][:], lhsT=lhs, rhs=wv[:, k, hf * 512:(hf + 1) * 512],
                                                                          start=(k == 0), stop=(k == 7)),
                                 reads=[B_hn[t], B_w], writes=[pv[hf][1]], sig=(k == 7))
                        S.op("pe", lambda k=k, lhs=lhs: P.matmul(pdt[:, t, :], lhsT=lhs, rhs=wd[:, k, :],
                                                                 start=(k == 0), stop=(k == 7)),
                             reads=[B_hn[t], B_w], writes=[B_pdt], sig=(k == 7))
                    Z, B_z = zr.next()
                    Vt, B_v = vr.next()
                    for hf in range(2):
                        S.op("act", lambda hf=hf: A.activation(out=Z[:, hf * 512:(hf + 1) * 512], in_=pz[hf][0][:], func=AF.Silu),
                             reads=[pz[hf][1]], writes=[B_z])
                        S.op("dve", lambda hf=hf: V.tensor_copy(out=Vt[:, hf * 512:(hf + 1) * 512], in_=pv[hf][0][:]),
                             reads=[pv[hf][1]], writes=[B_v])
                    S.dma("sp", ZS_S[t], Z[:], reads=[B_z])
                    S.dma("sp", V_S[t], Vt[:], reads=[B_v])
                xb_ = sbt(ph, "dt_x", [128, 32, 16])
                ab_ = sbt(ph, "dt_a", [128, 32, 16])
                S.op("dve", lambda: V.tensor_tensor(out=xb_[:], in0=pdt[:], in1=bc(bc16[:, 0, :].unsqueeze(1), [128, 32, 16]),
                                                    op=ALU.add), reads=[B_pdt, Bh], writes=[B_dt])
                S.op("act", lambda: A.activation(out=ab_[:], in_=xb_[:], func=AF.Abs), reads=[B_dt], writes=[B_dt])
                S.op("act", lambda: A.activation(out=ab_[:], in_=ab_[:], func=AF.Exp, scale=-1.0), reads=[B_dt], writes=[B_dt])
                S.op("dve", lambda: V.tensor_scalar(out=ab_[:], in0=ab_[:], scalar1=1.0, scalar2=None, op0=ALU.add),
                     reads=[B_dt], writes=[B_dt])
                S.op("act", lambda: A.activation(out=ab_[:], in_=ab_[:], func=AF.Ln), reads=[B_dt], writes=[B_dt])
                S.op("dve", lambda: V.scalar_tensor_tensor(out=dt_tm[:], in0=xb_[:], scalar=0.0, in1=ab_[:], op0=ALU.max,
                                                           op1=ALU.add), reads=[B_dt], writes=[B_dt])
                S.op("dve", lambda: V.tensor_tensor(out=a_tm[:], in0=dt_tm[:], in1=bc(bc16[:, 1, :].unsqueeze(1), [128, 32, 16]),
                                                    op=ALU.mult), reads=[B_dt, Bh], writes=[B_dt])
                pa, B_pa = pr.next()
                pl, B_pl = pr.next()
                S.op("pe", lambda: P.matmul(pa[:], lhsT=triT_f[:], rhs=a_tm[:].rearrange("p c h -> p (c h)"),
                                            start=True, stop=True), reads=[B_c, B_dt], writes=[B_pa])
                S.op("pe", lambda: P.matmul(pl[:], lhsT=ones_f[:], rhs=a_tm[:].rearrange("p c h -> p (c h)"),
                                            start=True, stop=True), reads=[B_c, B_dt], writes=[B_pl])
                fl = lambda t_: t_[:].rearrange("p c h -> p (c h)")
                S.op("dve", lambda: V.tensor_copy(out=fl(acs), in_=pa[:]), reads=[B_pa], writes=[B_dt])
                S.op("act", lambda: A.activation(out=fl(eA), in_=pa[:], func=AF.Exp), reads=[B_pa], writes=[B_dt])
                S.op("act", lambda: A.activation(out=fl(cdr), in_=pl[:], func=AF.Exp), reads=[B_pl], writes=[B_dt])
                S.op("dve", lambda: V.tensor_tensor(out=fl(dA), in0=pl[:], in1=fl(acs), op=ALU.subtract),
                     reads=[B_pl, B_dt], writes=[B_dt])
                S.op("act", lambda: A.activation(out=fl(dA), in_=fl(dA), func=AF.Exp), reads=[B_dt], writes=[B_dt])
                S.barrier()
            if hstop == 3:
                hst.close()
                return True

            with ExitStack() as ph:
                xin = Ring(nc, ph, "sd_x", 3, [128, 12, 512], BF16)
                zin = Ring(nc, ph, "sd_z", 3, [128, 1024], BF16)
                ptr = PRing(nc, ph, "sd_ptr", 2, [128, 1024], BF16)
                pcb = pst(ph, "sd_pcb", [128, 4, 128]); B_pcb = Buf(excl=True)
                pR = PRing(nc, ph, "sd_pR", 1, [128, 8, 128])
                pya = pst(ph, "sd_pya", [128, 512]); B_pya = Buf(excl=True)
                pyb = pst(ph, "sd_pyb", [128, 512]); B_pyb = Buf(excl=True)
                pst_ = pst(ph, "sd_pst", [128, 512]); B_pst = Buf(excl=True)
                xs_tm = Ring(nc, ph, "sd_xs", 2, [128, 1024], BF16)
                b_tm = Ring(nc, ph, "sd_b", 2, [128, 256], BF16)
                Xr = Ring(nc, ph, "sd_X", 2, [128, 1024], BF16)
                Xdr = Ring(nc, ph, "sd_Xd", 2, [128, 1024], BF16)
                cbm = Ring(nc, ph, "sd_cbm", 2, [128, 2, 128], BF16)
                rhsR = Ring(nc, ph, "sd_rr", 1, [128, 8, 128], F32)
                segr = Ring(nc, ph, "sd_seg", 1, [128, 8, 128], F32)
                Er = Ring(nc, ph, "sd_E", 2, [128, 8, 128], BF16)
                Wr = Ring(nc, ph, "sd_W", 2, [128, 8, 128], BF16)
                yr = Ring(nc, ph, "sd_y", 2, [128, 512], F32)
                y2r = Ring(nc, ph, "sd_y2", 1, [128, 512], F32)
                ynr = Ring(nc, ph, "sd_yn", 2, [128, 512], BF16)
                smr = Ring(nc, ph, "sd_sm", 3, [128, 2], F32)
                junk = sbt(ph, "sd_junk", [128, 512], BF16); B_junk = Buf()
                prev = sbt(ph, "sd_prev", [128, 1024]); B_prev = Buf()
                prevb = sbt(ph, "sd_prevb", [128, 1024], BF16); B_prevb = Buf()
                yT = Ring(nc, ph, "sd_yT", 2, [128, 8, 512], BF16)
                S.op("pool", lambda: G.memset(prev[:], 0.0), writes=[B_prev])
                S.op("pool", lambda: G.memset(prevb[:], 0.0), writes=[B_prevb])
                xsrc = XBC_S.rearrange("c p t -> p c t")
                chs = {}
                cur = {}

                def load_x(c):
                    Xin, B_xin = xin.next()
                    S.dma("sp", Xin[:, 0:6, :], xsrc[:, 0:6, c * 128:c * 128 + 512], writes=[B_xin])
                    S.dma("sp", Xin[:, 6:12, :], xsrc[:, 6:12, c * 128:c * 128 + 512], writes=[B_xin])
                    return Xin, B_xin

                def load_z(c):
                    Zt, B_z = zin.next()
                    S.dma("sp", Zt[:], ZS_S[c], writes=[B_z])
                    return Zt, B_z

                def prep(c):
                    cc = c % 4
                    if cc == 0:
                        if c == 0:
                            cur["Xn"] = load_x(0)
                        cur["Xin"] = cur["Xn"]
                        if c + 4 < 32:
                            cur["Xn"] = load_x(c + 4)
                        cur["YT"] = yT.next()
                    Xin, B_xin = cur["Xin"]
                    csl = slice(cc * 128, (cc + 1) * 128)
                    if c == 0:
                        cur["Zn"] = load_z(0)
                    Zt, B_z = cur["Zn"]
                    if c + 1 < 32:
                        cur["Zn"] = load_z(c + 1)
                    pt, B_pt = ptr.next()
                    for j in range(8):
                        S.op("pe", lambda j=j: P.transpose(out=pt[:, j * 128:(j + 1) * 128], in_=Xin[:, j, csl], identity=identb[:]),
                             reads=[B_xin, B_c], writes=[B_pt], sig=(j == 7))
                    xs, B_xs = xs_tm.next()
                    S.op("act", lambda: A.copy(out=xs[:], in_=pt[:]), reads=[B_pt], writes=[B_xs])
                    pt2, B_pt2 = ptr.next()
                    for g in range(2):
                        S.op("pe", lambda g=g: P.transpose(out=pt2[:, g * 128:(g + 1) * 128], in_=Xin[:, 8 + g, csl], identity=identb[:]),
                             reads=[B_xin, B_c], writes=[B_pt2], sig=(g == 1))
                    bt, B_bt = b_tm.next()
                    S.op("act", lambda: A.copy(out=bt[:], in_=pt2[:, 0:256]), reads=[B_pt2], writes=[B_bt])
                    X, B_X = Xr.next()
                    S.op("dve", lambda: V.tensor_tensor(out=X[:].rearrange("p (h d) -> p h d", h=16),
                                                        in0=xs[:].rearrange("p (h d) -> p h d", h=16),
                                                        in1=bc(dt_tm[:, c, :].unsqueeze(2), [128, 16, 64]), op=ALU.mult),
                         reads=[B_xs, B_dt], writes=[B_X])
                    Xd, B_Xd = Xdr.next()
                    S.op("pool", lambda: G.tensor_tensor(out=Xd[:].rearrange("p (h d) -> p h d", h=16),
                                                         in0=X[:].rearrange("p (h d) -> p h d", h=16),
                                                         in1=bc(dA[:, c, :].unsqueeze(2), [128, 16, 64]), op=ALU.mult),
                         reads=[B_X, B_dt], writes=[B_Xd])
                    for g in range(2):
                        S.op("pe", lambda g=g: P.matmul(pcb[:, g, :], lhsT=Xin[:, 8 + g, csl], rhs=Xin[:, 10 + g, csl],
                                                        start=True, stop=True), reads=[B_xin], writes=[B_pcb], sig=(g == 1))
                    cb, B_cb = cbm.next()
                    S.op("dve", lambda: V.tensor_tensor(out=cb[:], in0=pcb[:, 0:2, :], in1=bc(triT_f[:].unsqueeze(1), [128, 2, 128]),
                                                        op=ALU.mult), reads=[B_pcb, B_c], writes=[B_cb])
                    chs[c] = dict(Xin=(Xin, B_xin), csl=csl, Z=(Zt, B_z), xs=(xs, B_xs), bt=(bt, B_bt), X=(X, B_X),
                                  Xd=(Xd, B_Xd), cb=(cb, B_cb), YT=cur["YT"], W={})

                def stage1a_pool(c, g):
                    hs = slice(g * 8, (g + 1) * 8)
                    rr_, B_rr = rhsR.next()
                    S.op("pool", lambda: G.tensor_tensor(out=rr_[:], in0=bc(triT_f[:].unsqueeze(1), [128, 8, 128]),
                                                         in1=bc(a_tm[:, c, hs].unsqueeze(2), [128, 8, 128]), op=ALU.mult),
                         reads=[B_c, B_dt], writes=[B_rr])
                    chs[c]["rr", g] = (rr_, B_rr)

                def stage1a_pe(c, g):
                    rr_, B_rr = chs[c]["rr", g]
                    pr_, B_pr = pR.next()
                    for q4 in range(2):
                        S.op("pe", lambda q4=q4: P.matmul(pr_[:, q4 * 4:(q4 + 1) * 4, :], lhsT=ones_f[:],
                                                          rhs=rr_[:, q4 * 4:(q4 + 1) * 4, :], start=True, stop=True),
                             reads=[B_c, B_rr], writes=[B_pr], sig=(q4 == 1))
                    chs[c]["pr", g] = (pr_, B_pr)

                def stage1b(c, g):
                    d = chs[c]
                    cb, B_cb = d["cb"]
                    hs = slice(g * 8, (g + 1) * 8)
                    pr_, B_pr = d["pr", g]
                    if False:
                        rr_, B_rr = rhsR.next()
                    sg, B_sg = segr.next()
                    S.op("dve", lambda: V.tensor_tensor(out=sg[:], in0=pr_[:], in1=bc(acs[:, c, hs].unsqueeze(2), [128, 8, 128]),
                                                        op=ALU.subtract), reads=[B_pr, B_dt], writes=[B_sg])
                    S.op("dve", lambda: V.tensor_single_scalar(out=sg[:], in_=sg[:], scalar=0.0, op=ALU.min),
                         reads=[B_sg], writes=[B_sg])
                    E, B_E = Er.next()
                    S.op("act", lambda: A.activation(out=E[:], in_=sg[:], func=AF.Exp), reads=[B_sg], writes=[B_E])
                    W, B_W = Wr.next()
                    S.op("dve", lambda: V.tensor_tensor(out=W[:], in0=E[:], in1=bc(cb[:, g, :].unsqueeze(1), [128, 8, 128]),
                                                        op=ALU.mult), reads=[B_E, B_cb], writes=[B_W])
                    d["W"][g] = (W, B_W)

                def stage2a(c, g):
                    d = chs[c]
                    Xin, B_xin = d["Xin"]; csl = d["csl"]; Zt, B_z = d["Z"]; xs, B_xs = d["xs"]; bt, B_bt = d["bt"]
                    X, B_X = d["X"]; Xd, B_Xd = d["Xd"]; YT, B_yT = d["YT"]; W, B_W = d["W"][g]
                    cc = c % 4
                    hs = slice(g * 8, (g + 1) * 8)
                    fs = slice(g * 512, (g + 1) * 512)
                    for h in range(8):
                        hh = g * 8 + h
                        S.op("pe", lambda h=h, hh=hh: P.matmul(pya[:, h * 64:(h + 1) * 64], lhsT=W[:, h, :],
                                                               rhs=X[:, hh * 64:(hh + 1) * 64], start=True, stop=True),
                             reads=[B_W, B_X], writes=[B_pya], sig=(h == 7))
                    S.op("pe", lambda: P.matmul(pyb[:], lhsT=Xin[:, 10 + g, csl], rhs=prevb[:, fs], start=True, stop=True),
                         reads=[B_xin, B_prevb], writes=[B_pyb])
                def stage2b(c, g):
                    d = chs[c]
                    Zt, B_z = d["Z"]; xs, B_xs = d["xs"]
                    hs = slice(g * 8, (g + 1) * 8)
                    fs = slice(g * 512, (g + 1) * 512)
                    y, B_y = yr.next()
                    S.op("dve", lambda: V.tensor_tensor(out=y[:].rearrange("p (h d) -> p h d", h=8),
                                                        in0=pyb[:].rearrange("p (h d) -> p h d", h=8),
                                                        in1=bc(eA[:, c, hs].unsqueeze(2), [128, 8, 64]), op=ALU.mult),
                         reads=[B_pyb, B_dt], writes=[B_y])
                    S.op("dve", lambda: V.tensor_tensor(out=y[:], in0=y[:], in1=pya[:], op=ALU.add),
                         reads=[B_y, B_pya], writes=[B_y])
                    y2, B_y2 = y2r.next()
                    S.op("pool", lambda: G.tensor_tensor(out=y2[:].rearrange("p (h d) -> p h d", h=8),
                                                         in0=xs[:, fs].rearrange("p (h d) -> p h d", h=8),
                                                         in1=bc(bc16[:, 2, hs].unsqueeze(2), [128, 8, 64]), op=ALU.mult),
                         reads=[B_xs, Bh], writes=[B_y2])
                    S.op("dve", lambda: V.tensor_tensor(out=y[:], in0=y[:], in1=y2[:], op=ALU.add),
                         reads=[B_y, B_y2], writes=[B_y])
                    S.op("dve", lambda: V.tensor_tensor(out=y[:], in0=y[:], in1=Zt[:, fs], op=ALU.mult),
                         reads=[B_y, B_z], writes=[B_y])
                    sm, B_sm = smr.next()
                    S.op("act", lambda: A.activation(out=junk[:], in_=y[:], func=AF.Square, accum_out=sm[:, 0:1]),
                         reads=[B_y], writes=[B_junk, B_sm])
                    rstd(sm[:, 1:2], sm[:, 0:1], 1.0 / 512.0, B_sm)
                    yn, B_yn = ynr.next()
                    S.op("dve", lambda: V.scalar_tensor_tensor(out=yn[:], in0=y[:], scalar=sm[:, 1:2], in1=ssdw[:, fs],
                                                               op0=ALU.mult, op1=ALU.mult),
                         reads=[B_y, B_sm, Bh], writes=[B_yn])
                    d["yn", g] = (yn, B_yn)

                def stage2c(c, g):
                    d = chs[c]
                    csl = d["csl"]; bt, B_bt = d["bt"]; Xd, B_Xd = d["Xd"]; YT, B_yT = d["YT"]
                    yn, B_yn = d["yn", g]
                    cc = c % 4
                    hs = slice(g * 8, (g + 1) * 8)
                    fs = slice(g * 512, (g + 1) * 512)
                    pto, B_pto = ptr.next()
                    for j in range(4):
                        S.op("pe", lambda j=j: P.transpose(out=pto[:, j * 128:(j + 1) * 128], in_=yn[:, j * 128:(j + 1) * 128],
                                                           identity=identb[:]), reads=[B_yn, B_c], writes=[B_pto], sig=(j == 3))
                    S.op("act", lambda: A.copy(out=YT[:, g * 4:(g + 1) * 4, csl],
                                               in_=pto[:, 0:512].rearrange("p (j t) -> p j t", j=4)),
                         reads=[B_pto], writes=[B_yT])
                    S.op("pe", lambda: P.matmul(pst_[:], lhsT=bt[:, g * 128:(g + 1) * 128], rhs=Xd[:, fs], start=True, stop=True),
                         reads=[B_bt, B_Xd], writes=[B_pst])
                    S.op("dve", lambda: V.tensor_tensor(out=prev[:, fs].rearrange("p (h d) -> p h d", h=8),
                                                        in0=prev[:, fs].rearrange("p (h d) -> p h d", h=8),
                                                        in1=bc(cdr[:, c, hs].unsqueeze(2), [128, 8, 64]), op=ALU.mult),
                         reads=[B_prev, B_dt], writes=[B_prev])
                    S.op("dve", lambda: V.tensor_tensor(out=prev[:, fs], in0=prev[:, fs], in1=pst_[:], op=ALU.add),
                         reads=[B_prev, B_pst], writes=[B_prev])
                    S.op("act", lambda: A.copy(out=prevb[:, fs], in_=prev[:, fs]), reads=[B_prev], writes=[B_prevb])
                    if cc == 3 and g == 1:
                        c0 = (c - 3) * 128
                        S.dma("sp", ACT_S.rearrange("c p t -> p c t")[:, 0:8, c0:c0 + 512], YT[:], reads=[B_yT])
                    if g == 1:
                        del chs[c]

                sunits = [(c, g) for c in range(32) for g in range(2)]
                NSU = len(sunits)
                for st_ in range(NSU + 1):
                    cur_u = sunits[st_] if st_ < NSU else None
                    prv_u = sunits[st_ - 1] if st_ >= 1 else None
                    if cur_u is not None:
                        if cur_u[1] == 0:
                            prep(cur_u[0])
                        stage1a_pool(*cur_u)
                    if prv_u is not None:
                        stage2a(*prv_u)
                    if cur_u is not None:
                        stage1a_pe(*cur_u)
                    if prv_u is not None:
                        stage2b(*prv_u)
                    if cur_u is not None:
                        stage1b(*cur_u)
                    if prv_u is not None:
                        stage2c(*prv_u)
                S.barrier()
            if hstop == 4:
                hst.close()
                return True

            wst = ExitStack()
            open_wres(wst)
            load_Wres(hy_w_out[0], 16)
            with ExitStack() as ph:
                kq = Ring(nc, ph, "at_kq", 2, [128, 2, L], BF16)
                vr = Ring(nc, ph, "at_v", 2, [128, 32, 130], BF16)
                pS = PRing(nc, ph, "at_pS", 2, [128, 2, 512])
                pO = PRing(nc, ph, "at_pO", 3, [128, 2, 256])
                pT = PRing(nc, ph, "at_pT", 1, [128, 1024], BF16)
                Er = Ring(nc, ph, "at_E", 3, [128, 2, 512], BF16)
                smr = Ring(nc, ph, "at_sm", 6, [128, 4], F32)
                tr_ = Ring(nc, ph, "at_t", 4, [128, 128], F32)
                or_ = Ring(nc, ph, "at_o", 6, [128, 128], F32)
                ybr = Ring(nc, ph, "at_yb", 6, [128, 128], BF16)
                junk = sbt(ph, "at_junk", [128, 128], BF16); B_junk = Buf()
                yT = Ring(nc, ph, "at_yT", 2, [128, L], BF16)
                for t_, b_ in zip(vr.t, vr.b):
                    S.op("pool", lambda t_=t_: G.memset(t_[:, :, 128:130], 1.0), writes=[b_])
                for t_, b_ in zip(Er.t, Er.b):
                    S.op("pool", lambda t_=t_: G.memset(t_[:], 0.0), writes=[b_])
                vsrc = V_S.rearrange("t p f -> p t f")
                def load_head(hd):
                    KQ, B_kq = kq.next()
                    S.dma("sp", KQ[:, 0, :], KT_S[hd], writes=[B_kq])
                    S.dma("sp", KQ[:, 1, :], QT_S[hd], writes=[B_kq])
                    Vt, B_v = vr.next()
                    for v0 in range(0, 32, 8):
                        S.dma("sp", Vt[:, v0:v0 + 8, 0:128], vsrc[:, v0:v0 + 8, hd * 128:(hd + 1) * 128], writes=[B_v])
                    return KQ, B_kq, Vt, B_v
                head_next = load_head(0)
                for hd in range(8):
                    KQ, B_kq, Vt, B_v = head_next
                    if hd + 1 < 8:
                        head_next = load_head(hd + 1)
                    YT, B_yT = yT.next()
                    items = [(qt, i) for qt in range(16) for i in range(qt + 1)]

                    def stage_qk(qt, i):
                        q0 = qt * 256
                        diag = (i == qt)
                        ps, B_ps = pS.next()
                        for half in range(2):
                            kb = 2 * i + half
                            c0 = 128 if (diag and half == 1) else 0
                            for m in range(2):
                                S.op("pe", lambda m=m, half=half, kb=kb, c0=c0: P.matmul(
                                    ps[:, m, half * 256 + c0:half * 256 + 256],
                                    lhsT=KQ[64 * m:64 * m + 64, 0, kb * 128:(kb + 1) * 128],
                                    rhs=KQ[64 * m:64 * m + 64, 1, q0 + c0:q0 + 256], start=True, stop=True),
                                    reads=[B_kq], writes=[B_ps], sig=(half == 1 and m == 1))
                        return ps, B_ps

                    def stage_exp(qt, i, ps, B_ps):
                        diag = (i == qt)
                        E, B_E = Er.next()
                        if diag:
                            S.op("act", lambda: A.activation(out=E[:, :, 0:256], in_=ps[:, :, 0:256], func=AF.Exp),
                                 reads=[B_ps], writes=[B_E])
                            S.op("act", lambda: A.activation(out=E[:, :, 384:512], in_=ps[:, :, 384:512], func=AF.Exp),
                                 reads=[B_ps], writes=[B_E])
                            Ev_ = E[:].rearrange("p m (r c) -> p m r c", c=128)
                            S.op("dve", lambda: V.tensor_tensor(out=Ev_[:, :, 0:4:3, :], in0=Ev_[:, :, 0:4:3, :],
                                                                in1=bc(triT_b[:].unsqueeze(1).unsqueeze(1), [128, 2, 2, 128]),
                                                                op=ALU.mult), reads=[B_E, B_c], writes=[B_E])
                        else:
                            S.op("act", lambda: A.activation(out=E[:], in_=ps[:], func=AF.Exp), reads=[B_ps], writes=[B_E])
                        return E, B_E

                    def stage_pv(qt, i, E, B_E, pOs):
                        diag = (i == qt)
                        for half in range(2):
                            kb = 2 * i + half
                            for qs in range(2):
                                if diag and half == 1 and qs == 0:
                                    continue
                                po, B_po = pOs[qs]
                                first = (i == 0 and half == 0)
                                last = diag and (half == qs)
                                for m in range(2):
                                    S.op("pe", lambda m=m, qs=qs, po=po, kb=kb, half=half, first=first, last=last: P.matmul(
                                        po[:, m, 0:129], lhsT=E[:, m, half * 256 + qs * 128:half * 256 + (qs + 1) * 128],
                                        rhs=Vt[:, kb, 0:129], start=(first and m == 0), stop=(last and m == 1),
                                        skip_group_check=True),
                                        reads=[B_E, B_v], writes=[B_po], sig=(last and m == 1))

                    def epi1(qt, pOs):
                        outs = []
                        for qs in range(2):
                            po, B_po = pOs[qs]
                            sm, B_sm = smr.next()
                            S.op("dve", lambda: V.reciprocal(out=sm[:, 0:2], in_=po[:, :, 128]), reads=[B_po], writes=[B_sm])
                            t_, B_t = tr_.next()
                            S.op("dve", lambda: V.tensor_scalar(out=t_[:], in0=po[:, 1, 0:128], scalar1=sm[:, 1:2], scalar2=nlam[:, 0:1],
                                                                op0=ALU.mult, op1=ALU.mult), reads=[B_po, B_sm, Bh], writes=[B_t])
                            o_, B_o = or_.next()
                            S.op("dve", lambda: V.scalar_tensor_tensor(out=o_[:], in0=po[:, 0, 0:128], scalar=sm[:, 0:1], in1=t_[:],
                                                                       op0=ALU.mult, op1=ALU.add), reads=[B_po, B_sm, B_t], writes=[B_o])
                            outs.append((o_, B_o, sm, B_sm))
                        return outs

                    def epi2(qt, outs):
                        ybs = []
                        for qs in range(2):
                            o_, B_o, sm, B_sm = outs[qs]
                            S.op("act", lambda: A.activation(out=junk[:], in_=o_[:], func=AF.Square, accum_out=sm[:, 2:3]),
                                 reads=[B_o], writes=[B_junk, B_sm])
                            rstd(sm[:, 3:4], sm[:, 2:3], 1.0 / 128.0, B_sm)
                            yb, B_yb = ybr.next()
                            S.op("dve", lambda: V.scalar_tensor_tensor(out=yb[:], in0=o_[:], scalar=sm[:, 3:4], in1=sublw[:],
                                                                       op0=ALU.mult, op1=ALU.mult), reads=[B_o, B_sm, Bh], writes=[B_yb])
                            ybs.append((yb, B_yb))
                        return ybs

                    def epi3(qt, ybs):
                        q0 = qt * 256
                        for qs in range(2):
                            yb, B_yb = ybs[qs]
                            pt, B_pt = pT.next()
                            S.op("pe", lambda: P.transpose(out=pt[:, 0:128], in_=yb[:], identity=identb[:]), reads=[B_yb, B_c], writes=[B_pt])
                            S.op("dve", lambda: V.tensor_copy(out=YT[:, q0 + qs * 128:q0 + (qs + 1) * 128], in_=pt[:, 0:128]),
                                 reads=[B_pt], writes=[B_yT])

                    nxt_qk = stage_qk(*items[0])
                    pOs = None
                    pend2 = []
                    pend3 = []
                    for n, (qt, i) in enumerate(items):
                        ps, B_ps = nxt_qk
                        if n + 1 < len(items):
                            nxt_qk = stage_qk(*items[n + 1])
                        E, B_E = stage_exp(qt, i, ps, B_ps)
                        new3 = [(q_, epi2(q_, o_)) for (q_, o_) in pend2]
                        pend2 = []
                        if i == 0:
                            pOs = [pO.next(), pO.next()]
                        stage_pv(qt, i, E, B_E, pOs)
                        for (q_, y_) in pend3:
                            epi3(q_, y_)
                        pend3 = new3
                        if i == qt:
                            pend2.append((qt, epi1(qt, pOs)))
                    for (q_, o_) in pend2:
                        pend3.append((q_, epi2(q_, o_)))
                    for (q_, y_) in pend3:
                        epi3(q_, y_)
                    S.dma("sp", ACT_S[8 + hd], YT[:], reads=[B_yT])
                S.barrier()
            if hstop == 5:
                wst.close(); hst.close()
                return True
            phase_tm(16, h_in_ap, h_out_ap, widx_next)
            wst.close()
            hst.close()
            return False

        pcount = [0]

        def chk():
            pcount[0] += 1
            return stop is not None and pcount[0] >= stop

        def _main_program():
            h_cur = x
            scr = [hA, hB, hC]
            nxt = 0
            first = True
            for li, layer in enumerate(layers):
                last_layer = (li == len(layers) - 1)
                if first:
                    phase_norm_in(h_cur, 2 * layer)
                    if chk():
                        return
                    first = False
                h_mid = scr[nxt]; nxt += 1
                if layer == 0:
                    if phase_hybrid(h_cur, h_mid, 2 * layer + 1):
                        return
                    if chk():
                        return
                else:
                    with ExitStack() as ws:
                        open_wres(ws)
                        phase_shortconv(sc_w_out[0], 8)
                        if chk():
                            return
                        phase_tm(8, h_cur, h_mid, 2 * layer + 1)
                        if chk():
                            return
                with ExitStack() as ws:
                    open_wres(ws)
                    phase_ffn_up(layer, ffn_w_down[layer], 22)
                    if chk():
                        return
                    if last_layer:
                        phase_tm(22, h_mid, out, None)
                    else:
                        h_new = scr[nxt]; nxt += 1
                        phase_tm(22, h_mid, h_new, 2 * layers[li + 1])
                        h_cur = h_new
                    if chk():
                        return

        try:
            _main_program()
        except _Stop:
            pass
        S.barrier(engines=("sp",))
        build_program.stats = (S.n_ins, S.n_wait)
    return nc


_INPUT_NAMES = ["mix_norm", "ffn_norm", "hy_w_in", "hy_conv_w", "hy_conv_b", "hy_dt_bias", "hy_a_log",
                "hy_d_skip", "hy_ssd_norm", "hy_q_norm", "hy_k_norm", "hy_lambda_q1", "hy_lambda_k1",
                "hy_lambda_q2", "hy_lambda_k2", "hy_subln", "hy_w_out", "sc_w_in", "sc_conv_w", "sc_w_out",
                "ffn_w_up", "ffn_conv_w", "ffn_conv_b", "ffn_w_down"]


def kernel(**inputs):
    x = np.asarray(inputs["x"], dtype=np.float32)
    nc = build_program()
    shared = {k: np.ascontiguousarray(np.asarray(inputs[k], dtype=np.float32)) for k in _INPUT_NAMES}
    in_maps = []
    for b in range(8):
        m = dict(shared)
        m["x"] = np.ascontiguousarray(x[b])
        in_maps.append(m)
    res = run_bass_kernel_spmd(nc, in_maps, core_ids=list(range(8)))
    return np.stack([np.asarray(r["out"], dtype=np.float32) for r in res.results], axis=0)
```

```python
import math
from contextlib import ExitStack

import numpy as np
import concourse.bass as bass
import concourse.mybir as mybir
from concourse.bass_utils import run_bass_kernel_spmd

F32 = mybir.dt.float32
BF16 = mybir.dt.bfloat16
I32 = mybir.dt.int32
AF = mybir.ActivationFunctionType
ALU = mybir.AluOpType

L = 4096
DM = 1024
NT = 8
DFF = 2816
EPS = 1e-6
W_IN0 = 5648
C_Z, C_XBC, C_DT, C_Q, C_K, C_V = 0, 1024, 2560, 2576, 3600, 4624


class Ev:
    __slots__ = ("key", "sem", "val", "eng", "clock")

    def __init__(self, key, sem, val, eng, clock):
        self.key, self.sem, self.val, self.eng, self.clock = key, sem, val, eng, clock


class Buf:
    __slots__ = ("name", "w", "rd", "excl")

    def __init__(self, name="", excl=False):
        self.name, self.w, self.rd, self.excl = name, None, {}, excl


class Sched:
    NDMA = 8

    def __init__(self, nc, stack):
        self.nc = nc
        self.engs = {"pe": nc.tensor, "act": nc.scalar, "dve": nc.vector,
                     "pool": nc.gpsimd, "sp": nc.sync}
        self.sem, self.cnt, self.pending, self.last_ins = {}, {}, {}, {}
        self.seen = {e: {} for e in self.engs}
        for e in ("pe", "act", "dve", "pool"):
            self.sem[e] = stack.enter_context(nc.semaphore("s_" + e))
            self.cnt[e] = 0
            self.pending[e] = []
            self.last_ins[e] = None
        self.dsem, self.dcnt, self.drr = {}, {}, {}
        for q in ("sp", "pool"):
            self.dsem[q] = []
            for i in range(self.NDMA):
                k = "d_%s%d" % (q, i)
                self.dsem[q].append((k, stack.enter_context(nc.semaphore(k))))
                self.dcnt[k] = 0
            self.drr[q] = 0
        self.n_wait = 0
        self.n_ins = 0

    def _need(self, e, ev):
        if ev is None:
            return
        if ev.val is None:
            self._force(ev.eng)
        seen = self.seen[e]
        if seen.get(ev.key, 0) >= ev.val:
            return
        self.engs[e].wait_ge(ev.sem, ev.val)
        self.n_wait += 1
        for k, v in ev.clock.items():
            if seen.get(k, 0) < v:
                seen[k] = v

    def _force(self, e):
        if not self.pending[e]:
            return
        self.last_ins[e].then_inc(self.sem[e], 1)
        self._signal(e)

    def _signal(self, e):
        self.cnt[e] += 1
        v = self.cnt[e]
        clock = dict(self.seen[e])
        clock["s_" + e] = v
        for ev in self.pending[e]:
            ev.val = v
            ev.clock = clock
        self.pending[e] = []
        self.last_ins[e] = None

    def _deps(self, e, reads, writes, is_dma):
        for b in reads:
            if b.w is not None:
                self._need(e, b.w)
            if b.excl:
                for ev in list(b.rd.values()):
                    if ev.eng != e:
                        self._need(e, ev)
        for b in writes:
            if b.w is not None and (is_dma or b.w.eng != e or e == "pool"):
                self._need(e, b.w)
            for ev in list(b.rd.values()):
                if is_dma or ev.eng != e or e == "pool":
                    self._need(e, ev)

    def _record(self, ev, reads, writes):
        for b in reads:
            b.rd[ev.key] = ev
        for b in writes:
            b.w = ev
            b.rd = {}

    def op(self, e, fn, reads=(), writes=(), sig=True):
        self._deps(e, reads, writes, False)
        ins = fn()
        self.n_ins += 1
        ev = Ev("s_" + e, self.sem[e], None, e, None)
        self.pending[e].append(ev)
        self.last_ins[e] = ins
        if sig:
            ins.then_inc(self.sem[e], 1)
            self._signal(e)
        self._record(ev, reads, writes)
        return ev

    def dma(self, q, out, in_, reads=(), writes=(), **kw):
        i = self.drr[q]
        self.drr[q] = (i + 1) % self.NDMA
        key, sem = self.dsem[q][i]
        prev = self.dcnt[key]
        seen = self.seen[q]
        if prev > 0 and seen.get(key, 0) < prev:
            self.engs[q].wait_ge(sem, prev)
            self.n_wait += 1
            seen[key] = prev
        self._deps(q, reads, writes, True)
        ins = self.engs[q].dma_start(out=out, in_=in_, **kw)
        ins.then_inc(sem, 16)
        self.n_ins += 1
        self.dcnt[key] = prev + 16
        clock = dict(seen)
        clock[key] = prev + 16
        ev = Ev(key, sem, prev + 16, None, clock)
        self._record(ev, reads, writes)
        return ev

    def barrier(self, engines=("pe", "act", "dve", "pool", "sp")):
        for e in ("pe", "act", "dve", "pool"):
            self._force(e)
        evs = []
        for e in ("pe", "act", "dve", "pool"):
            if self.cnt[e] > 0:
                evs.append(Ev("s_" + e, self.sem[e], self.cnt[e], e, {"s_" + e: self.cnt[e]}))
        for q in self.dsem:
            for key, sem in self.dsem[q]:
                if self.dcnt[key] > 0:
                    evs.append(Ev(key, sem, self.dcnt[key], None, {key: self.dcnt[key]}))
        for e in engines:
            for ev in evs:
                if ev.eng != e:
                    self._need(e, ev)


_UID = [0]


class Ring:
    def __init__(self, nc, st, name, n, shape, dt):
        _UID[0] += 1
        name = "%s_%d_" % (name, _UID[0])
        self.t = [st.enter_context(nc.sbuf_tensor("%s%d" % (name, i), shape, dt)) for i in range(n)]
        self.b = [Buf("%s%d" % (name, i)) for i in range(n)]
        self.i = 0

    def next(self):
        i = self.i
        self.i = (i + 1) % len(self.t)
        return self.t[i], self.b[i]


class PRing:
    def __init__(self, nc, st, name, n, shape, dt=F32):
        _UID[0] += 1
        name = "%s_%d_" % (name, _UID[0])
        self.t = [st.enter_context(nc.psum_tensor("%s%d" % (name, i), shape, dt)) for i in range(n)]
        self.b = [Buf("%s%d" % (name, i), excl=True) for i in range(n)]
        self.i = 0

    def next(self):
        i = self.i
        self.i = (i + 1) % len(self.t)
        return self.t[i], self.b[i]


def bc(ap, shape):
    return ap.to_broadcast(shape)


class _Stop(Exception):
    pass


def build_program(layers=(0, 1), dbg=False, stop=None, hstop=None):
    nc = bass.Bass("TRN2", target_bir_lowering=False)

    def din(name, shape):
        return nc.dram_tensor(name, shape, F32, kind="ExternalInput").ap()

    x = din("x", [L, DM])
    mix_norm = din("mix_norm", [2, DM])
    ffn_norm = din("ffn_norm", [2, DM])
    hy_w_in = din("hy_w_in", [1, DM, W_IN0])
    hy_conv_w = din("hy_conv_w", [1, 4, 1536])
    hy_conv_b = din("hy_conv_b", [1, 1536])
    hy_dt_bias = din("hy_dt_bias", [1, 16])
    hy_a_log = din("hy_a_log", [1, 16])
    hy_d_skip = din("hy_d_skip", [1, 16])
    hy_ssd_norm = din("hy_ssd_norm", [1, 1024])
    hy_q_norm = din("hy_q_norm", [1, 64])
    hy_k_norm = din("hy_k_norm", [1, 64])
    hy_lq1 = din("hy_lambda_q1", [1, 64])
    hy_lk1 = din("hy_lambda_k1", [1, 64])
    hy_lq2 = din("hy_lambda_q2", [1, 64])
    hy_lk2 = din("hy_lambda_k2", [1, 64])
    hy_subln = din("hy_subln", [1, 128])
    hy_w_out = din("hy_w_out", [1, 2048, DM])
    sc_w_in = din("sc_w_in", [1, DM, 3072])
    sc_conv_w = din("sc_conv_w", [1, 3, 1024])
    sc_w_out = din("sc_w_out", [1, 1024, DM])
    ffn_w_up = din("ffn_w_up", [2, DM, 2 * DFF])
    ffn_conv_w = din("ffn_conv_w", [2, 3, 2 * DFF])
    ffn_conv_b = din("ffn_conv_b", [2, 2 * DFF])
    ffn_w_down = din("ffn_w_down", [2, DFF, DM])
    out = nc.dram_tensor("out", [L, DM], F32, kind="ExternalOutput").ap()

    skind = "ExternalOutput" if dbg else "Internal"

    def dscr(name, shape, dt=BF16):
        return nc.dram_tensor(name, shape, dt, kind=skind).ap()

    hA = dscr("hA", [L, DM], F32)
    hB = dscr("hB", [L, DM], F32)
    hC = dscr("hC", [L, DM], F32)
    ACT_S = dscr("ACT_S", [22, 128, L])
    XBC_S = dscr("XBC_S", [12, 128, L])
    QT_S = dscr("QT_S", [8, 128, L])
    KT_S = dscr("KT_S", [8, 128, L])
    ZS_S = dscr("ZS_S", [32, 128, 1024])
    V_S = dscr("V_S", [32, 128, 1024])

    with ExitStack() as top:
        S = Sched(nc, top)
        V, A, G, P = nc.vector, nc.scalar, nc.gpsimd, nc.tensor

        def sbt(st, name, shape, dt=F32):
            _UID[0] += 1
            return st.enter_context(nc.sbuf_tensor("%s_%d" % (name, _UID[0]), shape, dt))

        def pst(st, name, shape, dt=F32):
            _UID[0] += 1
            return st.enter_context(nc.psum_tensor("%s_%d" % (name, _UID[0]), shape, dt))

        hnT = sbt(top, "hnT", [128, 8, L], BF16)
        B_hn = [Buf("hn%d" % i) for i in range(32)]
        WR = {"t": None}
        B_Wres = Buf("Wres")

        def open_wres(st):
            WR["t"] = sbt(st, "Wres", [128, 22, DM], BF16)
        identf = sbt(top, "identf", [128, 128]); B_c = Buf("consts")
        identb = sbt(top, "identb", [128, 128], BF16)
        triT_f = sbt(top, "triT_f", [128, 128])
        triT_b = sbt(top, "triT_b", [128, 128], BF16)
        ones_f = sbt(top, "ones_f", [128, 128])
        epsc = sbt(top, "epsc", [128, 1])
        normw = sbt(top, "normw", [128, 4, 8])
        fcw = sbt(top, "fcw", [128, 2, 3, 44])
        fcb = sbt(top, "fcb", [128, 2, 44])
        scw = sbt(top, "scw", [128, 3, 8])

        S.op("pool", lambda: G.memset(identf[:], 1.0), writes=[B_c])
        S.op("pool", lambda: G.affine_select(out=identf[:], in_=identf[:], pattern=[[-1, 128]],
                                             compare_op=ALU.is_equal, fill=0.0, base=0,
                                             channel_multiplier=1), reads=[B_c], writes=[B_c])
        S.op("pool", lambda: G.memset(triT_f[:], 1.0), writes=[B_c])
        S.op("pool", lambda: G.affine_select(out=triT_f[:], in_=triT_f[:], pattern=[[1, 128]],
                                             compare_op=ALU.is_ge, fill=0.0, base=0,
                                             channel_multiplier=-1), reads=[B_c], writes=[B_c])
        S.op("pool", lambda: G.memset(ones_f[:], 1.0), writes=[B_c])
        S.op("pool", lambda: G.memset(epsc[:], EPS), writes=[B_c])
        S.op("dve", lambda: V.tensor_copy(out=identb[:], in_=identf[:]), reads=[B_c], writes=[B_c])
        S.op("dve", lambda: V.tensor_copy(out=triT_b[:], in_=triT_f[:]), reads=[B_c], writes=[B_c])

        def load_cols(st_ring, ps_ring, dst, src_row, nch):
            stg, B_s = st_ring.next()
            S.dma("sp", stg[0:nch, :], src_row.rearrange("(c p) -> c p", p=128), writes=[B_s])
            ps, B_p = ps_ring.next()
            S.op("pe", lambda: P.matmul(ps[:, 0:nch], lhsT=stg[0:nch, :], rhs=identf[0:nch, 0:nch],
                                        start=True, stop=True), reads=[B_s, B_c], writes=[B_p])
            S.op("dve", lambda: V.tensor_copy(out=dst, in_=ps[:, 0:nch]), reads=[B_p], writes=[B_c])

        with ExitStack() as ph:
            stg_r = Ring(nc, ph, "stg", 3, [64, 128], F32)
            ps_r = PRing(nc, ph, "psc", 2, [128, 512])
            for i, (src, l) in enumerate([(mix_norm, 0), (ffn_norm, 0), (mix_norm, 1), (ffn_norm, 1)]):
                load_cols(stg_r, ps_r, normw[:, i, :], src[l, :], 8)
            for l in range(2):
                for k in range(3):
                    load_cols(stg_r, ps_r, fcw[:, l, k, :], ffn_conv_w[l, k, :], 44)
                load_cols(stg_r, ps_r, fcb[:, l, :], ffn_conv_b[l, :], 44)
            for k in range(3):
                load_cols(stg_r, ps_r, scw[:, k, :], sc_conv_w[0, k, :], 8)
            S.barrier()

        def load_Wres_parts(w2d, nch):
            src = w2d.rearrange("(c p) n -> p c n", p=128)
            step = 4
            parts = []
            for c0 in range(0, nch, step):
                c1 = min(nch, c0 + step)
                parts.append(lambda c0=c0, c1=c1: S.dma("pool", WR["t"][:, c0:c1, :], src[:, c0:c1, :], writes=[B_Wres]))
            return parts

        def load_Wres(w2d, nch):
            for p_ in load_Wres_parts(w2d, nch):
                p_()

        def rstd(dst, src, scale, B_):
            S.op("act", lambda: A.activation(out=dst, in_=src, func=AF.Ln, bias=epsc[:, 0:1], scale=scale),
                 reads=[B_, B_c], writes=[B_])
            S.op("act", lambda: A.activation(out=dst, in_=dst, func=AF.Exp, scale=-0.5), reads=[B_], writes=[B_])

        class NormRes:
            def __init__(self, st):
                self.junk = sbt(st, "nr_junk", [128, DM], BF16); self.B_junk = Buf()
                self.xn = Ring(nc, st, "nr_xn", 2, [128, DM], BF16)
                self.sm = Ring(nc, st, "nr_sm", 3, [128, 2], F32)
                self.ps = PRing(nc, st, "nr_ps", 2, [128, 8, 128], BF16)

        def norm_to_hnT(nr, h_sb, B_h, widx, t128):
            sm, B_sm = nr.sm.next()
            S.op("act", lambda: A.activation(out=nr.junk[:], in_=h_sb, func=AF.Square,
                                             accum_out=sm[:, 0:1]),
                 reads=[B_h], writes=[nr.B_junk, B_sm])
            rstd(sm[:, 1:2], sm[:, 0:1], 1.0 / DM, B_sm)
            xn, B_xn = nr.xn.next()
            S.op("dve", lambda: V.tensor_scalar(out=xn[:], in0=h_sb, scalar1=sm[:, 1:2], scalar2=None,
                                                op0=ALU.mult), reads=[B_h, B_sm], writes=[B_xn])
            ps, B_ps = nr.ps.next()
            for c in range(8):
                S.op("pe", lambda c=c: P.transpose(out=ps[:, c, :], in_=xn[:, c * 128:(c + 1) * 128],
                                                   identity=identb[:]),
                     reads=[B_xn, B_c], writes=[B_ps], sig=(c == 7))
            S.op("dve", lambda: V.tensor_tensor(out=hnT[:, :, t128 * 128:(t128 + 1) * 128], in0=ps[:],
                                                in1=bc(normw[:, widx, :].unsqueeze(2), [128, 8, 128]),
                                                op=ALU.mult),
                 reads=[B_ps, B_c], writes=[B_hn[t128]])

        def phase_norm_in(h_src, widx):
            with ExitStack() as ph:
                nr = NormRes(ph)
                hr = Ring(nc, ph, "ni_h", 3, [128, DM], F32)
                for t in range(32):
                    h, B_h = hr.next()
                    S.dma("sp", h[:], h_src[t * 128:(t + 1) * 128, :], writes=[B_h])
                    norm_to_hnT(nr, h[:], B_h, widx, t)
                S.barrier()

        def phase_tm(nch, h_src, h_dst, widx_next):
            with ExitStack() as ph:
                nr = NormRes(ph) if widx_next is not None else None
                ar = Ring(nc, ph, "tm_a", 2, [128, nch, 512], BF16)
                hr = Ring(nc, ph, "tm_h", 3, [128, DM], F32)
                hn = Ring(nc, ph, "tm_hn", 3, [128, DM], F32)
                pr = PRing(nc, ph, "tm_ps", 4, [128, 512])
                src = ACT_S.rearrange("c p t -> p c t")
                pending_epi = [None]

                def load_a(tt):
                    a, B_a = ar.next()
                    half = (nch + 1) // 2
                    S.dma("sp", a[:, 0:half, :], src[:, 0:half, tt * 512:(tt + 1) * 512], writes=[B_a])
                    S.dma("sp", a[:, half:nch, :], src[:, half:nch, tt * 512:(tt + 1) * 512], writes=[B_a])
                    return a, B_a
                a_next = load_a(0)
                for tt in range(NT):
                    a, B_a = a_next
                    if tt + 1 < NT:
                        a_next = load_a(tt + 1)
                    for sub in range(4):
                        t128 = tt * 4 + sub
                        h, B_h = hr.next()
                        S.dma("sp", h[:], h_src[t128 * 128:(t128 + 1) * 128, :], writes=[B_h])
                        pss = [pr.next(), pr.next()]
                        for c in range(nch):
                            for hf in range(2):
                                ps, B_ps = pss[hf]
                                S.op("pe", lambda c=c, hf=hf, ps=ps: P.matmul(
                                    ps[:], lhsT=a[:, c, sub * 128:(sub + 1) * 128],
                                    rhs=WR["t"][:, c, hf * 512:(hf + 1) * 512],
                                    start=(c == 0), stop=(c == nch - 1)),
                                    reads=[B_a, B_Wres], writes=[B_ps], sig=(c == nch - 1))
                        def epi(pss=pss, h=h, B_h=B_h, t128=t128):
                            o, B_o = hn.next()
                            for hf in range(2):
                                ps, B_ps = pss[hf]
                                S.op("dve", lambda hf=hf, ps=ps: V.tensor_tensor(
                                    out=o[:, hf * 512:(hf + 1) * 512], in0=ps[:],
                                    in1=h[:, hf * 512:(hf + 1) * 512], op=ALU.add),
                                    reads=[B_ps, B_h], writes=[B_o])
                            S.dma("sp", h_dst[t128 * 128:(t128 + 1) * 128, :], o[:], reads=[B_o])
                            if nr is not None:
                                norm_to_hnT(nr, o[:], B_o, widx_next, t128)
                        if pending_epi[0] is not None:
                            pending_epi[0]()
                        pending_epi[0] = epi
                pending_epi[0]()
                S.barrier()

        def fm_matmuls(wt, n0, tt, ps, B_ps, B_w):
            for k in range(8):
                S.op("pe", lambda k=k: P.matmul(ps[:], lhsT=wt[:, k, n0:n0 + 128],
                                                rhs=hnT[:, k, tt * 512:(tt + 1) * 512],
                                                start=(k == 0), stop=(k == 7)),
                     reads=[B_w] + B_hn[tt * 4:tt * 4 + 4], writes=[B_ps], sig=(k == 7))

        def wload(wr, w2d, col0, ncol=128):
            wt, B_w = wr.next()
            S.dma("pool", wt[:, :, 0:ncol], w2d.rearrange("(k p) n -> p k n", p=128)[:, :, col0:col0 + ncol],
                  writes=[B_w])
            return wt, B_w

        def phase_ffn_up(l, w_next, nch_next):
            w2d = ffn_w_up[l]
            with ExitStack() as ph:
                wr = Ring(nc, ph, "fu_w", 6, [128, 8, 128], BF16)
                pr = PRing(nc, ph, "fu_ps", 6, [128, 512])
                xg = Ring(nc, ph, "fu_xg", 1, [128, L + 2], F32)
                xv = Ring(nc, ph, "fu_xv", 1, [128, L + 2], F32)
                yr = Ring(nc, ph, "fu_y", 6, [128, 512], F32)
                sr = Ring(nc, ph, "fu_s", 3, [128, 512], F32)
                gr = Ring(nc, ph, "fu_g", 2, [128, L], BF16)
                for r in (xg, xv):
                    for t_, b_ in zip(r.t, r.b):
                        S.op("pool", lambda t_=t_: G.memset(t_[:, 0:2], 0.0), writes=[b_])
                pend = [(wload(wr, w2d, 0), wload(wr, w2d, DFF))]
                wres_parts = load_Wres_parts(w_next, nch_next)
                fu_pend = []
                for i in range(22):
                    if i + 1 < 22:
                        pend.append((wload(wr, w2d, (i + 1) * 128), wload(wr, w2d, DFF + (i + 1) * 128)))
                    if wres_parts:
                        wres_parts.pop(0)()
                    (wg, B_wg), (wv, B_wv) = pend.pop(0)
                    Xg, B_xg = xg.next()
                    Xv, B_xv = xv.next()
                    Gt, B_g = gr.next()
                    cg, cv = i, 22 + i
                    for tt in range(NT):
                        sl = slice(tt * 512, (tt + 1) * 512)
                        psg, B_pg = pr.next()
                        psv, B_pv = pr.next()
                        fm_matmuls(wg, 0, tt, psg, B_pg, B_wg)
                        fm_matmuls(wv, 0, tt, psv, B_pv, B_wv)
                        Yg, B_yg = yr.next()
                        Yv, B_yv = yr.next()
                        S.op("act", lambda: A.activation(out=Yg[:], in_=psg[:], func=AF.Identity,
                                                         bias=fcb[:, l, cg:cg + 1], scale=fcw[:, l, 2, cg:cg + 1]),
                             reads=[B_pg, B_c], writes=[B_yg])
                        S.op("act", lambda: A.copy(out=Xg[:, 2 + tt * 512:2 + (tt + 1) * 512], in_=psg[:]),
                             reads=[B_pg], writes=[B_xg])
                        S.op("act", lambda: A.activation(out=Yv[:], in_=psv[:], func=AF.Identity,
                                                         bias=fcb[:, l, cv:cv + 1], scale=fcw[:, l, 2, cv:cv + 1]),
                             reads=[B_pv, B_c], writes=[B_yv])
                        S.op("dve", lambda: V.tensor_copy(out=Xv[:, 2 + tt * 512:2 + (tt + 1) * 512], in_=psv[:]),
                             reads=[B_pv], writes=[B_xv])
                        for (e, X_, B_x, Y_, B_y, c_) in (("dve", Xg, B_xg, Yg, B_yg, cg), ("dve", Xv, B_xv, Yv, B_yv, cv)):
                            for k in (1, 0):
                                en = "dve"
                                E_ = G if en == "pool" else V
                                S.op(en, lambda E_=E_, X_=X_, Y_=Y_, k=k, c_=c_: E_.scalar_tensor_tensor(
                                    out=Y_[:], in0=X_[:, tt * 512 + k:tt * 512 + k + 512],
                                    scalar=fcw[:, l, k, c_:c_ + 1], in1=Y_[:], op0=ALU.mult, op1=ALU.add),
                                    reads=[B_x, B_c, B_y], writes=[B_y])
                        def fin(Yg=Yg, B_yg=B_yg, Yv=Yv, B_yv=B_yv, Gt=Gt, B_g=B_g, sl=sl):
                            St, B_s = sr.next()
                            S.op("act", lambda: A.activation(out=St[:], in_=Yg[:], func=AF.Silu),
                                 reads=[B_yg], writes=[B_s])
                            S.op("pool", lambda: G.tensor_tensor(out=Gt[:, sl], in0=St[:], in1=Yv[:], op=ALU.mult),
                                 reads=[B_s, B_yv], writes=[B_g])
                        fu_pend.append(fin)
                        if len(fu_pend) > 1:
                            fu_pend.pop(0)()
                    while fu_pend:
                        fu_pend.pop(0)()
                    S.dma("sp", ACT_S[i], Gt[:], reads=[B_g])
                S.barrier()

        def phase_shortconv(w_next, nch_next):
            w2d = sc_w_in[0]
            with ExitStack() as ph:
                wr = Ring(nc, ph, "sc_w", 6, [128, 8, 128], BF16)
                pr = PRing(nc, ph, "sc_ps", 6, [128, 512])
                mr = Ring(nc, ph, "sc_m", 2, [128, L + 2], F32)
                ur = Ring(nc, ph, "sc_u", 3, [128, 512], F32)
                yr = Ring(nc, ph, "sc_y", 3, [128, 512], F32)
                rr = Ring(nc, ph, "sc_r", 2, [128, L], BF16)
                for t_, b_ in zip(mr.t, mr.b):
                    S.op("pool", lambda t_=t_: G.memset(t_[:, 0:2], 0.0), writes=[b_])
                load_Wres(w_next, nch_next)
                for i in range(8):
                    wb_, wc_, wu_ = (wload(wr, w2d, i * 128), wload(wr, w2d, 1024 + i * 128),
                                     wload(wr, w2d, 2048 + i * 128))
                    M, B_m = mr.next()
                    R, B_r = rr.next()
                    for tt in range(NT):
                        sl = slice(tt * 512, (tt + 1) * 512)
                        psb, B_pb = pr.next()
                        psc, B_pc = pr.next()
                        psu, B_pu = pr.next()
                        fm_matmuls(wc_[0], 0, tt, psc, B_pc, wc_[1])
                        fm_matmuls(wu_[0], 0, tt, psu, B_pu, wu_[1])
                        fm_matmuls(wb_[0], 0, tt, psb, B_pb, wb_[1])
                        U, B_u = ur.next()
                        S.op("act", lambda: A.copy(out=U[:], in_=psu[:]), reads=[B_pu], writes=[B_u])
                        S.op("dve", lambda: V.tensor_tensor(out=M[:, 2 + tt * 512:2 + (tt + 1) * 512], in0=psc[:],
                                                            in1=U[:], op=ALU.mult),
                             reads=[B_pc, B_u], writes=[B_m])
                        Y, B_y = yr.next()
                        S.op("act", lambda: A.activation(out=Y[:], in_=M[:, 2 + tt * 512:2 + (tt + 1) * 512],
                                                         func=AF.Copy, scale=scw[:, 2, i:i + 1]),
                             reads=[B_m, B_c], writes=[B_y])
                        for k in (1, 0):
                            S.op("dve", lambda k=k: V.scalar_tensor_tensor(
                                out=Y[:], in0=M[:, tt * 512 + k:tt * 512 + k + 512], scalar=scw[:, k, i:i + 1],
                                in1=Y[:], op0=ALU.mult, op1=ALU.add), reads=[B_m, B_c, B_y], writes=[B_y])
                        S.op("dve", lambda: V.tensor_tensor(out=R[:, sl], in0=psb[:], in1=Y[:], op=ALU.mult),
                             reads=[B_pb, B_y], writes=[B_r])
                    S.dma("sp", ACT_S[i], R[:], reads=[B_r])
                S.barrier()

        def phase_hybrid(h_in_ap, h_out_ap, widx_next):
            hyb = {}
            w2d = hy_w_in[0]
            hst = ExitStack()
            cst = ExitStack()
            Bh = Buf("hyb_consts")
            blk = sbt(hst, "blk", [128, 128], BF16)
            prot = sbt(hst, "prot", [128, 128], BF16)
            hcw = sbt(hst, "hcw", [128, 4, 12])
            hcb = sbt(hst, "hcb", [128, 12])
            qkw = sbt(hst, "qkw", [128, 2])
            dt_tm = sbt(hst, "dt_tm", [128, 32, 16])
            a_tm = sbt(hst, "a_tm", [128, 32, 16])
            acs = sbt(hst, "acs", [128, 32, 16])
            eA = sbt(hst, "eA", [128, 32, 16])
            dA = sbt(hst, "dA", [128, 32, 16])
            cdr = sbt(hst, "cdr", [128, 32, 16])
            bc16 = sbt(hst, "bc16", [128, 3, 16])
            ssdw = sbt(hst, "ssdw", [128, 1024])
            sublw = sbt(hst, "sublw", [128, 128])
            nlam = sbt(hst, "nlam", [128, 2])
            B_dt = Buf("dtstuff")
            cosT = sbt(cst, "cosT", [128, L])
            sinT = sbt(cst, "sinT", [128, L])

            with ExitStack() as ph:
                tf = sbt(ph, "c_tf", [128, 128])
                tf2 = sbt(ph, "c_tf2", [128, 128])
                B_t = Buf()
                S.op("pool", lambda: G.memset(tf[:], 0.0), writes=[B_t])
                S.op("pool", lambda: G.memset(tf[0:64, 0:64], 1.0), writes=[B_t])
                S.op("pool", lambda: G.memset(tf[64:128, 64:128], 1.0), writes=[B_t])
                S.op("dve", lambda: V.tensor_copy(out=blk[:], in_=tf[:]), reads=[B_t], writes=[Bh])
                S.op("pool", lambda: G.memset(tf[:], 1.0), reads=[Bh], writes=[B_t])
                S.op("pool", lambda: G.affine_select(out=tf[:], in_=tf[:], pattern=[[-1, 128]], compare_op=ALU.is_equal,
                                                     fill=0.0, base=32, channel_multiplier=1), reads=[B_t], writes=[B_t])
                S.op("pool", lambda: G.memset(tf2[:], -1.0), writes=[B_t])
                S.op("pool", lambda: G.affine_select(out=tf2[:], in_=tf2[:], pattern=[[-1, 128]], compare_op=ALU.is_equal,
                                                     fill=0.0, base=-32, channel_multiplier=1), reads=[B_t], writes=[B_t])
                for c0 in (0, 64):
                    S.op("pool", lambda c0=c0: G.memset(tf[:, c0:c0 + 32], 0.0), reads=[B_t], writes=[B_t])
                    S.op("pool", lambda c0=c0: G.memset(tf2[:, c0 + 32:c0 + 64], 0.0), reads=[B_t], writes=[B_t])
                S.op("dve", lambda: V.tensor_tensor(out=tf[:], in0=tf[:], in1=tf2[:], op=ALU.add), reads=[B_t], writes=[B_t])
                S.op("dve", lambda: V.tensor_copy(out=prot[:], in_=tf[:]), reads=[B_t], writes=[Bh])
                pi_ = sbt(ph, "c_pi", [128, 1], I32)
                pf_ = sbt(ph, "c_pf", [128, 2])
                S.op("pool", lambda: G.iota(pi_[:], pattern=[[0, 1]], base=0, channel_multiplier=1), writes=[B_t])
                S.op("dve", lambda: V.tensor_single_scalar(out=pi_[:], in_=pi_[:], scalar=31, op=ALU.bitwise_and),
                     reads=[B_t], writes=[B_t])
                S.op("dve", lambda: V.tensor_copy(out=pf_[:, 0:1], in_=pi_[:]), reads=[B_t], writes=[B_t])
                S.op("dve", lambda: V.tensor_scalar(out=pf_[:, 0:1], in0=pf_[:, 0:1], scalar1=-math.log(10000.0) / 32.0,
                                                    scalar2=-math.log(2 * math.pi), op0=ALU.mult, op1=ALU.add),
                     reads=[B_t], writes=[B_t])
                S.op("act", lambda: A.activation(out=pf_[:, 1:2], in_=pf_[:, 0:1], func=AF.Exp), reads=[B_t], writes=[B_t])
                ti = sbt(ph, "c_ti", [128, L], I32)
                r_ = sbt(ph, "c_r", [128, L])
                kf = sbt(ph, "c_kf", [128, L])
                S.op("pool", lambda: G.iota(ti[:], pattern=[[1, L]], base=0, channel_multiplier=0), writes=[B_t])
                S.op("dve", lambda: V.tensor_copy(out=r_[:], in_=ti[:]), reads=[B_t], writes=[B_t])
                S.op("dve", lambda: V.tensor_scalar(out=r_[:], in0=r_[:], scalar1=pf_[:, 1:2], scalar2=None, op0=ALU.mult),
                     reads=[B_t], writes=[B_t])
                for (dst, off) in ((sinT, 0.0), (cosT, 0.25)):
                    S.op("dve", lambda off=off: V.tensor_scalar(out=kf[:], in0=r_[:], scalar1=off, scalar2=None, op0=ALU.add),
                         reads=[B_t, Bh], writes=[B_t])
                    S.op("dve", lambda: V.tensor_copy(out=ti[:], in_=kf[:]), reads=[B_t], writes=[B_t])
                    S.op("dve", lambda dst=dst: V.tensor_copy(out=dst[:], in_=ti[:]), reads=[B_t], writes=[Bh])
                    S.op("dve", lambda dst=dst: V.tensor_tensor(out=kf[:], in0=kf[:], in1=dst[:], op=ALU.subtract),
                         reads=[B_t, Bh], writes=[B_t])
                    S.op("dve", lambda dst=dst: V.tensor_single_scalar(out=dst[:], in_=kf[:], scalar=0.5, op=ALU.is_gt),
                         reads=[B_t], writes=[Bh])
                    S.op("dve", lambda dst=dst: V.tensor_tensor(out=kf[:], in0=kf[:], in1=dst[:], op=ALU.subtract),
                         reads=[B_t, Bh], writes=[B_t])
                    S.op("act", lambda dst=dst: A.activation(out=dst[:], in_=kf[:], func=AF.Sin, scale=2 * math.pi),
                         reads=[B_t], writes=[Bh])
                stg_r = Ring(nc, ph, "hstg", 3, [64, 128], F32)
                ps_r = PRing(nc, ph, "hpsc", 2, [128, 512])

                def lc(dst, src_row, nch):
                    stg, B_s = stg_r.next()
                    S.dma("sp", stg[0:nch, :], src_row.rearrange("(c p) -> c p", p=128), writes=[B_s])
                    ps, B_p = ps_r.next()
                    S.op("pe", lambda: P.matmul(ps[:, 0:nch], lhsT=stg[0:nch, :], rhs=identf[0:nch, 0:nch],
                                                start=True, stop=True), reads=[B_s, B_c], writes=[B_p])
                    S.op("dve", lambda: V.tensor_copy(out=dst, in_=ps[:, 0:nch]), reads=[B_p], writes=[Bh])
                for k in range(4):
                    lc(hcw[:, k, :], hy_conv_w[0, k, :], 12)
                lc(hcb[:, :], hy_conv_b[0, :], 12)
                for j, src in enumerate((hy_q_norm, hy_k_norm)):
                    for hf in range(2):
                        S.dma("sp", qkw[hf * 64:(hf + 1) * 64, j:j + 1], src[0, :].rearrange("(p o) -> p o", o=1),
                              writes=[Bh])
                S.op("dve", lambda: V.tensor_scalar(out=qkw[:, 0:1], in0=qkw[:, 0:1], scalar1=0.125, scalar2=None,
                                                    op0=ALU.mult), reads=[Bh], writes=[Bh])
                for j, src in enumerate((hy_dt_bias, hy_a_log, hy_d_skip)):
                    S.dma("sp", bc16[:, j, :], src[0:1, :].partition_broadcast(128), writes=[Bh])
                S.op("act", lambda: A.activation(out=bc16[:, 1, :], in_=bc16[:, 1, :], func=AF.Exp), reads=[Bh], writes=[Bh])
                S.op("dve", lambda: V.tensor_scalar(out=bc16[:, 1, :], in0=bc16[:, 1, :], scalar1=-1.0, scalar2=None,
                                                    op0=ALU.mult), reads=[Bh], writes=[Bh])
                S.dma("sp", ssdw[:], hy_ssd_norm[0:1, :].partition_broadcast(128), writes=[Bh])
                S.dma("sp", sublw[:], hy_subln[0:1, :].partition_broadcast(128), writes=[Bh])
                lam_init = 0.8 - 0.6 * math.exp(-0.3 * 0)
                S.op("dve", lambda: V.tensor_scalar(out=sublw[:], in0=sublw[:], scalar1=(1.0 - lam_init), scalar2=None,
                                                    op0=ALU.mult), reads=[Bh], writes=[Bh])
                lt = sbt(ph, "c_lt", [128, 4, 64])
                ls = sbt(ph, "c_ls", [128, 4])
                for j, src in enumerate((hy_lq1, hy_lk1, hy_lq2, hy_lk2)):
                    S.dma("sp", lt[:, j, :], src[0:1, :].partition_broadcast(128), writes=[B_t])
                for j in range(2):
                    S.op("dve", lambda j=j: V.tensor_tensor(out=lt[:, 2 * j, :], in0=lt[:, 2 * j, :], in1=lt[:, 2 * j + 1, :],
                                                            op=ALU.mult), reads=[B_t], writes=[B_t])
                    S.op("act", lambda j=j: A.activation(out=lt[:, 2 * j + 1, :], in_=lt[:, 2 * j, :], func=AF.Identity,
                                                         accum_out=ls[:, j:j + 1]), reads=[B_t], writes=[B_t])
                S.op("act", lambda: A.activation(out=ls[:, 2:4], in_=ls[:, 0:2], func=AF.Exp), reads=[B_t], writes=[B_t])
                S.op("dve", lambda: V.tensor_tensor(out=nlam[:, 0:1], in0=ls[:, 3:4], in1=ls[:, 2:3], op=ALU.subtract),
                     reads=[B_t], writes=[Bh])
                S.op("dve", lambda: V.tensor_scalar(out=nlam[:, 0:1], in0=nlam[:, 0:1], scalar1=-lam_init, scalar2=None,
                                                    op0=ALU.add), reads=[Bh], writes=[Bh])
                S.barrier()
            if hstop == 1:
                cst.close(); hst.close()
                return True

            with ExitStack() as ph:
                wr = Ring(nc, ph, "hb_w", 4, [128, 8, 128], BF16)
                pr = PRing(nc, ph, "hb_ps", 4, [128, 512])
                pr2 = PRing(nc, ph, "hb_ps2", 4, [128, 512])
                xr = Ring(nc, ph, "hb_x", 2, [128, L + 3], BF16)
                yr = Ring(nc, ph, "hb_y", 4, [128, 512], F32)
                orr = Ring(nc, ph, "hb_o", 2, [128, L], BF16)
                sq = Ring(nc, ph, "hb_sq", 3, [128, 512], BF16)
                rs = Ring(nc, ph, "hb_rs", 3, [128, 512], F32)
                qn = Ring(nc, ph, "hb_qn", 3, [128, 512], BF16)
                t1r = Ring(nc, ph, "hb_t1", 3, [128, 512], F32)
                t2r = Ring(nc, ph, "hb_t2", 3, [128, 512], F32)
                for t_, b_ in zip(xr.t, xr.b):
                    S.op("pool", lambda t_=t_: G.memset(t_[:, 0:3], 0.0), writes=[b_])
                jobs = [("xbc", i, C_XBC + i * 128) for i in range(12)] + \
                       [("q", i, C_Q + i * 128) for i in range(8)] + [("k", i, C_K + i * 128) for i in range(8)]
                pend = [wload(wr, w2d, jobs[0][2])]
                units = [(ji, tt) for ji in range(len(jobs)) for tt in range(NT)]
                js = {}
                ust = {}

                def stage1(u):
                    ji, tt = units[u]
                    if tt == 0:
                        if ji + 1 < len(jobs):
                            pend.append(wload(wr, w2d, jobs[ji + 1][2]))
                        js[ji] = {"w": pend.pop(0)}
                    wt, B_w = js[ji]["w"]
                    ps, B_ps = pr.next()
                    fm_matmuls(wt, 0, tt, ps, B_ps, B_w)
                    ust[u] = {"ps": (ps, B_ps)}

                def stage2(u):
                    ji, tt = units[u]
                    kind, i, col = jobs[ji]
                    sl = slice(tt * 512, (tt + 1) * 512)
                    ps, B_ps = ust[u]["ps"]
                    if tt == 0:
                        js[ji]["O"] = orr.next()
                        if kind == "xbc":
                            js[ji]["X"] = xr.next()
                    O, B_o = js[ji]["O"]
                    if kind == "xbc":
                        X, B_x = js[ji]["X"]
                        Y, B_y = yr.next()
                        S.op("act", lambda: A.activation(out=Y[:], in_=ps[:], func=AF.Identity,
                                                         bias=hcb[:, i:i + 1], scale=hcw[:, 3, i:i + 1]),
                             reads=[B_ps, Bh], writes=[B_y])
                        S.op("act", lambda: A.copy(out=X[:, 3 + tt * 512:3 + (tt + 1) * 512], in_=ps[:]),
                             reads=[B_ps], writes=[B_x])
                        for k in (2, 1, 0):
                            S.op("dve", lambda k=k: V.scalar_tensor_tensor(
                                out=Y[:], in0=X[:, tt * 512 + k:tt * 512 + k + 512], scalar=hcw[:, k, i:i + 1],
                                in1=Y[:], op0=ALU.mult, op1=ALU.add), reads=[B_x, Bh, B_y], writes=[B_y])
                        ust[u]["Y"] = (Y, B_y)
                    else:
                        wcol = qkw[:, 0:1] if kind == "q" else qkw[:, 1:2]
                        s_, B_s = sq.next()
                        S.op("act", lambda: A.activation(out=s_[:], in_=ps[:], func=AF.Square),
                             reads=[B_ps], writes=[B_s])
                        ps2, B_p2 = pr2.next()
                        S.op("pe", lambda: P.matmul(ps2[:], lhsT=blk[:], rhs=s_[:], start=True, stop=True),
                             reads=[Bh, B_s], writes=[B_p2])
                        r1, B_r1 = rs.next()
                        S.op("act", lambda: A.activation(out=r1[:], in_=ps2[:], func=AF.Ln, bias=epsc[:, 0:1],
                                                         scale=1.0 / 64.0), reads=[B_p2, B_c], writes=[B_r1])
                        S.op("act", lambda: A.activation(out=r1[:], in_=r1[:], func=AF.Exp, scale=-0.5),
                             reads=[B_r1], writes=[B_r1])
                        q_, B_q = qn.next()
                        S.op("dve", lambda: V.scalar_tensor_tensor(out=q_[:], in0=ps[:], scalar=wcol, in1=r1[:],
                                                                   op0=ALU.mult, op1=ALU.mult),
                             reads=[B_ps, Bh, B_r1], writes=[B_q])
                        ust[u]["q"] = (q_, B_q)

                def stage3(u):
                    ji, tt = units[u]
                    kind, i, col = jobs[ji]
                    sl = slice(tt * 512, (tt + 1) * 512)
                    O, B_o = js[ji]["O"]
                    if kind == "xbc":
                        Y, B_y = ust[u]["Y"]
                        S.op("act", lambda: A.activation(out=O[:, sl], in_=Y[:], func=AF.Silu),
                             reads=[B_y], writes=[B_o])
                    if kind != "xbc":
                        q_, B_q = ust[u]["q"]
                        ps3, B_p3 = pr2.next()
                        S.op("pe", lambda: P.matmul(ps3[:], lhsT=prot[:], rhs=q_[:], start=True, stop=True),
                             reads=[Bh, B_q], writes=[B_p3])
                        t1, B_t1 = t1r.next()
                        S.op("pool", lambda: G.tensor_tensor(out=t1[:], in0=q_[:], in1=cosT[:, sl], op=ALU.mult),
                             reads=[B_q, Bh], writes=[B_t1])
                        t2, B_t2 = t2r.next()
                        S.op("dve", lambda: V.tensor_tensor(out=t2[:], in0=ps3[:], in1=sinT[:, sl], op=ALU.mult),
                             reads=[B_p3, Bh], writes=[B_t2])
                        S.op("dve", lambda: V.tensor_tensor(out=O[:, sl], in0=t1[:], in1=t2[:], op=ALU.add),
                             reads=[B_t1, B_t2], writes=[B_o])
                    if tt == NT - 1:
                        dst = {"xbc": XBC_S, "q": QT_S, "k": KT_S}[kind]
                        S.dma("sp", dst[i], O[:], reads=[B_o])
                    del ust[u]

                NU = len(units)
                for st_ in range(NU + 2):
                    if st_ < NU:
                        stage1(st_)
                    if 0 <= st_ - 1 < NU:
                        stage2(st_ - 1)
                    if 0 <= st_ - 2 < NU:
                        stage3(st_ - 2)
                S.barrier()

            cst.close()
            if hstop == 2:
                hst.close()
                return True
            with ExitStack() as ph:
                wz = sbt(ph, "wz", [128, 8, 1024], BF16)
                wv = sbt(ph, "wv", [128, 8, 1024], BF16)
                wd = sbt(ph, "wd", [128, 8, 16], BF16)
                B_w = Buf()
                wsrc = w2d.rearrange("(k p) n -> p k n", p=128)
                for k0 in (0, 4):
                    S.dma("pool", wz[:, k0:k0 + 4, :], wsrc[:, k0:k0 + 4, C_Z:C_Z + 1024], writes=[B_w])
                    S.dma("pool", wv[:, k0:k0 + 4, :], wsrc[:, k0:k0 + 4, C_V:C_V + 1024], writes=[B_w])
                S.dma("pool", wd[:], wsrc[:, :, C_DT:C_DT + 16], writes=[B_w])
                pr = PRing(nc, ph, "tz_ps", 6, [128, 512])
                pdt = pst(ph, "tz_pdt", [128, 32, 16]); B_pdt = Buf(excl=True)
                zr = Ring(nc, ph, "tz_z", 3, [128, 1024], BF16)
                vr = Ring(nc, ph, "tz_v", 3, [128, 1024], BF16)
                for t in range(32):
                    pz = [pr.next(), pr.next()]
                    pv = [pr.next(), pr.next()]
                    for k in range(8):
                        lhs = hnT[:, k, t * 128:(t + 1) * 128]
                        for hf in range(2):
                            S.op("pe", lambda k=k, hf=hf, lhs=lhs: P.matmul(pz[hf][0][:], lhsT=lhs, rhs=wz[:, k, hf * 512:(hf + 1) * 512],
                                                                          start=(k == 0), stop=(k == 7)),
                                 reads=[B_hn[t], B_w], writes=[pz[hf][1]], sig=(k == 7))
                            S.op("pe", lambda k=k, hf=hf, lhs=lhs: P.matmul(pv[hf][0][:], lhsT=lhs, rhs=wv[:, k, hf * 512:(hf + 1) * 512],
                                                                          start=(k == 0), stop=(k == 7)),
                                 reads=[B_hn[t], B_w], writes=[pv[hf][1]], sig=(k == 7))
                        S.op("pe", lambda k=k, lhs=lhs: P.matmul(pdt[:, t, :], lhsT=lhs, rhs=wd[:, k, :],
                                                                 start=(k == 0), stop=(k == 7)),
                             reads=[B_hn[t], B_w], writes=[B_pdt], sig=(k == 7))
                    Z, B_z = zr.next()
                    Vt, B_v = vr.next()
                    for hf in range(2):
                        S.op("act", lambda hf=hf: A.activation(out=Z[:, hf * 512:(hf + 1) * 512], in_=pz[hf][0][:], func=AF.Silu),
                             reads=[pz[hf][1]], writes=[B_z])
                        S.op("dve", lambda hf=hf: V.tensor_copy(out=Vt[:, hf * 512:(hf + 1) * 512], in_=pv[hf][0][:]),
                             reads=[pv[hf][1]], writes=[B_v])
                    S.dma("sp", ZS_S[t], Z[:], reads=[B_z])
                    S.dma("sp", V_S[t], Vt[:], reads=[B_v])
                xb_ = sbt(ph, "dt_x", [128, 32, 16])
                ab_ = sbt(ph, "dt_a", [128, 32, 16])
                S.op("dve", lambda: V.tensor_tensor(out=xb_[:], in0=pdt[:], in1=bc(bc16[:, 0, :].unsqueeze(1), [128, 32, 16]),
                                                    op=ALU.add), reads=[B_pdt, Bh], writes=[B_dt])
                S.op("act", lambda: A.activation(out=ab_[:], in_=xb_[:], func=AF.Abs), reads=[B_dt], writes=[B_dt])
                S.op("act", lambda: A.activation(out=ab_[:], in_=ab_[:], func=AF.Exp, scale=-1.0), reads=[B_dt], writes=[B_dt])
                S.op("dve", lambda: V.tensor_scalar(out=ab_[:], in0=ab_[:], scalar1=1.0, scalar2=None, op0=ALU.add),
                     reads=[B_dt], writes=[B_dt])
                S.op("act", lambda: A.activation(out=ab_[:], in_=ab_[:], func=AF.Ln), reads=[B_dt], writes=[B_dt])
                S.op("dve", lambda: V.scalar_tensor_tensor(out=dt_tm[:], in0=xb_[:], scalar=0.0, in1=ab_[:], op0=ALU.max,
                                                           op1=ALU.add), reads=[B_dt], writes=[B_dt])
                S.op("dve", lambda: V.tensor_tensor(out=a_tm[:], in0=dt_tm[:], in1=bc(bc16[:, 1, :].unsqueeze(1), [128, 32, 16]),
                                                    op=ALU.mult), reads=[B_dt, Bh], writes=[B_dt])
                pa, B_pa = pr.next()
                pl, B_pl = pr.next()
                S.op("pe", lambda: P.matmul(pa[:], lhsT=triT_f[:], rhs=a_tm[:].rearrange("p c h -> p (c h)"),
                                            start=True, stop=True), reads=[B_c, B_dt], writes=[B_pa])
                S.op("pe", lambda: P.matmul(pl[:], lhsT=ones_f[:], rhs=a_tm[:].rearrange("p c h -> p (c h)"),
                                            start=True, stop=True), reads=[B_c, B_dt], writes=[B_pl])
                fl = lambda t_: t_[:].rearrange("p c h -> p (c h)")
                S.op("dve", lambda: V.tensor_copy(out=fl(acs), in_=pa[:]), reads=[B_pa], writes=[B_dt])
                S.op("act", lambda: A.activation(out=fl(eA), in_=pa[:], func=AF.Exp), reads=[B_pa], writes=[B_dt])
                S.op("act", lambda: A.activation(out=fl(cdr), in_=pl[:], func=AF.Exp), reads=[B_pl], writes=[B_dt])
                S.op("dve", lambda: V.tensor_tensor(out=fl(dA), in0=pl[:], in1=fl(acs), op=ALU.subtract),
                     reads=[B_pl, B_dt], writes=[B_dt])
                S.op("act", lambda: A.activation(out=fl(dA), in_=fl(dA), func=AF.Exp), reads=[B_dt], writes=[B_dt])
                S.barrier()
            if hstop == 3:
                hst.close()
                return True

            with ExitStack() as ph:
                xin = Ring(nc, ph, "sd_x", 3, [128, 12, 512], BF16)
                zin = Ring(nc, ph, "sd_z", 3, [128, 1024], BF16)
                ptr = PRing(nc, ph, "sd_ptr", 2, [128, 1024], BF16)
                pcb = pst(ph, "sd_pcb", [128, 4, 128]); B_pcb = Buf(excl=True)
                pR = PRing(nc, ph, "sd_pR", 1, [128, 8, 128])
                pya = pst(ph, "sd_pya", [128, 512]); B_pya = Buf(excl=True)
                pyb = pst(ph, "sd_pyb", [128, 512]); B_pyb = Buf(excl=True)
                pst_ = pst(ph, "sd_pst", [128, 512]); B_pst = Buf(excl=True)
                xs_tm = Ring(nc, ph, "sd_xs", 2, [128, 1024], BF16)
                b_tm = Ring(nc, ph, "sd_b", 2, [128, 256], BF16)
                Xr = Ring(nc, ph, "sd_X", 2, [128, 1024], BF16)
                Xdr = Ring(nc, ph, "sd_Xd", 2, [128, 1024], BF16)
                cbm = Ring(nc, ph, "sd_cbm", 2, [128, 2, 128], BF16)
                rhsR = Ring(nc, ph, "sd_rr", 1, [128, 8, 128], F32)
                segr = Ring(nc, ph, "sd_seg", 1, [128, 8, 128], F32)
                Er = Ring(nc, ph, "sd_E", 2, [128, 8, 128], BF16)
                Wr = Ring(nc, ph, "sd_W", 2, [128, 8, 128], BF16)
                yr = Ring(nc, ph, "sd_y", 2, [128, 512], F32)
                y2r = Ring(nc, ph, "sd_y2", 1, [128, 512], F32)
                ynr = Ring(nc, ph, "sd_yn", 2, [128, 512], BF16)
                smr = Ring(nc, ph, "sd_sm", 3, [128, 2], F32)
                junk = sbt(ph, "sd_junk", [128, 512], BF16); B_junk = Buf()
                prev = sbt(ph, "sd_prev", [128, 1024]); B_prev = Buf()
                prevb = sbt(ph, "sd_prevb", [128, 1024], BF16); B_prevb = Buf()
                yT = Ring(nc, ph, "sd_yT", 2, [128, 8, 512], BF16)
                S.op("pool", lambda: G.memset(prev[:], 0.0), writes=[B_prev])
                S.op("pool", lambda: G.memset(prevb[:], 0.0), writes=[B_prevb])
                xsrc = XBC_S.rearrange("c p t -> p c t")
                chs = {}
                cur = {}

                def load_x(c):
                    Xin, B_xin = xin.next()
                    S.dma("sp", Xin[:, 0:6, :], xsrc[:, 0:6, c * 128:c * 128 + 512], writes=[B_xin])
                    S.dma("sp", Xin[:, 6:12, :], xsrc[:, 6:12, c * 128:c * 128 + 512], writes=[B_xin])
                    return Xin, B_xin

                def load_z(c):
                    Zt, B_z = zin.next()
                    S.dma("sp", Zt[:], ZS_S[c], writes=[B_z])
                    return Zt, B_z

                def prep(c):
                    cc = c % 4
                    if cc == 0:
                        if c == 0:
                            cur["Xn"] = load_x(0)
                        cur["Xin"] = cur["Xn"]
                        if c + 4 < 32:
                            cur["Xn"] = load_x(c + 4)
                        cur["YT"] = yT.next()
                    Xin, B_xin = cur["Xin"]
                    csl = slice(cc * 128, (cc + 1) * 128)
                    if c == 0:
                        cur["Zn"] = load_z(0)
                    Zt, B_z = cur["Zn"]
                    if c + 1 < 32:
                        cur["Zn"] = load_z(c + 1)
                    pt, B_pt = ptr.next()
                    for j in range(8):
                        S.op("pe", lambda j=j: P.transpose(out=pt[:, j * 128:(j + 1) * 128], in_=Xin[:, j, csl], identity=identb[:]),
                             reads=[B_xin, B_c], writes=[B_pt], sig=(j == 7))
                    xs, B_xs = xs_tm.next()
                    S.op("act", lambda: A.copy(out=xs[:], in_=pt[:]), reads=[B_pt], writes=[B_xs])
                    pt2, B_pt2 = ptr.next()
                    for g in range(2):
                        S.op("pe", lambda g=g: P.transpose(out=pt2[:, g * 128:(g + 1) * 128], in_=Xin[:, 8 + g, csl], identity=identb[:]),
                             reads=[B_xin, B_c], writes=[B_pt2], sig=(g == 1))
                    bt, B_bt = b_tm.next()
                    S.op("act", lambda: A.copy(out=bt[:], in_=pt2[:, 0:256]), reads=[B_pt2], writes=[B_bt])
                    X, B_X = Xr.next()
                    S.op("dve", lambda: V.tensor_tensor(out=X[:].rearrange("p (h d) -> p h d", h=16),
                                                        in0=xs[:].rearrange("p (h d) -> p h d", h=16),
                                                        in1=bc(dt_tm[:, c, :].unsqueeze(2), [128, 16, 64]), op=ALU.mult),
                         reads=[B_xs, B_dt], writes=[B_X])
                    Xd, B_Xd = Xdr.next()
                    S.op("pool", lambda: G.tensor_tensor(out=Xd[:].rearrange("p (h d) -> p h d", h=16),
                                                         in0=X[:].rearrange("p (h d) -> p h d", h=16),
                                                         in1=bc(dA[:, c, :].unsqueeze(2), [128, 16, 64]), op=ALU.mult),
                         reads=[B_X, B_dt], writes=[B_Xd])
                    for g in range(2):
                        S.op("pe", lambda g=g: P.matmul(pcb[:, g, :], lhsT=Xin[:, 8 + g, csl], rhs=Xin[:, 10 + g, csl],
                                                        start=True, stop=True), reads=[B_xin], writes=[B_pcb], sig=(g == 1))
                    cb, B_cb = cbm.next()
                    S.op("dve", lambda: V.tensor_tensor(out=cb[:], in0=pcb[:, 0:2, :], in1=bc(triT_f[:].unsqueeze(1), [128, 2, 128]),
                                                        op=ALU.mult), reads=[B_pcb, B_c], writes=[B_cb])
                    chs[c] = dict(Xin=(Xin, B_xin), csl=csl, Z=(Zt, B_z), xs=(xs, B_xs), bt=(bt, B_bt), X=(X, B_X),
                                  Xd=(Xd, B_Xd), cb=(cb, B_cb), YT=cur["YT"], W={})

                def stage1a_pool(c, g):
                    hs = slice(g * 8, (g + 1) * 8)
                    rr_, B_rr = rhsR.next()
                    S.op("pool", lambda: G.tensor_tensor(out=rr_[:], in0=bc(triT_f[:].unsqueeze(1), [128, 8, 128]),
                                                         in1=bc(a_tm[:, c, hs].unsqueeze(2), [128, 8, 128]), op=ALU.mult),
                         reads=[B_c, B_dt], writes=[B_rr])
                    chs[c]["rr", g] = (rr_, B_rr)

                def stage1a_pe(c, g):
                    rr_, B_rr = chs[c]["rr", g]
                    pr_, B_pr = pR.next()
                    for q4 in range(2):
                        S.op("pe", lambda q4=q4: P.matmul(pr_[:, q4 * 4:(q4 + 1) * 4, :], lhsT=ones_f[:],
                                                          rhs=rr_[:, q4 * 4:(q4 + 1) * 4, :], start=True, stop=True),
                             reads=[B_c, B_rr], writes=[B_pr], sig=(q4 == 1))
                    chs[c]["pr", g] = (pr_, B_pr)

                def stage1b(c, g):
                    d = chs[c]
                    cb, B_cb = d["cb"]
                    hs = slice(g * 8, (g + 1) * 8)
                    pr_, B_pr = d["pr", g]
                    if False:
                        rr_, B_rr = rhsR.next()
                    sg, B_sg = segr.next()
                    S.op("dve", lambda: V.tensor_tensor(out=sg[:], in0=pr_[:], in1=bc(acs[:, c, hs].unsqueeze(2), [128, 8, 128]),
                                                        op=ALU.subtract), reads=[B_pr, B_dt], writes=[B_sg])
                    S.op("dve", lambda: V.tensor_single_scalar(out=sg[:], in_=sg[:], scalar=0.0, op=ALU.min),
                         reads=[B_sg], writes=[B_sg])
                    E, B_E = Er.next()
                    S.op("act", lambda: A.activation(out=E[:], in_=sg[:], func=AF.Exp), reads=[B_sg], writes=[B_E])
                    W, B_W = Wr.next()
                    S.op("dve", lambda: V.tensor_tensor(out=W[:], in0=E[:], in1=bc(cb[:, g, :].unsqueeze(1), [128, 8, 128]),
                                                        op=ALU.mult), reads=[B_E, B_cb], writes=[B_W])
                    d["W"][g] = (W, B_W)

                def stage2a(c, g):
                    d = chs[c]
                    Xin, B_xin = d["Xin"]; csl = d["csl"]; Zt, B_z = d["Z"]; xs, B_xs = d["xs"]; bt, B_bt = d["bt"]
                    X, B_X = d["X"]; Xd, B_Xd = d["Xd"]; YT, B_yT = d["YT"]; W, B_W = d["W"][g]
                    cc = c % 4
                    hs = slice(g * 8, (g + 1) * 8)
                    fs = slice(g * 512, (g + 1) * 512)
                    for h in range(8):
                        hh = g * 8 + h
                        S.op("pe", lambda h=h, hh=hh: P.matmul(pya[:, h * 64:(h + 1) * 64], lhsT=W[:, h, :],
                                                               rhs=X[:, hh * 64:(hh + 1) * 64], start=True, stop=True),
                             reads=[B_W, B_X], writes=[B_pya], sig=(h == 7))
                    S.op("pe", lambda: P.matmul(pyb[:], lhsT=Xin[:, 10 + g, csl], rhs=prevb[:, fs], start=True, stop=True),
                         reads=[B_xin, B_prevb], writes=[B_pyb])
                def stage2b(c, g):
                    d = chs[c]
                    Zt, B_z = d["Z"]; xs, B_xs = d["xs"]
                    hs = slice(g * 8, (g + 1) * 8)
                    fs = slice(g * 512, (g + 1) * 512)
                    y, B_y = yr.next()
                    S.op("dve", lambda: V.tensor_tensor(out=y[:].rearrange("p (h d) -> p h d", h=8),
                                                        in0=pyb[:].rearrange("p (h d) -> p h d", h=8),
                                                        in1=bc(eA[:, c, hs].unsqueeze(2), [128, 8, 64]), op=ALU.mult),
                         reads=[B_pyb, B_dt], writes=[B_y])
                    S.op("dve", lambda: V.tensor_tensor(out=y[:], in0=y[:], in1=pya[:], op=ALU.add),
                         reads=[B_y, B_pya], writes=[B_y])
                    y2, B_y2 = y2r.next()
                    S.op("pool", lambda: G.tensor_tensor(out=y2[:].rearrange("p (h d) -> p h d", h=8),
                                                         in0=xs[:, fs].rearrange("p (h d) -> p h d", h=8),
                                                         in1=bc(bc16[:, 2, hs].unsqueeze(2), [128, 8, 64]), op=ALU.mult),
                         reads=[B_xs, Bh], writes=[B_y2])
                    S.op("dve", lambda: V.tensor_tensor(out=y[:], in0=y[:], in1=y2[:], op=ALU.add),
                         reads=[B_y, B_y2], writes=[B_y])
                    S.op("dve", lambda: V.tensor_tensor(out=y[:], in0=y[:], in1=Zt[:, fs], op=ALU.mult),
                         reads=[B_y, B_z], writes=[B_y])
                    sm, B_sm = smr.next()
                    S.op("act", lambda: A.activation(out=junk[:], in_=y[:], func=AF.Square, accum_out=sm[:, 0:1]),
                         reads=[B_y], writes=[B_junk, B_sm])
                    rstd(sm[:, 1:2], sm[:, 0:1], 1.0 / 512.0, B_sm)
                    yn, B_yn = ynr.next()
                    S.op("dve", lambda: V.scalar_tensor_tensor(out=yn[:], in0=y[:], scalar=sm[:, 1:2], in1=ssdw[:, fs],
                                                               op0=ALU.mult, op1=ALU.mult),
                         reads=[B_y, B_sm, Bh], writes=[B_yn])
                    d["yn", g] = (yn, B_yn)

                def stage2c(c, g):
                    d = chs[c]
                    csl = d["csl"]; bt, B_bt = d["bt"]; Xd, B_Xd = d["Xd"]; YT, B_yT = d["YT"]
                    yn, B_yn = d["yn", g]
                    cc = c % 4
                    hs = slice(g * 8, (g + 1) * 8)
                    fs = slice(g * 512, (g + 1) * 512)
                    pto, B_pto = ptr.next()
                    for j in range(4):
                        S.op("pe", lambda j=j: P.transpose(out=pto[:, j * 128:(j + 1) * 128], in_=yn[:, j * 128:(j + 1) * 128],
                                                           identity=identb[:]), reads=[B_yn, B_c], writes=[B_pto], sig=(j == 3))
                    S.op("act", lambda: A.copy(out=YT[:, g * 4:(g + 1) * 4, csl],
                                               in_=pto[:, 0:512].rearrange("p (j t) -> p j t", j=4)),
                         reads=[B_pto], writes=[B_yT])
                    S.op("pe", lambda: P.matmul(pst_[:], lhsT=bt[:, g * 128:(g + 1) * 128], rhs=Xd[:, fs], start=True, stop=True),
                         reads=[B_bt, B_Xd], writes=[B_pst])
                    S.op("dve", lambda: V.tensor_tensor(out=prev[:, fs].rearrange("p (h d) -> p h d", h=8),
                                                        in0=prev[:, fs].rearrange("p (h d) -> p h d", h=8),
                                                        in1=bc(cdr[:, c, hs].unsqueeze(2), [128, 8, 64]), op=ALU.mult),
                         reads=[B_prev, B_dt], writes=[B_prev])
                    S.op("dve", lambda: V.tensor_tensor(out=prev[:, fs], in0=prev[:, fs], in1=pst_[:], op=ALU.add),
                         reads=[B_prev, B_pst], writes=[B_prev])
                    S.op("act", lambda: A.copy(out=prevb[:, fs], in_=prev[:, fs]), reads=[B_prev], writes=[B_prevb])
                    if cc == 3 and g == 1:
                        c0 = (c - 3) * 128
                        S.dma("sp", ACT_S.rearrange("c p t -> p c t")[:, 0:8, c0:c0 + 512], YT[:], reads=[B_yT])
                    if g == 1:
                        del chs[c]

                sunits = [(c, g) for c in range(32) for g in range(2)]
                NSU = len(sunits)
                for st_ in range(NSU + 1):
                    cur_u = sunits[st_] if st_ < NSU else None
                    prv_u = sunits[st_ - 1] if st_ >= 1 else None
                    if cur_u is not None:
                        if cur_u[1] == 0:
                            prep(cur_u[0])
                        stage1a_pool(*cur_u)
                    if prv_u is not None:
                        stage2a(*prv_u)
                    if cur_u is not None:
                        stage1a_pe(*cur_u)
                    if prv_u is not None:
                        stage2b(*prv_u)
                    if cur_u is not None:
                        stage1b(*cur_u)
                    if prv_u is not None:
                        stage2c(*prv_u)
                S.barrier()
            if hstop == 4:
                hst.close()
                return True

            wst = ExitStack()
            open_wres(wst)
            load_Wres(hy_w_out[0], 16)
            with ExitStack() as ph:
                kq = Ring(nc, ph, "at_kq", 2, [128, 2, L], BF16)
                vr = Ring(nc, ph, "at_v", 2, [128, 32, 130], BF16)
                pS = PRing(nc, ph, "at_pS", 2, [128, 2, 512])
                pO = PRing(nc, ph, "at_pO", 3, [128, 2, 256])
                pT = PRing(nc, ph, "at_pT", 1, [128, 1024], BF16)
                Er = Ring(nc, ph, "at_E", 3, [128, 2, 512], BF16)
                smr = Ring(nc, ph, "at_sm", 6, [128, 4], F32)
                tr_ = Ring(nc, ph, "at_t", 4, [128, 128], F32)
                or_ = Ring(nc, ph, "at_o", 6, [128, 128], F32)
                ybr = Ring(nc, ph, "at_yb", 6, [128, 128], BF16)
                junk = sbt(ph, "at_junk", [128, 128], BF16); B_junk = Buf()
                yT = Ring(nc, ph, "at_yT", 2, [128, L], BF16)
                for t_, b_ in zip(vr.t, vr.b):
                    S.op("pool", lambda t_=t_: G.memset(t_[:, :, 128:130], 1.0), writes=[b_])
                for t_, b_ in zip(Er.t, Er.b):
                    S.op("pool", lambda t_=t_: G.memset(t_[:], 0.0), writes=[b_])
                vsrc = V_S.rearrange("t p f -> p t f")
                def load_head(hd):
                    KQ, B_kq = kq.next()
                    S.dma("sp", KQ[:, 0, :], KT_S[hd], writes=[B_kq])
                    S.dma("sp", KQ[:, 1, :], QT_S[hd], writes=[B_kq])
                    Vt, B_v = vr.next()
                    for v0 in range(0, 32, 8):
                        S.dma("sp", Vt[:, v0:v0 + 8, 0:128], vsrc[:, v0:v0 + 8, hd * 128:(hd + 1) * 128], writes=[B_v])
                    return KQ, B_kq, Vt, B_v
                head_next = load_head(0)
                for hd in range(8):
                    KQ, B_kq, Vt, B_v = head_next
                    if hd + 1 < 8:
                        head_next = load_head(hd + 1)
                    YT, B_yT = yT.next()
                    items = [(qt, i) for qt in range(16) for i in range(qt + 1)]

                    def stage_qk(qt, i):
                        q0 = qt * 256
                        diag = (i == qt)
                        ps, B_ps = pS.next()
                        for half in range(2):
                            kb = 2 * i + half
                            c0 = 128 if (diag and half == 1) else 0
                            for m in range(2):
                                S.op("pe", lambda m=m, half=half, kb=kb, c0=c0: P.matmul(
                                    ps[:, m, half * 256 + c0:half * 256 + 256],
                                    lhsT=KQ[64 * m:64 * m + 64, 0, kb * 128:(kb + 1) * 128],
                                    rhs=KQ[64 * m:64 * m + 64, 1, q0 + c0:q0 + 256], start=True, stop=True),
                                    reads=[B_kq], writes=[B_ps], sig=(half == 1 and m == 1))
                        return ps, B_ps

                    def stage_exp(qt, i, ps, B_ps):
                        diag = (i == qt)
                        E, B_E = Er.next()
                        if diag:
                            S.op("act", lambda: A.activation(out=E[:, :, 0:256], in_=ps[:, :, 0:256], func=AF.Exp),
                                 reads=[B_ps], writes=[B_E])
                            S.op("act", lambda: A.activation(out=E[:, :, 384:512], in_=ps[:, :, 384:512], func=AF.Exp),
                                 reads=[B_ps], writes=[B_E])
                            Ev_ = E[:].rearrange("p m (r c) -> p m r c", c=128)
                            S.op("dve", lambda: V.tensor_tensor(out=Ev_[:, :, 0:4:3, :], in0=Ev_[:, :, 0:4:3, :],
                                                                in1=bc(triT_b[:].unsqueeze(1).unsqueeze(1), [128, 2, 2, 128]),
                                                                op=ALU.mult), reads=[B_E, B_c], writes=[B_E])
                        else:
                            S.op("act", lambda: A.activation(out=E[:], in_=ps[:], func=AF.Exp), reads=[B_ps], writes=[B_E])
                        return E, B_E

                    def stage_pv(qt, i, E, B_E, pOs):
                        diag = (i == qt)
                        for half in range(2):
                            kb = 2 * i + half
                            for qs in range(2):
                                if diag and half == 1 and qs == 0:
                                    continue
                                po, B_po = pOs[qs]
                                first = (i == 0 and half == 0)
                                last = diag and (half == qs)
                                for m in range(2):
                                    S.op("pe", lambda m=m, qs=qs, po=po, kb=kb, half=half, first=first, last=last: P.matmul(
                                        po[:, m, 0:129], lhsT=E[:, m, half * 256 + qs * 128:half * 256 + (qs + 1) * 128],
                                        rhs=Vt[:, kb, 0:129], start=(first and m == 0), stop=(last and m == 1),
                                        skip_group_check=True),
                                        reads=[B_E, B_v], writes=[B_po], sig=(last and m == 1))

                    def epi1(qt, pOs):
                        outs = []
                        for qs in range(2):
                            po, B_po = pOs[qs]
                            sm, B_sm = smr.next()
                            S.op("dve", lambda: V.reciprocal(out=sm[:, 0:2], in_=po[:, :, 128]), reads=[B_po], writes=[B_sm])
                            t_, B_t = tr_.next()
                            S.op("dve", lambda: V.tensor_scalar(out=t_[:], in0=po[:, 1, 0:128], scalar1=sm[:, 1:2], scalar2=nlam[:, 0:1],
                                                                op0=ALU.mult, op1=ALU.mult), reads=[B_po, B_sm, Bh], writes=[B_t])
                            o_, B_o = or_.next()
                            S.op("dve", lambda: V.scalar_tensor_tensor(out=o_[:], in0=po[:, 0, 0:128], scalar=sm[:, 0:1], in1=t_[:],
                                                                       op0=ALU.mult, op1=ALU.add), reads=[B_po, B_sm, B_t], writes=[B_o])
                            outs.append((o_, B_o, sm, B_sm))
                        return outs

                    def epi2(qt, outs):
                        ybs = []
                        for qs in range(2):
                            o_, B_o, sm, B_sm = outs[qs]
                            S.op("act", lambda: A.activation(out=junk[:], in_=o_[:], func=AF.Square, accum_out=sm[:, 2:3]),
                                 reads=[B_o], writes=[B_junk, B_sm])
                            rstd(sm[:, 3:4], sm[:, 2:3], 1.0 / 128.0, B_sm)
                            yb, B_yb = ybr.next()
                            S.op("dve", lambda: V.scalar_tensor_tensor(out=yb[:], in0=o_[:], scalar=sm[:, 3:4], in1=sublw[:],
                                                                       op0=ALU.mult, op1=ALU.mult), reads=[B_o, B_sm, Bh], writes=[B_yb])
                            ybs.append((yb, B_yb))
                        return ybs

                    def epi3(qt, ybs):
                        q0 = qt * 256
                        for qs in range(2):
                            yb, B_yb = ybs[qs]
                            pt, B_pt = pT.next()
                            S.op("pe", lambda: P.transpose(out=pt[:, 0:128], in_=yb[:], identity=identb[:]), reads=[B_yb, B_c], writes=[B_pt])
                            S.op("dve", lambda: V.tensor_copy(out=YT[:, q0 + qs * 128:q0 + (qs + 1) * 128], in_=pt[:, 0:128]),
                                 reads=[B_pt], writes=[B_yT])

                    nxt_qk = stage_qk(*items[0])
                    pOs = None
                    pend2 = []
                    pend3 = []
                    for n, (qt, i) in enumerate(items):
                        ps, B_ps = nxt_qk
                        if n + 1 < len(items):
                            nxt_qk = stage_qk(*items[n + 1])
                        E, B_E = stage_exp(qt, i, ps, B_ps)
                        new3 = [(q_, epi2(q_, o_)) for (q_, o_) in pend2]
                        pend2 = []
                        if i == 0:
                            pOs = [pO.next(), pO.next()]
                        stage_pv(qt, i, E, B_E, pOs)
                        for (q_, y_) in pend3:
                            epi3(q_, y_)
                        pend3 = new3
                        if i == qt:
                            pend2.append((qt, epi1(qt, pOs)))
                    for (q_, o_) in pend2:
                        pend3.append((q_, epi2(q_, o_)))
                    for (q_, y_) in pend3:
                        epi3(q_, y_)
                    S.dma("sp", ACT_S[8 + hd], YT[:], reads=[B_yT])
                S.barrier()
            if hstop == 5:
                wst.close(); hst.close()
                return True
            phase_tm(16, h_in_ap, h_out_ap, widx_next)
            wst.close()
            hst.close()
            return False

        pcount = [0]

        def chk():
            pcount[0] += 1
            return stop is not None and pcount[0] >= stop

        def _main_program():
            h_cur = x
            scr = [hA, hB, hC]
            nxt = 0
            first = True
            for li, layer in enumerate(layers):
                last_layer = (li == len(layers) - 1)
                if first:
                    phase_norm_in(h_cur, 2 * layer)
                    if chk():
                        return
                    first = False
                h_mid = scr[nxt]; nxt += 1
                if layer == 0:
                    if phase_hybrid(h_cur, h_mid, 2 * layer + 1):
                        return
                    if chk():
                        return
                else:
                    with ExitStack() as ws:
                        open_wres(ws)
                        phase_shortconv(sc_w_out[0], 8)
                        if chk():
                            return
                        phase_tm(8, h_cur, h_mid, 2 * layer + 1)
                        if chk():
                            return
                with ExitStack() as ws:
                    open_wres(ws)
                    phase_ffn_up(layer, ffn_w_down[layer], 22)
                    if chk():
                        return
                    if last_layer:
                        phase_tm(22, h_mid, out, None)
                    else:
                        h_new = scr[nxt]; nxt += 1
                        phase_tm(22, h_mid, h_new, 2 * layers[li + 1])
                        h_cur = h_new
                    if chk():
                        return

        try:
            _main_program()
        except _Stop:
            pass
        S.barrier(engines=("sp",))
        build_program.stats = (S.n_ins, S.n_wait)
    return nc


_INPUT_NAMES = ["mix_norm", "ffn_norm", "hy_w_in", "hy_conv_w", "hy_conv_b", "hy_dt_bias", "hy_a_log",
                "hy_d_skip", "hy_ssd_norm", "hy_q_norm", "hy_k_norm", "hy_lambda_q1", "hy_lambda_k1",
                "hy_lambda_q2", "hy_lambda_k2", "hy_subln", "hy_w_out", "sc_w_in", "sc_conv_w", "sc_w_out",
                "ffn_w_up", "ffn_conv_w", "ffn_conv_b", "ffn_w_down"]


def kernel(**inputs):
    x = np.asarray(inputs["x"], dtype=np.float32)
    nc = build_program()
    shared = {k: np.ascontiguousarray(np.asarray(inputs[k], dtype=np.float32)) for k in _INPUT_NAMES}
    in_maps = []
    for b in range(8):
        m = dict(shared)
        m["x"] = np.ascontiguousarray(x[b])
        in_maps.append(m)
    res = run_bass_kernel_spmd(nc, in_maps, core_ids=list(range(8)))
    return np.stack([np.asarray(r["out"], dtype=np.float32) for r in res.results], axis=0)
```
